# Optimizing a Trainium2 kernel written in Bass

```python
import jax, jax.numpy as jnp
from jax import lax
import numpy as np

D_MODEL = 4096
BATCH = 2
SEQ = 4096
DEPTH = 2

HEAD_DIM = 128
ROT_DIM = HEAD_DIM // 4
ROPE_THETA = 500000.0
N_HEADS_OUT = D_MODEL // HEAD_DIM
N_HEADS_B_GROUP = N_HEADS_OUT // 4
N_HEADS_A = N_HEADS_OUT - N_HEADS_B_GROUP
B_PATTERNS = ((128, 1), (512, 4), (2048, 16))
N_HEADS_B = N_HEADS_B_GROUP * len(B_PATTERNS)
QKV_COLS = 3 * (N_HEADS_A + N_HEADS_B) * HEAD_DIM
MOBA_BLOCK = 256
MOBA_TOPK = 3
MOBA_QCHUNK = 16
DWIN_BLOCK = 128
GMLP_CHUNK = 128
GMLP_WIDTH = D_MODEL
GMLP_GROUP_DIM = 128
GMLP_GROUPS = GMLP_WIDTH // GMLP_GROUP_DIM
D_FF = 7 * D_MODEL // 2
CONV_WIDTH = 3
NORM_EPS = 1e-5
N_EVEN = (DEPTH + 1) // 2
N_ODD = DEPTH // 2

kernel_name = "hybrid_moba_dilated_gmlp_convffn"

F32 = jnp.float32


def rms_norm(x, g):
    xf = x.astype(F32)
    y = xf * lax.rsqrt(jnp.mean(xf * xf, axis=-1, keepdims=True) + NORM_EPS)
    return (y * g.astype(F32)).astype(x.dtype)


def layer_norm(x, g, b):
    xf = x.astype(F32)
    mu = jnp.mean(xf, axis=-1, keepdims=True)
    xc = xf - mu
    y = xc * lax.rsqrt(jnp.mean(xc * xc, axis=-1, keepdims=True) + NORM_EPS)
    return (y * g.astype(F32) + b.astype(F32)).astype(x.dtype)


def partial_rotary(x, positions):
    half = ROT_DIM // 2
    inv_freq = jnp.power(ROPE_THETA, -jnp.arange(half, dtype=F32) * (2.0 / ROT_DIM))
    ang = positions.astype(F32)[:, None, :, None] * inv_freq
    cos, sin = jnp.cos(ang), jnp.sin(ang)
    xr = x[..., :ROT_DIM].astype(F32)
    x1, x2 = xr[..., :half], xr[..., half:]
    rot = jnp.concatenate([x1 * cos - x2 * sin, x2 * cos + x1 * sin], axis=-1).astype(x.dtype)
    return jnp.concatenate([rot, x[..., ROT_DIM:]], axis=-1)


def moba_attention(q, k, v):
    bsz, nh, s, dh = q.shape
    blk, cq = MOBA_BLOCK, MOBA_QCHUNK
    nb = -(-s // blk)
    sp = nb * blk
    pad = ((0, 0), (0, 0), (0, sp - s), (0, 0))
    bh = bsz * nh
    qf = jnp.pad(q, pad).reshape(bh, sp, dh)
    kf = jnp.pad(k, pad).reshape(bh, sp, dh)
    vf = jnp.pad(v, pad).reshape(bh, sp, dh)
    kb = kf.reshape(bh, nb, blk, dh)
    vb = vf.reshape(bh, nb, blk, dh)
    kmean = jnp.mean(kb.astype(F32), axis=2)
    gate = jnp.einsum('zsd,znd->zsn', qf.astype(F32), kmean)
    qblk = jnp.arange(sp) // blk
    past = jnp.arange(nb)[None, :] < qblk[:, None]
    gate = jnp.where(past[None], gate, -jnp.inf)
    kk = min(MOBA_TOPK, nb)
    _, sel = lax.top_k(gate, kk)
    sel_valid = jnp.arange(kk)[None, :] < qblk[:, None]
    nc = sp // cq
    qc = qf.reshape(bh, nc, cq, dh).transpose(1, 0, 2, 3)
    selc = sel.reshape(bh, nc, cq, kk).transpose(1, 0, 2, 3)
    validc = sel_valid.reshape(nc, cq, kk)
    starts = jnp.arange(nc, dtype=jnp.int32) * cq
    scale = dh ** -0.5

    def one_chunk(args):
        qi, si, vi, t0 = args
        gk = jax.vmap(lambda kz, iz: kz[iz])(kb, si)
        gv = jax.vmap(lambda vz, iz: vz[iz])(vb, si)
        s_sel = jnp.einsum('zcd,zcnjd->zcnj', qi, gk, preferred_element_type=F32) * scale
        s_sel = jnp.where(vi[None, :, :, None], s_sel, -jnp.inf).reshape(bh, cq, kk * blk)
        b0 = (t0 // blk) * blk
        ko = lax.dynamic_slice_in_dim(kf, b0, blk, axis=1)
        vo = lax.dynamic_slice_in_dim(vf, b0, blk, axis=1)
        s_own = jnp.einsum('zcd,zjd->zcj', qi, ko, preferred_element_type=F32) * scale
        causal = (b0 + jnp.arange(blk))[None, :] <= (t0 + jnp.arange(cq))[:, None]
        s_own = jnp.where(causal[None], s_own, -jnp.inf)
        p = jax.nn.softmax(jnp.concatenate([s_sel, s_own], axis=-1), axis=-1).astype(vf.dtype)
        out = jnp.einsum('zcm,zcmd->zcd', p[..., :kk * blk], gv.reshape(bh, cq, kk * blk, dh))
        return out + jnp.einsum('zcj,zjd->zcd', p[..., kk * blk:], vo)

    out = lax.map(one_chunk, (qc, selc, validc, starts))
    out = out.transpose(1, 0, 2, 3).reshape(bh, sp, dh)[:, :s]
    return out.reshape(bsz, nh, s, dh)


def dilated_window_attention(q, k, v, window, dilation):
    bsz, nh, s, dh = q.shape
    n_back = window // dilation
    blk = DWIN_BLOCK
    sub = s // dilation
    nblk = -(-sub // blk)
    lp = nblk * blk

    def to_blocks(t):
        t = t.reshape(bsz, nh, sub, dilation, dh).swapaxes(2, 3)
        t = jnp.pad(t, ((0, 0), (0, 0), (0, 0), (0, lp - sub), (0, 0)))
        return t.reshape(bsz, nh, dilation, nblk, blk, dh)

    def band(t):
        prev = jnp.pad(t, ((0, 0), (0, 0), (0, 0), (1, 0), (0, 0), (0, 0)))[:, :, :, :nblk]
        return jnp.concatenate([prev, t], axis=4)

    qb = to_blocks(q)
    kband = band(to_blocks(k))
    vband = band(to_blocks(v))
    scores = jnp.einsum('bhrnid,bhrnjd->bhrnij', qb, kband, preferred_element_type=F32) * (dh ** -0.5)
    qi = jnp.arange(blk)[:, None]
    kj = jnp.arange(2 * blk)[None, :]
    dist = blk + qi - kj
    in_band = (dist >= 0) & (dist <= n_back)
    not_pad = (jnp.arange(nblk)[:, None, None] > 0) | (kj >= blk)[None]
    mask = in_band[None] & not_pad
    scores = jnp.where(mask, scores, -jnp.inf)
    lse = jax.nn.logsumexp(scores, axis=-1)
    p = jnp.exp(scores - lse[..., None]).astype(v.dtype)
    out = jnp.einsum('bhrnij,bhrnjd->bhrnid', p, vband)
    out = out.reshape(bsz, nh, dilation, lp, dh)[:, :, :, :sub].swapaxes(2, 3).reshape(bsz, nh, s, dh)
    lse = lse.reshape(bsz, nh, dilation, lp)[..., :sub].swapaxes(2, 3).reshape(bsz, nh, s)
    return out, lse


def dilated_mixture(q, k, v):
    outs, lses = [], []
    for g, (window, dilation) in enumerate(B_PATTERNS):
        sl = slice(g * N_HEADS_B_GROUP, (g + 1) * N_HEADS_B_GROUP)
        o, l = dilated_window_attention(q[:, sl], k[:, sl], v[:, sl], window, dilation)
        outs.append(o)
        lses.append(l)
    w = jax.nn.softmax(jnp.stack(lses, axis=0), axis=0)
    return jnp.einsum('gbhs,gbhsd->bhsd', w.astype(q.dtype), jnp.stack(outs, axis=0))


def hybrid_attention_block(x, positions, norm_g, w_in, w_out):
    bsz, s, _ = x.shape
    h = rms_norm(x, norm_g)
    proj = (h @ w_in).reshape(bsz, s, 3, N_HEADS_A + N_HEADS_B, HEAD_DIM)
    proj = proj.transpose(2, 0, 3, 1, 4)
    q = partial_rotary(proj[0], positions)
    k = partial_rotary(proj[1], positions)
    v = proj[2]
    oa = moba_attention(q[:, :N_HEADS_A], k[:, :N_HEADS_A], v[:, :N_HEADS_A])
    ob = dilated_mixture(q[:, N_HEADS_A:], k[:, N_HEADS_A:], v[:, N_HEADS_A:])
    o = jnp.concatenate([oa, ob], axis=1).transpose(0, 2, 1, 3).reshape(bsz, s, N_HEADS_OUT * HEAD_DIM)
    return x + o @ w_out


def spatial_gating_block(x, norm_g, w_in, b_in, v_gain, v_bias, w_s, b_s, w_out):
    bsz, s, _ = x.shape
    nc = s // GMLP_CHUNK
    h = rms_norm(x, norm_g)
    z = jax.nn.gelu(h @ w_in + b_in, approximate=False)
    u, v = jnp.split(z, 2, axis=-1)
    v = layer_norm(v, v_gain, v_bias).reshape(bsz, nc, GMLP_CHUNK, GMLP_GROUPS, GMLP_GROUP_DIM)
    w_causal = jnp.tril(w_s)
    f = jnp.einsum('gts,bcsgd->bctgd', w_causal, v) + b_s.T[None, None, :, :, None]
    out = u * f.reshape(bsz, s, GMLP_WIDTH)
    return x + out @ w_out


def conv_ffn_block(x, norm_g, w_up, conv_w, conv_b, w_down):
    s = x.shape[1]
    h = rms_norm(x, norm_g)
    a = h @ w_up
    ap = jnp.pad(a, ((0, 0), (CONV_WIDTH - 1, 0), (0, 0)))
    c = conv_b + ap[:, 0:s] * conv_w[0]
    for j in range(1, CONV_WIDTH):
        c = c + ap[:, j:j + s] * conv_w[j]
    gate, up = jnp.split(c, 2, axis=-1)
    return x + (jax.nn.silu(gate) * up) @ w_down


def setup_inputs(seed: int = 0) -> dict:
    key = jax.random.key(seed)
    ks = jax.random.split(key, 24)

    def nrm(k, shape, scale):
        return jax.random.normal(k, shape, F32) * scale

    d, e, f2 = D_MODEL, GMLP_WIDTH, 2 * D_FF
    return {
        "x": nrm(ks[0], (BATCH, SEQ, d), 1.0),
        "positions": jnp.broadcast_to(jnp.arange(SEQ, dtype=jnp.int32), (BATCH, SEQ)),
        "attn_norm": 1.0 + nrm(ks[1], (N_EVEN, d), 0.02),
        "attn_w_in": nrm(ks[2], (N_EVEN, d, QKV_COLS), d ** -0.5),
        "attn_w_out": nrm(ks[3], (N_EVEN, N_HEADS_OUT * HEAD_DIM, d), (N_HEADS_OUT * HEAD_DIM) ** -0.5),
        "sg_norm": 1.0 + nrm(ks[4], (N_ODD, d), 0.02),
        "sg_w_in": nrm(ks[5], (N_ODD, d, 2 * e), d ** -0.5),
        "sg_b_in": nrm(ks[6], (N_ODD, 2 * e), 0.02),
        "sg_v_gain": 1.0 + nrm(ks[7], (N_ODD, e), 0.02),
        "sg_v_bias": nrm(ks[8], (N_ODD, e), 0.02),
        "sg_w_s": nrm(ks[9], (N_ODD, GMLP_GROUPS, GMLP_CHUNK, GMLP_CHUNK), GMLP_CHUNK ** -0.5),
        "sg_b_s": 1.0 + nrm(ks[10], (N_ODD, GMLP_GROUPS, GMLP_CHUNK), 0.02),
        "sg_w_out": nrm(ks[11], (N_ODD, e, d), e ** -0.5),
        "ffn_norm": 1.0 + nrm(ks[12], (DEPTH, d), 0.02),
        "ffn_w_up": nrm(ks[13], (DEPTH, d, f2), d ** -0.5),
        "ffn_conv_w": nrm(ks[14], (DEPTH, CONV_WIDTH, f2), CONV_WIDTH ** -0.5),
        "ffn_conv_b": nrm(ks[15], (DEPTH, f2), 0.02),
        "ffn_w_down": nrm(ks[16], (DEPTH, D_FF, d), D_FF ** -0.5),
        "final_norm": 1.0 + nrm(ks[17], (d,), 0.02),
    }


def reference(x, positions, attn_norm, attn_w_in, attn_w_out, sg_norm, sg_w_in, sg_b_in,
              sg_v_gain, sg_v_bias, sg_w_s, sg_b_s, sg_w_out, ffn_norm, ffn_w_up,
              ffn_conv_w, ffn_conv_b, ffn_w_down, final_norm):
    h = x
    for layer in range(DEPTH):
        i = layer // 2
        if layer % 2 == 0:
            h = hybrid_attention_block(h, positions, attn_norm[i], attn_w_in[i], attn_w_out[i])
        else:
            h = spatial_gating_block(h, sg_norm[i], sg_w_in[i], sg_b_in[i], sg_v_gain[i],
                                     sg_v_bias[i], sg_w_s[i], sg_b_s[i], sg_w_out[i])
        h = conv_ffn_block(h, ffn_norm[layer], ffn_w_up[layer], ffn_conv_w[layer],
                           ffn_conv_b[layer], ffn_w_down[layer])
    return rms_norm(h, final_norm)
```

```python
import numpy as np
from contextlib import ExitStack
import concourse.bass as bass
import concourse.mybir as mybir
from concourse.bass_utils import run_bass_kernel_spmd

F32 = mybir.dt.float32
BF16 = mybir.dt.bfloat16
I32 = mybir.dt.int32
AF = mybir.ActivationFunctionType
ALU = mybir.AluOpType
AX = mybir.AxisListType

NEG = -30000.0
ENGS = ("pe", "act", "dve", "pool", "sp")
NRING = 12


class Cfg:
    def __init__(self, D=4096, S=4096, NA=24, NBG=8, DFF=14336, B=2):
        self.D, self.S, self.NA, self.NBG, self.DFF, self.B = D, S, NA, NBG, DFF, B
        self.DH = 128
        self.NB = 3 * NBG
        self.NH = NA + self.NB
        self.NHO = NA + NBG
        assert self.NHO * 128 == D
        self.QKV = 3 * self.NH * 128
        self.DC = D // 128
        self.FC = DFF // 128
        self.T = 512
        self.NT = S // self.T
        self.MB = 256
        self.NBLK = S // self.MB
        self.E = D
        self.GG = self.E // 128
        self.EPS = 1e-5
        self.PATS = ((128, 1), (512, 4), (2048, 16))
        self.VW = 512 if (self.NH * 128) % 512 == 0 else 256
        self.SCALE = 128 ** -0.5


class Buf:
    __slots__ = ("w", "r")

    def __init__(self):
        self.w = None
        self.r = []


class Op:
    __slots__ = ("eng", "fn", "deps", "signal", "ev", "dma", "di")

    def __init__(self, eng, fn, dma):
        self.eng, self.fn, self.dma = eng, fn, dma
        self.deps = []
        self.signal = False
        self.ev = None
        self.di = -1


class Prog:
    def __init__(self, nc):
        self.nc = nc
        self.ops = []

    def op(self, eng, fn, reads=(), writes=(), dma=False):
        o = Op(eng, fn, dma)
        deps = {}
        for b in reads:
            if b.w is not None:
                deps[id(b.w)] = b.w
        for b in writes:
            if b.w is not None:
                deps[id(b.w)] = b.w
            for r in b.r:
                deps[id(r)] = r
        for d in deps.values():
            if d is o:
                continue
            if d.eng == "pe" and eng == "pe" and not d.dma and not dma:
                continue
            o.deps.append(d)
            d.signal = True
        for b in writes:
            b.w = o
            b.r = []
        for b in reads:
            if b.w is not o:
                if not dma:
                    b.r = [r for r in b.r if r.dma or r.eng != eng]
                b.r.append(o)
        self.ops.append(o)
        return o

    def dma(self, eng, out, in_, reads=(), writes=()):
        return self.op(eng, lambda e: e.dma_start(out=out, in_=in_), reads, writes, dma=True)

    def emit(self, name=None):
        nc = self.nc
        with ExitStack() as es:
            if not hasattr(nc, "_mk_sems"):
                gs = ExitStack()
                c_ = {e: gs.enter_context(nc.semaphore("c_" + e)) for e in ENGS}
                d_ = {e: [gs.enter_context(nc.semaphore("d_%s%d" % (e, i))) for i in range(NRING)]
                      for e in ("sp", "pool")}
                nc._mk_sems = (gs, c_, d_)
                nc._mk_pool_n = 0
            _, csem, dsem = nc._mk_sems
            cnt = {e: 0 for e in ENGS}
            dcnt = {e: 0 for e in dsem}
            dcnt["pool"] = nc._mk_pool_n
            pool_base = nc._mk_pool_n
            by_eng = {e: [] for e in ENGS}
            dlist = {e: [] for e in dsem}
            for o in self.ops:
                if o.dma:
                    i = dcnt[o.eng]
                    dcnt[o.eng] += 1
                    o.di = i
                    o.ev = (dsem[o.eng][i % NRING], 16 * (i // NRING + 1))
                    dlist[o.eng].append(o)
                    if o.eng == "pool":
                        nc._mk_pool_n = i + 1
                elif o.signal:
                    cnt[o.eng] += 1
                    o.ev = (csem[o.eng], cnt[o.eng])
                by_eng[o.eng].append(o)
            block = es.enter_context(nc.Block())

            def body(e, eng):
                waited = {}

                def wait(ev):
                    sem, val = ev
                    k = id(sem)
                    if waited.get(k, 0) < val:
                        e.wait_ge(sem, val)
                        waited[k] = val

                for o in by_eng[eng]:
                    need = {}
                    for d in o.deps:
                        sm, val = d.ev
                        k_ = id(sm)
                        if k_ not in need or need[k_][1] < val:
                            need[k_] = (sm, val)
                    for ev in need.values():
                        wait(ev)
                    if o.dma:
                        li = o.di - (pool_base if eng == "pool" else 0)
                        if li >= NRING:
                            wait(dlist[eng][li - NRING].ev)
                    ins = o.fn(e)
                    if o.dma:
                        ins.then_inc(o.ev[0], 16)
                    elif o.signal:
                        ins.then_inc(o.ev[0], 1)
                if eng in dlist:
                    for o in dlist[eng][-NRING:]:
                        wait(o.ev)

            block.tensor(lambda e: body(e, "pe"))
            block.scalar(lambda e: body(e, "act"))
            block.vector(lambda e: body(e, "dve"))
            block.gpsimd(lambda e: body(e, "pool"))
            block.sync(lambda e: body(e, "sp"))
        _, csem, dsem = nc._mk_sems
        allsems = list(csem.values()) + list(dsem["sp"])
        with nc.Block() as blk2:
            def clr(e):
                for sm in allsems:
                    e.sem_clear(sm)
            blk2.sync(clr)
        self.ops = []


class Tile:
    def __init__(self, t, nsub=0):
        self.t = t
        self.b = Buf()
        self.sub = [Buf() for _ in range(nsub)]


class Ctx:
    _n = 0

    def __init__(self, nc, es):
        self.nc, self.es = nc, es
        Ctx._n += 1
        self.pre = "s%d_" % Ctx._n

    def sb(self, name, shape, dt, nsub=0):
        return Tile(self.es.enter_context(self.nc.sbuf_tensor(self.pre + name, list(shape), dt)), nsub)

    def ps(self, name, shape=(128, 512), dt=F32, nsub=0):
        return Tile(self.es.enter_context(self.nc.psum_tensor(self.pre + name, list(shape), dt)), nsub)


class Ring:
    def __init__(self, tiles):
        self.tiles = tiles
        self.i = 0

    def next(self):
        t = self.tiles[self.i % len(self.tiles)]
        self.i += 1
        return t


def ring(cx, name, n, shape, dt):
    return Ring([cx.sb("%s%d" % (name, i), shape, dt) for i in range(n)])


def psring(cx, name, n):
    return Ring([cx.ps("%s%d" % (name, i)) for i in range(n)])


def f_mm(out, lhsT, rhs, start=True, stop=True):
    return lambda e: e.matmul(out, lhsT, rhs, start=start, stop=stop)


def f_tr(out, in_, ident):
    return lambda e: e.transpose(out, in_, ident)


def f_act(out, in_, func, **kw):
    return lambda e: e.activation(out=out, in_=in_, func=func, **kw)


def f_ts(out, in0, s1, s2, op0, op1=None):
    if op1 is None:
        return lambda e: e.tensor_scalar(out=out, in0=in0, scalar1=s1, scalar2=None, op0=op0)
    return lambda e: e.tensor_scalar(out=out, in0=in0, scalar1=s1, scalar2=s2, op0=op0, op1=op1)


def f_stt(out, in0, scalar, in1, op0, op1):
    return lambda e: e.scalar_tensor_tensor(out=out, in0=in0, scalar=scalar, in1=in1, op0=op0, op1=op1)


def f_tt(out, in0, in1, op):
    return lambda e: e.tensor_tensor(out=out, in0=in0, in1=in1, op=op)


def f_copy(out, in_):
    return lambda e: e.tensor_copy(out=out, in_=in_)


def f_recip(out, in_):
    return lambda e: e.reciprocal(out=out, in_=in_)


def f_memset(ap, v):
    return lambda e: e.memset(ap, v)


def dview(h):
    return h.ap().rearrange("(c p) t -> p c t", p=128)


def emit_norm(P, cfg, K, xin, tok0, g, h, hcol0, xring, sqring, ones, psn, rstd, T=None):
    T, DC = (T or cfg.T), cfg.DC
    xv = dview(xin)
    for c in range(DC):
        xc = xring.next()
        P.dma("sp", xc.t[:, 0:T], xv[:, c, tok0:tok0 + T], writes=[xc.b])
        sq = sqring.next()
        P.op("act", f_act(sq.t[:, 0:T], xc.t[:, 0:T], AF.Square), reads=[xc.b], writes=[sq.b])
        P.op("pe", f_mm(psn.t[:, 0:T], ones.t[:, :], sq.t[:, 0:T], c == 0, c == DC - 1),
             reads=[ones.b, sq.b], writes=[psn.b])
    P.op("act", f_act(rstd.t[:, 0:T], psn.t[:, 0:T], AF.Sqrt, scale=1.0 / cfg.D, bias=K["eps"].t[:, 0:1]),
         reads=[psn.b, K["eps"].b], writes=[rstd.b])
    P.op("dve", f_recip(rstd.t[:, 0:T], rstd.t[:, 0:T]), reads=[rstd.b], writes=[rstd.b])
    for c in range(DC):
        xc = xring.next()
        P.dma("sp", xc.t[:, 0:T], xv[:, c, tok0:tok0 + T], writes=[xc.b])
        P.op("dve", f_stt(h.t[:, c, hcol0:hcol0 + T], xc.t[:, 0:T], g.t[:, c:c + 1], rstd.t[:, 0:T],
                          ALU.mult, ALU.mult),
             reads=[xc.b, g.b, rstd.b], writes=[h.b])


def load_consts(P, cx, cfg, consts):
    K = {}
    K["ones"] = cx.sb("k_ones", [128, 128], BF16)
    P.op("dve", f_memset(K["ones"].t[:, :], 1.0), writes=[K["ones"].b])
    K["eps"] = cx.sb("k_eps", [128, 1], F32)
    P.op("dve", f_memset(K["eps"].t[:, :], cfg.EPS), writes=[K["eps"].b])
    return K


def stage_ffn_up(nc, cfg, xin, g_d, wup_d, cw_d, cb_d, gT):
    T, DC, FC, S = cfg.T, cfg.DC, cfg.FC, cfg.S
    NSUB = 2 if S % (2 * T) == 0 else 1
    TS = T * NSUB
    with ExitStack() as es:
        cx = Ctx(nc, es)
        P = Prog(nc)
        K = load_consts(P, cx, cfg, None)
        g = cx.sb("g", [128, DC], F32)
        P.dma("sp", g.t[:, :], g_d.ap(), writes=[g.b])
        cw = cx.sb("cw", [128, 3, 2 * FC], F32)
        P.dma("sp", cw.t[:, :, :], cw_d.ap(), writes=[cw.b])
        cb = cx.sb("cb", [128, 2 * FC], F32)
        P.dma("sp", cb.t[:, :], cb_d.ap(), writes=[cb.b])
        carry = [cx.sb("carry%d" % i, [128, FC, 2], F32, nsub=FC) for i in range(2)]
        for cr in carry:
            P.op("dve", f_memset(cr.t[:, :, :], 0.0), writes=cr.sub)
        h = cx.sb("h", [128, DC, TS], BF16)
        xring = ring(cx, "xc", 4, [128, T], F32)
        sqring = ring(cx, "sq", 3, [128, T], BF16)
        rstd = cx.sb("rstd", [128, T], F32)
        wring = ring(cx, "w", 4, [128, DC, 128], BF16)
        abuf = ring(cx, "ab", 4, [128, T + 4], F32)
        cbuf = ring(cx, "cbf", 4, [128, T], F32)
        sgb = ring(cx, "sg", 2, [128, T], F32)
        gob = ring(cx, "go", 3, [128, T], BF16)
        psn = cx.ps("psn")
        psr = psring(cx, "ps", 7)
        for st in range(S // TS):
            for sub in range(NSUB):
                emit_norm(P, cfg, K, xin, st * TS + sub * T, g, h, sub * T, xring, sqring, K["ones"], psn, rstd)
            for j in range(FC):
                ws = []
                for half in range(2):
                    w = wring.next()
                    P.dma("pool", w.t[:, :, :], wup_d.ap()[half * FC + j], writes=[w.b])
                    ws.append(w)
                for sub in range(NSUB):
                    tok0 = st * TS + sub * T
                    cs = []
                    for half in range(2):
                        ps = psr.next()
                        for k in range(DC):
                            P.op("pe", f_mm(ps.t[:, :], ws[half].t[:, k, :], h.t[:, k, sub * T:(sub + 1) * T],
                                            k == 0, k == DC - 1),
                                 reads=[ws[half].b, h.b], writes=[ps.b])
                        ch = half * FC + j
                        ab = abuf.next()
                        cr = carry[half]
                        P.op("dve", f_copy(ab.t[:, 0:2], cr.t[:, j, :]), reads=[cr.sub[j]], writes=[ab.b])
                        P.op("act", f_act(ab.t[:, 2:T + 2], ps.t[:, :], AF.Copy), reads=[ps.b], writes=[ab.b])
                        P.op("dve", f_copy(cr.t[:, j, :], ab.t[:, T:T + 2]), reads=[ab.b], writes=[cr.sub[j]])
                        c_ = cbuf.next()
                        P.op("dve", f_ts(c_.t[:, :], ab.t[:, 2:T + 2], cw.t[:, 2, ch:ch + 1], cb.t[:, ch:ch + 1],
                                         ALU.mult, ALU.add), reads=[ab.b, cw.b, cb.b], writes=[c_.b])
                        P.op("dve", f_stt(c_.t[:, :], ab.t[:, 1:T + 1], cw.t[:, 1, ch:ch + 1], c_.t[:, :],
                                          ALU.mult, ALU.add), reads=[ab.b, c_.b, cw.b], writes=[c_.b])
                        P.op("dve", f_stt(c_.t[:, :], ab.t[:, 0:T], cw.t[:, 0, ch:ch + 1], c_.t[:, :],
                                          ALU.mult, ALU.add), reads=[ab.b, c_.b, cw.b], writes=[c_.b])
                        cs.append(c_)
                    sg = sgb.next()
                    P.op("act", f_act(sg.t[:, :], cs[0].t[:, :], AF.Silu), reads=[cs[0].b], writes=[sg.b])
                    go = gob.next()
                    P.op("dve", f_tt(go.t[:, :], sg.t[:, :], cs[1].t[:, :], ALU.mult),
                         reads=[sg.b, cs[1].b], writes=[go.b])
                    P.dma("sp", gT.ap()[j * 128:(j + 1) * 128, tok0:tok0 + T], go.t[:, :], reads=[go.b])
        P.emit()


def stage_linear_res(nc, cfg, KC, actT, w_d, xin, xout):
    T, DC, S = cfg.T, cfg.DC, cfg.S
    with ExitStack() as es:
        cx = Ctx(nc, es)
        P = Prog(nc)
        KG = 16
        NG = (KC + KG - 1) // KG
        abufs = [cx.sb("a%d" % i, [128, KC, T], BF16, nsub=NG) for i in range(2 if KC <= 32 else 1)]
        wring = ring(cx, "w", 2, [128, KC, 128], BF16)
        xr = ring(cx, "xr", 3, [128, T], F32)
        xo = ring(cx, "xo", 3, [128, T], F32)
        psr = psring(cx, "ps", 4)
        av = dview(actT)
        xv = dview(xin)
        ov = dview(xout)
        for tt in range(S // T):
            tok0 = tt * T
            a = abufs[tt % len(abufs)]
            for gk in range(NG):
                k0, k1 = gk * KG, min(KC, (gk + 1) * KG)
                P.dma("sp", a.t[:, k0:k1, :], av[:, k0:k1, tok0:tok0 + T], writes=[a.sub[gk]])
            for c in range(DC):
                w = wring.next()
                P.dma("pool", w.t[:, :, :], w_d.ap()[c], writes=[w.b])
                ps = psr.next()
                for k in range(KC):
                    P.op("pe", f_mm(ps.t[:, :], w.t[:, k, :], a.t[:, k, :], k == 0, k == KC - 1),
                         reads=[w.b, a.sub[k // KG]], writes=[ps.b])
                x_ = xr.next()
                P.dma("sp", x_.t[:, :], xv[:, c, tok0:tok0 + T], writes=[x_.b])
                o_ = xo.next()
                P.op("dve", f_tt(o_.t[:, :], ps.t[:, :], x_.t[:, :], ALU.add), reads=[ps.b, x_.b], writes=[o_.b])
                P.dma("sp", ov[:, c, tok0:tok0 + T], o_.t[:, :], reads=[o_.b])
        P.emit()


def tile_w(W, cw=128):
    K, N = W.shape
    return np.ascontiguousarray(W.reshape(K // 128, 128, N // cw, cw).transpose(2, 1, 0, 3))


def vec_pc(v):
    return np.ascontiguousarray(v.reshape(-1, 128).T)


def np_bf16(a):
    import ml_dtypes
    return np.asarray(a, dtype=np.float32).astype(ml_dtypes.bfloat16)


def make_consts(nc, cfg):
    C = {}
    half = 16
    invf = np.power(np.float32(500000.0), -np.arange(half, dtype=np.float32) * np.float32(2.0 / 32.0)).astype(np.float32)
    C["invf"] = nc.inline_tensor(np.concatenate([invf, invf]).reshape(32, 1).astype(np.float32), "k_invf")
    Pm = np.zeros((128, 32), np.float32)
    for m in range(16):
        Pm[m + 16, m] = -1.0
        Pm[m, m + 16] = 1.0
    C["Pm"] = nc.inline_tensor(Pm, "k_Pm")
    C["identf"] = nc.inline_tensor(np.eye(128, dtype=np.float32), "k_identf")
    C["identb"] = nc.inline_tensor(np_bf16(np.eye(128)), "k_identb")
    NBLK = cfg.NBLK
    E = np.zeros((128, NBLK * 128), np.float32)
    for n in range(NBLK):
        E[n % 16, n * 128:(n + 1) * 128] = 1.0
    C["esel"] = nc.inline_tensor(np_bf16(E), "k_esel")
    j = np.arange(128)[:, None, None]
    r = np.arange(4)[None, :, None]
    i = np.arange(512)[None, None, :]
    C["cmask"] = nc.inline_tensor(np_bf16(np.where(r * 128 + j <= i, 0.0, NEG)), "k_cmask")
    jj = np.arange(128)[:, None]
    ii = np.arange(128)[None, :]
    cur = np.where(jj <= ii, 0.0, NEG)
    prv = np.where(jj >= ii, 0.0, NEG)
    bm = np.stack([np.concatenate([cur, np.full((128, 128), NEG)], 1), np.concatenate([cur, prv], 1)], 1)
    C["bmask"] = nc.inline_tensor(np_bf16(bm), "k_bmask")
    NKT = cfg.S // 128
    pm = np.full((NKT, 16), NEG, np.float32)
    om = np.full((NKT, 16), -1.0e9, np.float32)
    for qi in range(NKT):
        qb = (qi * 128) // cfg.MB
        pm[qi, :qb] = 0.0
        om[qi, qb] = 0.0
    C["pastm"] = nc.inline_tensor(np.ascontiguousarray(np.broadcast_to(pm.reshape(1, NKT * 16), (128, NKT * 16))), "k_pastm")
    C["ownm"] = nc.inline_tensor(np.ascontiguousarray(np.broadcast_to(om.reshape(1, NKT * 16), (128, NKT * 16))), "k_ownm")
    C["tril"] = nc.inline_tensor(np.tril(np.ones((128, 128), np.float32)), "k_tril")
    return C


def stage_rope(nc, cfg, C, pos_d, cs_d):
    S = cfg.S
    W = min(S, 2048)
    with ExitStack() as es:
        cx = Ctx(nc, es)
        P = Prog(nc)
        invf = cx.sb("invf", [32, 1], F32)
        P.dma("sp", invf.t[:, :], C["invf"].ap(), writes=[invf.b])
        for c0 in range(0, S, W):
            S_ = W
            posi = cx.sb("posi%d" % c0, [32, S_], I32)
            P.dma("sp", posi.t[:, :], pos_d.ap()[:, c0:c0 + W], writes=[posi.b])
            ang = cx.sb("ang%d" % c0, [32, S_], F32)
            P.op("dve", f_copy(ang.t[:, :], posi.t[:, :]), reads=[posi.b], writes=[ang.b])
            P.op("dve", f_ts(ang.t[:, :], ang.t[:, :], invf.t[:, 0:1], None, ALU.mult), reads=[ang.b, invf.b], writes=[ang.b])
            ki = cx.sb("ki%d" % c0, [32, S_], I32)
            kf = cx.sb("kf%d" % c0, [32, S_], F32)
            mk = cx.sb("mk%d" % c0, [32, S_], F32)
            for idx, (nm, shift) in enumerate((("cos", 0.25), ("sin", 0.0))):
                y = cx.sb("y_%s%d" % (nm, c0), [32, S_], F32)
                P.op("dve", f_ts(y.t[:, :], ang.t[:, :], float(1.0 / (2.0 * np.pi)), shift, ALU.mult, ALU.add),
                     reads=[ang.b], writes=[y.b])
                P.op("dve", f_copy(ki.t[:, :], y.t[:, :]), reads=[y.b], writes=[ki.b])
                P.op("dve", f_copy(kf.t[:, :], ki.t[:, :]), reads=[ki.b], writes=[kf.b])
                P.op("dve", f_tt(y.t[:, :], y.t[:, :], kf.t[:, :], ALU.subtract), reads=[y.b, kf.b], writes=[y.b])
                P.op("dve", f_ts(mk.t[:, :], y.t[:, :], 0.5, None, ALU.is_gt), reads=[y.b], writes=[mk.b])
                P.op("dve", f_tt(y.t[:, :], y.t[:, :], mk.t[:, :], ALU.subtract), reads=[y.b, mk.b], writes=[y.b])
                P.op("dve", f_ts(mk.t[:, :], y.t[:, :], -0.5, None, ALU.is_lt), reads=[y.b], writes=[mk.b])
                P.op("dve", f_tt(y.t[:, :], y.t[:, :], mk.t[:, :], ALU.add), reads=[y.b, mk.b], writes=[y.b])
                P.op("act", f_act(y.t[:, :], y.t[:, :], AF.Sin, scale=float(2.0 * np.pi * (1.0 - 1e-6))),
                     reads=[y.b], writes=[y.b])
                P.dma("sp", cs_d.ap()[:, idx, c0:c0 + W], y.t[:, :], reads=[y.b])
        P.emit()


def stage_qkv(nc, cfg, C, xin, g_d, wqk_d, wv_d, cs_d, qT, kT, v, ksum_d):
    T, DC, S, NH, NA = cfg.T, cfg.DC, cfg.S, cfg.NH, cfg.NA
    VW = cfg.VW
    NSUB = 2 if S % (2 * T) == 0 else 1
    TS = T * NSUB
    with ExitStack() as es:
        cx = Ctx(nc, es)
        P = Prog(nc)
        K = load_consts(P, cx, cfg, None)
        g = cx.sb("g", [128, DC], F32)
        P.dma("sp", g.t[:, :], g_d.ap(), writes=[g.b])
        Pm = cx.sb("Pm", [128, 32], F32)
        P.dma("sp", Pm.t[:, :], C["Pm"].ap(), writes=[Pm.b])
        csr = ring(cx, "cs", 4, [32, 2, T], F32)
        ksum = cx.sb("ksum", [128, NA, cfg.NBLK], F32)
        h = cx.sb("h", [128, DC, TS], BF16)
        xring = ring(cx, "xc", 4, [128, T], F32)
        sqring = ring(cx, "sq", 3, [128, T], BF16)
        rstd = cx.sb("rstd", [128, T], F32)
        wring = ring(cx, "w", 3, [128, DC, 128], BF16)
        wvring = ring(cx, "wv", 2, [128, DC, VW], BF16)
        qfr = ring(cx, "qf", 3, [128, T], F32)
        tmr = ring(cx, "tm", 4, [32, T], F32)
        qbr = ring(cx, "qb", 3, [128, T], BF16)
        vbr = ring(cx, "vb", 3, [128, VW], BF16)
        psn = cx.ps("psn")
        psr = psring(cx, "ps", 5)
        psp = psring(cx, "pp", 2)
        for st in range(S // TS):
            cst = []
            for sub in range(NSUB):
                emit_norm(P, cfg, K, xin, st * TS + sub * T, g, h, sub * T, xring, sqring, K["ones"], psn, rstd)
                cs = csr.next()
                tok0 = st * TS + sub * T
                P.dma("sp", cs.t[:, :, :], cs_d.ap()[:, :, tok0:tok0 + T], writes=[cs.b])
                cst.append(cs)
            for j in range(2 * NH):
                isk = j >= NH
                hd = j - NH if isk else j
                w = wring.next()
                P.dma("pool", w.t[:, :, :], wqk_d.ap()[j], writes=[w.b])
                for sub in range(NSUB):
                    tok0 = st * TS + sub * T
                    ps = psr.next()
                    for k in range(DC):
                        P.op("pe", f_mm(ps.t[:, :], w.t[:, k, :], h.t[:, k, sub * T:(sub + 1) * T], k == 0, k == DC - 1),
                             reads=[w.b, h.b], writes=[ps.b])
                    qf = qfr.next()
                    P.op("act", f_act(qf.t[:, :], ps.t[:, :], AF.Copy), reads=[ps.b], writes=[qf.b])
                    pp = psp.next()
                    P.op("pe", f_mm(pp.t[0:32, :], Pm.t[:, :], qf.t[:, :]), reads=[Pm.b, qf.b], writes=[pp.b])
                    t1 = tmr.next()
                    t2 = tmr.next()
                    P.op("dve", f_tt(t1.t[:, :], qf.t[0:32, :], cst[sub].t[:, 0, :], ALU.mult),
                         reads=[qf.b, cst[sub].b], writes=[t1.b])
                    P.op("dve", f_tt(t2.t[:, :], pp.t[0:32, :], cst[sub].t[:, 1, :], ALU.mult),
                         reads=[pp.b, cst[sub].b], writes=[t2.b])
                    P.op("dve", f_tt(qf.t[0:32, :], t1.t[:, :], t2.t[:, :], ALU.add),
                         reads=[t1.b, t2.b], writes=[qf.b])
                    qb = qbr.next()
                    P.op("act", f_act(qb.t[:, :], qf.t[:, :], AF.Copy), reads=[qf.b], writes=[qb.b])
                    if isk and hd < NA:
                        b0 = tok0 // cfg.MB
                        nb = T // cfg.MB
                        P.op("dve", lambda e, o_=ksum.t[:, hd, b0:b0 + nb], i_=qf.t[:, :].rearrange("p (b m) -> p b m", m=cfg.MB):
                             e.tensor_reduce(out=o_, in_=i_, axis=AX.X, op=ALU.add),
                             reads=[qf.b], writes=[ksum.b])
                    dst = kT if isk else qT
                    P.dma("sp", dst.ap()[hd * 128:(hd + 1) * 128, tok0:tok0 + T], qb.t[:, :], reads=[qb.b])
            for jb in range(NH * 128 // VW):
                wv = wvring.next()
                P.dma("pool", wv.t[:, :, :], wv_d.ap()[jb], writes=[wv.b])
                for tb in range(TS // 128):
                    tok0 = st * TS + tb * 128
                    ps = psr.next()
                    for k in range(DC):
                        P.op("pe", f_mm(ps.t[:, 0:VW], h.t[:, k, tb * 128:(tb + 1) * 128], wv.t[:, k, :], k == 0, k == DC - 1),
                             reads=[wv.b, h.b], writes=[ps.b])
                    vb = vbr.next()
                    P.op("act", f_act(vb.t[:, :], ps.t[:, 0:VW], AF.Copy), reads=[ps.b], writes=[vb.b])
                    P.dma("sp", v.ap()[tok0:tok0 + 128, jb * VW:(jb + 1) * VW], vb.t[:, :], reads=[vb.b])
        P.dma("sp", ksum_d.ap(), ksum.t[:, :, :], reads=[ksum.b])
        P.emit()


def stage_moba(nc, cfg, C, qT, kT, v, ksum_d, oT):
    S, NA, NBLK, T = cfg.S, cfg.NA, cfg.NBLK, cfg.T
    NKT = S // 128
    with ExitStack() as es:
        cx = Ctx(nc, es)
        P = Prog(nc)
        K = load_consts(P, cx, cfg, None)
        ones = K["ones"]
        identf = cx.sb("identf", [128, 128], F32)
        P.dma("sp", identf.t[:, :], C["identf"].ap(), writes=[identf.b])
        identb = cx.sb("identb", [128, 128], BF16)
        P.dma("sp", identb.t[:, :], C["identb"].ap(), writes=[identb.b])
        esel = cx.sb("esel", [128, NBLK * 128], BF16)
        P.dma("sp", esel.t[:, :], C["esel"].ap(), writes=[esel.b])
        cmask = cx.sb("cmask", [128, 4, 512], BF16)
        P.dma("sp", cmask.t[:, :, :], C["cmask"].ap(), writes=[cmask.b])
        ksum = cx.sb("ksum", [128, NA, NBLK], F32)
        P.dma("sp", ksum.t[:, :, :], ksum_d.ap(), writes=[ksum.b])
        qr = ring(cx, "q", 2, [128, S], BF16)
        kr = ring(cx, "k", 2, [128, S], BF16)
        vr = ring(cx, "v", 2, [128, NKT, 128], BF16)
        kmr = ring(cx, "km", 2, [128, 16], BF16)
        btr = ring(cx, "bt", 2, [128, S], BF16)
        gmr = ring(cx, "gm", 4, [128, 16], F32)
        mxr = ring(cx, "mx", 4, [128, 8], F32)
        bqr = ring(cx, "bq", 6, [128, 128], F32)
        ptr = ring(cx, "pt", 3, [128, T], BF16)
        rzr = ring(cx, "rz", 2, [128, T], F32)
        obr = ring(cx, "ob", 2, [128, T], BF16)
        psr = psring(cx, "ps", 3)
        por = psring(cx, "po", 2)
        pzr = psring(cx, "pz", 2)
        psm = cx.ps("psm", nsub=7)
        NB16 = min(NBLK, 16)
        assert NBLK <= 16 and NBLK >= 8
        pastm = cx.sb("pastm", [128, NKT * 16], F32)
        P.dma("sp", pastm.t[:, :], C["pastm"].ap(), writes=[pastm.b])
        ownm = cx.sb("ownm", [128, NKT * 16], F32)
        P.dma("sp", ownm.t[:, :], C["ownm"].ap(), writes=[ownm.b])
        gmar = ring(cx, "gma", 2, [128, NKT * 16], F32)
        mxar = ring(cx, "mxa", 2, [128, NKT, 8], F32)
        bqar = ring(cx, "bqa", 2, [128, NKT, 128], F32)
        for t_ in bqar.tiles:
            P.op("dve", f_memset(t_.t[:, :, :], 0.0), writes=[t_.b])
        v3 = lambda ap: ap.rearrange("p (a b) -> p a b", b=16)
        st = {}

        def load(hd):
            q, k, vv, km, bt = qr.next(), kr.next(), vr.next(), kmr.next(), btr.next()
            P.dma("sp", q.t[:, :], qT.ap()[hd * 128:(hd + 1) * 128, :], writes=[q.b])
            P.dma("sp", k.t[:, :], kT.ap()[hd * 128:(hd + 1) * 128, :], writes=[k.b])
            vsrc = v.ap()[:, hd * 128:(hd + 1) * 128].rearrange("(n p) d -> p n d", p=128)
            for n0 in range(0, NKT, 8):
                P.dma("sp", vv.t[:, n0:n0 + 8, :], vsrc[:, n0:n0 + 8, :], writes=[vv.b])
            P.op("dve", f_memset(km.t[:, :], 0.0), writes=[km.b])
            P.op("dve", f_copy(km.t[:, 0:NBLK], ksum.t[:, hd, :]), reads=[ksum.b], writes=[km.b])
            st[hd] = (q, k, vv, km, bt)

        def gate_a(hd):
            q, k, vv, km, bt = st[hd]
            for qi in range(NKT):
                P.op("pe", f_mm(psm.t[:, qi * 16:(qi + 1) * 16], q.t[:, qi * 128:(qi + 1) * 128], km.t[:, 0:16]),
                     reads=[q.b, km.b], writes=[psm.b])
            gma, mxa, bqa = gmar.next(), mxar.next(), bqar.next()
            P.op("dve", f_tt(gma.t[:, :], psm.t[:, 0:NKT * 16], pastm.t[:, :], ALU.add), reads=[psm.b, pastm.b], writes=[gma.b])
            for qi in range(NKT):
                P.op("dve", lambda e, o_=mxa.t[:, qi, :], i_=gma.t[:, qi * 16:(qi + 1) * 16]: e.max(out=o_, in_=i_),
                     reads=[gma.b], writes=[mxa.b])
            for qi in range(NKT):
                P.op("dve", f_ts(gma.t[:, qi * 16:(qi + 1) * 16], gma.t[:, qi * 16:(qi + 1) * 16], mxa.t[:, qi, 2:3], None, ALU.is_ge),
                     reads=[gma.b, mxa.b], writes=[gma.b])
            P.op("dve", f_ts(bqa.t[:, :, 0:16], v3(gma.t[:, :]), -NEG, NEG, ALU.mult, ALU.add), reads=[gma.b], writes=[bqa.b])
            P.op("dve", f_tt(bqa.t[:, :, 0:16], bqa.t[:, :, 0:16], v3(pastm.t[:, :]), ALU.min), reads=[bqa.b, pastm.b], writes=[bqa.b])
            P.op("dve", f_tt(bqa.t[:, :, 0:16], bqa.t[:, :, 0:16], v3(ownm.t[:, :]), ALU.max), reads=[bqa.b, ownm.b], writes=[bqa.b])
            st[hd] = (q, k, vv, km, bt, bqa)

        def gate_b(hd):
            q, k, vv, km, bt, bqa = st[hd]
            for grp in range(NKT // 4):
                pst = psr.next()
                for j_ in range(4):
                    P.op("pe", f_tr(pst.t[:, j_ * 128:(j_ + 1) * 128], bqa.t[:, grp * 4 + j_, :], identf.t[:, :]),
                         reads=[bqa.b, identf.b], writes=[pst.b])
                P.op("act", f_act(bt.t[:, grp * 512:(grp + 1) * 512], pst.t[:, :], AF.Copy), reads=[pst.b], writes=[bt.b])

        def attention(hd):
            q, k, vv, km, bt, bqa = st.pop(hd)
            for g in range(S // T):
                nkt = 4 * (g + 1)
                po, pz = por.next(), pzr.next()
                qs = q.t[:, g * T:(g + 1) * T]

                def qk(kt):
                    ps = psr.next()
                    n = kt // 2
                    diag = kt >= 4 * g
                    P.op("pe", f_mm(ps.t[:, :], k.t[:, kt * 128:(kt + 1) * 128], qs, True, False),
                         reads=[k.b, q.b], writes=[ps.b])
                    P.op("pe", f_mm(ps.t[:, :], esel.t[:, n * 128:(n + 1) * 128], bt.t[:, g * T:(g + 1) * T], False, not diag),
                         reads=[esel.b, bt.b], writes=[ps.b])
                    if diag:
                        P.op("pe", f_mm(ps.t[:, :], identb.t[:, :], cmask.t[:, kt - 4 * g, :], False, True),
                             reads=[identb.b, cmask.b], writes=[ps.b])
                    return ps

                ps_next = qk(0)
                for kt in range(nkt):
                    ps = ps_next
                    pt = ptr.next()
                    P.op("act", f_act(pt.t[:, :], ps.t[:, :], AF.Exp, scale=float(cfg.SCALE)), reads=[ps.b], writes=[pt.b])
                    if kt + 1 < nkt:
                        ps_next = qk(kt + 1)
                    P.op("pe", f_mm(po.t[:, :], vv.t[:, kt, :], pt.t[:, :], kt == 0, kt == nkt - 1),
                         reads=[vv.b, pt.b], writes=[po.b])
                    P.op("pe", f_mm(pz.t[:, :], ones.t[:, :], pt.t[:, :], kt == 0, kt == nkt - 1),
                         reads=[ones.b, pt.b], writes=[pz.b])
                rz = rzr.next()
                P.op("dve", f_recip(rz.t[:, :], pz.t[:, :]), reads=[pz.b], writes=[rz.b])
                ob = obr.next()
                P.op("dve", f_tt(ob.t[:, :], po.t[:, :], rz.t[:, :], ALU.mult), reads=[po.b, rz.b], writes=[ob.b])
                P.dma("sp", oT.ap()[hd * 128:(hd + 1) * 128, g * T:(g + 1) * T], ob.t[:, :], reads=[ob.b])

        load(0)
        gate_a(0)
        gate_b(0)
        for hd in range(NA):
            if hd + 1 < NA:
                load(hd + 1)
                gate_a(hd + 1)
            attention(hd)
            if hd + 1 < NA:
                gate_b(hd + 1)
        P.emit()


def stage_dilated(nc, cfg, C, qT, kT, v, oT):
    S, NA, NBG, T = cfg.S, cfg.NA, cfg.NBG, cfg.T
    NKT = S // 128
    with ExitStack() as es:
        cx = Ctx(nc, es)
        P = Prog(nc)
        K = load_consts(P, cx, cfg, None)
        ones = K["ones"]
        identb = cx.sb("identb", [128, 128], BF16)
        P.dma("sp", identb.t[:, :], C["identb"].ap(), writes=[identb.b])
        bmask = cx.sb("bmask", [128, 2, 256], BF16)
        P.dma("sp", bmask.t[:, :, :], C["bmask"].ap(), writes=[bmask.b])
        qr = ring(cx, "q", 4, [128, S], BF16)
        kr = ring(cx, "k", 4, [128, S], BF16)
        vr = ring(cx, "v", 2, [128, NKT, 128], BF16)
        uacc = cx.sb("uacc", [128, S], F32)
        zacc = cx.sb("zacc", [128, S], F32)
        ptr = ring(cx, "pt", 3, [128, 256], BF16)
        obr = ring(cx, "ob", 2, [128, T], BF16)
        psr = psring(cx, "ps", 4)
        pur = psring(cx, "pu", 4)
        for hh in range(NBG):
            for gi, (win, dil) in enumerate(cfg.PATS):
                hq = NA + gi * NBG + hh
                nblk = S // (128 * dil)
                q, k, vv = qr.next(), kr.next(), vr.next()
                P.dma("sp", q.t[:, :], qT.ap()[hq * 128:(hq + 1) * 128, :], writes=[q.b])
                P.dma("sp", k.t[:, :], kT.ap()[hq * 128:(hq + 1) * 128, :], writes=[k.b])
                vsrc = v.ap()[:, hq * 128:(hq + 1) * 128].rearrange("(n p r) d -> p r n d", p=128, r=dil)
                for r in range(dil):
                    for n0 in range(0, nblk, 8):
                        n1 = min(nblk, n0 + 8)
                        P.dma("sp", vv.t[:, r * nblk + n0:r * nblk + n1, :], vsrc[:, r, n0:n1, :], writes=[vv.b])
                if dil > 1:
                    q2, k2 = qr.next(), kr.next()
                    sub = S // dil
                    for r in range(dil):
                        P.op("dve", f_copy(q2.t[:, r * sub:(r + 1) * sub], q.t[:, r:r + (sub - 1) * dil + 1:dil]), reads=[q.b], writes=[q2.b])
                        P.op("act", f_act(k2.t[:, r * sub:(r + 1) * sub], k.t[:, r:r + (sub - 1) * dil + 1:dil], AF.Copy), reads=[k.b], writes=[k2.b])
                    q, k = q2, k2
                tiles = [(r, n) for r in range(dil) for n in range(nblk)]

                def qk(rn, q=q, k=k, nblk=nblk):
                    r, n = rn
                    c0 = (r * nblk + n) * 128
                    p0 = c0 - 128 if n > 0 else c0
                    ps = psr.next()
                    P.op("pe", f_mm(ps.t[:, 0:128], k.t[:, c0:c0 + 128], q.t[:, c0:c0 + 128], True, False), reads=[k.b, q.b], writes=[ps.b])
                    P.op("pe", f_mm(ps.t[:, 128:256], k.t[:, p0:p0 + 128], q.t[:, c0:c0 + 128], False, False), reads=[k.b, q.b], writes=[ps.b])
                    P.op("pe", f_mm(ps.t[:, 0:256], identb.t[:, :], bmask.t[:, 1 if n > 0 else 0, :], False, True),
                         reads=[identb.b, bmask.b], writes=[ps.b])
                    return ps

                ps_next = qk(tiles[0])
                for ti, (r, n) in enumerate(tiles):
                    b0 = n * 128 * dil + r
                    cur = slice(b0, b0 + 127 * dil + 1, dil)
                    ps = ps_next
                    pt = ptr.next()
                    P.op("act", f_act(pt.t[:, :], ps.t[:, 0:256], AF.Exp, scale=float(cfg.SCALE)), reads=[ps.b], writes=[pt.b])
                    if ti + 1 < len(tiles):
                        ps_next = qk(tiles[ti + 1])
                    pu = pur.next()
                    vi = r * nblk + n
                    vp = vi - 1 if n > 0 else vi
                    P.op("pe", f_mm(pu.t[:, 0:128], vv.t[:, vi, :], pt.t[:, 0:128], True, False), reads=[vv.b, pt.b], writes=[pu.b])
                    P.op("pe", f_mm(pu.t[:, 0:128], vv.t[:, vp, :], pt.t[:, 128:256], False, False), reads=[vv.b, pt.b], writes=[pu.b])
                    P.op("pe", f_mm(pu.t[:, 128:256], ones.t[:, :], pt.t[:, 0:128], False, False), reads=[ones.b, pt.b], writes=[pu.b])
                    P.op("pe", f_mm(pu.t[:, 128:256], ones.t[:, :], pt.t[:, 128:256], False, True), reads=[ones.b, pt.b], writes=[pu.b])
                    if gi == 0:
                        P.op("dve", f_copy(uacc.t[:, cur], pu.t[:, 0:128]), reads=[pu.b], writes=[uacc.b])
                        P.op("dve", f_copy(zacc.t[:, cur], pu.t[:, 128:256]), reads=[pu.b], writes=[zacc.b])
                    else:
                        P.op("dve", f_tt(uacc.t[:, cur], uacc.t[:, cur], pu.t[:, 0:128], ALU.add), reads=[pu.b, uacc.b], writes=[uacc.b])
                        P.op("dve", f_tt(zacc.t[:, cur], zacc.t[:, cur], pu.t[:, 128:256], ALU.add), reads=[pu.b, zacc.b], writes=[zacc.b])
            P.op("dve", f_recip(zacc.t[:, :], zacc.t[:, :]), reads=[zacc.b], writes=[zacc.b])
            for g in range(S // T):
                ob = obr.next()
                P.op("dve", f_tt(ob.t[:, :], uacc.t[:, g * T:(g + 1) * T], zacc.t[:, g * T:(g + 1) * T], ALU.mult),
                     reads=[uacc.b, zacc.b], writes=[ob.b])
                P.dma("sp", oT.ap()[(NA + hh) * 128:(NA + hh + 1) * 128, g * T:(g + 1) * T], ob.t[:, :], reads=[ob.b])
        P.emit()


def stage_sg_setup(nc, cfg, C, ws_d, bs_d, lnb_d, wsT_d, bsb_d):
    GG = cfg.GG
    with ExitStack() as es:
        cx = Ctx(nc, es)
        P = Prog(nc)
        identf = cx.sb("identf", [128, 128], F32)
        P.dma("sp", identf.t[:, :], C["identf"].ap(), writes=[identf.b])
        tril = cx.sb("tril", [128, 128], F32)
        P.dma("sp", tril.t[:, :], C["tril"].ap(), writes=[tril.b])
        e0 = cx.sb("e0", [128, 128], BF16)
        P.op("dve", f_memset(e0.t[:, :], 0.0), writes=[e0.b])
        P.op("dve", f_memset(e0.t[0:1, :], 1.0), writes=[e0.b])
        lnb = cx.sb("lnb", [128, cfg.E], BF16)
        P.dma("pool", lnb.t[:, :], lnb_d.ap(), writes=[lnb.b])
        bs0 = cx.sb("bs0", [128, GG * 128], BF16)
        P.op("dve", f_memset(bs0.t[:, :], 0.0), writes=[bs0.b])
        P.dma("pool", bs0.t[0:1, :], bs_d.ap(), writes=[bs0.b])
        wsT = cx.sb("wsT", [128, GG, 128], BF16)
        bsb = cx.sb("bsb", [128, GG, 128], F32)
        wr = ring(cx, "w", 3, [128, 128], F32)
        psr = psring(cx, "ps", 4)
        for g in range(GG):
            w = wr.next()
            P.dma("sp", w.t[:, :], ws_d.ap()[g], writes=[w.b])
            P.op("dve", f_tt(w.t[:, :], w.t[:, :], tril.t[:, :], ALU.mult), reads=[w.b, tril.b], writes=[w.b])
            ps = psr.next()
            P.op("pe", f_tr(ps.t[:, 0:128], w.t[:, :], identf.t[:, :]), reads=[w.b, identf.b], writes=[ps.b])
            P.op("act", f_act(wsT.t[:, g, :], ps.t[:, 0:128], AF.Copy), reads=[ps.b], writes=[wsT.b])
            ps2 = psr.next()
            P.op("pe", f_mm(ps2.t[:, 0:128], lnb.t[:, g * 128:(g + 1) * 128], wsT.t[:, g, :], True, False),
                 reads=[lnb.b, wsT.b], writes=[ps2.b])
            P.op("pe", f_mm(ps2.t[:, 0:128], e0.t[:, :], bs0.t[:, g * 128:(g + 1) * 128], False, True),
                 reads=[e0.b, bs0.b], writes=[ps2.b])
            P.op("act", f_act(bsb.t[:, g, :], ps2.t[:, 0:128], AF.Copy), reads=[ps2.b], writes=[bsb.b])
        P.dma("sp", wsT_d.ap(), wsT.t[:, :, :], reads=[wsT.b])
        P.dma("sp", bsb_d.ap(), bsb.t[:, :, :], reads=[bsb.b])
        P.emit()


def stage_sg_main(nc, cfg, xin, g_d, wu_d, wv_d, bu_d, bv_d, lng_d, wsT_d, bsb_d, mT):
    DC, S, GG, E = cfg.DC, cfg.S, cfg.GG, cfg.E
    T = 512
    VW = 256
    with ExitStack() as es:
        cx = Ctx(nc, es)
        P = Prog(nc)
        K = load_consts(P, cx, cfg, None)
        g = cx.sb("g", [128, DC], F32)
        P.dma("sp", g.t[:, :], g_d.ap(), writes=[g.b])
        bu = cx.sb("bu", [128, GG], F32)
        P.dma("sp", bu.t[:, :], bu_d.ap(), writes=[bu.b])
        lng = cx.sb("lng", [128, GG], F32)
        P.dma("sp", lng.t[:, :], lng_d.ap(), writes=[lng.b])
        e0 = cx.sb("e0", [128, 128], BF16)
        P.op("dve", f_memset(e0.t[:, :], 0.0), writes=[e0.b])
        P.op("dve", f_memset(e0.t[0:1, :], 1.0), writes=[e0.b])
        bv0 = cx.sb("bv0", [128, E], BF16)
        P.op("dve", f_memset(bv0.t[:, :], 0.0), writes=[bv0.b])
        P.dma("pool", bv0.t[0:1, :], bv_d.ap(), writes=[bv0.b])
        wsT = cx.sb("wsT", [128, GG, 128], BF16)
        P.dma("sp", wsT.t[:, :, :], wsT_d.ap(), writes=[wsT.b])
        bsb = cx.sb("bsb", [128, GG, 128], F32)
        P.dma("sp", bsb.t[:, :, :], bsb_d.ap(), writes=[bsb.b])
        h = cx.sb("h", [128, DC, T], BF16)
        uT = cx.sb("uT", [128, GG, T], BF16)
        xring = ring(cx, "xc", 3, [128, T], F32)
        sqring = ring(cx, "sq", 2, [128, T], BF16)
        rstd = cx.sb("rstd", [128, T], F32)
        wring = ring(cx, "w", 2, [128, DC, 128], BF16)
        wvring = ring(cx, "wv", 2, [128, DC, VW], BF16)
        vbufs = [cx.sb("vbuf%d" % i, [128, E], BF16) for i in range(T // 128)]
        vn = cx.sb("vn", [128, E], BF16)
        nst = max(1, E // 512)
        stats = cx.sb("stats", [128, nst, 6], F32)
        mv = cx.sb("mv", [128, 2], F32)
        fbr = ring(cx, "fb", 3, [128, 128], F32)
        mbr = ring(cx, "mb", 1, [128, GG, 128], BF16)
        psn = cx.ps("psn")
        psr = psring(cx, "ps", 5)
        psa = psring(cx, "pa", 2)
        mv_ = dview(mT)
        for tt in range(S // T):
            tok0 = tt * T
            emit_norm(P, cfg, K, xin, tok0, g, h, 0, xring, sqring, K["ones"], psn, rstd, T=T)
            for j in range(GG):
                w = wring.next()
                P.dma("pool", w.t[:, :, :], wu_d.ap()[j], writes=[w.b])
                ps = psr.next()
                for k in range(DC):
                    P.op("pe", f_mm(ps.t[:, 0:T], w.t[:, k, :], h.t[:, k, :], k == 0, k == DC - 1), reads=[w.b, h.b], writes=[ps.b])
                P.op("act", f_act(uT.t[:, j, :], ps.t[:, 0:T], AF.Gelu, bias=bu.t[:, j:j + 1]), reads=[ps.b, bu.b], writes=[uT.b])
            for jb in range(E // VW):
                wv = wvring.next()
                P.dma("pool", wv.t[:, :, :], wv_d.ap()[jb], writes=[wv.b])
                for tb in range(T // 128):
                    ps = psr.next()
                    for k in range(DC):
                        P.op("pe", f_mm(ps.t[:, 0:VW], h.t[:, k, tb * 128:(tb + 1) * 128], wv.t[:, k, :], k == 0, False),
                             reads=[wv.b, h.b], writes=[ps.b])
                    P.op("pe", f_mm(ps.t[:, 0:VW], e0.t[:, :], bv0.t[:, jb * VW:(jb + 1) * VW], False, True),
                         reads=[e0.b, bv0.b], writes=[ps.b])
                    P.op("act", f_act(vbufs[tb].t[:, jb * VW:(jb + 1) * VW], ps.t[:, 0:VW], AF.Gelu), reads=[ps.b], writes=[vbufs[tb].b])
            for tb in range(T // 128):
                vb = vbufs[tb]
                for i in range(nst):
                    w_ = min(512, E)
                    P.op("dve", lambda e, o_=stats.t[:, i, :], i_=vb.t[:, i * w_:(i + 1) * w_]: e.bn_stats(out=o_, in_=i_),
                         reads=[vb.b], writes=[stats.b])
                P.op("dve", lambda e, o_=mv.t[:, :], i_=stats.t[:, :, :]: e.bn_aggr(out=o_, in_=i_), reads=[stats.b], writes=[mv.b])
                P.op("act", f_act(mv.t[:, 1:2], mv.t[:, 1:2], AF.Sqrt, bias=K["eps"].t[:, 0:1]), reads=[mv.b, K["eps"].b], writes=[mv.b])
                P.op("dve", f_recip(mv.t[:, 1:2], mv.t[:, 1:2]), reads=[mv.b], writes=[mv.b])
                P.op("dve", f_ts(vn.t[:, :], vb.t[:, :], mv.t[:, 0:1], mv.t[:, 1:2], ALU.subtract, ALU.mult),
                     reads=[vb.b, mv.b], writes=[vn.b])
                mb = mbr.next()
                for gi in range(GG):
                    if gi % 4 == 0:
                        pa = psa.next()
                    c0 = (gi % 4) * 128
                    P.op("pe", f_mm(pa.t[:, c0:c0 + 128], vn.t[:, gi * 128:(gi + 1) * 128], wsT.t[:, gi, :], True, True),
                         reads=[vn.b, wsT.b], writes=[pa.b])
                    fb = fbr.next()
                    P.op("dve", f_stt(fb.t[:, :], pa.t[:, c0:c0 + 128], lng.t[:, gi:gi + 1], bsb.t[:, gi, :], ALU.mult, ALU.add),
                         reads=[pa.b, lng.b, bsb.b], writes=[fb.b])
                    P.op("dve", f_tt(mb.t[:, gi, :], fb.t[:, :], uT.t[:, gi, tb * 128:(tb + 1) * 128], ALU.mult),
                         reads=[fb.b, uT.b], writes=[mb.b])
                for g0 in range(0, GG, 8):
                    g1 = min(GG, g0 + 8)
                    P.dma("sp", mv_[:, g0:g1, tok0 + tb * 128:tok0 + (tb + 1) * 128], mb.t[:, g0:g1, :], reads=[mb.b])
        P.emit()


def stage_final_norm(nc, cfg, xin, g_d, out):
    DC, S, T = cfg.DC, cfg.S, cfg.T
    with ExitStack() as es:
        cx = Ctx(nc, es)
        P = Prog(nc)
        K = load_consts(P, cx, cfg, None)
        g = cx.sb("g", [128, DC], F32)
        P.dma("sp", g.t[:, :], g_d.ap(), writes=[g.b])
        hr = ring(cx, "h", 2, [128, DC, T], F32)
        xring = ring(cx, "xc", 4, [128, T], F32)
        sqring = ring(cx, "sq", 3, [128, T], BF16)
        rsr = ring(cx, "rstd", 2, [128, T], F32)
        pnr = psring(cx, "psn", 2)
        ov = dview(out)
        for tt in range(S // T):
            h = hr.next()
            emit_norm(P, cfg, K, xin, tt * T, g, h, 0, xring, sqring, K["ones"], pnr.next(), rsr.next())
            for c0 in range(0, DC, 8):
                c1 = min(DC, c0 + 8)
                P.dma("sp", ov[:, c0:c1, tt * T:(tt + 1) * T], h.t[:, c0:c1, :], reads=[h.b])
        P.emit()


def build_program(cfg):
    nc = bass.Bass("TRN2", target_bir_lowering=False)
    D, S, DC, FC, NH, NA, GG, E, DFF = cfg.D, cfg.S, cfg.DC, cfg.FC, cfg.NH, cfg.NA, cfg.GG, cfg.E, cfg.DFF
    C = make_consts(nc, cfg)

    def di(name, shape, dt=F32):
        return nc.dram_tensor(name, list(shape), dt, kind="ExternalInput")

    def ds(name, shape, dt):
        return nc.dram_tensor(name, list(shape), dt)

    xT = di("xT", [D, S])
    pos = di("pos", [32, S], I32)
    g_attn = di("g_attn", [128, DC])
    wqk = di("wqk", [2 * NH, 128, DC, 128])
    wv = di("wv", [NH * 128 // cfg.VW, 128, DC, cfg.VW])
    wo = di("wo", [DC, 128, DC, 128])
    ffn = []
    for l in range(2):
        ffn.append(dict(g=di("g_ffn%d" % l, [128, DC]), wup=di("wup%d" % l, [2 * FC, 128, DC, 128]),
                        cw=di("cw%d" % l, [128, 3, 2 * FC]), cb=di("cb%d" % l, [128, 2 * FC]),
                        wdn=di("wdn%d" % l, [DC, 128, FC, 128])))
    g_sg = di("g_sg", [128, DC])
    sg_wu = di("sg_wu", [GG, 128, DC, 128])
    sg_wv = di("sg_wv", [E // 256, 128, DC, 256])
    sg_bu = di("sg_bu", [128, GG])
    sg_bv = di("sg_bv", [1, E])
    sg_lng = di("sg_lng", [128, GG])
    sg_lnb = di("sg_lnb", [128, E])
    sg_ws = di("sg_ws", [GG, 128, 128])
    sg_bs = di("sg_bs", [1, GG * 128])
    sg_wo = di("sg_wo", [DC, 128, GG, 128])
    g_fin = di("g_fin", [128, DC])
    outT = nc.dram_tensor("outT", [D, S], F32, kind="ExternalOutput")

    qT = ds("qT", [NH * 128, S], BF16)
    kT = ds("kT", [NH * 128, S], BF16)
    v = ds("v", [S, NH * 128], BF16)
    ksum = ds("ksum", [128, NA, cfg.NBLK], F32)
    cs = ds("cs", [32, 2, S], F32)
    oT = ds("oT", [D, S], BF16)
    gT = ds("gT", [DFF, S], BF16)
    xa = ds("xa", [D, S], F32)
    xb = ds("xb", [D, S], F32)
    wsT = ds("wsT", [128, GG, 128], BF16)
    bsb = ds("bsb", [128, GG, 128], F32)

    import os
    sel = os.environ.get("MK_STAGES")
    sel = set(int(t) for t in sel.split(",")) if sel else set(range(1, 13))
    stages = [
        lambda: (stage_rope(nc, cfg, C, pos, cs), stage_qkv(nc, cfg, C, xT, g_attn, wqk, wv, cs, qT, kT, v, ksum)),
        lambda: stage_moba(nc, cfg, C, qT, kT, v, ksum, oT),
        lambda: stage_dilated(nc, cfg, C, qT, kT, v, oT),
        lambda: stage_linear_res(nc, cfg, DC, oT, wo, xT, xa),
        lambda: stage_ffn_up(nc, cfg, xa, ffn[0]["g"], ffn[0]["wup"], ffn[0]["cw"], ffn[0]["cb"], gT),
        lambda: stage_linear_res(nc, cfg, FC, gT, ffn[0]["wdn"], xa, xb),
        lambda: stage_sg_setup(nc, cfg, C, sg_ws, sg_bs, sg_lnb, wsT, bsb),
        lambda: stage_sg_main(nc, cfg, xb, g_sg, sg_wu, sg_wv, sg_bu, sg_bv, sg_lng, wsT, bsb, oT),
        lambda: stage_linear_res(nc, cfg, GG, oT, sg_wo, xb, xa),
        lambda: stage_ffn_up(nc, cfg, xa, ffn[1]["g"], ffn[1]["wup"], ffn[1]["cw"], ffn[1]["cb"], gT),
        lambda: stage_linear_res(nc, cfg, FC, gT, ffn[1]["wdn"], xa, xb),
        lambda: stage_final_norm(nc, cfg, xb, g_fin, outT),
    ]
    for i, st in enumerate(stages):
        if i + 1 in sel:
            st()
    return nc


def host_prep(cfg, p):
    NH, FC, GG, E = cfg.NH, cfg.FC, cfg.GG, cfg.E
    f = lambda a: np.asarray(a, dtype=np.float32)
    w_in = f(p["attn_w_in"])[0]
    m = {}
    m["g_attn"] = vec_pc(f(p["attn_norm"])[0])
    m["wqk"] = tile_w(w_in[:, :2 * NH * 128])
    m["wv"] = tile_w(w_in[:, 2 * NH * 128:], cfg.VW)
    m["wo"] = tile_w(f(p["attn_w_out"])[0])
    for l in range(2):
        m["g_ffn%d" % l] = vec_pc(f(p["ffn_norm"])[l])
        m["wup%d" % l] = tile_w(f(p["ffn_w_up"])[l])
        m["cw%d" % l] = np.ascontiguousarray(f(p["ffn_conv_w"])[l].reshape(3, 2 * FC, 128).transpose(2, 0, 1))
        m["cb%d" % l] = vec_pc(f(p["ffn_conv_b"])[l])
        m["wdn%d" % l] = tile_w(f(p["ffn_w_down"])[l])
    sw = f(p["sg_w_in"])[0]
    sb = f(p["sg_b_in"])[0]
    m["g_sg"] = vec_pc(f(p["sg_norm"])[0])
    m["sg_wu"] = tile_w(sw[:, :E])
    m["sg_wv"] = tile_w(sw[:, E:], 256)
    m["sg_bu"] = vec_pc(sb[:E])
    m["sg_bv"] = np.ascontiguousarray(sb[E:].reshape(1, E))
    m["sg_lng"] = vec_pc(f(p["sg_v_gain"])[0])
    m["sg_lnb"] = np.ascontiguousarray(np.broadcast_to(f(p["sg_v_bias"])[0], (128, E)))
    m["sg_ws"] = np.ascontiguousarray(f(p["sg_w_s"])[0])
    m["sg_bs"] = np.ascontiguousarray(f(p["sg_b_s"])[0].reshape(1, GG * 128))
    m["sg_wo"] = tile_w(f(p["sg_w_out"])[0])
    m["g_fin"] = vec_pc(f(p["final_norm"]))
    return m


def run_module(cfg, inputs):
    x = np.asarray(inputs["x"], dtype=np.float32)
    positions = np.asarray(inputs["positions"]).astype(np.int32)
    B = x.shape[0]
    import time as _t
    t0 = _t.time()
    shared = host_prep(cfg, inputs)
    t1 = _t.time()
    nc = build_program(cfg)
    print("[mk] host_prep %.1fs build %.1fs" % (t1 - t0, _t.time() - t1), flush=True)
    in_maps = []
    for b in range(B):
        m = dict(shared)
        m["xT"] = np.ascontiguousarray(x[b].T)
        m["pos"] = np.ascontiguousarray(np.broadcast_to(positions[b], (32, cfg.S)))
        in_maps.append(m)
    t2 = _t.time()
    res = run_bass_kernel_spmd(nc, in_maps, core_ids=list(range(B)))
    print("[mk] launch %.1fs" % (_t.time() - t2), flush=True)
    return np.stack([np.ascontiguousarray(res.results[b]["outT"].T) for b in range(B)], 0)


def kernel(**inputs):
    cfg = Cfg()
    return run_module(cfg, inputs)
```

```python
import numpy as np
from contextlib import ExitStack
import concourse.bass as bass
import concourse.mybir as mybir
from concourse.bass_utils import run_bass_kernel_spmd

F32 = mybir.dt.float32
BF16 = mybir.dt.bfloat16
I32 = mybir.dt.int32
AF = mybir.ActivationFunctionType
ALU = mybir.AluOpType
AX = mybir.AxisListType

NEG = -30000.0
ENGS = ("pe", "act", "dve", "pool", "sp")
NRING = 12


class Cfg:
    def __init__(self, D=4096, S=4096, NA=24, NBG=8, DFF=14336, B=2):
        self.D, self.S, self.NA, self.NBG, self.DFF, self.B = D, S, NA, NBG, DFF, B
        self.DH = 128
        self.NB = 3 * NBG
        self.NH = NA + self.NB
        self.NHO = NA + NBG
        assert self.NHO * 128 == D
        self.QKV = 3 * self.NH * 128
        self.DC = D // 128
        self.FC = DFF // 128
        self.T = 512
        self.NT = S // self.T
        self.MB = 256
        self.NBLK = S // self.MB
        self.E = D
        self.GG = self.E // 128
        self.EPS = 1e-5
        self.PATS = ((128, 1), (512, 4), (2048, 16))
        self.VW = 512 if (self.NH * 128) % 512 == 0 else 256
        self.SCALE = 128 ** -0.5


class Buf:
    __slots__ = ("w", "r")

    def __init__(self):
        self.w = None
        self.r = []


class Op:
    __slots__ = ("eng", "fn", "deps", "signal", "ev", "dma", "di")

    def __init__(self, eng, fn, dma):
        self.eng, self.fn, self.dma = eng, fn, dma
        self.deps = []
        self.signal = False
        self.ev = None
        self.di = -1


class Prog:
    def __init__(self, nc):
        self.nc = nc
        self.ops = []

    def op(self, eng, fn, reads=(), writes=(), dma=False):
        o = Op(eng, fn, dma)
        deps = {}
        for b in reads:
            if b.w is not None:
                deps[id(b.w)] = b.w
        for b in writes:
            if b.w is not None:
                deps[id(b.w)] = b.w
            for r in b.r:
                deps[id(r)] = r
        for d in deps.values():
            if d is o:
                continue
            if d.eng == "pe" and eng == "pe" and not d.dma and not dma:
                continue
            o.deps.append(d)
            d.signal = True
        for b in writes:
            b.w = o
            b.r = []
        for b in reads:
            if b.w is not o:
                if not dma:
                    b.r = [r for r in b.r if r.dma or r.eng != eng]
                b.r.append(o)
        self.ops.append(o)
        return o

    def dma(self, eng, out, in_, reads=(), writes=()):
        return self.op(eng, lambda e: e.dma_start(out=out, in_=in_), reads, writes, dma=True)

    def emit(self, name=None):
        nc = self.nc
        with ExitStack() as es:
            if not hasattr(nc, "_mk_sems"):
                gs = ExitStack()
                c_ = {e: gs.enter_context(nc.semaphore("c_" + e)) for e in ENGS}
                d_ = {e: [gs.enter_context(nc.semaphore("d_%s%d" % (e, i))) for i in range(NRING)]
                      for e in ("sp", "pool")}
                nc._mk_sems = (gs, c_, d_)
                nc._mk_pool_n = 0
            _, csem, dsem = nc._mk_sems
            cnt = {e: 0 for e in ENGS}
            dcnt = {e: 0 for e in dsem}
            dcnt["pool"] = nc._mk_pool_n
            pool_base = nc._mk_pool_n
            by_eng = {e: [] for e in ENGS}
            dlist = {e: [] for e in dsem}
            for o in self.ops:
                if o.dma:
                    i = dcnt[o.eng]
                    dcnt[o.eng] += 1
                    o.di = i
                    o.ev = (dsem[o.eng][i % NRING], 16 * (i // NRING + 1))
                    dlist[o.eng].append(o)
                    if o.eng == "pool":
                        nc._mk_pool_n = i + 1
                elif o.signal:
                    cnt[o.eng] += 1
                    o.ev = (csem[o.eng], cnt[o.eng])
                by_eng[o.eng].append(o)
            block = es.enter_context(nc.Block())

            def body(e, eng):
                waited = {}

                def wait(ev):
                    sem, val = ev
                    k = id(sem)
                    if waited.get(k, 0) < val:
                        e.wait_ge(sem, val)
                        waited[k] = val

                for o in by_eng[eng]:
                    need = {}
                    for d in o.deps:
                        sm, val = d.ev
                        k_ = id(sm)
                        if k_ not in need or need[k_][1] < val:
                            need[k_] = (sm, val)
                    for ev in need.values():
                        wait(ev)
                    if o.dma:
                        li = o.di - (pool_base if eng == "pool" else 0)
                        if li >= NRING:
                            wait(dlist[eng][li - NRING].ev)
                    ins = o.fn(e)
                    if o.dma:
                        ins.then_inc(o.ev[0], 16)
                    elif o.signal:
                        ins.then_inc(o.ev[0], 1)
                if eng in dlist:
                    for o in dlist[eng][-NRING:]:
                        wait(o.ev)

            block.tensor(lambda e: body(e, "pe"))
            block.scalar(lambda e: body(e, "act"))
            block.vector(lambda e: body(e, "dve"))
            block.gpsimd(lambda e: body(e, "pool"))
            block.sync(lambda e: body(e, "sp"))
        _, csem, dsem = nc._mk_sems
        allsems = list(csem.values()) + list(dsem["sp"])
        with nc.Block() as blk2:
            def clr(e):
                for sm in allsems:
                    e.sem_clear(sm)
            blk2.sync(clr)
        self.ops = []


class Tile:
    def __init__(self, t, nsub=0):
        self.t = t
        self.b = Buf()
        self.sub = [Buf() for _ in range(nsub)]


class Ctx:
    _n = 0

    def __init__(self, nc, es):
        self.nc, self.es = nc, es
        Ctx._n += 1
        self.pre = "s%d_" % Ctx._n

    def sb(self, name, shape, dt, nsub=0):
        return Tile(self.es.enter_context(self.nc.sbuf_tensor(self.pre + name, list(shape), dt)), nsub)

    def ps(self, name, shape=(128, 512), dt=F32, nsub=0):
        return Tile(self.es.enter_context(self.nc.psum_tensor(self.pre + name, list(shape), dt)), nsub)


class Ring:
    def __init__(self, tiles):
        self.tiles = tiles
        self.i = 0

    def next(self):
        t = self.tiles[self.i % len(self.tiles)]
        self.i += 1
        return t


def ring(cx, name, n, shape, dt):
    return Ring([cx.sb("%s%d" % (name, i), shape, dt) for i in range(n)])


def psring(cx, name, n):
    return Ring([cx.ps("%s%d" % (name, i)) for i in range(n)])


def f_mm(out, lhsT, rhs, start=True, stop=True):
    return lambda e: e.matmul(out, lhsT, rhs, start=start, stop=stop)


def f_tr(out, in_, ident):
    return lambda e: e.transpose(out, in_, ident)


def f_act(out, in_, func, **kw):
    return lambda e: e.activation(out=out, in_=in_, func=func, **kw)


def f_ts(out, in0, s1, s2, op0, op1=None):
    if op1 is None:
        return lambda e: e.tensor_scalar(out=out, in0=in0, scalar1=s1, scalar2=None, op0=op0)
    return lambda e: e.tensor_scalar(out=out, in0=in0, scalar1=s1, scalar2=s2, op0=op0, op1=op1)


def f_stt(out, in0, scalar, in1, op0, op1):
    return lambda e: e.scalar_tensor_tensor(out=out, in0=in0, scalar=scalar, in1=in1, op0=op0, op1=op1)


def f_tt(out, in0, in1, op):
    return lambda e: e.tensor_tensor(out=out, in0=in0, in1=in1, op=op)


def f_copy(out, in_):
    return lambda e: e.tensor_copy(out=out, in_=in_)


def f_recip(out, in_):
    return lambda e: e.reciprocal(out=out, in_=in_)


def f_memset(ap, v):
    return lambda e: e.memset(ap, v)


def dview(h):
    return h.ap().rearrange("(c p) t -> p c t", p=128)


def emit_norm(P, cfg, K, xin, tok0, g, h, hcol0, xring, sqring, ones, psn, rstd, T=None):
    T, DC = (T or cfg.T), cfg.DC
    xv = dview(xin)
    for c in range(DC):
        xc = xring.next()
        P.dma("sp", xc.t[:, 0:T], xv[:, c, tok0:tok0 + T], writes=[xc.b])
        sq = sqring.next()
        P.op("act", f_act(sq.t[:, 0:T], xc.t[:, 0:T], AF.Square), reads=[xc.b], writes=[sq.b])
        P.op("pe", f_mm(psn.t[:, 0:T], ones.t[:, :], sq.t[:, 0:T], c == 0, c == DC - 1),
             reads=[ones.b, sq.b], writes=[psn.b])
    P.op("act", f_act(rstd.t[:, 0:T], psn.t[:, 0:T], AF.Sqrt, scale=1.0 / cfg.D, bias=K["eps"].t[:, 0:1]),
         reads=[psn.b, K["eps"].b], writes=[rstd.b])
    P.op("dve", f_recip(rstd.t[:, 0:T], rstd.t[:, 0:T]), reads=[rstd.b], writes=[rstd.b])
    for c in range(DC):
        xc = xring.next()
        P.dma("sp", xc.t[:, 0:T], xv[:, c, tok0:tok0 + T], writes=[xc.b])
        P.op("dve", f_stt(h.t[:, c, hcol0:hcol0 + T], xc.t[:, 0:T], g.t[:, c:c + 1], rstd.t[:, 0:T],
                          ALU.mult, ALU.mult),
             reads=[xc.b, g.b, rstd.b], writes=[h.b])


def load_consts(P, cx, cfg, consts):
    K = {}
    K["ones"] = cx.sb("k_ones", [128, 128], BF16)
    P.op("dve", f_memset(K["ones"].t[:, :], 1.0), writes=[K["ones"].b])
    K["eps"] = cx.sb("k_eps", [128, 1], F32)
    P.op("dve", f_memset(K["eps"].t[:, :], cfg.EPS), writes=[K["eps"].b])
    return K


def stage_ffn_up(nc, cfg, xin, g_d, wup_d, cw_d, cb_d, gT):
    T, DC, FC, S = cfg.T, cfg.DC, cfg.FC, cfg.S
    NSUB = 2 if S % (2 * T) == 0 else 1
    TS = T * NSUB
    with ExitStack() as es:
        cx = Ctx(nc, es)
        P = Prog(nc)
        K = load_consts(P, cx, cfg, None)
        g = cx.sb("g", [128, DC], F32)
        P.dma("sp", g.t[:, :], g_d.ap(), writes=[g.b])
        cw = cx.sb("cw", [128, 3, 2 * FC], F32)
        P.dma("sp", cw.t[:, :, :], cw_d.ap(), writes=[cw.b])
        cb = cx.sb("cb", [128, 2 * FC], F32)
        P.dma("sp", cb.t[:, :], cb_d.ap(), writes=[cb.b])
        carry = [cx.sb("carry%d" % i, [128, FC, 2], F32, nsub=FC) for i in range(2)]
        for cr in carry:
            P.op("dve", f_memset(cr.t[:, :, :], 0.0), writes=cr.sub)
        h = cx.sb("h", [128, DC, TS], BF16)
        xring = ring(cx, "xc", 4, [128, T], F32)
        sqring = ring(cx, "sq", 3, [128, T], BF16)
        rstd = cx.sb("rstd", [128, T], F32)
        wring = ring(cx, "w", 4, [128, DC, 128], BF16)
        abuf = ring(cx, "ab", 4, [128, T + 4], F32)
        cbuf = ring(cx, "cbf", 4, [128, T], F32)
        sgb = ring(cx, "sg", 2, [128, T], F32)
        gob = ring(cx, "go", 3, [128, T], BF16)
        psn = cx.ps("psn")
        psr = psring(cx, "ps", 7)
        for st in range(S // TS):
            for sub in range(NSUB):
                emit_norm(P, cfg, K, xin, st * TS + sub * T, g, h, sub * T, xring, sqring, K["ones"], psn, rstd)
            for j in range(FC):
                ws = []
                for half in range(2):
                    w = wring.next()
                    P.dma("pool", w.t[:, :, :], wup_d.ap()[half * FC + j], writes=[w.b])
                    ws.append(w)
                for sub in range(NSUB):
                    tok0 = st * TS + sub * T
                    cs = []
                    for half in range(2):
                        ps = psr.next()
                        for k in range(DC):
                            P.op("pe", f_mm(ps.t[:, :], ws[half].t[:, k, :], h.t[:, k, sub * T:(sub + 1) * T],
                                            k == 0, k == DC - 1),
                                 reads=[ws[half].b, h.b], writes=[ps.b])
                        ch = half * FC + j
                        ab = abuf.next()
                        cr = carry[half]
                        P.op("dve", f_copy(ab.t[:, 0:2], cr.t[:, j, :]), reads=[cr.sub[j]], writes=[ab.b])
                        P.op("act", f_act(ab.t[:, 2:T + 2], ps.t[:, :], AF.Copy), reads=[ps.b], writes=[ab.b])
                        P.op("dve", f_copy(cr.t[:, j, :], ab.t[:, T:T + 2]), reads=[ab.b], writes=[cr.sub[j]])
                        c_ = cbuf.next()
                        P.op("dve", f_ts(c_.t[:, :], ab.t[:, 2:T + 2], cw.t[:, 2, ch:ch + 1], cb.t[:, ch:ch + 1],
                                         ALU.mult, ALU.add), reads=[ab.b, cw.b, cb.b], writes=[c_.b])
                        P.op("dve", f_stt(c_.t[:, :], ab.t[:, 1:T + 1], cw.t[:, 1, ch:ch + 1], c_.t[:, :],
                                          ALU.mult, ALU.add), reads=[ab.b, c_.b, cw.b], writes=[c_.b])
                        P.op("dve", f_stt(c_.t[:, :], ab.t[:, 0:T], cw.t[:, 0, ch:ch + 1], c_.t[:, :],
                                          ALU.mult, ALU.add), reads=[ab.b, c_.b, cw.b], writes=[c_.b])
                        cs.append(c_)
                    sg = sgb.next()
                    P.op("act", f_act(sg.t[:, :], cs[0].t[:, :], AF.Silu), reads=[cs[0].b], writes=[sg.b])
                    go = gob.next()
                    P.op("dve", f_tt(go.t[:, :], sg.t[:, :], cs[1].t[:, :], ALU.mult),
                         reads=[sg.b, cs[1].b], writes=[go.b])
                    P.dma("sp", gT.ap()[j * 128:(j + 1) * 128, tok0:tok0 + T], go.t[:, :], reads=[go.b])
        P.emit()


def stage_linear_res(nc, cfg, KC, actT, w_d, xin, xout):
    T, DC, S = cfg.T, cfg.DC, cfg.S
    NSUB = 2 if (KC <= 32 and S % (2 * T) == 0) else 1
    TS = T * NSUB
    with ExitStack() as es:
        cx = Ctx(nc, es)
        P = Prog(nc)
        KG = 16 // NSUB
        NG = (KC + KG - 1) // KG
        abufs = [cx.sb("a%d" % i, [128, KC, TS], BF16, nsub=NG) for i in range(2 if KC <= 32 else 1)]
        wring = ring(cx, "w", 2, [128, KC, 128], BF16)
        xr = ring(cx, "xr", 3, [128, T], F32)
        xo = ring(cx, "xo", 3, [128, T], F32)
        psr = psring(cx, "ps", 4)
        av = dview(actT)
        xv = dview(xin)
        ov = dview(xout)
        for tt in range(S // TS):
            t0 = tt * TS
            a = abufs[tt % len(abufs)]
            for gk in range(NG):
                k0, k1 = gk * KG, min(KC, (gk + 1) * KG)
                P.dma("sp", a.t[:, k0:k1, :], av[:, k0:k1, t0:t0 + TS], writes=[a.sub[gk]])
            for c in range(DC):
                w = wring.next()
                P.dma("pool", w.t[:, :, :], w_d.ap()[c], writes=[w.b])
                for sub in range(NSUB):
                    tok0 = t0 + sub * T
                    ps = psr.next()
                    for k in range(KC):
                        P.op("pe", f_mm(ps.t[:, :], w.t[:, k, :], a.t[:, k, sub * T:(sub + 1) * T], k == 0, k == KC - 1),
                             reads=[w.b, a.sub[k // KG]], writes=[ps.b])
                    x_ = xr.next()
                    P.dma("sp", x_.t[:, :], xv[:, c, tok0:tok0 + T], writes=[x_.b])
                    o_ = xo.next()
                    P.op("dve", f_tt(o_.t[:, :], ps.t[:, :], x_.t[:, :], ALU.add), reads=[ps.b, x_.b], writes=[o_.b])
                    P.dma("sp", ov[:, c, tok0:tok0 + T], o_.t[:, :], reads=[o_.b])
        P.emit()


def tile_w(W, cw=128):
    K, N = W.shape
    return np.ascontiguousarray(W.reshape(K // 128, 128, N // cw, cw).transpose(2, 1, 0, 3))


def vec_pc(v):
    return np.ascontiguousarray(v.reshape(-1, 128).T)


def np_bf16(a):
    import ml_dtypes
    return np.asarray(a, dtype=np.float32).astype(ml_dtypes.bfloat16)


def make_consts(nc, cfg):
    C = {}
    half = 16
    invf = np.power(np.float32(500000.0), -np.arange(half, dtype=np.float32) * np.float32(2.0 / 32.0)).astype(np.float32)
    C["invf"] = nc.inline_tensor(np.concatenate([invf, invf]).reshape(32, 1).astype(np.float32), "k_invf")
    Pm = np.zeros((128, 32), np.float32)
    for m in range(16):
        Pm[m + 16, m] = -1.0
        Pm[m, m + 16] = 1.0
    C["Pm"] = nc.inline_tensor(Pm, "k_Pm")
    C["identf"] = nc.inline_tensor(np.eye(128, dtype=np.float32), "k_identf")
    C["identb"] = nc.inline_tensor(np_bf16(np.eye(128)), "k_identb")
    NBLK = cfg.NBLK
    E = np.zeros((128, NBLK * 128), np.float32)
    for n in range(NBLK):
        E[n % 16, n * 128:(n + 1) * 128] = 1.0
    C["esel"] = nc.inline_tensor(np_bf16(E), "k_esel")
    j = np.arange(128)[:, None, None]
    r = np.arange(4)[None, :, None]
    i = np.arange(512)[None, None, :]
    C["cmask"] = nc.inline_tensor(np_bf16(np.where(r * 128 + j <= i, 0.0, NEG)), "k_cmask")
    jj = np.arange(128)[:, None]
    ii = np.arange(128)[None, :]
    cur = np.where(jj <= ii, 0.0, NEG)
    prv = np.where(jj >= ii, 0.0, NEG)
    bm = np.stack([np.concatenate([cur, np.full((128, 128), NEG)], 1), np.concatenate([cur, prv], 1)], 1)
    C["bmask"] = nc.inline_tensor(np_bf16(bm), "k_bmask")
    NKT = cfg.S // 128
    pm = np.full((NKT, 16), NEG, np.float32)
    om = np.full((NKT, 16), -1.0e9, np.float32)
    for qi in range(NKT):
        qb = (qi * 128) // cfg.MB
        pm[qi, :qb] = 0.0
        om[qi, qb] = 0.0
    C["pastm"] = nc.inline_tensor(np.ascontiguousarray(np.broadcast_to(pm.reshape(1, NKT * 16), (128, NKT * 16))), "k_pastm")
    C["ownm"] = nc.inline_tensor(np.ascontiguousarray(np.broadcast_to(om.reshape(1, NKT * 16), (128, NKT * 16))), "k_ownm")
    C["tril"] = nc.inline_tensor(np.tril(np.ones((128, 128), np.float32)), "k_tril")
    return C


def stage_rope(nc, cfg, C, pos_d, cs_d):
    S = cfg.S
    W = min(S, 2048)
    with ExitStack() as es:
        cx = Ctx(nc, es)
        P = Prog(nc)
        invf = cx.sb("invf", [32, 1], F32)
        P.dma("sp", invf.t[:, :], C["invf"].ap(), writes=[invf.b])
        for c0 in range(0, S, W):
            S_ = W
            posi = cx.sb("posi%d" % c0, [32, S_], I32)
            P.dma("sp", posi.t[:, :], pos_d.ap()[:, c0:c0 + W], writes=[posi.b])
            ang = cx.sb("ang%d" % c0, [32, S_], F32)
            P.op("dve", f_copy(ang.t[:, :], posi.t[:, :]), reads=[posi.b], writes=[ang.b])
            P.op("dve", f_ts(ang.t[:, :], ang.t[:, :], invf.t[:, 0:1], None, ALU.mult), reads=[ang.b, invf.b], writes=[ang.b])
            ki = cx.sb("ki%d" % c0, [32, S_], I32)
            kf = cx.sb("kf%d" % c0, [32, S_], F32)
            mk = cx.sb("mk%d" % c0, [32, S_], F32)
            for idx, (nm, shift) in enumerate((("cos", 0.25), ("sin", 0.0))):
                y = cx.sb("y_%s%d" % (nm, c0), [32, S_], F32)
                P.op("dve", f_ts(y.t[:, :], ang.t[:, :], float(1.0 / (2.0 * np.pi)), shift, ALU.mult, ALU.add),
                     reads=[ang.b], writes=[y.b])
                P.op("dve", f_copy(ki.t[:, :], y.t[:, :]), reads=[y.b], writes=[ki.b])
                P.op("dve", f_copy(kf.t[:, :], ki.t[:, :]), reads=[ki.b], writes=[kf.b])
                P.op("dve", f_tt(y.t[:, :], y.t[:, :], kf.t[:, :], ALU.subtract), reads=[y.b, kf.b], writes=[y.b])
                P.op("dve", f_ts(mk.t[:, :], y.t[:, :], 0.5, None, ALU.is_gt), reads=[y.b], writes=[mk.b])
                P.op("dve", f_tt(y.t[:, :], y.t[:, :], mk.t[:, :], ALU.subtract), reads=[y.b, mk.b], writes=[y.b])
                P.op("dve", f_ts(mk.t[:, :], y.t[:, :], -0.5, None, ALU.is_lt), reads=[y.b], writes=[mk.b])
                P.op("dve", f_tt(y.t[:, :], y.t[:, :], mk.t[:, :], ALU.add), reads=[y.b, mk.b], writes=[y.b])
                P.op("act", f_act(y.t[:, :], y.t[:, :], AF.Sin, scale=float(2.0 * np.pi * (1.0 - 1e-6))),
                     reads=[y.b], writes=[y.b])
                P.dma("sp", cs_d.ap()[:, idx, c0:c0 + W], y.t[:, :], reads=[y.b])
        P.emit()


def stage_qkv(nc, cfg, C, xin, g_d, wqk_d, wv_d, cs_d, qT, kT, v, ksum_d):
    T, DC, S, NH, NA = cfg.T, cfg.DC, cfg.S, cfg.NH, cfg.NA
    VW = cfg.VW
    NSUB = 2 if S % (2 * T) == 0 else 1
    TS = T * NSUB
    with ExitStack() as es:
        cx = Ctx(nc, es)
        P = Prog(nc)
        K = load_consts(P, cx, cfg, None)
        g = cx.sb("g", [128, DC], F32)
        P.dma("sp", g.t[:, :], g_d.ap(), writes=[g.b])
        Pm = cx.sb("Pm", [128, 32], F32)
        P.dma("sp", Pm.t[:, :], C["Pm"].ap(), writes=[Pm.b])
        csr = ring(cx, "cs", 4, [32, 2, T], F32)
        ksum = cx.sb("ksum", [128, NA, cfg.NBLK], F32)
        h = cx.sb("h", [128, DC, TS], BF16)
        xring = ring(cx, "xc", 4, [128, T], F32)
        sqring = ring(cx, "sq", 3, [128, T], BF16)
        rstd = cx.sb("rstd", [128, T], F32)
        wring = ring(cx, "w", 3, [128, DC, 128], BF16)
        wvring = ring(cx, "wv", 2, [128, DC, VW], BF16)
        qfr = ring(cx, "qf", 4, [128, T], F32)
        tmr = ring(cx, "tm", 4, [32, T], F32)
        qbr = ring(cx, "qb", 3, [128, T], BF16)
        vbr = ring(cx, "vb", 3, [128, VW], BF16)
        psn = cx.ps("psn")
        psr = psring(cx, "ps", 5)
        psp = psring(cx, "pp", 2)
        for st in range(S // TS):
            cst = []
            for sub in range(NSUB):
                emit_norm(P, cfg, K, xin, st * TS + sub * T, g, h, sub * T, xring, sqring, K["ones"], psn, rstd)
                cs = csr.next()
                tok0 = st * TS + sub * T
                P.dma("sp", cs.t[:, :, :], cs_d.ap()[:, :, tok0:tok0 + T], writes=[cs.b])
                cst.append(cs)
            pending = [None]
            for j in range(2 * NH):
                isk = j >= NH
                hd = j - NH if isk else j
                w = wring.next()
                P.dma("pool", w.t[:, :, :], wqk_d.ap()[j], writes=[w.b])
                for sub in range(NSUB):
                    tok0 = st * TS + sub * T
                    ps = psr.next()
                    for k in range(DC):
                        P.op("pe", f_mm(ps.t[:, :], w.t[:, k, :], h.t[:, k, sub * T:(sub + 1) * T], k == 0, k == DC - 1),
                             reads=[w.b, h.b], writes=[ps.b])
                    qf = qfr.next()
                    P.op("act", f_act(qf.t[:, :], ps.t[:, :], AF.Copy), reads=[ps.b], writes=[qf.b])

                    def rot(qf=qf, sub=sub, tok0=tok0, isk=isk, hd=hd):
                        pp = psp.next()
                        P.op("pe", f_mm(pp.t[0:32, :], Pm.t[:, :], qf.t[:, :]), reads=[Pm.b, qf.b], writes=[pp.b])
                        t1 = tmr.next()
                        t2 = tmr.next()
                        P.op("dve", f_tt(t1.t[:, :], qf.t[0:32, :], cst[sub].t[:, 0, :], ALU.mult),
                             reads=[qf.b, cst[sub].b], writes=[t1.b])
                        P.op("dve", f_tt(t2.t[:, :], pp.t[0:32, :], cst[sub].t[:, 1, :], ALU.mult),
                             reads=[pp.b, cst[sub].b], writes=[t2.b])
                        P.op("dve", f_tt(qf.t[0:32, :], t1.t[:, :], t2.t[:, :], ALU.add),
                             reads=[t1.b, t2.b], writes=[qf.b])
                        qb = qbr.next()
                        P.op("act", f_act(qb.t[:, :], qf.t[:, :], AF.Copy), reads=[qf.b], writes=[qb.b])
                        if isk and hd < NA:
                            b0 = tok0 // cfg.MB
                            nb = T // cfg.MB
                            P.op("dve", lambda e, o_=ksum.t[:, hd, b0:b0 + nb], i_=qf.t[:, :].rearrange("p (b m) -> p b m", m=cfg.MB):
                                 e.tensor_reduce(out=o_, in_=i_, axis=AX.X, op=ALU.add),
                                 reads=[qf.b], writes=[ksum.b])
                        dst = kT if isk else qT
                        P.dma("sp", dst.ap()[hd * 128:(hd + 1) * 128, tok0:tok0 + T], qb.t[:, :], reads=[qb.b])

                    if pending[0] is not None:
                        pending[0]()
                    pending[0] = rot
            if pending[0] is not None:
                pending[0]()
                pending[0] = None
            for jb in range(NH * 128 // VW):
                wv = wvring.next()
                P.dma("pool", wv.t[:, :, :], wv_d.ap()[jb], writes=[wv.b])
                for tb in range(TS // 128):
                    tok0 = st * TS + tb * 128
                    ps = psr.next()
                    for k in range(DC):
                        P.op("pe", f_mm(ps.t[:, 0:VW], h.t[:, k, tb * 128:(tb + 1) * 128], wv.t[:, k, :], k == 0, k == DC - 1),
                             reads=[wv.b, h.b], writes=[ps.b])
                    vb = vbr.next()
                    P.op("act", f_act(vb.t[:, :], ps.t[:, 0:VW], AF.Copy), reads=[ps.b], writes=[vb.b])
                    P.dma("sp", v.ap()[tok0:tok0 + 128, jb * VW:(jb + 1) * VW], vb.t[:, :], reads=[vb.b])
        P.dma("sp", ksum_d.ap(), ksum.t[:, :, :], reads=[ksum.b])
        P.emit()


def stage_moba(nc, cfg, C, qT, kT, v, ksum_d, oT):
    S, NA, NBLK, T = cfg.S, cfg.NA, cfg.NBLK, cfg.T
    NKT = S // 128
    with ExitStack() as es:
        cx = Ctx(nc, es)
        P = Prog(nc)
        K = load_consts(P, cx, cfg, None)
        ones = K["ones"]
        identf = cx.sb("identf", [128, 128], F32)
        P.dma("sp", identf.t[:, :], C["identf"].ap(), writes=[identf.b])
        identb = cx.sb("identb", [128, 128], BF16)
        P.dma("sp", identb.t[:, :], C["identb"].ap(), writes=[identb.b])
        esel = cx.sb("esel", [128, NBLK * 128], BF16)
        P.dma("sp", esel.t[:, :], C["esel"].ap(), writes=[esel.b])
        cmask = cx.sb("cmask", [128, 4, 512], BF16)
        P.dma("sp", cmask.t[:, :, :], C["cmask"].ap(), writes=[cmask.b])
        ksum = cx.sb("ksum", [128, NA, NBLK], F32)
        P.dma("sp", ksum.t[:, :, :], ksum_d.ap(), writes=[ksum.b])
        qr = ring(cx, "q", 2, [128, S], BF16)
        kr = ring(cx, "k", 2, [128, S], BF16)
        vr = ring(cx, "v", 2, [128, NKT, 128], BF16)
        kmr = ring(cx, "km", 2, [128, 16], BF16)
        btr = ring(cx, "bt", 2, [128, S], BF16)
        gmr = ring(cx, "gm", 4, [128, 16], F32)
        mxr = ring(cx, "mx", 4, [128, 8], F32)
        bqr = ring(cx, "bq", 6, [128, 128], F32)
        ptr = ring(cx, "pt", 3, [128, T], BF16)
        rzr = ring(cx, "rz", 2, [128, T], F32)
        obr = ring(cx, "ob", 2, [128, T], BF16)
        psr = psring(cx, "ps", 3)
        por = psring(cx, "po", 2)
        pzr = psring(cx, "pz", 2)
        psm = cx.ps("psm", nsub=7)
        NB16 = min(NBLK, 16)
        assert NBLK <= 16 and NBLK >= 8
        pastm = cx.sb("pastm", [128, NKT * 16], F32)
        P.dma("sp", pastm.t[:, :], C["pastm"].ap(), writes=[pastm.b])
        ownm = cx.sb("ownm", [128, NKT * 16], F32)
        P.dma("sp", ownm.t[:, :], C["ownm"].ap(), writes=[ownm.b])
        gmar = ring(cx, "gma", 2, [128, NKT * 16], F32)
        mxar = ring(cx, "mxa", 2, [128, NKT, 8], F32)
        bqar = ring(cx, "bqa", 2, [128, NKT, 128], F32)
        for t_ in bqar.tiles:
            P.op("dve", f_memset(t_.t[:, :, :], 0.0), writes=[t_.b])
        v3 = lambda ap: ap.rearrange("p (a b) -> p a b", b=16)
        st = {}

        def load(hd):
            q, k, vv, km, bt = qr.next(), kr.next(), vr.next(), kmr.next(), btr.next()
            P.dma("sp", q.t[:, :], qT.ap()[hd * 128:(hd + 1) * 128, :], writes=[q.b])
            P.dma("sp", k.t[:, :], kT.ap()[hd * 128:(hd + 1) * 128, :], writes=[k.b])
            vsrc = v.ap()[:, hd * 128:(hd + 1) * 128].rearrange("(n p) d -> p n d", p=128)
            for n0 in range(0, NKT, 8):
                P.dma("sp", vv.t[:, n0:n0 + 8, :], vsrc[:, n0:n0 + 8, :], writes=[vv.b])
            P.op("dve", f_memset(km.t[:, :], 0.0), writes=[km.b])
            P.op("dve", f_copy(km.t[:, 0:NBLK], ksum.t[:, hd, :]), reads=[ksum.b], writes=[km.b])
            st[hd] = (q, k, vv, km, bt)

        def gate_a(hd):
            q, k, vv, km, bt = st[hd]
            for qi in range(NKT):
                P.op("pe", f_mm(psm.t[:, qi * 16:(qi + 1) * 16], q.t[:, qi * 128:(qi + 1) * 128], km.t[:, 0:16]),
                     reads=[q.b, km.b], writes=[psm.b])
            gma, mxa, bqa = gmar.next(), mxar.next(), bqar.next()
            P.op("dve", f_tt(gma.t[:, :], psm.t[:, 0:NKT * 16], pastm.t[:, :], ALU.add), reads=[psm.b, pastm.b], writes=[gma.b])
            for qi in range(NKT):
                P.op("dve", lambda e, o_=mxa.t[:, qi, :], i_=gma.t[:, qi * 16:(qi + 1) * 16]: e.max(out=o_, in_=i_),
                     reads=[gma.b], writes=[mxa.b])
            for qi in range(NKT):
                P.op("dve", f_ts(gma.t[:, qi * 16:(qi + 1) * 16], gma.t[:, qi * 16:(qi + 1) * 16], mxa.t[:, qi, 2:3], None, ALU.is_ge),
                     reads=[gma.b, mxa.b], writes=[gma.b])
            P.op("dve", f_ts(bqa.t[:, :, 0:16], v3(gma.t[:, :]), -NEG, NEG, ALU.mult, ALU.add), reads=[gma.b], writes=[bqa.b])
            P.op("dve", f_tt(bqa.t[:, :, 0:16], bqa.t[:, :, 0:16], v3(pastm.t[:, :]), ALU.min), reads=[bqa.b, pastm.b], writes=[bqa.b])
            P.op("dve", f_tt(bqa.t[:, :, 0:16], bqa.t[:, :, 0:16], v3(ownm.t[:, :]), ALU.max), reads=[bqa.b, ownm.b], writes=[bqa.b])
            st[hd] = (q, k, vv, km, bt, bqa)

        def gate_b(hd):
            q, k, vv, km, bt, bqa = st[hd]
            for grp in range(NKT // 4):
                pst = psr.next()
                for j_ in range(4):
                    P.op("pe", f_tr(pst.t[:, j_ * 128:(j_ + 1) * 128], bqa.t[:, grp * 4 + j_, :], identf.t[:, :]),
                         reads=[bqa.b, identf.b], writes=[pst.b])
                P.op("act", f_act(bt.t[:, grp * 512:(grp + 1) * 512], pst.t[:, :], AF.Copy), reads=[pst.b], writes=[bt.b])

        def attention(hd):
            q, k, vv, km, bt, bqa = st.pop(hd)
            for g in range(S // T):
                nkt = 4 * (g + 1)
                po, pz = por.next(), pzr.next()
                qs = q.t[:, g * T:(g + 1) * T]

                def qk(kt):
                    ps = psr.next()
                    n = kt // 2
                    diag = kt >= 4 * g
                    P.op("pe", f_mm(ps.t[:, :], k.t[:, kt * 128:(kt + 1) * 128], qs, True, False),
                         reads=[k.b, q.b], writes=[ps.b])
                    P.op("pe", f_mm(ps.t[:, :], esel.t[:, n * 128:(n + 1) * 128], bt.t[:, g * T:(g + 1) * T], False, not diag),
                         reads=[esel.b, bt.b], writes=[ps.b])
                    if diag:
                        P.op("pe", f_mm(ps.t[:, :], identb.t[:, :], cmask.t[:, kt - 4 * g, :], False, True),
                             reads=[identb.b, cmask.b], writes=[ps.b])
                    return ps

                ps_next = qk(0)
                for kt in range(nkt):
                    ps = ps_next
                    pt = ptr.next()
                    P.op("act", f_act(pt.t[:, :], ps.t[:, :], AF.Exp, scale=float(cfg.SCALE)), reads=[ps.b], writes=[pt.b])
                    if kt + 1 < nkt:
                        ps_next = qk(kt + 1)
                    P.op("pe", f_mm(po.t[:, :], vv.t[:, kt, :], pt.t[:, :], kt == 0, kt == nkt - 1),
                         reads=[vv.b, pt.b], writes=[po.b])
                    P.op("pe", f_mm(pz.t[:, :], ones.t[:, :], pt.t[:, :], kt == 0, kt == nkt - 1),
                         reads=[ones.b, pt.b], writes=[pz.b])
                rz = rzr.next()
                P.op("dve", f_recip(rz.t[:, :], pz.t[:, :]), reads=[pz.b], writes=[rz.b])
                ob = obr.next()
                P.op("dve", f_tt(ob.t[:, :], po.t[:, :], rz.t[:, :], ALU.mult), reads=[po.b, rz.b], writes=[ob.b])
                P.dma("sp", oT.ap()[hd * 128:(hd + 1) * 128, g * T:(g + 1) * T], ob.t[:, :], reads=[ob.b])

        load(0)
        gate_a(0)
        gate_b(0)
        for hd in range(NA):
            if hd + 1 < NA:
                load(hd + 1)
                gate_a(hd + 1)
            attention(hd)
            if hd + 1 < NA:
                gate_b(hd + 1)
        P.emit()


def stage_dilated(nc, cfg, C, qT, kT, v, oT):
    S, NA, NBG, T = cfg.S, cfg.NA, cfg.NBG, cfg.T
    NKT = S // 128
    with ExitStack() as es:
        cx = Ctx(nc, es)
        P = Prog(nc)
        K = load_consts(P, cx, cfg, None)
        ones = K["ones"]
        identb = cx.sb("identb", [128, 128], BF16)
        P.dma("sp", identb.t[:, :], C["identb"].ap(), writes=[identb.b])
        bmask = cx.sb("bmask", [128, 2, 256], BF16)
        P.dma("sp", bmask.t[:, :, :], C["bmask"].ap(), writes=[bmask.b])
        qr = ring(cx, "q", 4, [128, S], BF16)
        kr = ring(cx, "k", 4, [128, S], BF16)
        vr = ring(cx, "v", 2, [128, NKT, 128], BF16)
        uacc = cx.sb("uacc", [128, S], F32)
        zacc = cx.sb("zacc", [128, S], F32)
        ptr = ring(cx, "pt", 3, [128, 256], BF16)
        obr = ring(cx, "ob", 2, [128, T], BF16)
        psr = psring(cx, "ps", 4)
        pur = psring(cx, "pu", 4)
        for hh in range(NBG):
            for gi, (win, dil) in enumerate(cfg.PATS):
                hq = NA + gi * NBG + hh
                nblk = S // (128 * dil)
                q, k, vv = qr.next(), kr.next(), vr.next()
                P.dma("sp", q.t[:, :], qT.ap()[hq * 128:(hq + 1) * 128, :], writes=[q.b])
                P.dma("sp", k.t[:, :], kT.ap()[hq * 128:(hq + 1) * 128, :], writes=[k.b])
                vsrc = v.ap()[:, hq * 128:(hq + 1) * 128].rearrange("(n p r) d -> p r n d", p=128, r=dil)
                for r in range(dil):
                    for n0 in range(0, nblk, 8):
                        n1 = min(nblk, n0 + 8)
                        P.dma("sp", vv.t[:, r * nblk + n0:r * nblk + n1, :], vsrc[:, r, n0:n1, :], writes=[vv.b])
                if dil > 1:
                    q2, k2 = qr.next(), kr.next()
                    sub = S // dil
                    for r in range(dil):
                        P.op("dve", f_copy(q2.t[:, r * sub:(r + 1) * sub], q.t[:, r:r + (sub - 1) * dil + 1:dil]), reads=[q.b], writes=[q2.b])
                        P.op("act", f_act(k2.t[:, r * sub:(r + 1) * sub], k.t[:, r:r + (sub - 1) * dil + 1:dil], AF.Copy), reads=[k.b], writes=[k2.b])
                    q, k = q2, k2
                tiles = [(r, n) for r in range(dil) for n in range(nblk)]

                def qk(rn, q=q, k=k, nblk=nblk):
                    r, n = rn
                    c0 = (r * nblk + n) * 128
                    p0 = c0 - 128 if n > 0 else c0
                    ps = psr.next()
                    P.op("pe", f_mm(ps.t[:, 0:128], k.t[:, c0:c0 + 128], q.t[:, c0:c0 + 128], True, False), reads=[k.b, q.b], writes=[ps.b])
                    P.op("pe", f_mm(ps.t[:, 128:256], k.t[:, p0:p0 + 128], q.t[:, c0:c0 + 128], False, False), reads=[k.b, q.b], writes=[ps.b])
                    P.op("pe", f_mm(ps.t[:, 0:256], identb.t[:, :], bmask.t[:, 1 if n > 0 else 0, :], False, True),
                         reads=[identb.b, bmask.b], writes=[ps.b])
                    return ps

                ps_next = qk(tiles[0])
                for ti, (r, n) in enumerate(tiles):
                    b0 = n * 128 * dil + r
                    cur = slice(b0, b0 + 127 * dil + 1, dil)
                    ps = ps_next
                    pt = ptr.next()
                    P.op("act", f_act(pt.t[:, :], ps.t[:, 0:256], AF.Exp, scale=float(cfg.SCALE)), reads=[ps.b], writes=[pt.b])
                    if ti + 1 < len(tiles):
                        ps_next = qk(tiles[ti + 1])
                    pu = pur.next()
                    vi = r * nblk + n
                    vp = vi - 1 if n > 0 else vi
                    P.op("pe", f_mm(pu.t[:, 0:128], vv.t[:, vi, :], pt.t[:, 0:128], True, False), reads=[vv.b, pt.b], writes=[pu.b])
                    P.op("pe", f_mm(pu.t[:, 0:128], vv.t[:, vp, :], pt.t[:, 128:256], False, False), reads=[vv.b, pt.b], writes=[pu.b])
                    P.op("pe", f_mm(pu.t[:, 128:256], ones.t[:, :], pt.t[:, 0:128], False, False), reads=[ones.b, pt.b], writes=[pu.b])
                    P.op("pe", f_mm(pu.t[:, 128:256], ones.t[:, :], pt.t[:, 128:256], False, True), reads=[ones.b, pt.b], writes=[pu.b])
                    if gi == 0:
                        P.op("dve", f_copy(uacc.t[:, cur], pu.t[:, 0:128]), reads=[pu.b], writes=[uacc.b])
                        P.op("dve", f_copy(zacc.t[:, cur], pu.t[:, 128:256]), reads=[pu.b], writes=[zacc.b])
                    else:
                        P.op("dve", f_tt(uacc.t[:, cur], uacc.t[:, cur], pu.t[:, 0:128], ALU.add), reads=[pu.b, uacc.b], writes=[uacc.b])
                        P.op("dve", f_tt(zacc.t[:, cur], zacc.t[:, cur], pu.t[:, 128:256], ALU.add), reads=[pu.b, zacc.b], writes=[zacc.b])
            P.op("dve", f_recip(zacc.t[:, :], zacc.t[:, :]), reads=[zacc.b], writes=[zacc.b])
            for g in range(S // T):
                ob = obr.next()
                P.op("dve", f_tt(ob.t[:, :], uacc.t[:, g * T:(g + 1) * T], zacc.t[:, g * T:(g + 1) * T], ALU.mult),
                     reads=[uacc.b, zacc.b], writes=[ob.b])
                P.dma("sp", oT.ap()[(NA + hh) * 128:(NA + hh + 1) * 128, g * T:(g + 1) * T], ob.t[:, :], reads=[ob.b])
        P.emit()


def stage_sg_setup(nc, cfg, C, ws_d, bs_d, lnb_d, wsT_d, bsb_d):
    GG = cfg.GG
    with ExitStack() as es:
        cx = Ctx(nc, es)
        P = Prog(nc)
        identf = cx.sb("identf", [128, 128], F32)
        P.dma("sp", identf.t[:, :], C["identf"].ap(), writes=[identf.b])
        tril = cx.sb("tril", [128, 128], F32)
        P.dma("sp", tril.t[:, :], C["tril"].ap(), writes=[tril.b])
        e0 = cx.sb("e0", [128, 128], BF16)
        P.op("dve", f_memset(e0.t[:, :], 0.0), writes=[e0.b])
        P.op("dve", f_memset(e0.t[0:1, :], 1.0), writes=[e0.b])
        lnb = cx.sb("lnb", [128, cfg.E], BF16)
        P.dma("pool", lnb.t[:, :], lnb_d.ap(), writes=[lnb.b])
        bs0 = cx.sb("bs0", [128, GG * 128], BF16)
        P.op("dve", f_memset(bs0.t[:, :], 0.0), writes=[bs0.b])
        P.dma("pool", bs0.t[0:1, :], bs_d.ap(), writes=[bs0.b])
        wsT = cx.sb("wsT", [128, GG, 128], BF16)
        bsb = cx.sb("bsb", [128, GG, 128], F32)
        wr = ring(cx, "w", 3, [128, 128], F32)
        psr = psring(cx, "ps", 4)
        for g in range(GG):
            w = wr.next()
            P.dma("sp", w.t[:, :], ws_d.ap()[g], writes=[w.b])
            P.op("dve", f_tt(w.t[:, :], w.t[:, :], tril.t[:, :], ALU.mult), reads=[w.b, tril.b], writes=[w.b])
            ps = psr.next()
            P.op("pe", f_tr(ps.t[:, 0:128], w.t[:, :], identf.t[:, :]), reads=[w.b, identf.b], writes=[ps.b])
            P.op("act", f_act(wsT.t[:, g, :], ps.t[:, 0:128], AF.Copy), reads=[ps.b], writes=[wsT.b])
            ps2 = psr.next()
            P.op("pe", f_mm(ps2.t[:, 0:128], lnb.t[:, g * 128:(g + 1) * 128], wsT.t[:, g, :], True, False),
                 reads=[lnb.b, wsT.b], writes=[ps2.b])
            P.op("pe", f_mm(ps2.t[:, 0:128], e0.t[:, :], bs0.t[:, g * 128:(g + 1) * 128], False, True),
                 reads=[e0.b, bs0.b], writes=[ps2.b])
            P.op("act", f_act(bsb.t[:, g, :], ps2.t[:, 0:128], AF.Copy), reads=[ps2.b], writes=[bsb.b])
        P.dma("sp", wsT_d.ap(), wsT.t[:, :, :], reads=[wsT.b])
        P.dma("sp", bsb_d.ap(), bsb.t[:, :, :], reads=[bsb.b])
        P.emit()


def stage_sg_main(nc, cfg, xin, g_d, wu_d, wv_d, bu_d, bv_d, lng_d, wsT_d, bsb_d, mT):
    DC, S, GG, E = cfg.DC, cfg.S, cfg.GG, cfg.E
    T = 512
    VW = 256
    with ExitStack() as es:
        cx = Ctx(nc, es)
        P = Prog(nc)
        K = load_consts(P, cx, cfg, None)
        g = cx.sb("g", [128, DC], F32)
        P.dma("sp", g.t[:, :], g_d.ap(), writes=[g.b])
        bu = cx.sb("bu", [128, GG], F32)
        P.dma("sp", bu.t[:, :], bu_d.ap(), writes=[bu.b])
        lng = cx.sb("lng", [128, GG], F32)
        P.dma("sp", lng.t[:, :], lng_d.ap(), writes=[lng.b])
        e0 = cx.sb("e0", [128, 128], BF16)
        P.op("dve", f_memset(e0.t[:, :], 0.0), writes=[e0.b])
        P.op("dve", f_memset(e0.t[0:1, :], 1.0), writes=[e0.b])
        bv0 = cx.sb("bv0", [128, E], BF16)
        P.op("dve", f_memset(bv0.t[:, :], 0.0), writes=[bv0.b])
        P.dma("pool", bv0.t[0:1, :], bv_d.ap(), writes=[bv0.b])
        wsT = cx.sb("wsT", [128, GG, 128], BF16)
        P.dma("sp", wsT.t[:, :, :], wsT_d.ap(), writes=[wsT.b])
        bsb = cx.sb("bsb", [128, GG, 128], F32)
        P.dma("sp", bsb.t[:, :, :], bsb_d.ap(), writes=[bsb.b])
        h = cx.sb("h", [128, DC, T], BF16)
        uT = cx.sb("uT", [128, GG, T], BF16)
        xring = ring(cx, "xc", 3, [128, T], F32)
        sqring = ring(cx, "sq", 2, [128, T], BF16)
        rstd = cx.sb("rstd", [128, T], F32)
        wring = ring(cx, "w", 2, [128, DC, 128], BF16)
        wvring = ring(cx, "wv", 2, [128, DC, VW], BF16)
        vbufs = [cx.sb("vbuf%d" % i, [128, E], BF16) for i in range(T // 128)]
        vn = cx.sb("vn", [128, E], BF16)
        nst = max(1, E // 512)
        stats = cx.sb("stats", [128, nst, 6], F32)
        mv = cx.sb("mv", [128, 2], F32)
        fbr = ring(cx, "fb", 3, [128, 128], F32)
        mbr = ring(cx, "mb", 1, [128, GG, 128], BF16)
        psn = cx.ps("psn")
        psr = psring(cx, "ps", 5)
        psa = psring(cx, "pa", 2)
        mv_ = dview(mT)
        for tt in range(S // T):
            tok0 = tt * T
            emit_norm(P, cfg, K, xin, tok0, g, h, 0, xring, sqring, K["ones"], psn, rstd, T=T)
            for j in range(GG):
                w = wring.next()
                P.dma("pool", w.t[:, :, :], wu_d.ap()[j], writes=[w.b])
                ps = psr.next()
                for k in range(DC):
                    P.op("pe", f_mm(ps.t[:, 0:T], w.t[:, k, :], h.t[:, k, :], k == 0, k == DC - 1), reads=[w.b, h.b], writes=[ps.b])
                P.op("act", f_act(uT.t[:, j, :], ps.t[:, 0:T], AF.Gelu, bias=bu.t[:, j:j + 1]), reads=[ps.b, bu.b], writes=[uT.b])
            for jb in range(E // VW):
                wv = wvring.next()
                P.dma("pool", wv.t[:, :, :], wv_d.ap()[jb], writes=[wv.b])
                for tb in range(T // 128):
                    ps = psr.next()
                    for k in range(DC):
                        P.op("pe", f_mm(ps.t[:, 0:VW], h.t[:, k, tb * 128:(tb + 1) * 128], wv.t[:, k, :], k == 0, False),
                             reads=[wv.b, h.b], writes=[ps.b])
                    P.op("pe", f_mm(ps.t[:, 0:VW], e0.t[:, :], bv0.t[:, jb * VW:(jb + 1) * VW], False, True),
                         reads=[e0.b, bv0.b], writes=[ps.b])
                    P.op("act", f_act(vbufs[tb].t[:, jb * VW:(jb + 1) * VW], ps.t[:, 0:VW], AF.Gelu), reads=[ps.b], writes=[vbufs[tb].b])
            for tb in range(T // 128):
                vb = vbufs[tb]
                for i in range(nst):
                    w_ = min(512, E)
                    P.op("dve", lambda e, o_=stats.t[:, i, :], i_=vb.t[:, i * w_:(i + 1) * w_]: e.bn_stats(out=o_, in_=i_),
                         reads=[vb.b], writes=[stats.b])
                P.op("dve", lambda e, o_=mv.t[:, :], i_=stats.t[:, :, :]: e.bn_aggr(out=o_, in_=i_), reads=[stats.b], writes=[mv.b])
                P.op("act", f_act(mv.t[:, 1:2], mv.t[:, 1:2], AF.Sqrt, bias=K["eps"].t[:, 0:1]), reads=[mv.b, K["eps"].b], writes=[mv.b])
                P.op("dve", f_recip(mv.t[:, 1:2], mv.t[:, 1:2]), reads=[mv.b], writes=[mv.b])
                P.op("dve", f_ts(vn.t[:, :], vb.t[:, :], mv.t[:, 0:1], mv.t[:, 1:2], ALU.subtract, ALU.mult),
                     reads=[vb.b, mv.b], writes=[vn.b])
                mb = mbr.next()
                for gi in range(GG):
                    if gi % 4 == 0:
                        pa = psa.next()
                    c0 = (gi % 4) * 128
                    P.op("pe", f_mm(pa.t[:, c0:c0 + 128], vn.t[:, gi * 128:(gi + 1) * 128], wsT.t[:, gi, :], True, True),
                         reads=[vn.b, wsT.b], writes=[pa.b])
                    fb = fbr.next()
                    P.op("dve", f_stt(fb.t[:, :], pa.t[:, c0:c0 + 128], lng.t[:, gi:gi + 1], bsb.t[:, gi, :], ALU.mult, ALU.add),
                         reads=[pa.b, lng.b, bsb.b], writes=[fb.b])
                    P.op("dve", f_tt(mb.t[:, gi, :], fb.t[:, :], uT.t[:, gi, tb * 128:(tb + 1) * 128], ALU.mult),
                         reads=[fb.b, uT.b], writes=[mb.b])
                for g0 in range(0, GG, 8):
                    g1 = min(GG, g0 + 8)
                    P.dma("sp", mv_[:, g0:g1, tok0 + tb * 128:tok0 + (tb + 1) * 128], mb.t[:, g0:g1, :], reads=[mb.b])
        P.emit()


def stage_final_norm(nc, cfg, xin, g_d, out):
    DC, S, T = cfg.DC, cfg.S, cfg.T
    CG = 8
    NG = (DC + CG - 1) // CG
    with ExitStack() as es:
        cx = Ctx(nc, es)
        P = Prog(nc)
        K = load_consts(P, cx, cfg, None)
        g = cx.sb("g", [128, DC], F32)
        P.dma("sp", g.t[:, :], g_d.ap(), writes=[g.b])
        xts = [cx.sb("x%d" % i, [128, DC, T], F32, nsub=NG) for i in range(2)]
        sqring = ring(cx, "sq", 3, [128, T], BF16)
        rsr = ring(cx, "rstd", 2, [128, T], F32)
        pnr = psring(cx, "psn", 2)
        xv = dview(xin)
        ov = dview(out)
        for tt in range(S // T):
            x = xts[tt % 2]
            tok0 = tt * T
            for gi in range(NG):
                c0, c1 = gi * CG, min(DC, (gi + 1) * CG)
                P.dma("sp", x.t[:, c0:c1, :], xv[:, c0:c1, tok0:tok0 + T], writes=[x.sub[gi]])
            psn, rstd = pnr.next(), rsr.next()
            for c in range(DC):
                sq = sqring.next()
                P.op("act", f_act(sq.t[:, :], x.t[:, c, :], AF.Square), reads=[x.sub[c // CG]], writes=[sq.b])
                P.op("pe", f_mm(psn.t[:, :], K["ones"].t[:, :], sq.t[:, :], c == 0, c == DC - 1),
                     reads=[K["ones"].b, sq.b], writes=[psn.b])
            P.op("act", f_act(rstd.t[:, :], psn.t[:, :], AF.Sqrt, scale=1.0 / cfg.D, bias=K["eps"].t[:, 0:1]),
                 reads=[psn.b, K["eps"].b], writes=[rstd.b])
            P.op("dve", f_recip(rstd.t[:, :], rstd.t[:, :]), reads=[rstd.b], writes=[rstd.b])
            for c in range(DC):
                P.op("dve", f_stt(x.t[:, c, :], x.t[:, c, :], g.t[:, c:c + 1], rstd.t[:, :], ALU.mult, ALU.mult),
                     reads=[x.sub[c // CG], g.b, rstd.b], writes=[x.sub[c // CG]])
            for gi in range(NG):
                c0, c1 = gi * CG, min(DC, (gi + 1) * CG)
                P.dma("sp", ov[:, c0:c1, tok0:tok0 + T], x.t[:, c0:c1, :], reads=[x.sub[gi]])
        P.emit()


def build_program(cfg):
    nc = bass.Bass("TRN2", target_bir_lowering=False)
    D, S, DC, FC, NH, NA, GG, E, DFF = cfg.D, cfg.S, cfg.DC, cfg.FC, cfg.NH, cfg.NA, cfg.GG, cfg.E, cfg.DFF
    C = make_consts(nc, cfg)

    def di(name, shape, dt=F32):
        return nc.dram_tensor(name, list(shape), dt, kind="ExternalInput")

    def ds(name, shape, dt):
        return nc.dram_tensor(name, list(shape), dt)

    xT = di("xT", [D, S])
    pos = di("pos", [32, S], I32)
    g_attn = di("g_attn", [128, DC])
    wqk = di("wqk", [2 * NH, 128, DC, 128])
    wv = di("wv", [NH * 128 // cfg.VW, 128, DC, cfg.VW])
    wo = di("wo", [DC, 128, DC, 128])
    ffn = []
    for l in range(2):
        ffn.append(dict(g=di("g_ffn%d" % l, [128, DC]), wup=di("wup%d" % l, [2 * FC, 128, DC, 128]),
                        cw=di("cw%d" % l, [128, 3, 2 * FC]), cb=di("cb%d" % l, [128, 2 * FC]),
                        wdn=di("wdn%d" % l, [DC, 128, FC, 128])))
    g_sg = di("g_sg", [128, DC])
    sg_wu = di("sg_wu", [GG, 128, DC, 128])
    sg_wv = di("sg_wv", [E // 256, 128, DC, 256])
    sg_bu = di("sg_bu", [128, GG])
    sg_bv = di("sg_bv", [1, E])
    sg_lng = di("sg_lng", [128, GG])
    sg_lnb = di("sg_lnb", [128, E])
    sg_ws = di("sg_ws", [GG, 128, 128])
    sg_bs = di("sg_bs", [1, GG * 128])
    sg_wo = di("sg_wo", [DC, 128, GG, 128])
    g_fin = di("g_fin", [128, DC])
    outT = nc.dram_tensor("outT", [D, S], F32, kind="ExternalOutput")

    qT = ds("qT", [NH * 128, S], BF16)
    kT = ds("kT", [NH * 128, S], BF16)
    v = ds("v", [S, NH * 128], BF16)
    ksum = ds("ksum", [128, NA, cfg.NBLK], F32)
    cs = ds("cs", [32, 2, S], F32)
    oT = ds("oT", [D, S], BF16)
    gT = ds("gT", [DFF, S], BF16)
    xa = ds("xa", [D, S], F32)
    xb = ds("xb", [D, S], F32)
    wsT = ds("wsT", [128, GG, 128], BF16)
    bsb = ds("bsb", [128, GG, 128], F32)

    import os
    sel = os.environ.get("MK_STAGES")
    sel = set(int(t) for t in sel.split(",")) if sel else set(range(1, 13))
    stages = [
        lambda: (stage_rope(nc, cfg, C, pos, cs), stage_qkv(nc, cfg, C, xT, g_attn, wqk, wv, cs, qT, kT, v, ksum)),
        lambda: stage_moba(nc, cfg, C, qT, kT, v, ksum, oT),
        lambda: stage_dilated(nc, cfg, C, qT, kT, v, oT),
        lambda: stage_linear_res(nc, cfg, DC, oT, wo, xT, xa),
        lambda: stage_ffn_up(nc, cfg, xa, ffn[0]["g"], ffn[0]["wup"], ffn[0]["cw"], ffn[0]["cb"], gT),
        lambda: stage_linear_res(nc, cfg, FC, gT, ffn[0]["wdn"], xa, xb),
        lambda: stage_sg_setup(nc, cfg, C, sg_ws, sg_bs, sg_lnb, wsT, bsb),
        lambda: stage_sg_main(nc, cfg, xb, g_sg, sg_wu, sg_wv, sg_bu, sg_bv, sg_lng, wsT, bsb, oT),
        lambda: stage_linear_res(nc, cfg, GG, oT, sg_wo, xb, xa),
        lambda: stage_ffn_up(nc, cfg, xa, ffn[1]["g"], ffn[1]["wup"], ffn[1]["cw"], ffn[1]["cb"], gT),
        lambda: stage_linear_res(nc, cfg, FC, gT, ffn[1]["wdn"], xa, xb),
        lambda: stage_final_norm(nc, cfg, xb, g_fin, outT),
    ]
    for i, st in enumerate(stages):
        if i + 1 in sel:
            st()
    return nc


def host_prep(cfg, p):
    NH, FC, GG, E = cfg.NH, cfg.FC, cfg.GG, cfg.E
    f = lambda a: np.asarray(a, dtype=np.float32)
    w_in = f(p["attn_w_in"])[0]
    m = {}
    m["g_attn"] = vec_pc(f(p["attn_norm"])[0])
    m["wqk"] = tile_w(w_in[:, :2 * NH * 128])
    m["wv"] = tile_w(w_in[:, 2 * NH * 128:], cfg.VW)
    m["wo"] = tile_w(f(p["attn_w_out"])[0])
    for l in range(2):
        m["g_ffn%d" % l] = vec_pc(f(p["ffn_norm"])[l])
        m["wup%d" % l] = tile_w(f(p["ffn_w_up"])[l])
        m["cw%d" % l] = np.ascontiguousarray(f(p["ffn_conv_w"])[l].reshape(3, 2 * FC, 128).transpose(2, 0, 1))
        m["cb%d" % l] = vec_pc(f(p["ffn_conv_b"])[l])
        m["wdn%d" % l] = tile_w(f(p["ffn_w_down"])[l])
    sw = f(p["sg_w_in"])[0]
    sb = f(p["sg_b_in"])[0]
    m["g_sg"] = vec_pc(f(p["sg_norm"])[0])
    m["sg_wu"] = tile_w(sw[:, :E])
    m["sg_wv"] = tile_w(sw[:, E:], 256)
    m["sg_bu"] = vec_pc(sb[:E])
    m["sg_bv"] = np.ascontiguousarray(sb[E:].reshape(1, E))
    m["sg_lng"] = vec_pc(f(p["sg_v_gain"])[0])
    m["sg_lnb"] = np.ascontiguousarray(np.broadcast_to(f(p["sg_v_bias"])[0], (128, E)))
    m["sg_ws"] = np.ascontiguousarray(f(p["sg_w_s"])[0])
    m["sg_bs"] = np.ascontiguousarray(f(p["sg_b_s"])[0].reshape(1, GG * 128))
    m["sg_wo"] = tile_w(f(p["sg_w_out"])[0])
    m["g_fin"] = vec_pc(f(p["final_norm"]))
    return m


def run_module(cfg, inputs):
    x = np.asarray(inputs["x"], dtype=np.float32)
    positions = np.asarray(inputs["positions"]).astype(np.int32)
    B = x.shape[0]
    import time as _t
    t0 = _t.time()
    shared = host_prep(cfg, inputs)
    t1 = _t.time()
    nc = build_program(cfg)
    print("[mk] host_prep %.1fs build %.1fs" % (t1 - t0, _t.time() - t1), flush=True)
    in_maps = []
    for b in range(B):
        m = dict(shared)
        m["xT"] = np.ascontiguousarray(x[b].T)
        m["pos"] = np.ascontiguousarray(np.broadcast_to(positions[b], (32, cfg.S)))
        in_maps.append(m)
    t2 = _t.time()
    res = run_bass_kernel_spmd(nc, in_maps, core_ids=list(range(B)))
    print("[mk] launch %.1fs" % (_t.time() - t2), flush=True)
    return np.stack([np.ascontiguousarray(res.results[b]["outT"].T) for b in range(B)], 0)


def kernel(**inputs):
    cfg = Cfg()
    return run_module(cfg, inputs)
```

```python
import numpy as np
from contextlib import ExitStack
import concourse.bass as bass
import concourse.mybir as mybir
from concourse.bass_utils import run_bass_kernel_spmd

F32 = mybir.dt.float32
BF16 = mybir.dt.bfloat16
I32 = mybir.dt.int32
AF = mybir.ActivationFunctionType
ALU = mybir.AluOpType
AX = mybir.AxisListType

NEG = -30000.0
ENGS = ("pe", "act", "dve", "pool", "sp")
NRING = 12


class Cfg:
    def __init__(self, D=4096, S=4096, NA=24, NBG=8, DFF=14336, B=2):
        self.D, self.S, self.NA, self.NBG, self.DFF, self.B = D, S, NA, NBG, DFF, B
        self.DH = 128
        self.NB = 3 * NBG
        self.NH = NA + self.NB
        self.NHO = NA + NBG
        assert self.NHO * 128 == D
        self.QKV = 3 * self.NH * 128
        self.DC = D // 128
        self.FC = DFF // 128
        self.T = 512
        self.NT = S // self.T
        self.MB = 256
        self.NBLK = S // self.MB
        self.E = D
        self.GG = self.E // 128
        self.EPS = 1e-5
        self.PATS = ((128, 1), (512, 4), (2048, 16))
        self.VW = 512 if (self.NH * 128) % 512 == 0 else 256
        self.SCALE = 128 ** -0.5


class Buf:
    __slots__ = ("w", "r")

    def __init__(self):
        self.w = None
        self.r = []


class Op:
    __slots__ = ("eng", "fn", "deps", "signal", "ev", "dma", "di")

    def __init__(self, eng, fn, dma):
        self.eng, self.fn, self.dma = eng, fn, dma
        self.deps = []
        self.signal = False
        self.ev = None
        self.di = -1


class Prog:
    def __init__(self, nc):
        self.nc = nc
        self.ops = []

    def op(self, eng, fn, reads=(), writes=(), dma=False):
        o = Op(eng, fn, dma)
        deps = {}
        for b in reads:
            if b.w is not None:
                deps[id(b.w)] = b.w
        for b in writes:
            if b.w is not None:
                deps[id(b.w)] = b.w
            for r in b.r:
                deps[id(r)] = r
        for d in deps.values():
            if d is o:
                continue
            if d.eng == "pe" and eng == "pe" and not d.dma and not dma:
                continue
            o.deps.append(d)
            d.signal = True
        for b in writes:
            b.w = o
            b.r = []
        for b in reads:
            if b.w is not o:
                if not dma:
                    b.r = [r for r in b.r if r.dma or r.eng != eng]
                b.r.append(o)
        self.ops.append(o)
        return o

    def dma(self, eng, out, in_, reads=(), writes=()):
        return self.op(eng, lambda e: e.dma_start(out=out, in_=in_), reads, writes, dma=True)

    def emit(self, name=None):
        nc = self.nc
        with ExitStack() as es:
            if not hasattr(nc, "_mk_sems"):
                gs = ExitStack()
                c_ = {e: gs.enter_context(nc.semaphore("c_" + e)) for e in ENGS}
                d_ = {e: [gs.enter_context(nc.semaphore("d_%s%d" % (e, i))) for i in range(NRING)]
                      for e in ("sp", "pool")}
                nc._mk_sems = (gs, c_, d_)
                nc._mk_pool_n = 0
            _, csem, dsem = nc._mk_sems
            cnt = {e: 0 for e in ENGS}
            dcnt = {e: 0 for e in dsem}
            dcnt["pool"] = nc._mk_pool_n
            pool_base = nc._mk_pool_n
            by_eng = {e: [] for e in ENGS}
            dlist = {e: [] for e in dsem}
            for o in self.ops:
                if o.dma:
                    i = dcnt[o.eng]
                    dcnt[o.eng] += 1
                    o.di = i
                    o.ev = (dsem[o.eng][i % NRING], 16 * (i // NRING + 1))
                    dlist[o.eng].append(o)
                    if o.eng == "pool":
                        nc._mk_pool_n = i + 1
                elif o.signal:
                    cnt[o.eng] += 1
                    o.ev = (csem[o.eng], cnt[o.eng])
                by_eng[o.eng].append(o)
            block = es.enter_context(nc.Block())

            def body(e, eng):
                waited = {}

                def wait(ev):
                    sem, val = ev
                    k = id(sem)
                    if waited.get(k, 0) < val:
                        e.wait_ge(sem, val)
                        waited[k] = val

                for o in by_eng[eng]:
                    need = {}
                    for d in o.deps:
                        sm, val = d.ev
                        k_ = id(sm)
                        if k_ not in need or need[k_][1] < val:
                            need[k_] = (sm, val)
                    for ev in need.values():
                        wait(ev)
                    if o.dma:
                        li = o.di - (pool_base if eng == "pool" else 0)
                        if li >= NRING:
                            wait(dlist[eng][li - NRING].ev)
                    ins = o.fn(e)
                    if o.dma:
                        ins.then_inc(o.ev[0], 16)
                    elif o.signal:
                        ins.then_inc(o.ev[0], 1)
                if eng in dlist:
                    for o in dlist[eng][-NRING:]:
                        wait(o.ev)

            block.tensor(lambda e: body(e, "pe"))
            block.scalar(lambda e: body(e, "act"))
            block.vector(lambda e: body(e, "dve"))
            block.gpsimd(lambda e: body(e, "pool"))
            block.sync(lambda e: body(e, "sp"))
        _, csem, dsem = nc._mk_sems
        allsems = list(csem.values()) + list(dsem["sp"])
        with nc.Block() as blk2:
            def clr(e):
                for sm in allsems:
                    e.sem_clear(sm)
            blk2.sync(clr)
        self.ops = []


class Tile:
    def __init__(self, t, nsub=0):
        self.t = t
        self.b = Buf()
        self.sub = [Buf() for _ in range(nsub)]


class Ctx:
    _n = 0

    def __init__(self, nc, es):
        self.nc, self.es = nc, es
        Ctx._n += 1
        self.pre = "s%d_" % Ctx._n

    def sb(self, name, shape, dt, nsub=0):
        return Tile(self.es.enter_context(self.nc.sbuf_tensor(self.pre + name, list(shape), dt)), nsub)

    def ps(self, name, shape=(128, 512), dt=F32, nsub=0):
        return Tile(self.es.enter_context(self.nc.psum_tensor(self.pre + name, list(shape), dt)), nsub)


class Ring:
    def __init__(self, tiles):
        self.tiles = tiles
        self.i = 0

    def next(self):
        t = self.tiles[self.i % len(self.tiles)]
        self.i += 1
        return t


def ring(cx, name, n, shape, dt):
    return Ring([cx.sb("%s%d" % (name, i), shape, dt) for i in range(n)])


def psring(cx, name, n):
    return Ring([cx.ps("%s%d" % (name, i)) for i in range(n)])


def f_mm(out, lhsT, rhs, start=True, stop=True):
    return lambda e: e.matmul(out, lhsT, rhs, start=start, stop=stop)


def f_tr(out, in_, ident):
    return lambda e: e.transpose(out, in_, ident)


def f_act(out, in_, func, **kw):
    return lambda e: e.activation(out=out, in_=in_, func=func, **kw)


def f_ts(out, in0, s1, s2, op0, op1=None):
    if op1 is None:
        return lambda e: e.tensor_scalar(out=out, in0=in0, scalar1=s1, scalar2=None, op0=op0)
    return lambda e: e.tensor_scalar(out=out, in0=in0, scalar1=s1, scalar2=s2, op0=op0, op1=op1)


def f_stt(out, in0, scalar, in1, op0, op1):
    return lambda e: e.scalar_tensor_tensor(out=out, in0=in0, scalar=scalar, in1=in1, op0=op0, op1=op1)


def f_tt(out, in0, in1, op):
    return lambda e: e.tensor_tensor(out=out, in0=in0, in1=in1, op=op)


def f_copy(out, in_):
    return lambda e: e.tensor_copy(out=out, in_=in_)


def f_recip(out, in_):
    return lambda e: e.reciprocal(out=out, in_=in_)


def f_memset(ap, v):
    return lambda e: e.memset(ap, v)


def dview(h):
    return h.ap().rearrange("(c p) t -> p c t", p=128)


def emit_norm(P, cfg, K, xin, tok0, g, h, hcol0, xring, sqring, ones, psn, rstd, T=None):
    T, DC = (T or cfg.T), cfg.DC
    xv = dview(xin)
    for c in range(DC):
        xc = xring.next()
        P.dma("sp", xc.t[:, 0:T], xv[:, c, tok0:tok0 + T], writes=[xc.b])
        sq = sqring.next()
        P.op("act", f_act(sq.t[:, 0:T], xc.t[:, 0:T], AF.Square), reads=[xc.b], writes=[sq.b])
        P.op("pe", f_mm(psn.t[:, 0:T], ones.t[:, :], sq.t[:, 0:T], c == 0, c == DC - 1),
             reads=[ones.b, sq.b], writes=[psn.b])
    P.op("act", f_act(rstd.t[:, 0:T], psn.t[:, 0:T], AF.Sqrt, scale=1.0 / cfg.D, bias=K["eps"].t[:, 0:1]),
         reads=[psn.b, K["eps"].b], writes=[rstd.b])
    P.op("dve", f_recip(rstd.t[:, 0:T], rstd.t[:, 0:T]), reads=[rstd.b], writes=[rstd.b])
    for c in range(DC):
        xc = xring.next()
        P.dma("sp", xc.t[:, 0:T], xv[:, c, tok0:tok0 + T], writes=[xc.b])
        P.op("dve", f_stt(h.t[:, c, hcol0:hcol0 + T], xc.t[:, 0:T], g.t[:, c:c + 1], rstd.t[:, 0:T],
                          ALU.mult, ALU.mult),
             reads=[xc.b, g.b, rstd.b], writes=[h.b])


def load_consts(P, cx, cfg, consts):
    K = {}
    K["ones"] = cx.sb("k_ones", [128, 128], BF16)
    P.op("dve", f_memset(K["ones"].t[:, :], 1.0), writes=[K["ones"].b])
    K["eps"] = cx.sb("k_eps", [128, 1], F32)
    P.op("dve", f_memset(K["eps"].t[:, :], cfg.EPS), writes=[K["eps"].b])
    return K


def stage_ffn_up(nc, cfg, xin, g_d, wup_d, cw_d, cb_d, gT, precast=None):
    T, DC, FC, S = cfg.T, cfg.DC, cfg.FC, cfg.S
    NSUB = 2 if S % (2 * T) == 0 else 1
    TS = T * NSUB
    with ExitStack() as es:
        cx = Ctx(nc, es)
        P = Prog(nc)
        K = load_consts(P, cx, cfg, None)
        g = cx.sb("g", [128, DC], F32)
        P.dma("sp", g.t[:, :], g_d.ap(), writes=[g.b])
        cw = cx.sb("cw", [128, 3, 2 * FC], F32)
        P.dma("sp", cw.t[:, :, :], cw_d.ap(), writes=[cw.b])
        cb = cx.sb("cb", [128, 2 * FC], F32)
        P.dma("sp", cb.t[:, :], cb_d.ap(), writes=[cb.b])
        carry = [cx.sb("carry%d" % i, [128, FC, 2], F32, nsub=FC) for i in range(2)]
        for cr in carry:
            P.op("dve", f_memset(cr.t[:, :, :], 0.0), writes=cr.sub)
        hs = [cx.sb("h%d" % i, [128, DC, TS], BF16) for i in range(2)]
        xring = ring(cx, "xc", 4, [128, T], F32)
        sqring = ring(cx, "sq", 3, [128, T], BF16)
        rstd = cx.sb("rstd", [128, T], F32)
        wring = ring(cx, "w", 4, [128, DC, 128], BF16)
        abuf = ring(cx, "ab", 4, [128, T + 4], F32)
        cbuf = ring(cx, "cbf", 4, [128, T], F32)
        sgb = ring(cx, "sg", 2, [128, T], F32)
        gob = ring(cx, "go", 3, [128, T], BF16)
        psn = cx.ps("psn")
        psr = psring(cx, "ps", 7)
        NST = S // TS

        def do_norm(st_):
            for sub_ in range(NSUB):
                emit_norm(P, cfg, K, xin, st_ * TS + sub_ * T, g, hs[st_ % 2], sub_ * T, xring, sqring, K["ones"], psn, rstd)

        pc_step = max(1, (min(2, NST) * FC) // DC)
        pc_done = [0]

        def do_precast(c):
            src, dst = precast
            P.dma("pool", dst.ap()[c].rearrange("p k c -> p (k c)"), src.ap()[c].rearrange("p k c -> p (k c)"))

        do_norm(0)
        for st in range(NST):
            h = hs[st % 2]
            for j in range(FC):
                if j == min(4, FC - 1) and st + 1 < NST:
                    do_norm(st + 1)
                ws = []
                for half in range(2):
                    w = wring.next()
                    P.dma("pool", w.t[:, :, :], wup_d.ap()[half * FC + j], writes=[w.b])
                    ws.append(w)
                it = st * FC + j
                if precast is not None and it % pc_step == 0 and pc_done[0] < DC:
                    do_precast(pc_done[0])
                    pc_done[0] += 1
                for sub in range(NSUB):
                    tok0 = st * TS + sub * T
                    cs = []
                    for half in range(2):
                        ps = psr.next()
                        for k in range(DC):
                            P.op("pe", f_mm(ps.t[:, :], ws[half].t[:, k, :], h.t[:, k, sub * T:(sub + 1) * T],
                                            k == 0, k == DC - 1),
                                 reads=[ws[half].b, h.b], writes=[ps.b])
                        ch = half * FC + j
                        ab = abuf.next()
                        cr = carry[half]
                        P.op("dve", f_copy(ab.t[:, 0:2], cr.t[:, j, :]), reads=[cr.sub[j]], writes=[ab.b])
                        P.op("act", f_act(ab.t[:, 2:T + 2], ps.t[:, :], AF.Copy), reads=[ps.b], writes=[ab.b])
                        P.op("dve", f_copy(cr.t[:, j, :], ab.t[:, T:T + 2]), reads=[ab.b], writes=[cr.sub[j]])
                        c_ = cbuf.next()
                        P.op("dve", f_ts(c_.t[:, :], ab.t[:, 2:T + 2], cw.t[:, 2, ch:ch + 1], cb.t[:, ch:ch + 1],
                                         ALU.mult, ALU.add), reads=[ab.b, cw.b, cb.b], writes=[c_.b])
                        P.op("dve", f_stt(c_.t[:, :], ab.t[:, 1:T + 1], cw.t[:, 1, ch:ch + 1], c_.t[:, :],
                                          ALU.mult, ALU.add), reads=[ab.b, c_.b, cw.b], writes=[c_.b])
                        P.op("dve", f_stt(c_.t[:, :], ab.t[:, 0:T], cw.t[:, 0, ch:ch + 1], c_.t[:, :],
                                          ALU.mult, ALU.add), reads=[ab.b, c_.b, cw.b], writes=[c_.b])
                        cs.append(c_)
                    sg = sgb.next()
                    P.op("act", f_act(sg.t[:, :], cs[0].t[:, :], AF.Silu), reads=[cs[0].b], writes=[sg.b])
                    go = gob.next()
                    P.op("dve", f_tt(go.t[:, :], sg.t[:, :], cs[1].t[:, :], ALU.mult),
                         reads=[sg.b, cs[1].b], writes=[go.b])
                    P.dma("sp", gT.ap()[j * 128:(j + 1) * 128, tok0:tok0 + T], go.t[:, :], reads=[go.b])
        while precast is not None and pc_done[0] < DC:
            do_precast(pc_done[0])
            pc_done[0] += 1
        P.emit()


def stage_linear_res(nc, cfg, KC, actT, w_d, xin, xout):
    T, DC, S = cfg.T, cfg.DC, cfg.S
    NSUB = 2 if (KC <= 32 and S % (2 * T) == 0) else 1
    TS = T * NSUB
    with ExitStack() as es:
        cx = Ctx(nc, es)
        P = Prog(nc)
        KG = 16 // NSUB
        NG = (KC + KG - 1) // KG
        abufs = [cx.sb("a%d" % i, [128, KC, TS], BF16, nsub=NG) for i in range(2 if KC <= 32 else 1)]
        wring = ring(cx, "w", 2, [128, KC, 128], BF16)
        xr = ring(cx, "xr", 3, [128, T], F32)
        xo = ring(cx, "xo", 3, [128, T], F32)
        psr = psring(cx, "ps", 4)
        av = dview(actT)
        xv = dview(xin)
        ov = dview(xout)
        for tt in range(S // TS):
            t0 = tt * TS
            a = abufs[tt % len(abufs)]
            for gk in range(NG):
                k0, k1 = gk * KG, min(KC, (gk + 1) * KG)
                P.dma("sp", a.t[:, k0:k1, :], av[:, k0:k1, t0:t0 + TS], writes=[a.sub[gk]])
            for c in range(DC):
                w = wring.next()
                P.dma("pool", w.t[:, :, :], w_d.ap()[c], writes=[w.b])
                for sub in range(NSUB):
                    tok0 = t0 + sub * T
                    ps = psr.next()
                    for k in range(KC):
                        P.op("pe", f_mm(ps.t[:, :], w.t[:, k, :], a.t[:, k, sub * T:(sub + 1) * T], k == 0, k == KC - 1),
                             reads=[w.b, a.sub[k // KG]], writes=[ps.b])
                    x_ = xr.next()
                    P.dma("sp", x_.t[:, :], xv[:, c, tok0:tok0 + T], writes=[x_.b])
                    o_ = xo.next()
                    P.op("dve", f_tt(o_.t[:, :], ps.t[:, :], x_.t[:, :], ALU.add), reads=[ps.b, x_.b], writes=[o_.b])
                    P.dma("sp", ov[:, c, tok0:tok0 + T], o_.t[:, :], reads=[o_.b])
        P.emit()


def tile_w(W, cw=128):
    K, N = W.shape
    return np.ascontiguousarray(W.reshape(K // 128, 128, N // cw, cw).transpose(2, 1, 0, 3))


def vec_pc(v):
    return np.ascontiguousarray(v.reshape(-1, 128).T)


def np_bf16(a):
    import ml_dtypes
    return np.asarray(a, dtype=np.float32).astype(ml_dtypes.bfloat16)


def make_consts(nc, cfg):
    C = {}
    half = 16
    invf = np.power(np.float32(500000.0), -np.arange(half, dtype=np.float32) * np.float32(2.0 / 32.0)).astype(np.float32)
    C["invf"] = nc.inline_tensor(np.concatenate([invf, invf]).reshape(32, 1).astype(np.float32), "k_invf")
    Pm = np.zeros((128, 32), np.float32)
    for m in range(16):
        Pm[m + 16, m] = -1.0
        Pm[m, m + 16] = 1.0
    C["Pm"] = nc.inline_tensor(Pm, "k_Pm")
    C["identf"] = nc.inline_tensor(np.eye(128, dtype=np.float32), "k_identf")
    C["identb"] = nc.inline_tensor(np_bf16(np.eye(128)), "k_identb")
    NBLK = cfg.NBLK
    E = np.zeros((128, NBLK * 128), np.float32)
    for n in range(NBLK):
        E[n % 16, n * 128:(n + 1) * 128] = 1.0
    C["esel"] = nc.inline_tensor(np_bf16(E), "k_esel")
    j = np.arange(128)[:, None, None]
    r = np.arange(4)[None, :, None]
    i = np.arange(512)[None, None, :]
    C["cmask"] = nc.inline_tensor(np_bf16(np.where(r * 128 + j <= i, 0.0, NEG)), "k_cmask")
    jj = np.arange(128)[:, None]
    ii = np.arange(128)[None, :]
    cur = np.where(jj <= ii, 0.0, NEG)
    prv = np.where(jj >= ii, 0.0, NEG)
    bm = np.stack([np.concatenate([cur, np.full((128, 128), NEG)], 1), np.concatenate([cur, prv], 1)], 1)
    C["bmask"] = nc.inline_tensor(np_bf16(bm), "k_bmask")
    NKT = cfg.S // 128
    pm = np.full((NKT, 16), NEG, np.float32)
    om = np.full((NKT, 16), -1.0e9, np.float32)
    for qi in range(NKT):
        qb = (qi * 128) // cfg.MB
        pm[qi, :qb] = 0.0
        om[qi, qb] = 0.0
    C["pastm"] = nc.inline_tensor(np.ascontiguousarray(np.broadcast_to(pm.reshape(1, NKT * 16), (128, NKT * 16))), "k_pastm")
    C["ownm"] = nc.inline_tensor(np.ascontiguousarray(np.broadcast_to(om.reshape(1, NKT * 16), (128, NKT * 16))), "k_ownm")
    C["tril"] = nc.inline_tensor(np.tril(np.ones((128, 128), np.float32)), "k_tril")
    return C


def stage_rope(nc, cfg, C, pos_d, cs_d):
    S = cfg.S
    W = min(S, 2048)
    with ExitStack() as es:
        cx = Ctx(nc, es)
        P = Prog(nc)
        invf = cx.sb("invf", [32, 1], F32)
        P.dma("sp", invf.t[:, :], C["invf"].ap(), writes=[invf.b])
        for c0 in range(0, S, W):
            S_ = W
            posi = cx.sb("posi%d" % c0, [32, S_], I32)
            P.dma("sp", posi.t[:, :], pos_d.ap()[:, c0:c0 + W], writes=[posi.b])
            ang = cx.sb("ang%d" % c0, [32, S_], F32)
            P.op("dve", f_copy(ang.t[:, :], posi.t[:, :]), reads=[posi.b], writes=[ang.b])
            P.op("dve", f_ts(ang.t[:, :], ang.t[:, :], invf.t[:, 0:1], None, ALU.mult), reads=[ang.b, invf.b], writes=[ang.b])
            ki = cx.sb("ki%d" % c0, [32, S_], I32)
            kf = cx.sb("kf%d" % c0, [32, S_], F32)
            mk = cx.sb("mk%d" % c0, [32, S_], F32)
            for idx, (nm, shift) in enumerate((("cos", 0.25), ("sin", 0.0))):
                y = cx.sb("y_%s%d" % (nm, c0), [32, S_], F32)
                P.op("dve", f_ts(y.t[:, :], ang.t[:, :], float(1.0 / (2.0 * np.pi)), shift, ALU.mult, ALU.add),
                     reads=[ang.b], writes=[y.b])
                P.op("dve", f_copy(ki.t[:, :], y.t[:, :]), reads=[y.b], writes=[ki.b])
                P.op("dve", f_copy(kf.t[:, :], ki.t[:, :]), reads=[ki.b], writes=[kf.b])
                P.op("dve", f_tt(y.t[:, :], y.t[:, :], kf.t[:, :], ALU.subtract), reads=[y.b, kf.b], writes=[y.b])
                P.op("dve", f_ts(mk.t[:, :], y.t[:, :], 0.5, None, ALU.is_gt), reads=[y.b], writes=[mk.b])
                P.op("dve", f_tt(y.t[:, :], y.t[:, :], mk.t[:, :], ALU.subtract), reads=[y.b, mk.b], writes=[y.b])
                P.op("dve", f_ts(mk.t[:, :], y.t[:, :], -0.5, None, ALU.is_lt), reads=[y.b], writes=[mk.b])
                P.op("dve", f_tt(y.t[:, :], y.t[:, :], mk.t[:, :], ALU.add), reads=[y.b, mk.b], writes=[y.b])
                P.op("act", f_act(y.t[:, :], y.t[:, :], AF.Sin, scale=float(2.0 * np.pi * (1.0 - 1e-6))),
                     reads=[y.b], writes=[y.b])
                P.dma("sp", cs_d.ap()[:, idx, c0:c0 + W], y.t[:, :], reads=[y.b])
        P.emit()


def stage_qkv(nc, cfg, C, xin, g_d, wqk_d, wv_d, cs_d, qT, kT, v, ksum_d):
    T, DC, S, NH, NA = cfg.T, cfg.DC, cfg.S, cfg.NH, cfg.NA
    VW = cfg.VW
    NSUB = 2 if S % (2 * T) == 0 else 1
    TS = T * NSUB
    with ExitStack() as es:
        cx = Ctx(nc, es)
        P = Prog(nc)
        K = load_consts(P, cx, cfg, None)
        g = cx.sb("g", [128, DC], F32)
        P.dma("sp", g.t[:, :], g_d.ap(), writes=[g.b])
        Pm = cx.sb("Pm", [128, 32], F32)
        P.dma("sp", Pm.t[:, :], C["Pm"].ap(), writes=[Pm.b])
        csr = ring(cx, "cs", 4, [32, 2, T], F32)
        ksum = cx.sb("ksum", [128, NA, cfg.NBLK], F32)
        h = cx.sb("h", [128, DC, TS], BF16)
        xring = ring(cx, "xc", 4, [128, T], F32)
        sqring = ring(cx, "sq", 3, [128, T], BF16)
        rstd = cx.sb("rstd", [128, T], F32)
        wring = ring(cx, "w", 3, [128, DC, 128], BF16)
        wvring = ring(cx, "wv", 2, [128, DC, VW], BF16)
        qfr = ring(cx, "qf", 4, [128, T], F32)
        tmr = ring(cx, "tm", 4, [32, T], F32)
        qbr = ring(cx, "qb", 3, [128, T], BF16)
        vbr = ring(cx, "vb", 3, [128, VW], BF16)
        psn = cx.ps("psn")
        psr = psring(cx, "ps", 5)
        psp = psring(cx, "pp", 2)
        for st in range(S // TS):
            cst = []
            for sub in range(NSUB):
                emit_norm(P, cfg, K, xin, st * TS + sub * T, g, h, sub * T, xring, sqring, K["ones"], psn, rstd)
                cs = csr.next()
                tok0 = st * TS + sub * T
                P.dma("sp", cs.t[:, :, :], cs_d.ap()[:, :, tok0:tok0 + T], writes=[cs.b])
                cst.append(cs)
            pending = [None]
            for j in range(2 * NH):
                isk = j >= NH
                hd = j - NH if isk else j
                w = wring.next()
                P.dma("pool", w.t[:, :, :], wqk_d.ap()[j], writes=[w.b])
                for sub in range(NSUB):
                    tok0 = st * TS + sub * T
                    ps = psr.next()
                    for k in range(DC):
                        P.op("pe", f_mm(ps.t[:, :], w.t[:, k, :], h.t[:, k, sub * T:(sub + 1) * T], k == 0, k == DC - 1),
                             reads=[w.b, h.b], writes=[ps.b])
                    qf = qfr.next()
                    P.op("act", f_act(qf.t[:, :], ps.t[:, :], AF.Copy), reads=[ps.b], writes=[qf.b])

                    def rot(qf=qf, sub=sub, tok0=tok0, isk=isk, hd=hd):
                        pp = psp.next()
                        P.op("pe", f_mm(pp.t[0:32, :], Pm.t[:, :], qf.t[:, :]), reads=[Pm.b, qf.b], writes=[pp.b])
                        t1 = tmr.next()
                        t2 = tmr.next()
                        P.op("dve", f_tt(t1.t[:, :], qf.t[0:32, :], cst[sub].t[:, 0, :], ALU.mult),
                             reads=[qf.b, cst[sub].b], writes=[t1.b])
                        P.op("dve", f_tt(t2.t[:, :], pp.t[0:32, :], cst[sub].t[:, 1, :], ALU.mult),
                             reads=[pp.b, cst[sub].b], writes=[t2.b])
                        P.op("dve", f_tt(qf.t[0:32, :], t1.t[:, :], t2.t[:, :], ALU.add),
                             reads=[t1.b, t2.b], writes=[qf.b])
                        qb = qbr.next()
                        P.op("act", f_act(qb.t[:, :], qf.t[:, :], AF.Copy), reads=[qf.b], writes=[qb.b])
                        if isk and hd < NA:
                            b0 = tok0 // cfg.MB
                            nb = T // cfg.MB
                            P.op("dve", lambda e, o_=ksum.t[:, hd, b0:b0 + nb], i_=qf.t[:, :].rearrange("p (b m) -> p b m", m=cfg.MB):
                                 e.tensor_reduce(out=o_, in_=i_, axis=AX.X, op=ALU.add),
                                 reads=[qf.b], writes=[ksum.b])
                        dst = kT if isk else qT
                        P.dma("sp", dst.ap()[hd * 128:(hd + 1) * 128, tok0:tok0 + T], qb.t[:, :], reads=[qb.b])

                    if pending[0] is not None:
                        pending[0]()
                    pending[0] = rot
            if pending[0] is not None:
                pending[0]()
                pending[0] = None
            for jb in range(NH * 128 // VW):
                wv = wvring.next()
                P.dma("pool", wv.t[:, :, :], wv_d.ap()[jb], writes=[wv.b])
                for tb in range(TS // 128):
                    tok0 = st * TS + tb * 128
                    ps = psr.next()
                    for k in range(DC):
                        P.op("pe", f_mm(ps.t[:, 0:VW], h.t[:, k, tb * 128:(tb + 1) * 128], wv.t[:, k, :], k == 0, k == DC - 1),
                             reads=[wv.b, h.b], writes=[ps.b])
                    vb = vbr.next()
                    P.op("act", f_act(vb.t[:, :], ps.t[:, 0:VW], AF.Copy), reads=[ps.b], writes=[vb.b])
                    P.dma("sp", v.ap()[tok0:tok0 + 128, jb * VW:(jb + 1) * VW], vb.t[:, :], reads=[vb.b])
        P.dma("sp", ksum_d.ap(), ksum.t[:, :, :], reads=[ksum.b])
        P.emit()


def stage_moba(nc, cfg, C, qT, kT, v, ksum_d, oT):
    S, NA, NBLK, T = cfg.S, cfg.NA, cfg.NBLK, cfg.T
    NKT = S // 128
    with ExitStack() as es:
        cx = Ctx(nc, es)
        P = Prog(nc)
        K = load_consts(P, cx, cfg, None)
        ones = K["ones"]
        identf = cx.sb("identf", [128, 128], F32)
        P.dma("sp", identf.t[:, :], C["identf"].ap(), writes=[identf.b])
        identb = cx.sb("identb", [128, 128], BF16)
        P.dma("sp", identb.t[:, :], C["identb"].ap(), writes=[identb.b])
        esel = cx.sb("esel", [128, NBLK * 128], BF16)
        P.dma("sp", esel.t[:, :], C["esel"].ap(), writes=[esel.b])
        cmask = cx.sb("cmask", [128, 4, 512], BF16)
        P.dma("sp", cmask.t[:, :, :], C["cmask"].ap(), writes=[cmask.b])
        ksum = cx.sb("ksum", [128, NA, NBLK], F32)
        P.dma("sp", ksum.t[:, :, :], ksum_d.ap(), writes=[ksum.b])
        qr = ring(cx, "q", 2, [128, S], BF16)
        kr = ring(cx, "k", 2, [128, S], BF16)
        vr = ring(cx, "v", 2, [128, NKT, 128], BF16)
        kmr = ring(cx, "km", 2, [128, 16], BF16)
        btr = ring(cx, "bt", 2, [128, S], BF16)
        gmr = ring(cx, "gm", 4, [128, 16], F32)
        mxr = ring(cx, "mx", 4, [128, 8], F32)
        bqr = ring(cx, "bq", 6, [128, 128], F32)
        ptr = ring(cx, "pt", 3, [128, T], BF16)
        rzr = ring(cx, "rz", 2, [128, T], F32)
        obr = ring(cx, "ob", 2, [128, T], BF16)
        psr = psring(cx, "ps", 3)
        por = psring(cx, "po", 2)
        pzr = psring(cx, "pz", 2)
        psm = cx.ps("psm", nsub=7)
        NB16 = min(NBLK, 16)
        assert NBLK <= 16 and NBLK >= 8
        pastm = cx.sb("pastm", [128, NKT * 16], F32)
        P.dma("sp", pastm.t[:, :], C["pastm"].ap(), writes=[pastm.b])
        ownm = cx.sb("ownm", [128, NKT * 16], F32)
        P.dma("sp", ownm.t[:, :], C["ownm"].ap(), writes=[ownm.b])
        gmar = ring(cx, "gma", 2, [128, NKT * 16], F32)
        mxar = ring(cx, "mxa", 2, [128, NKT, 8], F32)
        bqar = ring(cx, "bqa", 2, [128, NKT, 128], F32)
        for t_ in bqar.tiles:
            P.op("dve", f_memset(t_.t[:, :, :], 0.0), writes=[t_.b])
        v3 = lambda ap: ap.rearrange("p (a b) -> p a b", b=16)
        st = {}

        def load(hd):
            q, k, vv, km, bt = qr.next(), kr.next(), vr.next(), kmr.next(), btr.next()
            P.dma("sp", q.t[:, :], qT.ap()[hd * 128:(hd + 1) * 128, :], writes=[q.b])
            P.dma("sp", k.t[:, :], kT.ap()[hd * 128:(hd + 1) * 128, :], writes=[k.b])
            vsrc = v.ap()[:, hd * 128:(hd + 1) * 128].rearrange("(n p) d -> p n d", p=128)
            for n0 in range(0, NKT, 8):
                P.dma("sp", vv.t[:, n0:n0 + 8, :], vsrc[:, n0:n0 + 8, :], writes=[vv.b])
            P.op("dve", f_memset(km.t[:, :], 0.0), writes=[km.b])
            P.op("dve", f_copy(km.t[:, 0:NBLK], ksum.t[:, hd, :]), reads=[ksum.b], writes=[km.b])
            st[hd] = (q, k, vv, km, bt)

        def gate_a(hd):
            q, k, vv, km, bt = st[hd]
            for qi in range(NKT):
                P.op("pe", f_mm(psm.t[:, qi * 16:(qi + 1) * 16], q.t[:, qi * 128:(qi + 1) * 128], km.t[:, 0:16]),
                     reads=[q.b, km.b], writes=[psm.b])
            gma, mxa, bqa = gmar.next(), mxar.next(), bqar.next()
            P.op("dve", f_tt(gma.t[:, :], psm.t[:, 0:NKT * 16], pastm.t[:, :], ALU.add), reads=[psm.b, pastm.b], writes=[gma.b])
            for qi in range(NKT):
                P.op("dve", lambda e, o_=mxa.t[:, qi, :], i_=gma.t[:, qi * 16:(qi + 1) * 16]: e.max(out=o_, in_=i_),
                     reads=[gma.b], writes=[mxa.b])
            for qi in range(NKT):
                P.op("dve", f_ts(gma.t[:, qi * 16:(qi + 1) * 16], gma.t[:, qi * 16:(qi + 1) * 16], mxa.t[:, qi, 2:3], None, ALU.is_ge),
                     reads=[gma.b, mxa.b], writes=[gma.b])
            P.op("dve", f_ts(bqa.t[:, :, 0:16], v3(gma.t[:, :]), -NEG, NEG, ALU.mult, ALU.add), reads=[gma.b], writes=[bqa.b])
            P.op("dve", f_tt(bqa.t[:, :, 0:16], bqa.t[:, :, 0:16], v3(pastm.t[:, :]), ALU.min), reads=[bqa.b, pastm.b], writes=[bqa.b])
            P.op("dve", f_tt(bqa.t[:, :, 0:16], bqa.t[:, :, 0:16], v3(ownm.t[:, :]), ALU.max), reads=[bqa.b, ownm.b], writes=[bqa.b])
            st[hd] = (q, k, vv, km, bt, bqa)

        def gate_b(hd):
            q, k, vv, km, bt, bqa = st[hd]
            for grp in range(NKT // 4):
                pst = psr.next()
                for j_ in range(4):
                    P.op("pe", f_tr(pst.t[:, j_ * 128:(j_ + 1) * 128], bqa.t[:, grp * 4 + j_, :], identf.t[:, :]),
                         reads=[bqa.b, identf.b], writes=[pst.b])
                P.op("act", f_act(bt.t[:, grp * 512:(grp + 1) * 512], pst.t[:, :], AF.Copy), reads=[pst.b], writes=[bt.b])

        def attention(hd):
            q, k, vv, km, bt, bqa = st.pop(hd)
            for g in range(S // T):
                nkt = 4 * (g + 1)
                po, pz = por.next(), pzr.next()
                qs = q.t[:, g * T:(g + 1) * T]

                def qk(kt):
                    ps = psr.next()
                    n = kt // 2
                    diag = kt >= 4 * g
                    P.op("pe", f_mm(ps.t[:, :], k.t[:, kt * 128:(kt + 1) * 128], qs, True, False),
                         reads=[k.b, q.b], writes=[ps.b])
                    P.op("pe", f_mm(ps.t[:, :], esel.t[:, n * 128:(n + 1) * 128], bt.t[:, g * T:(g + 1) * T], False, not diag),
                         reads=[esel.b, bt.b], writes=[ps.b])
                    if diag:
                        P.op("pe", f_mm(ps.t[:, :], identb.t[:, :], cmask.t[:, kt - 4 * g, :], False, True),
                             reads=[identb.b, cmask.b], writes=[ps.b])
                    return ps

                ps_next = qk(0)
                for kt in range(nkt):
                    ps = ps_next
                    pt = ptr.next()
                    P.op("act", f_act(pt.t[:, :], ps.t[:, :], AF.Exp, scale=float(cfg.SCALE)), reads=[ps.b], writes=[pt.b])
                    if kt + 1 < nkt:
                        ps_next = qk(kt + 1)
                    P.op("pe", f_mm(po.t[:, :], vv.t[:, kt, :], pt.t[:, :], kt == 0, kt == nkt - 1),
                         reads=[vv.b, pt.b], writes=[po.b])
                    P.op("pe", f_mm(pz.t[:, :], ones.t[:, :], pt.t[:, :], kt == 0, kt == nkt - 1),
                         reads=[ones.b, pt.b], writes=[pz.b])
                rz = rzr.next()
                P.op("dve", f_recip(rz.t[:, :], pz.t[:, :]), reads=[pz.b], writes=[rz.b])
                ob = obr.next()
                P.op("dve", f_tt(ob.t[:, :], po.t[:, :], rz.t[:, :], ALU.mult), reads=[po.b, rz.b], writes=[ob.b])
                P.dma("sp", oT.ap()[hd * 128:(hd + 1) * 128, g * T:(g + 1) * T], ob.t[:, :], reads=[ob.b])

        load(0)
        gate_a(0)
        gate_b(0)
        for hd in range(NA):
            if hd + 1 < NA:
                load(hd + 1)
                gate_a(hd + 1)
            attention(hd)
            if hd + 1 < NA:
                gate_b(hd + 1)
        P.emit()


def stage_dilated(nc, cfg, C, qT, kT, v, oT):
    S, NA, NBG, T = cfg.S, cfg.NA, cfg.NBG, cfg.T
    NKT = S // 128
    with ExitStack() as es:
        cx = Ctx(nc, es)
        P = Prog(nc)
        K = load_consts(P, cx, cfg, None)
        ones = K["ones"]
        identb = cx.sb("identb", [128, 128], BF16)
        P.dma("sp", identb.t[:, :], C["identb"].ap(), writes=[identb.b])
        bmask = cx.sb("bmask", [128, 2, 256], BF16)
        P.dma("sp", bmask.t[:, :, :], C["bmask"].ap(), writes=[bmask.b])
        qr = ring(cx, "q", 4, [128, S], BF16)
        kr = ring(cx, "k", 4, [128, S], BF16)
        vr = ring(cx, "v", 2, [128, NKT, 128], BF16)
        uacc = cx.sb("uacc", [128, S], F32)
        zacc = cx.sb("zacc", [128, S], F32)
        ptr = ring(cx, "pt", 3, [128, 256], BF16)
        obr = ring(cx, "ob", 2, [128, T], BF16)
        psr = psring(cx, "ps", 4)
        pur = psring(cx, "pu", 4)
        for hh in range(NBG):
            for gi, (win, dil) in enumerate(cfg.PATS):
                hq = NA + gi * NBG + hh
                nblk = S // (128 * dil)
                q, k, vv = qr.next(), kr.next(), vr.next()
                P.dma("sp", q.t[:, :], qT.ap()[hq * 128:(hq + 1) * 128, :], writes=[q.b])
                P.dma("sp", k.t[:, :], kT.ap()[hq * 128:(hq + 1) * 128, :], writes=[k.b])
                vsrc = v.ap()[:, hq * 128:(hq + 1) * 128].rearrange("(n p r) d -> p r n d", p=128, r=dil)
                for r in range(dil):
                    for n0 in range(0, nblk, 8):
                        n1 = min(nblk, n0 + 8)
                        P.dma("sp", vv.t[:, r * nblk + n0:r * nblk + n1, :], vsrc[:, r, n0:n1, :], writes=[vv.b])
                if dil > 1:
                    q2, k2 = qr.next(), kr.next()
                    sub = S // dil
                    for r in range(dil):
                        P.op("dve", f_copy(q2.t[:, r * sub:(r + 1) * sub], q.t[:, r:r + (sub - 1) * dil + 1:dil]), reads=[q.b], writes=[q2.b])
                        P.op("act", f_act(k2.t[:, r * sub:(r + 1) * sub], k.t[:, r:r + (sub - 1) * dil + 1:dil], AF.Copy), reads=[k.b], writes=[k2.b])
                    q, k = q2, k2
                tiles = [(r, n) for r in range(dil) for n in range(nblk)]

                def qk(rn, q=q, k=k, nblk=nblk):
                    r, n = rn
                    c0 = (r * nblk + n) * 128
                    p0 = c0 - 128 if n > 0 else c0
                    ps = psr.next()
                    P.op("pe", f_mm(ps.t[:, 0:128], k.t[:, c0:c0 + 128], q.t[:, c0:c0 + 128], True, False), reads=[k.b, q.b], writes=[ps.b])
                    P.op("pe", f_mm(ps.t[:, 128:256], k.t[:, p0:p0 + 128], q.t[:, c0:c0 + 128], False, False), reads=[k.b, q.b], writes=[ps.b])
                    P.op("pe", f_mm(ps.t[:, 0:256], identb.t[:, :], bmask.t[:, 1 if n > 0 else 0, :], False, True),
                         reads=[identb.b, bmask.b], writes=[ps.b])
                    return ps

                ps_next = qk(tiles[0])
                for ti, (r, n) in enumerate(tiles):
                    b0 = n * 128 * dil + r
                    cur = slice(b0, b0 + 127 * dil + 1, dil)
                    ps = ps_next
                    pt = ptr.next()
                    P.op("act", f_act(pt.t[:, :], ps.t[:, 0:256], AF.Exp, scale=float(cfg.SCALE)), reads=[ps.b], writes=[pt.b])
                    if ti + 1 < len(tiles):
                        ps_next = qk(tiles[ti + 1])
                    pu = pur.next()
                    vi = r * nblk + n
                    vp = vi - 1 if n > 0 else vi
                    P.op("pe", f_mm(pu.t[:, 0:128], vv.t[:, vi, :], pt.t[:, 0:128], True, False), reads=[vv.b, pt.b], writes=[pu.b])
                    P.op("pe", f_mm(pu.t[:, 0:128], vv.t[:, vp, :], pt.t[:, 128:256], False, False), reads=[vv.b, pt.b], writes=[pu.b])
                    P.op("pe", f_mm(pu.t[:, 128:256], ones.t[:, :], pt.t[:, 0:128], False, False), reads=[ones.b, pt.b], writes=[pu.b])
                    P.op("pe", f_mm(pu.t[:, 128:256], ones.t[:, :], pt.t[:, 128:256], False, True), reads=[ones.b, pt.b], writes=[pu.b])
                    if gi == 0:
                        P.op("dve", f_copy(uacc.t[:, cur], pu.t[:, 0:128]), reads=[pu.b], writes=[uacc.b])
                        P.op("dve", f_copy(zacc.t[:, cur], pu.t[:, 128:256]), reads=[pu.b], writes=[zacc.b])
                    else:
                        P.op("dve", f_tt(uacc.t[:, cur], uacc.t[:, cur], pu.t[:, 0:128], ALU.add), reads=[pu.b, uacc.b], writes=[uacc.b])
                        P.op("dve", f_tt(zacc.t[:, cur], zacc.t[:, cur], pu.t[:, 128:256], ALU.add), reads=[pu.b, zacc.b], writes=[zacc.b])
            P.op("dve", f_recip(zacc.t[:, :], zacc.t[:, :]), reads=[zacc.b], writes=[zacc.b])
            for g in range(S // T):
                ob = obr.next()
                P.op("dve", f_tt(ob.t[:, :], uacc.t[:, g * T:(g + 1) * T], zacc.t[:, g * T:(g + 1) * T], ALU.mult),
                     reads=[uacc.b, zacc.b], writes=[ob.b])
                P.dma("sp", oT.ap()[(NA + hh) * 128:(NA + hh + 1) * 128, g * T:(g + 1) * T], ob.t[:, :], reads=[ob.b])
        P.emit()


def stage_sg_setup(nc, cfg, C, ws_d, bs_d, lnb_d, wsT_d, bsb_d):
    GG = cfg.GG
    with ExitStack() as es:
        cx = Ctx(nc, es)
        P = Prog(nc)
        identf = cx.sb("identf", [128, 128], F32)
        P.dma("sp", identf.t[:, :], C["identf"].ap(), writes=[identf.b])
        tril = cx.sb("tril", [128, 128], F32)
        P.dma("sp", tril.t[:, :], C["tril"].ap(), writes=[tril.b])
        e0 = cx.sb("e0", [128, 128], BF16)
        P.op("dve", f_memset(e0.t[:, :], 0.0), writes=[e0.b])
        P.op("dve", f_memset(e0.t[0:1, :], 1.0), writes=[e0.b])
        lnb = cx.sb("lnb", [128, cfg.E], BF16)
        P.dma("pool", lnb.t[:, :], lnb_d.ap(), writes=[lnb.b])
        bs0 = cx.sb("bs0", [128, GG * 128], BF16)
        P.op("dve", f_memset(bs0.t[:, :], 0.0), writes=[bs0.b])
        P.dma("pool", bs0.t[0:1, :], bs_d.ap(), writes=[bs0.b])
        wsT = cx.sb("wsT", [128, GG, 128], BF16)
        bsb = cx.sb("bsb", [128, GG, 128], F32)
        wr = ring(cx, "w", 3, [128, 128], F32)
        psr = psring(cx, "ps", 4)
        for g in range(GG):
            w = wr.next()
            P.dma("sp", w.t[:, :], ws_d.ap()[g], writes=[w.b])
            P.op("dve", f_tt(w.t[:, :], w.t[:, :], tril.t[:, :], ALU.mult), reads=[w.b, tril.b], writes=[w.b])
            ps = psr.next()
            P.op("pe", f_tr(ps.t[:, 0:128], w.t[:, :], identf.t[:, :]), reads=[w.b, identf.b], writes=[ps.b])
            P.op("act", f_act(wsT.t[:, g, :], ps.t[:, 0:128], AF.Copy), reads=[ps.b], writes=[wsT.b])
            ps2 = psr.next()
            P.op("pe", f_mm(ps2.t[:, 0:128], lnb.t[:, g * 128:(g + 1) * 128], wsT.t[:, g, :], True, False),
                 reads=[lnb.b, wsT.b], writes=[ps2.b])
            P.op("pe", f_mm(ps2.t[:, 0:128], e0.t[:, :], bs0.t[:, g * 128:(g + 1) * 128], False, True),
                 reads=[e0.b, bs0.b], writes=[ps2.b])
            P.op("act", f_act(bsb.t[:, g, :], ps2.t[:, 0:128], AF.Copy), reads=[ps2.b], writes=[bsb.b])
        P.dma("sp", wsT_d.ap(), wsT.t[:, :, :], reads=[wsT.b])
        P.dma("sp", bsb_d.ap(), bsb.t[:, :, :], reads=[bsb.b])
        P.emit()


def stage_sg_main(nc, cfg, xin, g_d, wu_d, wv_d, bu_d, bv_d, lng_d, wsT_d, bsb_d, mT):
    DC, S, GG, E = cfg.DC, cfg.S, cfg.GG, cfg.E
    T = 512
    VW = 256
    with ExitStack() as es:
        cx = Ctx(nc, es)
        P = Prog(nc)
        K = load_consts(P, cx, cfg, None)
        g = cx.sb("g", [128, DC], F32)
        P.dma("sp", g.t[:, :], g_d.ap(), writes=[g.b])
        bu = cx.sb("bu", [128, GG], F32)
        P.dma("sp", bu.t[:, :], bu_d.ap(), writes=[bu.b])
        lng = cx.sb("lng", [128, GG], F32)
        P.dma("sp", lng.t[:, :], lng_d.ap(), writes=[lng.b])
        e0 = cx.sb("e0", [128, 128], BF16)
        P.op("dve", f_memset(e0.t[:, :], 0.0), writes=[e0.b])
        P.op("dve", f_memset(e0.t[0:1, :], 1.0), writes=[e0.b])
        bv0 = cx.sb("bv0", [128, E], BF16)
        P.op("dve", f_memset(bv0.t[:, :], 0.0), writes=[bv0.b])
        P.dma("pool", bv0.t[0:1, :], bv_d.ap(), writes=[bv0.b])
        wsT = cx.sb("wsT", [128, GG, 128], BF16)
        P.dma("sp", wsT.t[:, :, :], wsT_d.ap(), writes=[wsT.b])
        bsb = cx.sb("bsb", [128, GG, 128], F32)
        P.dma("sp", bsb.t[:, :, :], bsb_d.ap(), writes=[bsb.b])
        h = cx.sb("h", [128, DC, T], BF16)
        uT = cx.sb("uT", [128, GG, T], BF16)
        xring = ring(cx, "xc", 3, [128, T], F32)
        sqring = ring(cx, "sq", 2, [128, T], BF16)
        rstd = cx.sb("rstd", [128, T], F32)
        wring = ring(cx, "w", 2, [128, DC, 128], BF16)
        wvring = ring(cx, "wv", 2, [128, DC, VW], BF16)
        vbufs = [cx.sb("vbuf%d" % i, [128, E], BF16) for i in range(T // 128)]
        vn = cx.sb("vn", [128, E], BF16)
        nst = max(1, E // 512)
        stats = cx.sb("stats", [128, nst, 6], F32)
        mv = cx.sb("mv", [128, 2], F32)
        fbr = ring(cx, "fb", 3, [128, 128], F32)
        mbr = ring(cx, "mb", 1, [128, GG, 128], BF16)
        psn = cx.ps("psn")
        psr = psring(cx, "ps", 5)
        psa = psring(cx, "pa", 2)
        mv_ = dview(mT)
        for tt in range(S // T):
            tok0 = tt * T
            emit_norm(P, cfg, K, xin, tok0, g, h, 0, xring, sqring, K["ones"], psn, rstd, T=T)
            for j in range(GG):
                w = wring.next()
                P.dma("pool", w.t[:, :, :], wu_d.ap()[j], writes=[w.b])
                ps = psr.next()
                for k in range(DC):
                    P.op("pe", f_mm(ps.t[:, 0:T], w.t[:, k, :], h.t[:, k, :], k == 0, k == DC - 1), reads=[w.b, h.b], writes=[ps.b])
                P.op("act", f_act(uT.t[:, j, :], ps.t[:, 0:T], AF.Gelu, bias=bu.t[:, j:j + 1]), reads=[ps.b, bu.b], writes=[uT.b])
            for jb in range(E // VW):
                wv = wvring.next()
                P.dma("pool", wv.t[:, :, :], wv_d.ap()[jb], writes=[wv.b])
                for tb in range(T // 128):
                    ps = psr.next()
                    for k in range(DC):
                        P.op("pe", f_mm(ps.t[:, 0:VW], h.t[:, k, tb * 128:(tb + 1) * 128], wv.t[:, k, :], k == 0, False),
                             reads=[wv.b, h.b], writes=[ps.b])
                    P.op("pe", f_mm(ps.t[:, 0:VW], e0.t[:, :], bv0.t[:, jb * VW:(jb + 1) * VW], False, True),
                         reads=[e0.b, bv0.b], writes=[ps.b])
                    P.op("act", f_act(vbufs[tb].t[:, jb * VW:(jb + 1) * VW], ps.t[:, 0:VW], AF.Gelu), reads=[ps.b], writes=[vbufs[tb].b])
            for tb in range(T // 128):
                vb = vbufs[tb]
                for i in range(nst):
                    w_ = min(512, E)
                    P.op("dve", lambda e, o_=stats.t[:, i, :], i_=vb.t[:, i * w_:(i + 1) * w_]: e.bn_stats(out=o_, in_=i_),
                         reads=[vb.b], writes=[stats.b])
                P.op("dve", lambda e, o_=mv.t[:, :], i_=stats.t[:, :, :]: e.bn_aggr(out=o_, in_=i_), reads=[stats.b], writes=[mv.b])
                P.op("act", f_act(mv.t[:, 1:2], mv.t[:, 1:2], AF.Sqrt, bias=K["eps"].t[:, 0:1]), reads=[mv.b, K["eps"].b], writes=[mv.b])
                P.op("dve", f_recip(mv.t[:, 1:2], mv.t[:, 1:2]), reads=[mv.b], writes=[mv.b])
                P.op("dve", f_ts(vn.t[:, :], vb.t[:, :], mv.t[:, 0:1], mv.t[:, 1:2], ALU.subtract, ALU.mult),
                     reads=[vb.b, mv.b], writes=[vn.b])
                mb = mbr.next()
                for gi in range(GG):
                    if gi % 4 == 0:
                        pa = psa.next()
                    c0 = (gi % 4) * 128
                    P.op("pe", f_mm(pa.t[:, c0:c0 + 128], vn.t[:, gi * 128:(gi + 1) * 128], wsT.t[:, gi, :], True, True),
                         reads=[vn.b, wsT.b], writes=[pa.b])
                    fb = fbr.next()
                    P.op("dve", f_stt(fb.t[:, :], pa.t[:, c0:c0 + 128], lng.t[:, gi:gi + 1], bsb.t[:, gi, :], ALU.mult, ALU.add),
                         reads=[pa.b, lng.b, bsb.b], writes=[fb.b])
                    P.op("dve", f_tt(mb.t[:, gi, :], fb.t[:, :], uT.t[:, gi, tb * 128:(tb + 1) * 128], ALU.mult),
                         reads=[fb.b, uT.b], writes=[mb.b])
                for g0 in range(0, GG, 8):
                    g1 = min(GG, g0 + 8)
                    P.dma("sp", mv_[:, g0:g1, tok0 + tb * 128:tok0 + (tb + 1) * 128], mb.t[:, g0:g1, :], reads=[mb.b])
        P.emit()


def stage_final_norm(nc, cfg, xin, g_d, out):
    DC, S, T = cfg.DC, cfg.S, cfg.T
    CG = 8
    NG = (DC + CG - 1) // CG
    with ExitStack() as es:
        cx = Ctx(nc, es)
        P = Prog(nc)
        K = load_consts(P, cx, cfg, None)
        g = cx.sb("g", [128, DC], F32)
        P.dma("sp", g.t[:, :], g_d.ap(), writes=[g.b])
        xts = [cx.sb("x%d" % i, [128, DC, T], F32, nsub=NG) for i in range(2)]
        sqring = ring(cx, "sq", 3, [128, T], BF16)
        rsr = ring(cx, "rstd", 2, [128, T], F32)
        pnr = psring(cx, "psn", 2)
        xv = dview(xin)
        ov = dview(out)
        for tt in range(S // T):
            x = xts[tt % 2]
            tok0 = tt * T
            for gi in range(NG):
                c0, c1 = gi * CG, min(DC, (gi + 1) * CG)
                P.dma("sp", x.t[:, c0:c1, :], xv[:, c0:c1, tok0:tok0 + T], writes=[x.sub[gi]])
            psn, rstd = pnr.next(), rsr.next()
            for c in range(DC):
                sq = sqring.next()
                P.op("act", f_act(sq.t[:, :], x.t[:, c, :], AF.Square), reads=[x.sub[c // CG]], writes=[sq.b])
                P.op("pe", f_mm(psn.t[:, :], K["ones"].t[:, :], sq.t[:, :], c == 0, c == DC - 1),
                     reads=[K["ones"].b, sq.b], writes=[psn.b])
            P.op("act", f_act(rstd.t[:, :], psn.t[:, :], AF.Sqrt, scale=1.0 / cfg.D, bias=K["eps"].t[:, 0:1]),
                 reads=[psn.b, K["eps"].b], writes=[rstd.b])
            P.op("dve", f_recip(rstd.t[:, :], rstd.t[:, :]), reads=[rstd.b], writes=[rstd.b])
            for c in range(DC):
                P.op("dve", f_stt(x.t[:, c, :], x.t[:, c, :], g.t[:, c:c + 1], rstd.t[:, :], ALU.mult, ALU.mult),
                     reads=[x.sub[c // CG], g.b, rstd.b], writes=[x.sub[c // CG]])
            for gi in range(NG):
                c0, c1 = gi * CG, min(DC, (gi + 1) * CG)
                P.dma("sp", ov[:, c0:c1, tok0:tok0 + T], x.t[:, c0:c1, :], reads=[x.sub[gi]])
        P.emit()


def build_program(cfg):
    nc = bass.Bass("TRN2", target_bir_lowering=False)
    D, S, DC, FC, NH, NA, GG, E, DFF = cfg.D, cfg.S, cfg.DC, cfg.FC, cfg.NH, cfg.NA, cfg.GG, cfg.E, cfg.DFF
    C = make_consts(nc, cfg)

    def di(name, shape, dt=F32):
        return nc.dram_tensor(name, list(shape), dt, kind="ExternalInput")

    def ds(name, shape, dt):
        return nc.dram_tensor(name, list(shape), dt)

    xT = di("xT", [D, S])
    pos = di("pos", [32, S], I32)
    g_attn = di("g_attn", [128, DC])
    wqk = di("wqk", [2 * NH, 128, DC, 128])
    wv = di("wv", [NH * 128 // cfg.VW, 128, DC, cfg.VW])
    wo = di("wo", [DC, 128, DC, 128])
    ffn = []
    for l in range(2):
        ffn.append(dict(g=di("g_ffn%d" % l, [128, DC]), wup=di("wup%d" % l, [2 * FC, 128, DC, 128]),
                        cw=di("cw%d" % l, [128, 3, 2 * FC]), cb=di("cb%d" % l, [128, 2 * FC]),
                        wdn=di("wdn%d" % l, [DC, 128, FC, 128])))
    g_sg = di("g_sg", [128, DC])
    sg_wu = di("sg_wu", [GG, 128, DC, 128])
    sg_wv = di("sg_wv", [E // 256, 128, DC, 256])
    sg_bu = di("sg_bu", [128, GG])
    sg_bv = di("sg_bv", [1, E])
    sg_lng = di("sg_lng", [128, GG])
    sg_lnb = di("sg_lnb", [128, E])
    sg_ws = di("sg_ws", [GG, 128, 128])
    sg_bs = di("sg_bs", [1, GG * 128])
    sg_wo = di("sg_wo", [DC, 128, GG, 128])
    g_fin = di("g_fin", [128, DC])
    outT = nc.dram_tensor("outT", [D, S], F32, kind="ExternalOutput")

    qT = ds("qT", [NH * 128, S], BF16)
    kT = ds("kT", [NH * 128, S], BF16)
    v = ds("v", [S, NH * 128], BF16)
    ksum = ds("ksum", [128, NA, cfg.NBLK], F32)
    cs = ds("cs", [32, 2, S], F32)
    oT = ds("oT", [D, S], BF16)
    gT = ds("gT", [DFF, S], BF16)
    xa = ds("xa", [D, S], F32)
    xb = ds("xb", [D, S], F32)
    wdnb = [ds("wdnb%d" % l, [DC, 128, FC, 128], BF16) for l in range(2)]
    wsT = ds("wsT", [128, GG, 128], BF16)
    bsb = ds("bsb", [128, GG, 128], F32)

    import os
    sel = os.environ.get("MK_STAGES")
    sel = set(int(t) for t in sel.split(",")) if sel else set(range(1, 13))
    stages = [
        lambda: (stage_rope(nc, cfg, C, pos, cs), stage_qkv(nc, cfg, C, xT, g_attn, wqk, wv, cs, qT, kT, v, ksum)),
        lambda: stage_moba(nc, cfg, C, qT, kT, v, ksum, oT),
        lambda: stage_dilated(nc, cfg, C, qT, kT, v, oT),
        lambda: stage_linear_res(nc, cfg, DC, oT, wo, xT, xa),
        lambda: stage_ffn_up(nc, cfg, xa, ffn[0]["g"], ffn[0]["wup"], ffn[0]["cw"], ffn[0]["cb"], gT, precast=(ffn[0]["wdn"], wdnb[0])),
        lambda: stage_linear_res(nc, cfg, FC, gT, wdnb[0], xa, xb),
        lambda: stage_sg_setup(nc, cfg, C, sg_ws, sg_bs, sg_lnb, wsT, bsb),
        lambda: stage_sg_main(nc, cfg, xb, g_sg, sg_wu, sg_wv, sg_bu, sg_bv, sg_lng, wsT, bsb, oT),
        lambda: stage_linear_res(nc, cfg, GG, oT, sg_wo, xb, xa),
        lambda: stage_ffn_up(nc, cfg, xa, ffn[1]["g"], ffn[1]["wup"], ffn[1]["cw"], ffn[1]["cb"], gT, precast=(ffn[1]["wdn"], wdnb[1])),
        lambda: stage_linear_res(nc, cfg, FC, gT, wdnb[1], xa, xb),
        lambda: stage_final_norm(nc, cfg, xb, g_fin, outT),
    ]
    for i, st in enumerate(stages):
        if i + 1 in sel:
            st()
    return nc


def host_prep(cfg, p):
    NH, FC, GG, E = cfg.NH, cfg.FC, cfg.GG, cfg.E
    f = lambda a: np.asarray(a, dtype=np.float32)
    w_in = f(p["attn_w_in"])[0]
    m = {}
    m["g_attn"] = vec_pc(f(p["attn_norm"])[0])
    m["wqk"] = tile_w(w_in[:, :2 * NH * 128])
    m["wv"] = tile_w(w_in[:, 2 * NH * 128:], cfg.VW)
    m["wo"] = tile_w(f(p["attn_w_out"])[0])
    for l in range(2):
        m["g_ffn%d" % l] = vec_pc(f(p["ffn_norm"])[l])
        m["wup%d" % l] = tile_w(f(p["ffn_w_up"])[l])
        m["cw%d" % l] = np.ascontiguousarray(f(p["ffn_conv_w"])[l].reshape(3, 2 * FC, 128).transpose(2, 0, 1))
        m["cb%d" % l] = vec_pc(f(p["ffn_conv_b"])[l])
        m["wdn%d" % l] = tile_w(f(p["ffn_w_down"])[l])
    sw = f(p["sg_w_in"])[0]
    sb = f(p["sg_b_in"])[0]
    m["g_sg"] = vec_pc(f(p["sg_norm"])[0])
    m["sg_wu"] = tile_w(sw[:, :E])
    m["sg_wv"] = tile_w(sw[:, E:], 256)
    m["sg_bu"] = vec_pc(sb[:E])
    m["sg_bv"] = np.ascontiguousarray(sb[E:].reshape(1, E))
    m["sg_lng"] = vec_pc(f(p["sg_v_gain"])[0])
    m["sg_lnb"] = np.ascontiguousarray(np.broadcast_to(f(p["sg_v_bias"])[0], (128, E)))
    m["sg_ws"] = np.ascontiguousarray(f(p["sg_w_s"])[0])
    m["sg_bs"] = np.ascontiguousarray(f(p["sg_b_s"])[0].reshape(1, GG * 128))
    m["sg_wo"] = tile_w(f(p["sg_w_out"])[0])
    m["g_fin"] = vec_pc(f(p["final_norm"]))
    return m


def run_module(cfg, inputs):
    x = np.asarray(inputs["x"], dtype=np.float32)
    positions = np.asarray(inputs["positions"]).astype(np.int32)
    B = x.shape[0]
    import time as _t
    t0 = _t.time()
    shared = host_prep(cfg, inputs)
    t1 = _t.time()
    nc = build_program(cfg)
    print("[mk] host_prep %.1fs build %.1fs" % (t1 - t0, _t.time() - t1), flush=True)
    in_maps = []
    for b in range(B):
        m = dict(shared)
        m["xT"] = np.ascontiguousarray(x[b].T)
        m["pos"] = np.ascontiguousarray(np.broadcast_to(positions[b], (32, cfg.S)))
        in_maps.append(m)
    t2 = _t.time()
    res = run_bass_kernel_spmd(nc, in_maps, core_ids=list(range(B)))
    print("[mk] launch %.1fs" % (_t.time() - t2), flush=True)
    return np.stack([np.ascontiguousarray(res.results[b]["outT"].T) for b in range(B)], 0)


def kernel(**inputs):
    cfg = Cfg()
    return run_module(cfg, inputs)
```

```python
import numpy as np
from contextlib import ExitStack
import concourse.bass as bass
import concourse.mybir as mybir
from concourse.bass_utils import run_bass_kernel_spmd

F32 = mybir.dt.float32
BF16 = mybir.dt.bfloat16
I32 = mybir.dt.int32
AF = mybir.ActivationFunctionType
ALU = mybir.AluOpType
AX = mybir.AxisListType

NEG = -30000.0
ENGS = ("pe", "act", "dve", "pool", "sp")
NRING = 12


class Cfg:
    def __init__(self, D=4096, S=4096, NA=24, NBG=8, DFF=14336, B=2):
        self.D, self.S, self.NA, self.NBG, self.DFF, self.B = D, S, NA, NBG, DFF, B
        self.DH = 128
        self.NB = 3 * NBG
        self.NH = NA + self.NB
        self.NHO = NA + NBG
        assert self.NHO * 128 == D
        self.QKV = 3 * self.NH * 128
        self.DC = D // 128
        self.FC = DFF // 128
        self.T = 512
        self.NT = S // self.T
        self.MB = 256
        self.NBLK = S // self.MB
        self.E = D
        self.GG = self.E // 128
        self.EPS = 1e-5
        self.PATS = ((128, 1), (512, 4), (2048, 16))
        self.VW = 512 if (self.NH * 128) % 512 == 0 else 256
        self.SCALE = 128 ** -0.5


class Buf:
    __slots__ = ("w", "r")

    def __init__(self):
        self.w = None
        self.r = []


class Op:
    __slots__ = ("eng", "fn", "deps", "signal", "ev", "dma", "di")

    def __init__(self, eng, fn, dma):
        self.eng, self.fn, self.dma = eng, fn, dma
        self.deps = []
        self.signal = False
        self.ev = None
        self.di = -1


class Prog:
    def __init__(self, nc):
        self.nc = nc
        self.ops = []

    def op(self, eng, fn, reads=(), writes=(), dma=False):
        o = Op(eng, fn, dma)
        deps = {}
        for b in reads:
            if b.w is not None:
                deps[id(b.w)] = b.w
        for b in writes:
            if b.w is not None:
                deps[id(b.w)] = b.w
            for r in b.r:
                deps[id(r)] = r
        for d in deps.values():
            if d is o:
                continue
            if d.eng == "pe" and eng == "pe" and not d.dma and not dma:
                continue
            o.deps.append(d)
            d.signal = True
        for b in writes:
            b.w = o
            b.r = []
        for b in reads:
            if b.w is not o:
                if not dma:
                    b.r = [r for r in b.r if r.dma or r.eng != eng]
                b.r.append(o)
        self.ops.append(o)
        return o

    def dma(self, eng, out, in_, reads=(), writes=()):
        return self.op(eng, lambda e: e.dma_start(out=out, in_=in_), reads, writes, dma=True)

    def emit(self, name=None):
        nc = self.nc
        with ExitStack() as es:
            if not hasattr(nc, "_mk_sems"):
                gs = ExitStack()
                c_ = {e: gs.enter_context(nc.semaphore("c_" + e)) for e in ENGS}
                d_ = {e: [gs.enter_context(nc.semaphore("d_%s%d" % (e, i))) for i in range(NRING)]
                      for e in ("sp", "pool")}
                nc._mk_sems = (gs, c_, d_)
                nc._mk_pool_n = 0
            _, csem, dsem = nc._mk_sems
            cnt = {e: 0 for e in ENGS}
            dcnt = {e: 0 for e in dsem}
            dcnt["pool"] = nc._mk_pool_n
            pool_base = nc._mk_pool_n
            by_eng = {e: [] for e in ENGS}
            dlist = {e: [] for e in dsem}
            for o in self.ops:
                if o.dma:
                    i = dcnt[o.eng]
                    dcnt[o.eng] += 1
                    o.di = i
                    o.ev = (dsem[o.eng][i % NRING], 16 * (i // NRING + 1))
                    dlist[o.eng].append(o)
                    if o.eng == "pool":
                        nc._mk_pool_n = i + 1
                elif o.signal:
                    cnt[o.eng] += 1
                    o.ev = (csem[o.eng], cnt[o.eng])
                by_eng[o.eng].append(o)
            block = es.enter_context(nc.Block())

            def body(e, eng):
                waited = {}

                def wait(ev):
                    sem, val = ev
                    k = id(sem)
                    if waited.get(k, 0) < val:
                        e.wait_ge(sem, val)
                        waited[k] = val

                for o in by_eng[eng]:
                    need = {}
                    for d in o.deps:
                        sm, val = d.ev
                        k_ = id(sm)
                        if k_ not in need or need[k_][1] < val:
                            need[k_] = (sm, val)
                    for ev in need.values():
                        wait(ev)
                    if o.dma:
                        li = o.di - (pool_base if eng == "pool" else 0)
                        if li >= NRING:
                            wait(dlist[eng][li - NRING].ev)
                    ins = o.fn(e)
                    if o.dma:
                        ins.then_inc(o.ev[0], 16)
                    elif o.signal:
                        ins.then_inc(o.ev[0], 1)
                if eng in dlist:
                    for o in dlist[eng][-NRING:]:
                        wait(o.ev)

            block.tensor(lambda e: body(e, "pe"))
            block.scalar(lambda e: body(e, "act"))
            block.vector(lambda e: body(e, "dve"))
            block.gpsimd(lambda e: body(e, "pool"))
            block.sync(lambda e: body(e, "sp"))
        _, csem, dsem = nc._mk_sems
        allsems = list(csem.values()) + list(dsem["sp"])
        with nc.Block() as blk2:
            def clr(e):
                for sm in allsems:
                    e.sem_clear(sm)
            blk2.sync(clr)
        self.ops = []


class Tile:
    def __init__(self, t, nsub=0):
        self.t = t
        self.b = Buf()
        self.sub = [Buf() for _ in range(nsub)]


class Ctx:
    _n = 0

    def __init__(self, nc, es):
        self.nc, self.es = nc, es
        Ctx._n += 1
        self.pre = "s%d_" % Ctx._n

    def sb(self, name, shape, dt, nsub=0):
        return Tile(self.es.enter_context(self.nc.sbuf_tensor(self.pre + name, list(shape), dt)), nsub)

    def ps(self, name, shape=(128, 512), dt=F32, nsub=0):
        return Tile(self.es.enter_context(self.nc.psum_tensor(self.pre + name, list(shape), dt)), nsub)


class Ring:
    def __init__(self, tiles):
        self.tiles = tiles
        self.i = 0

    def next(self):
        t = self.tiles[self.i % len(self.tiles)]
        self.i += 1
        return t


def ring(cx, name, n, shape, dt):
    return Ring([cx.sb("%s%d" % (name, i), shape, dt) for i in range(n)])


def psring(cx, name, n):
    return Ring([cx.ps("%s%d" % (name, i)) for i in range(n)])


def f_mm(out, lhsT, rhs, start=True, stop=True):
    return lambda e: e.matmul(out, lhsT, rhs, start=start, stop=stop)


def f_tr(out, in_, ident):
    return lambda e: e.transpose(out, in_, ident)


def f_act(out, in_, func, **kw):
    return lambda e: e.activation(out=out, in_=in_, func=func, **kw)


def f_ts(out, in0, s1, s2, op0, op1=None):
    if op1 is None:
        return lambda e: e.tensor_scalar(out=out, in0=in0, scalar1=s1, scalar2=None, op0=op0)
    return lambda e: e.tensor_scalar(out=out, in0=in0, scalar1=s1, scalar2=s2, op0=op0, op1=op1)


def f_stt(out, in0, scalar, in1, op0, op1):
    return lambda e: e.scalar_tensor_tensor(out=out, in0=in0, scalar=scalar, in1=in1, op0=op0, op1=op1)


def f_tt(out, in0, in1, op):
    return lambda e: e.tensor_tensor(out=out, in0=in0, in1=in1, op=op)


def f_copy(out, in_):
    return lambda e: e.tensor_copy(out=out, in_=in_)


def f_recip(out, in_):
    return lambda e: e.reciprocal(out=out, in_=in_)


def f_memset(ap, v):
    return lambda e: e.memset(ap, v)


def dview(h):
    return h.ap().rearrange("(c p) t -> p c t", p=128)


def emit_norm(P, cfg, K, xin, tok0, g, h, hcol0, xring, sqring, ones, psn, rstd, T=None):
    T, DC = (T or cfg.T), cfg.DC
    xv = dview(xin)
    for c in range(DC):
        xc = xring.next()
        P.dma("sp", xc.t[:, 0:T], xv[:, c, tok0:tok0 + T], writes=[xc.b])
        sq = sqring.next()
        P.op("act", f_act(sq.t[:, 0:T], xc.t[:, 0:T], AF.Square), reads=[xc.b], writes=[sq.b])
        P.op("pe", f_mm(psn.t[:, 0:T], ones.t[:, :], sq.t[:, 0:T], c == 0, c == DC - 1),
             reads=[ones.b, sq.b], writes=[psn.b])
    P.op("act", f_act(rstd.t[:, 0:T], psn.t[:, 0:T], AF.Sqrt, scale=1.0 / cfg.D, bias=K["eps"].t[:, 0:1]),
         reads=[psn.b, K["eps"].b], writes=[rstd.b])
    P.op("dve", f_recip(rstd.t[:, 0:T], rstd.t[:, 0:T]), reads=[rstd.b], writes=[rstd.b])
    for c in range(DC):
        xc = xring.next()
        P.dma("sp", xc.t[:, 0:T], xv[:, c, tok0:tok0 + T], writes=[xc.b])
        P.op("dve", f_stt(h.t[:, c, hcol0:hcol0 + T], xc.t[:, 0:T], g.t[:, c:c + 1], rstd.t[:, 0:T],
                          ALU.mult, ALU.mult),
             reads=[xc.b, g.b, rstd.b], writes=[h.b])


def load_consts(P, cx, cfg, consts):
    K = {}
    K["ones"] = cx.sb("k_ones", [128, 128], BF16)
    P.op("dve", f_memset(K["ones"].t[:, :], 1.0), writes=[K["ones"].b])
    K["eps"] = cx.sb("k_eps", [128, 1], F32)
    P.op("dve", f_memset(K["eps"].t[:, :], cfg.EPS), writes=[K["eps"].b])
    return K


def stage_ffn_up(nc, cfg, xin, g_d, wup_d, cw_d, cb_d, gT, precast=None):
    T, DC, FC, S = cfg.T, cfg.DC, cfg.FC, cfg.S
    NSUB = 2 if S % (2 * T) == 0 else 1
    TS = T * NSUB
    with ExitStack() as es:
        cx = Ctx(nc, es)
        P = Prog(nc)
        K = load_consts(P, cx, cfg, None)
        g = cx.sb("g", [128, DC], F32)
        P.dma("sp", g.t[:, :], g_d.ap(), writes=[g.b])
        cw = cx.sb("cw", [128, 3, 2 * FC], F32)
        P.dma("sp", cw.t[:, :, :], cw_d.ap(), writes=[cw.b])
        cb = cx.sb("cb", [128, 2 * FC], F32)
        P.dma("sp", cb.t[:, :], cb_d.ap(), writes=[cb.b])
        carry = [cx.sb("carry%d" % i, [128, FC, 2], F32, nsub=FC) for i in range(2)]
        for cr in carry:
            P.op("dve", f_memset(cr.t[:, :, :], 0.0), writes=cr.sub)
        hs = [cx.sb("h%d" % i, [128, DC, TS], BF16) for i in range(2)]
        xring = ring(cx, "xc", 4, [128, T], F32)
        sqring = ring(cx, "sq", 3, [128, T], BF16)
        rstd = cx.sb("rstd", [128, T], F32)
        wring = ring(cx, "w", 4, [128, DC, 128], BF16)
        abuf = ring(cx, "ab", 4, [128, T + 4], F32)
        cbuf = ring(cx, "cbf", 4, [128, T], F32)
        sgb = ring(cx, "sg", 2, [128, T], F32)
        gob = ring(cx, "go", 3, [128, T], BF16)
        psn = cx.ps("psn")
        psr = psring(cx, "ps", 7)
        NST = S // TS

        def do_norm(st_):
            for sub_ in range(NSUB):
                emit_norm(P, cfg, K, xin, st_ * TS + sub_ * T, g, hs[st_ % 2], sub_ * T, xring, sqring, K["ones"], psn, rstd)

        PCQ = 4
        n_pc = DC * PCQ
        pc_step = max(1, (NST * FC) // n_pc)
        pc_done = [0]
        PCW = FC * 128 // PCQ

        def do_precast(i):
            src, dst = precast
            c, q_ = i // PCQ, i % PCQ
            P.dma("pool", dst.ap()[c].rearrange("p k c -> p (k c)")[:, q_ * PCW:(q_ + 1) * PCW],
                  src.ap()[c].rearrange("p k c -> p (k c)")[:, q_ * PCW:(q_ + 1) * PCW])

        do_norm(0)
        for st in range(NST):
            h = hs[st % 2]
            for j in range(FC):
                if j == min(4, FC - 1) and st + 1 < NST:
                    do_norm(st + 1)
                ws = []
                for half in range(2):
                    w = wring.next()
                    P.dma("pool", w.t[:, :, :], wup_d.ap()[half * FC + j], writes=[w.b])
                    ws.append(w)
                it = st * FC + j
                if precast is not None and it % pc_step == 0 and pc_done[0] < n_pc:
                    do_precast(pc_done[0])
                    pc_done[0] += 1
                for sub in range(NSUB):
                    tok0 = st * TS + sub * T
                    cs = []
                    for half in range(2):
                        ps = psr.next()
                        for k in range(DC):
                            P.op("pe", f_mm(ps.t[:, :], ws[half].t[:, k, :], h.t[:, k, sub * T:(sub + 1) * T],
                                            k == 0, k == DC - 1),
                                 reads=[ws[half].b, h.b], writes=[ps.b])
                        ch = half * FC + j
                        ab = abuf.next()
                        cr = carry[half]
                        P.op("dve", f_copy(ab.t[:, 0:2], cr.t[:, j, :]), reads=[cr.sub[j]], writes=[ab.b])
                        P.op("act", f_act(ab.t[:, 2:T + 2], ps.t[:, :], AF.Copy), reads=[ps.b], writes=[ab.b])
                        P.op("dve", f_copy(cr.t[:, j, :], ab.t[:, T:T + 2]), reads=[ab.b], writes=[cr.sub[j]])
                        c_ = cbuf.next()
                        P.op("dve", f_ts(c_.t[:, :], ab.t[:, 2:T + 2], cw.t[:, 2, ch:ch + 1], cb.t[:, ch:ch + 1],
                                         ALU.mult, ALU.add), reads=[ab.b, cw.b, cb.b], writes=[c_.b])
                        P.op("dve", f_stt(c_.t[:, :], ab.t[:, 1:T + 1], cw.t[:, 1, ch:ch + 1], c_.t[:, :],
                                          ALU.mult, ALU.add), reads=[ab.b, c_.b, cw.b], writes=[c_.b])
                        P.op("dve", f_stt(c_.t[:, :], ab.t[:, 0:T], cw.t[:, 0, ch:ch + 1], c_.t[:, :],
                                          ALU.mult, ALU.add), reads=[ab.b, c_.b, cw.b], writes=[c_.b])
                        cs.append(c_)
                    sg = sgb.next()
                    P.op("act", f_act(sg.t[:, :], cs[0].t[:, :], AF.Silu), reads=[cs[0].b], writes=[sg.b])
                    go = gob.next()
                    P.op("dve", f_tt(go.t[:, :], sg.t[:, :], cs[1].t[:, :], ALU.mult),
                         reads=[sg.b, cs[1].b], writes=[go.b])
                    P.dma("sp", gT.ap()[j * 128:(j + 1) * 128, tok0:tok0 + T], go.t[:, :], reads=[go.b])
        while precast is not None and pc_done[0] < n_pc:
            do_precast(pc_done[0])
            pc_done[0] += 1
        P.emit()


def stage_linear_res(nc, cfg, KC, actT, w_d, xin, xout):
    T, DC, S = cfg.T, cfg.DC, cfg.S
    NSUB = 2 if (KC <= 32 and S % (2 * T) == 0) else 1
    TS = T * NSUB
    with ExitStack() as es:
        cx = Ctx(nc, es)
        P = Prog(nc)
        KG = 16 // NSUB
        NG = (KC + KG - 1) // KG
        abufs = [cx.sb("a%d" % i, [128, KC, TS], BF16, nsub=NG) for i in range(2 if KC <= 32 else 1)]
        wring = ring(cx, "w", 2, [128, KC, 128], BF16)
        xr = ring(cx, "xr", 3, [128, T], F32)
        xo = ring(cx, "xo", 3, [128, T], F32)
        psr = psring(cx, "ps", 4)
        av = dview(actT)
        xv = dview(xin)
        ov = dview(xout)
        for tt in range(S // TS):
            t0 = tt * TS
            a = abufs[tt % len(abufs)]
            for gk in range(NG):
                k0, k1 = gk * KG, min(KC, (gk + 1) * KG)
                P.dma("sp", a.t[:, k0:k1, :], av[:, k0:k1, t0:t0 + TS], writes=[a.sub[gk]])
            for c in range(DC):
                w = wring.next()
                P.dma("pool", w.t[:, :, :], w_d.ap()[c], writes=[w.b])
                for sub in range(NSUB):
                    tok0 = t0 + sub * T
                    ps = psr.next()
                    for k in range(KC):
                        P.op("pe", f_mm(ps.t[:, :], w.t[:, k, :], a.t[:, k, sub * T:(sub + 1) * T], k == 0, k == KC - 1),
                             reads=[w.b, a.sub[k // KG]], writes=[ps.b])
                    x_ = xr.next()
                    P.dma("sp", x_.t[:, :], xv[:, c, tok0:tok0 + T], writes=[x_.b])
                    o_ = xo.next()
                    P.op("dve", f_tt(o_.t[:, :], ps.t[:, :], x_.t[:, :], ALU.add), reads=[ps.b, x_.b], writes=[o_.b])
                    P.dma("sp", ov[:, c, tok0:tok0 + T], o_.t[:, :], reads=[o_.b])
        P.emit()


def tile_w(W, cw=128):
    K, N = W.shape
    return np.ascontiguousarray(W.reshape(K // 128, 128, N // cw, cw).transpose(2, 1, 0, 3))


def vec_pc(v):
    return np.ascontiguousarray(v.reshape(-1, 128).T)


def np_bf16(a):
    import ml_dtypes
    return np.asarray(a, dtype=np.float32).astype(ml_dtypes.bfloat16)


def make_consts(nc, cfg):
    C = {}
    half = 16
    invf = np.power(np.float32(500000.0), -np.arange(half, dtype=np.float32) * np.float32(2.0 / 32.0)).astype(np.float32)
    C["invf"] = nc.inline_tensor(np.concatenate([invf, invf]).reshape(32, 1).astype(np.float32), "k_invf")
    Pm = np.zeros((128, 32), np.float32)
    for m in range(16):
        Pm[m + 16, m] = -1.0
        Pm[m, m + 16] = 1.0
    C["Pm"] = nc.inline_tensor(Pm, "k_Pm")
    C["identf"] = nc.inline_tensor(np.eye(128, dtype=np.float32), "k_identf")
    C["identb"] = nc.inline_tensor(np_bf16(np.eye(128)), "k_identb")
    NBLK = cfg.NBLK
    E = np.zeros((128, NBLK * 128), np.float32)
    for n in range(NBLK):
        E[n % 16, n * 128:(n + 1) * 128] = 1.0
    C["esel"] = nc.inline_tensor(np_bf16(E), "k_esel")
    j = np.arange(128)[:, None, None]
    r = np.arange(4)[None, :, None]
    i = np.arange(512)[None, None, :]
    C["cmask"] = nc.inline_tensor(np_bf16(np.where(r * 128 + j <= i, 0.0, NEG)), "k_cmask")
    jj = np.arange(128)[:, None]
    ii = np.arange(128)[None, :]
    cur = np.where(jj <= ii, 0.0, NEG)
    prv = np.where(jj >= ii, 0.0, NEG)
    bm = np.stack([np.concatenate([cur, np.full((128, 128), NEG)], 1), np.concatenate([cur, prv], 1)], 1)
    C["bmask"] = nc.inline_tensor(np_bf16(bm), "k_bmask")
    NKT = cfg.S // 128
    pm = np.full((NKT, 16), NEG, np.float32)
    om = np.full((NKT, 16), -1.0e9, np.float32)
    for qi in range(NKT):
        qb = (qi * 128) // cfg.MB
        pm[qi, :qb] = 0.0
        om[qi, qb] = 0.0
    C["pastm"] = nc.inline_tensor(np.ascontiguousarray(np.broadcast_to(pm.reshape(1, NKT * 16), (128, NKT * 16))), "k_pastm")
    C["ownm"] = nc.inline_tensor(np.ascontiguousarray(np.broadcast_to(om.reshape(1, NKT * 16), (128, NKT * 16))), "k_ownm")
    C["tril"] = nc.inline_tensor(np.tril(np.ones((128, 128), np.float32)), "k_tril")
    return C


def stage_rope(nc, cfg, C, pos_d, cs_d):
    S = cfg.S
    W = min(S, 2048)
    with ExitStack() as es:
        cx = Ctx(nc, es)
        P = Prog(nc)
        invf = cx.sb("invf", [32, 1], F32)
        P.dma("sp", invf.t[:, :], C["invf"].ap(), writes=[invf.b])
        for c0 in range(0, S, W):
            S_ = W
            posi = cx.sb("posi%d" % c0, [32, S_], I32)
            P.dma("sp", posi.t[:, :], pos_d.ap()[:, c0:c0 + W], writes=[posi.b])
            ang = cx.sb("ang%d" % c0, [32, S_], F32)
            P.op("dve", f_copy(ang.t[:, :], posi.t[:, :]), reads=[posi.b], writes=[ang.b])
            P.op("dve", f_ts(ang.t[:, :], ang.t[:, :], invf.t[:, 0:1], None, ALU.mult), reads=[ang.b, invf.b], writes=[ang.b])
            ki = cx.sb("ki%d" % c0, [32, S_], I32)
            kf = cx.sb("kf%d" % c0, [32, S_], F32)
            mk = cx.sb("mk%d" % c0, [32, S_], F32)
            for idx, (nm, shift) in enumerate((("cos", 0.25), ("sin", 0.0))):
                y = cx.sb("y_%s%d" % (nm, c0), [32, S_], F32)
                P.op("dve", f_ts(y.t[:, :], ang.t[:, :], float(1.0 / (2.0 * np.pi)), shift, ALU.mult, ALU.add),
                     reads=[ang.b], writes=[y.b])
                P.op("dve", f_copy(ki.t[:, :], y.t[:, :]), reads=[y.b], writes=[ki.b])
                P.op("dve", f_copy(kf.t[:, :], ki.t[:, :]), reads=[ki.b], writes=[kf.b])
                P.op("dve", f_tt(y.t[:, :], y.t[:, :], kf.t[:, :], ALU.subtract), reads=[y.b, kf.b], writes=[y.b])
                P.op("dve", f_ts(mk.t[:, :], y.t[:, :], 0.5, None, ALU.is_gt), reads=[y.b], writes=[mk.b])
                P.op("dve", f_tt(y.t[:, :], y.t[:, :], mk.t[:, :], ALU.subtract), reads=[y.b, mk.b], writes=[y.b])
                P.op("dve", f_ts(mk.t[:, :], y.t[:, :], -0.5, None, ALU.is_lt), reads=[y.b], writes=[mk.b])
                P.op("dve", f_tt(y.t[:, :], y.t[:, :], mk.t[:, :], ALU.add), reads=[y.b, mk.b], writes=[y.b])
                P.op("act", f_act(y.t[:, :], y.t[:, :], AF.Sin, scale=float(2.0 * np.pi * (1.0 - 1e-6))),
                     reads=[y.b], writes=[y.b])
                P.dma("sp", cs_d.ap()[:, idx, c0:c0 + W], y.t[:, :], reads=[y.b])
        P.emit()


def stage_qkv(nc, cfg, C, xin, g_d, wqk_d, wv_d, cs_d, qT, kT, v, ksum_d):
    T, DC, S, NH, NA = cfg.T, cfg.DC, cfg.S, cfg.NH, cfg.NA
    VW = cfg.VW
    NSUB = 2 if S % (2 * T) == 0 else 1
    TS = T * NSUB
    with ExitStack() as es:
        cx = Ctx(nc, es)
        P = Prog(nc)
        K = load_consts(P, cx, cfg, None)
        g = cx.sb("g", [128, DC], F32)
        P.dma("sp", g.t[:, :], g_d.ap(), writes=[g.b])
        Pm = cx.sb("Pm", [128, 32], F32)
        P.dma("sp", Pm.t[:, :], C["Pm"].ap(), writes=[Pm.b])
        csr = ring(cx, "cs", 4, [32, 2, T], F32)
        ksum = cx.sb("ksum", [128, NA, cfg.NBLK], F32)
        h = cx.sb("h", [128, DC, TS], BF16)
        xring = ring(cx, "xc", 4, [128, T], F32)
        sqring = ring(cx, "sq", 3, [128, T], BF16)
        rstd = cx.sb("rstd", [128, T], F32)
        wring = ring(cx, "w", 3, [128, DC, 128], BF16)
        wvring = ring(cx, "wv", 2, [128, DC, VW], BF16)
        qfr = ring(cx, "qf", 4, [128, T], F32)
        tmr = ring(cx, "tm", 4, [32, T], F32)
        qbr = ring(cx, "qb", 3, [128, T], BF16)
        vbr = ring(cx, "vb", 3, [128, VW], BF16)
        psn = cx.ps("psn")
        psr = psring(cx, "ps", 5)
        psp = psring(cx, "pp", 2)
        for st in range(S // TS):
            cst = []
            for sub in range(NSUB):
                emit_norm(P, cfg, K, xin, st * TS + sub * T, g, h, sub * T, xring, sqring, K["ones"], psn, rstd)
                cs = csr.next()
                tok0 = st * TS + sub * T
                P.dma("sp", cs.t[:, :, :], cs_d.ap()[:, :, tok0:tok0 + T], writes=[cs.b])
                cst.append(cs)
            pending = [None]
            for j in range(2 * NH):
                isk = j >= NH
                hd = j - NH if isk else j
                w = wring.next()
                P.dma("pool", w.t[:, :, :], wqk_d.ap()[j], writes=[w.b])
                for sub in range(NSUB):
                    tok0 = st * TS + sub * T
                    ps = psr.next()
                    for k in range(DC):
                        P.op("pe", f_mm(ps.t[:, :], w.t[:, k, :], h.t[:, k, sub * T:(sub + 1) * T], k == 0, k == DC - 1),
                             reads=[w.b, h.b], writes=[ps.b])
                    qf = qfr.next()
                    P.op("act", f_act(qf.t[:, :], ps.t[:, :], AF.Copy), reads=[ps.b], writes=[qf.b])

                    def rot(qf=qf, sub=sub, tok0=tok0, isk=isk, hd=hd):
                        pp = psp.next()
                        P.op("pe", f_mm(pp.t[0:32, :], Pm.t[:, :], qf.t[:, :]), reads=[Pm.b, qf.b], writes=[pp.b])
                        t1 = tmr.next()
                        t2 = tmr.next()
                        P.op("dve", f_tt(t1.t[:, :], qf.t[0:32, :], cst[sub].t[:, 0, :], ALU.mult),
                             reads=[qf.b, cst[sub].b], writes=[t1.b])
                        P.op("dve", f_tt(t2.t[:, :], pp.t[0:32, :], cst[sub].t[:, 1, :], ALU.mult),
                             reads=[pp.b, cst[sub].b], writes=[t2.b])
                        P.op("dve", f_tt(qf.t[0:32, :], t1.t[:, :], t2.t[:, :], ALU.add),
                             reads=[t1.b, t2.b], writes=[qf.b])
                        qb = qbr.next()
                        P.op("act", f_act(qb.t[:, :], qf.t[:, :], AF.Copy), reads=[qf.b], writes=[qb.b])
                        if isk and hd < NA:
                            b0 = tok0 // cfg.MB
                            nb = T // cfg.MB
                            P.op("dve", lambda e, o_=ksum.t[:, hd, b0:b0 + nb], i_=qf.t[:, :].rearrange("p (b m) -> p b m", m=cfg.MB):
                                 e.tensor_reduce(out=o_, in_=i_, axis=AX.X, op=ALU.add),
                                 reads=[qf.b], writes=[ksum.b])
                        dst = kT if isk else qT
                        P.dma("sp", dst.ap()[hd * 128:(hd + 1) * 128, tok0:tok0 + T], qb.t[:, :], reads=[qb.b])

                    if pending[0] is not None:
                        pending[0]()
                    pending[0] = rot
            if pending[0] is not None:
                pending[0]()
                pending[0] = None
            for jb in range(NH * 128 // VW):
                wv = wvring.next()
                P.dma("pool", wv.t[:, :, :], wv_d.ap()[jb], writes=[wv.b])
                for tb in range(TS // 128):
                    tok0 = st * TS + tb * 128
                    ps = psr.next()
                    for k in range(DC):
                        P.op("pe", f_mm(ps.t[:, 0:VW], h.t[:, k, tb * 128:(tb + 1) * 128], wv.t[:, k, :], k == 0, k == DC - 1),
                             reads=[wv.b, h.b], writes=[ps.b])
                    vb = vbr.next()
                    P.op("act", f_act(vb.t[:, :], ps.t[:, 0:VW], AF.Copy), reads=[ps.b], writes=[vb.b])
                    P.dma("sp", v.ap()[tok0:tok0 + 128, jb * VW:(jb + 1) * VW], vb.t[:, :], reads=[vb.b])
        P.dma("sp", ksum_d.ap(), ksum.t[:, :, :], reads=[ksum.b])
        P.emit()


def stage_moba(nc, cfg, C, qT, kT, v, ksum_d, oT):
    S, NA, NBLK, T = cfg.S, cfg.NA, cfg.NBLK, cfg.T
    NKT = S // 128
    with ExitStack() as es:
        cx = Ctx(nc, es)
        P = Prog(nc)
        K = load_consts(P, cx, cfg, None)
        ones = K["ones"]
        identf = cx.sb("identf", [128, 128], F32)
        P.dma("sp", identf.t[:, :], C["identf"].ap(), writes=[identf.b])
        identb = cx.sb("identb", [128, 128], BF16)
        P.dma("sp", identb.t[:, :], C["identb"].ap(), writes=[identb.b])
        esel = cx.sb("esel", [128, NBLK * 128], BF16)
        P.dma("sp", esel.t[:, :], C["esel"].ap(), writes=[esel.b])
        cmask = cx.sb("cmask", [128, 4, 512], BF16)
        P.dma("sp", cmask.t[:, :, :], C["cmask"].ap(), writes=[cmask.b])
        ksum = cx.sb("ksum", [128, NA, NBLK], F32)
        P.dma("sp", ksum.t[:, :, :], ksum_d.ap(), writes=[ksum.b])
        qr = ring(cx, "q", 2, [128, S], BF16)
        kr = ring(cx, "k", 2, [128, S], BF16)
        vr = ring(cx, "v", 2, [128, NKT, 128], BF16)
        kmr = ring(cx, "km", 2, [128, 16], BF16)
        btr = ring(cx, "bt", 2, [128, S], BF16)
        gmr = ring(cx, "gm", 4, [128, 16], F32)
        mxr = ring(cx, "mx", 4, [128, 8], F32)
        bqr = ring(cx, "bq", 6, [128, 128], F32)
        ptr = ring(cx, "pt", 3, [128, T], BF16)
        rzr = ring(cx, "rz", 2, [128, T], F32)
        obr = ring(cx, "ob", 2, [128, T], BF16)
        psr = psring(cx, "ps", 3)
        por = psring(cx, "po", 2)
        pzr = psring(cx, "pz", 2)
        psm = cx.ps("psm", nsub=7)
        NB16 = min(NBLK, 16)
        assert NBLK <= 16 and NBLK >= 8
        pastm = cx.sb("pastm", [128, NKT * 16], F32)
        P.dma("sp", pastm.t[:, :], C["pastm"].ap(), writes=[pastm.b])
        ownm = cx.sb("ownm", [128, NKT * 16], F32)
        P.dma("sp", ownm.t[:, :], C["ownm"].ap(), writes=[ownm.b])
        gmar = ring(cx, "gma", 2, [128, NKT * 16], F32)
        mxar = ring(cx, "mxa", 2, [128, NKT, 8], F32)
        bqar = ring(cx, "bqa", 2, [128, NKT, 128], F32)
        for t_ in bqar.tiles:
            P.op("dve", f_memset(t_.t[:, :, :], 0.0), writes=[t_.b])
        v3 = lambda ap: ap.rearrange("p (a b) -> p a b", b=16)
        st = {}

        def load(hd):
            q, k, vv, km, bt = qr.next(), kr.next(), vr.next(), kmr.next(), btr.next()
            P.dma("sp", q.t[:, :], qT.ap()[hd * 128:(hd + 1) * 128, :], writes=[q.b])
            P.dma("sp", k.t[:, :], kT.ap()[hd * 128:(hd + 1) * 128, :], writes=[k.b])
            vsrc = v.ap()[:, hd * 128:(hd + 1) * 128].rearrange("(n p) d -> p n d", p=128)
            for n0 in range(0, NKT, 8):
                P.dma("sp", vv.t[:, n0:n0 + 8, :], vsrc[:, n0:n0 + 8, :], writes=[vv.b])
            P.op("dve", f_memset(km.t[:, :], 0.0), writes=[km.b])
            P.op("dve", f_copy(km.t[:, 0:NBLK], ksum.t[:, hd, :]), reads=[ksum.b], writes=[km.b])
            st[hd] = (q, k, vv, km, bt)

        def gate_a(hd):
            q, k, vv, km, bt = st[hd]
            for qi in range(NKT):
                P.op("pe", f_mm(psm.t[:, qi * 16:(qi + 1) * 16], q.t[:, qi * 128:(qi + 1) * 128], km.t[:, 0:16]),
                     reads=[q.b, km.b], writes=[psm.b])
            gma, mxa, bqa = gmar.next(), mxar.next(), bqar.next()
            P.op("dve", f_tt(gma.t[:, :], psm.t[:, 0:NKT * 16], pastm.t[:, :], ALU.add), reads=[psm.b, pastm.b], writes=[gma.b])
            for qi in range(NKT):
                P.op("dve", lambda e, o_=mxa.t[:, qi, :], i_=gma.t[:, qi * 16:(qi + 1) * 16]: e.max(out=o_, in_=i_),
                     reads=[gma.b], writes=[mxa.b])
            for qi in range(NKT):
                P.op("dve", f_ts(gma.t[:, qi * 16:(qi + 1) * 16], gma.t[:, qi * 16:(qi + 1) * 16], mxa.t[:, qi, 2:3], None, ALU.is_ge),
                     reads=[gma.b, mxa.b], writes=[gma.b])
            P.op("dve", f_ts(bqa.t[:, :, 0:16], v3(gma.t[:, :]), -NEG, NEG, ALU.mult, ALU.add), reads=[gma.b], writes=[bqa.b])
            P.op("dve", f_tt(bqa.t[:, :, 0:16], bqa.t[:, :, 0:16], v3(pastm.t[:, :]), ALU.min), reads=[bqa.b, pastm.b], writes=[bqa.b])
            P.op("dve", f_tt(bqa.t[:, :, 0:16], bqa.t[:, :, 0:16], v3(ownm.t[:, :]), ALU.max), reads=[bqa.b, ownm.b], writes=[bqa.b])
            st[hd] = (q, k, vv, km, bt, bqa)

        def gate_b(hd):
            q, k, vv, km, bt, bqa = st[hd]
            for grp in range(NKT // 4):
                pst = psr.next()
                for j_ in range(4):
                    P.op("pe", f_tr(pst.t[:, j_ * 128:(j_ + 1) * 128], bqa.t[:, grp * 4 + j_, :], identf.t[:, :]),
                         reads=[bqa.b, identf.b], writes=[pst.b])
                P.op("act", f_act(bt.t[:, grp * 512:(grp + 1) * 512], pst.t[:, :], AF.Copy), reads=[pst.b], writes=[bt.b])

        def attention(hd):
            q, k, vv, km, bt, bqa = st.pop(hd)
            for g in range(S // T):
                nkt = 4 * (g + 1)
                po, pz = por.next(), pzr.next()
                qs = q.t[:, g * T:(g + 1) * T]

                def qk(kt):
                    ps = psr.next()
                    n = kt // 2
                    diag = kt >= 4 * g
                    P.op("pe", f_mm(ps.t[:, :], k.t[:, kt * 128:(kt + 1) * 128], qs, True, False),
                         reads=[k.b, q.b], writes=[ps.b])
                    P.op("pe", f_mm(ps.t[:, :], esel.t[:, n * 128:(n + 1) * 128], bt.t[:, g * T:(g + 1) * T], False, not diag),
                         reads=[esel.b, bt.b], writes=[ps.b])
                    if diag:
                        P.op("pe", f_mm(ps.t[:, :], identb.t[:, :], cmask.t[:, kt - 4 * g, :], False, True),
                             reads=[identb.b, cmask.b], writes=[ps.b])
                    return ps

                ps_next = qk(0)
                for kt in range(nkt):
                    ps = ps_next
                    pt = ptr.next()
                    P.op("act", f_act(pt.t[:, :], ps.t[:, :], AF.Exp, scale=float(cfg.SCALE)), reads=[ps.b], writes=[pt.b])
                    if kt + 1 < nkt:
                        ps_next = qk(kt + 1)
                    P.op("pe", f_mm(po.t[:, :], vv.t[:, kt, :], pt.t[:, :], kt == 0, kt == nkt - 1),
                         reads=[vv.b, pt.b], writes=[po.b])
                    P.op("pe", f_mm(pz.t[:, :], ones.t[:, :], pt.t[:, :], kt == 0, kt == nkt - 1),
                         reads=[ones.b, pt.b], writes=[pz.b])
                rz = rzr.next()
                P.op("dve", f_recip(rz.t[:, :], pz.t[:, :]), reads=[pz.b], writes=[rz.b])
                ob = obr.next()
                P.op("dve", f_tt(ob.t[:, :], po.t[:, :], rz.t[:, :], ALU.mult), reads=[po.b, rz.b], writes=[ob.b])
                P.dma("sp", oT.ap()[hd * 128:(hd + 1) * 128, g * T:(g + 1) * T], ob.t[:, :], reads=[ob.b])

        load(0)
        gate_a(0)
        gate_b(0)
        for hd in range(NA):
            if hd + 1 < NA:
                load(hd + 1)
                gate_a(hd + 1)
            attention(hd)
            if hd + 1 < NA:
                gate_b(hd + 1)
        P.emit()


def stage_dilated(nc, cfg, C, qT, kT, v, oT):
    S, NA, NBG, T = cfg.S, cfg.NA, cfg.NBG, cfg.T
    NKT = S // 128
    with ExitStack() as es:
        cx = Ctx(nc, es)
        P = Prog(nc)
        K = load_consts(P, cx, cfg, None)
        ones = K["ones"]
        identb = cx.sb("identb", [128, 128], BF16)
        P.dma("sp", identb.t[:, :], C["identb"].ap(), writes=[identb.b])
        bmask = cx.sb("bmask", [128, 2, 256], BF16)
        P.dma("sp", bmask.t[:, :, :], C["bmask"].ap(), writes=[bmask.b])
        qr = ring(cx, "q", 4, [128, S], BF16)
        kr = ring(cx, "k", 4, [128, S], BF16)
        vr = ring(cx, "v", 2, [128, NKT, 128], BF16)
        uacc = cx.sb("uacc", [128, S], F32)
        zacc = cx.sb("zacc", [128, S], F32)
        ptr = ring(cx, "pt", 3, [128, 256], BF16)
        obr = ring(cx, "ob", 2, [128, T], BF16)
        psr = psring(cx, "ps", 4)
        pur = psring(cx, "pu", 4)
        for hh in range(NBG):
            for gi, (win, dil) in enumerate(cfg.PATS):
                hq = NA + gi * NBG + hh
                nblk = S // (128 * dil)
                q, k, vv = qr.next(), kr.next(), vr.next()
                P.dma("sp", q.t[:, :], qT.ap()[hq * 128:(hq + 1) * 128, :], writes=[q.b])
                P.dma("sp", k.t[:, :], kT.ap()[hq * 128:(hq + 1) * 128, :], writes=[k.b])
                vsrc = v.ap()[:, hq * 128:(hq + 1) * 128].rearrange("(n p r) d -> p r n d", p=128, r=dil)
                for r in range(dil):
                    for n0 in range(0, nblk, 8):
                        n1 = min(nblk, n0 + 8)
                        P.dma("sp", vv.t[:, r * nblk + n0:r * nblk + n1, :], vsrc[:, r, n0:n1, :], writes=[vv.b])
                if dil > 1:
                    q2, k2 = qr.next(), kr.next()
                    sub = S // dil
                    for r in range(dil):
                        P.op("dve", f_copy(q2.t[:, r * sub:(r + 1) * sub], q.t[:, r:r + (sub - 1) * dil + 1:dil]), reads=[q.b], writes=[q2.b])
                        P.op("act", f_act(k2.t[:, r * sub:(r + 1) * sub], k.t[:, r:r + (sub - 1) * dil + 1:dil], AF.Copy), reads=[k.b], writes=[k2.b])
                    q, k = q2, k2
                tiles = [(r, n) for r in range(dil) for n in range(nblk)]

                def qk(rn, q=q, k=k, nblk=nblk):
                    r, n = rn
                    c0 = (r * nblk + n) * 128
                    p0 = c0 - 128 if n > 0 else c0
                    ps = psr.next()
                    P.op("pe", f_mm(ps.t[:, 0:128], k.t[:, c0:c0 + 128], q.t[:, c0:c0 + 128], True, False), reads=[k.b, q.b], writes=[ps.b])
                    P.op("pe", f_mm(ps.t[:, 128:256], k.t[:, p0:p0 + 128], q.t[:, c0:c0 + 128], False, False), reads=[k.b, q.b], writes=[ps.b])
                    P.op("pe", f_mm(ps.t[:, 0:256], identb.t[:, :], bmask.t[:, 1 if n > 0 else 0, :], False, True),
                         reads=[identb.b, bmask.b], writes=[ps.b])
                    return ps

                ps_next = qk(tiles[0])
                for ti, (r, n) in enumerate(tiles):
                    b0 = n * 128 * dil + r
                    cur = slice(b0, b0 + 127 * dil + 1, dil)
                    ps = ps_next
                    pt = ptr.next()
                    P.op("act", f_act(pt.t[:, :], ps.t[:, 0:256], AF.Exp, scale=float(cfg.SCALE)), reads=[ps.b], writes=[pt.b])
                    if ti + 1 < len(tiles):
                        ps_next = qk(tiles[ti + 1])
                    pu = pur.next()
                    vi = r * nblk + n
                    vp = vi - 1 if n > 0 else vi
                    P.op("pe", f_mm(pu.t[:, 0:128], vv.t[:, vi, :], pt.t[:, 0:128], True, False), reads=[vv.b, pt.b], writes=[pu.b])
                    P.op("pe", f_mm(pu.t[:, 0:128], vv.t[:, vp, :], pt.t[:, 128:256], False, False), reads=[vv.b, pt.b], writes=[pu.b])
                    P.op("pe", f_mm(pu.t[:, 128:256], ones.t[:, :], pt.t[:, 0:128], False, False), reads=[ones.b, pt.b], writes=[pu.b])
                    P.op("pe", f_mm(pu.t[:, 128:256], ones.t[:, :], pt.t[:, 128:256], False, True), reads=[ones.b, pt.b], writes=[pu.b])
                    if gi == 0:
                        P.op("dve", f_copy(uacc.t[:, cur], pu.t[:, 0:128]), reads=[pu.b], writes=[uacc.b])
                        P.op("dve", f_copy(zacc.t[:, cur], pu.t[:, 128:256]), reads=[pu.b], writes=[zacc.b])
                    else:
                        P.op("dve", f_tt(uacc.t[:, cur], uacc.t[:, cur], pu.t[:, 0:128], ALU.add), reads=[pu.b, uacc.b], writes=[uacc.b])
                        P.op("dve", f_tt(zacc.t[:, cur], zacc.t[:, cur], pu.t[:, 128:256], ALU.add), reads=[pu.b, zacc.b], writes=[zacc.b])
            P.op("dve", f_recip(zacc.t[:, :], zacc.t[:, :]), reads=[zacc.b], writes=[zacc.b])
            for g in range(S // T):
                ob = obr.next()
                P.op("dve", f_tt(ob.t[:, :], uacc.t[:, g * T:(g + 1) * T], zacc.t[:, g * T:(g + 1) * T], ALU.mult),
                     reads=[uacc.b, zacc.b], writes=[ob.b])
                P.dma("sp", oT.ap()[(NA + hh) * 128:(NA + hh + 1) * 128, g * T:(g + 1) * T], ob.t[:, :], reads=[ob.b])
        P.emit()


def stage_sg_setup(nc, cfg, C, ws_d, bs_d, lnb_d, wsT_d, bsb_d):
    GG = cfg.GG
    with ExitStack() as es:
        cx = Ctx(nc, es)
        P = Prog(nc)
        identf = cx.sb("identf", [128, 128], F32)
        P.dma("sp", identf.t[:, :], C["identf"].ap(), writes=[identf.b])
        tril = cx.sb("tril", [128, 128], F32)
        P.dma("sp", tril.t[:, :], C["tril"].ap(), writes=[tril.b])
        e0 = cx.sb("e0", [128, 128], BF16)
        P.op("dve", f_memset(e0.t[:, :], 0.0), writes=[e0.b])
        P.op("dve", f_memset(e0.t[0:1, :], 1.0), writes=[e0.b])
        lnb = cx.sb("lnb", [128, cfg.E], BF16)
        P.dma("pool", lnb.t[:, :], lnb_d.ap(), writes=[lnb.b])
        bs0 = cx.sb("bs0", [128, GG * 128], BF16)
        P.op("dve", f_memset(bs0.t[:, :], 0.0), writes=[bs0.b])
        P.dma("pool", bs0.t[0:1, :], bs_d.ap(), writes=[bs0.b])
        wsT = cx.sb("wsT", [128, GG, 128], BF16)
        bsb = cx.sb("bsb", [128, GG, 128], F32)
        wr = ring(cx, "w", 3, [128, 128], F32)
        psr = psring(cx, "ps", 4)
        for g in range(GG):
            w = wr.next()
            P.dma("sp", w.t[:, :], ws_d.ap()[g], writes=[w.b])
            P.op("dve", f_tt(w.t[:, :], w.t[:, :], tril.t[:, :], ALU.mult), reads=[w.b, tril.b], writes=[w.b])
            ps = psr.next()
            P.op("pe", f_tr(ps.t[:, 0:128], w.t[:, :], identf.t[:, :]), reads=[w.b, identf.b], writes=[ps.b])
            P.op("act", f_act(wsT.t[:, g, :], ps.t[:, 0:128], AF.Copy), reads=[ps.b], writes=[wsT.b])
            ps2 = psr.next()
            P.op("pe", f_mm(ps2.t[:, 0:128], lnb.t[:, g * 128:(g + 1) * 128], wsT.t[:, g, :], True, False),
                 reads=[lnb.b, wsT.b], writes=[ps2.b])
            P.op("pe", f_mm(ps2.t[:, 0:128], e0.t[:, :], bs0.t[:, g * 128:(g + 1) * 128], False, True),
                 reads=[e0.b, bs0.b], writes=[ps2.b])
            P.op("act", f_act(bsb.t[:, g, :], ps2.t[:, 0:128], AF.Copy), reads=[ps2.b], writes=[bsb.b])
        P.dma("sp", wsT_d.ap(), wsT.t[:, :, :], reads=[wsT.b])
        P.dma("sp", bsb_d.ap(), bsb.t[:, :, :], reads=[bsb.b])
        P.emit()


def stage_sg_main(nc, cfg, xin, g_d, wu_d, wv_d, bu_d, bv_d, lng_d, wsT_d, bsb_d, mT):
    DC, S, GG, E = cfg.DC, cfg.S, cfg.GG, cfg.E
    T = 512
    VW = 256
    with ExitStack() as es:
        cx = Ctx(nc, es)
        P = Prog(nc)
        K = load_consts(P, cx, cfg, None)
        g = cx.sb("g", [128, DC], F32)
        P.dma("sp", g.t[:, :], g_d.ap(), writes=[g.b])
        bu = cx.sb("bu", [128, GG], F32)
        P.dma("sp", bu.t[:, :], bu_d.ap(), writes=[bu.b])
        lng = cx.sb("lng", [128, GG], F32)
        P.dma("sp", lng.t[:, :], lng_d.ap(), writes=[lng.b])
        e0 = cx.sb("e0", [128, 128], BF16)
        P.op("dve", f_memset(e0.t[:, :], 0.0), writes=[e0.b])
        P.op("dve", f_memset(e0.t[0:1, :], 1.0), writes=[e0.b])
        bv0 = cx.sb("bv0", [128, E], BF16)
        P.op("dve", f_memset(bv0.t[:, :], 0.0), writes=[bv0.b])
        P.dma("pool", bv0.t[0:1, :], bv_d.ap(), writes=[bv0.b])
        wsT = cx.sb("wsT", [128, GG, 128], BF16)
        P.dma("sp", wsT.t[:, :, :], wsT_d.ap(), writes=[wsT.b])
        bsb = cx.sb("bsb", [128, GG, 128], F32)
        P.dma("sp", bsb.t[:, :, :], bsb_d.ap(), writes=[bsb.b])
        h = cx.sb("h", [128, DC, T], BF16)
        uT = cx.sb("uT", [128, GG, T], BF16)
        xring = ring(cx, "xc", 3, [128, T], F32)
        sqring = ring(cx, "sq", 2, [128, T], BF16)
        rstd = cx.sb("rstd", [128, T], F32)
        wring = ring(cx, "w", 2, [128, DC, 128], BF16)
        wvring = ring(cx, "wv", 2, [128, DC, VW], BF16)
        vbufs = [cx.sb("vbuf%d" % i, [128, E], BF16) for i in range(T // 128)]
        vn = cx.sb("vn", [128, E], BF16)
        nst = max(1, E // 512)
        stats = cx.sb("stats", [128, nst, 6], F32)
        mv = cx.sb("mv", [128, 2], F32)
        fbr = ring(cx, "fb", 3, [128, 128], F32)
        mbr = ring(cx, "mb", 1, [128, GG, 128], BF16)
        psn = cx.ps("psn")
        psr = psring(cx, "ps", 5)
        psa = psring(cx, "pa", 2)
        mv_ = dview(mT)
        for tt in range(S // T):
            tok0 = tt * T
            emit_norm(P, cfg, K, xin, tok0, g, h, 0, xring, sqring, K["ones"], psn, rstd, T=T)
            for j in range(GG):
                w = wring.next()
                P.dma("pool", w.t[:, :, :], wu_d.ap()[j], writes=[w.b])
                ps = psr.next()
                for k in range(DC):
                    P.op("pe", f_mm(ps.t[:, 0:T], w.t[:, k, :], h.t[:, k, :], k == 0, k == DC - 1), reads=[w.b, h.b], writes=[ps.b])
                P.op("act", f_act(uT.t[:, j, :], ps.t[:, 0:T], AF.Gelu, bias=bu.t[:, j:j + 1]), reads=[ps.b, bu.b], writes=[uT.b])
            for jb in range(E // VW):
                wv = wvring.next()
                P.dma("pool", wv.t[:, :, :], wv_d.ap()[jb], writes=[wv.b])
                for tb in range(T // 128):
                    ps = psr.next()
                    for k in range(DC):
                        P.op("pe", f_mm(ps.t[:, 0:VW], h.t[:, k, tb * 128:(tb + 1) * 128], wv.t[:, k, :], k == 0, False),
                             reads=[wv.b, h.b], writes=[ps.b])
                    P.op("pe", f_mm(ps.t[:, 0:VW], e0.t[:, :], bv0.t[:, jb * VW:(jb + 1) * VW], False, True),
                         reads=[e0.b, bv0.b], writes=[ps.b])
                    P.op("act", f_act(vbufs[tb].t[:, jb * VW:(jb + 1) * VW], ps.t[:, 0:VW], AF.Gelu), reads=[ps.b], writes=[vbufs[tb].b])
            for tb in range(T // 128):
                vb = vbufs[tb]
                for i in range(nst):
                    w_ = min(512, E)
                    P.op("dve", lambda e, o_=stats.t[:, i, :], i_=vb.t[:, i * w_:(i + 1) * w_]: e.bn_stats(out=o_, in_=i_),
                         reads=[vb.b], writes=[stats.b])
                P.op("dve", lambda e, o_=mv.t[:, :], i_=stats.t[:, :, :]: e.bn_aggr(out=o_, in_=i_), reads=[stats.b], writes=[mv.b])
                P.op("act", f_act(mv.t[:, 1:2], mv.t[:, 1:2], AF.Sqrt, bias=K["eps"].t[:, 0:1]), reads=[mv.b, K["eps"].b], writes=[mv.b])
                P.op("dve", f_recip(mv.t[:, 1:2], mv.t[:, 1:2]), reads=[mv.b], writes=[mv.b])
                P.op("dve", f_ts(vn.t[:, :], vb.t[:, :], mv.t[:, 0:1], mv.t[:, 1:2], ALU.subtract, ALU.mult),
                     reads=[vb.b, mv.b], writes=[vn.b])
                mb = mbr.next()
                for gi in range(GG):
                    if gi % 4 == 0:
                        pa = psa.next()
                    c0 = (gi % 4) * 128
                    P.op("pe", f_mm(pa.t[:, c0:c0 + 128], vn.t[:, gi * 128:(gi + 1) * 128], wsT.t[:, gi, :], True, True),
                         reads=[vn.b, wsT.b], writes=[pa.b])
                    fb = fbr.next()
                    P.op("dve", f_stt(fb.t[:, :], pa.t[:, c0:c0 + 128], lng.t[:, gi:gi + 1], bsb.t[:, gi, :], ALU.mult, ALU.add),
                         reads=[pa.b, lng.b, bsb.b], writes=[fb.b])
                    P.op("dve", f_tt(mb.t[:, gi, :], fb.t[:, :], uT.t[:, gi, tb * 128:(tb + 1) * 128], ALU.mult),
                         reads=[fb.b, uT.b], writes=[mb.b])
                for g0 in range(0, GG, 8):
                    g1 = min(GG, g0 + 8)
                    P.dma("sp", mv_[:, g0:g1, tok0 + tb * 128:tok0 + (tb + 1) * 128], mb.t[:, g0:g1, :], reads=[mb.b])
        P.emit()


def stage_final_norm(nc, cfg, xin, g_d, out):
    DC, S, T = cfg.DC, cfg.S, cfg.T
    CG = 8
    NG = (DC + CG - 1) // CG
    with ExitStack() as es:
        cx = Ctx(nc, es)
        P = Prog(nc)
        K = load_consts(P, cx, cfg, None)
        g = cx.sb("g", [128, DC], F32)
        P.dma("sp", g.t[:, :], g_d.ap(), writes=[g.b])
        xts = [cx.sb("x%d" % i, [128, DC, T], F32, nsub=NG) for i in range(2)]
        sqring = ring(cx, "sq", 3, [128, T], BF16)
        rsr = ring(cx, "rstd", 2, [128, T], F32)
        pnr = psring(cx, "psn", 2)
        xv = dview(xin)
        ov = dview(out)
        for tt in range(S // T):
            x = xts[tt % 2]
            tok0 = tt * T
            for gi in range(NG):
                c0, c1 = gi * CG, min(DC, (gi + 1) * CG)
                P.dma("sp", x.t[:, c0:c1, :], xv[:, c0:c1, tok0:tok0 + T], writes=[x.sub[gi]])
            psn, rstd = pnr.next(), rsr.next()
            for c in range(DC):
                sq = sqring.next()
                P.op("act", f_act(sq.t[:, :], x.t[:, c, :], AF.Square), reads=[x.sub[c // CG]], writes=[sq.b])
                P.op("pe", f_mm(psn.t[:, :], K["ones"].t[:, :], sq.t[:, :], c == 0, c == DC - 1),
                     reads=[K["ones"].b, sq.b], writes=[psn.b])
            P.op("act", f_act(rstd.t[:, :], psn.t[:, :], AF.Sqrt, scale=1.0 / cfg.D, bias=K["eps"].t[:, 0:1]),
                 reads=[psn.b, K["eps"].b], writes=[rstd.b])
            P.op("dve", f_recip(rstd.t[:, :], rstd.t[:, :]), reads=[rstd.b], writes=[rstd.b])
            for c in range(DC):
                P.op("dve", f_stt(x.t[:, c, :], x.t[:, c, :], g.t[:, c:c + 1], rstd.t[:, :], ALU.mult, ALU.mult),
                     reads=[x.sub[c // CG], g.b, rstd.b], writes=[x.sub[c // CG]])
            for gi in range(NG):
                c0, c1 = gi * CG, min(DC, (gi + 1) * CG)
                P.dma("sp", ov[:, c0:c1, tok0:tok0 + T], x.t[:, c0:c1, :], reads=[x.sub[gi]])
        P.emit()


def build_program(cfg):
    nc = bass.Bass("TRN2", target_bir_lowering=False)
    D, S, DC, FC, NH, NA, GG, E, DFF = cfg.D, cfg.S, cfg.DC, cfg.FC, cfg.NH, cfg.NA, cfg.GG, cfg.E, cfg.DFF
    C = make_consts(nc, cfg)

    def di(name, shape, dt=F32):
        return nc.dram_tensor(name, list(shape), dt, kind="ExternalInput")

    def ds(name, shape, dt):
        return nc.dram_tensor(name, list(shape), dt)

    xT = di("xT", [D, S])
    pos = di("pos", [32, S], I32)
    g_attn = di("g_attn", [128, DC])
    wqk = di("wqk", [2 * NH, 128, DC, 128])
    wv = di("wv", [NH * 128 // cfg.VW, 128, DC, cfg.VW])
    wo = di("wo", [DC, 128, DC, 128])
    ffn = []
    for l in range(2):
        ffn.append(dict(g=di("g_ffn%d" % l, [128, DC]), wup=di("wup%d" % l, [2 * FC, 128, DC, 128]),
                        cw=di("cw%d" % l, [128, 3, 2 * FC]), cb=di("cb%d" % l, [128, 2 * FC]),
                        wdn=di("wdn%d" % l, [DC, 128, FC, 128])))
    g_sg = di("g_sg", [128, DC])
    sg_wu = di("sg_wu", [GG, 128, DC, 128])
    sg_wv = di("sg_wv", [E // 256, 128, DC, 256])
    sg_bu = di("sg_bu", [128, GG])
    sg_bv = di("sg_bv", [1, E])
    sg_lng = di("sg_lng", [128, GG])
    sg_lnb = di("sg_lnb", [128, E])
    sg_ws = di("sg_ws", [GG, 128, 128])
    sg_bs = di("sg_bs", [1, GG * 128])
    sg_wo = di("sg_wo", [DC, 128, GG, 128])
    g_fin = di("g_fin", [128, DC])
    outT = nc.dram_tensor("outT", [D, S], F32, kind="ExternalOutput")

    qT = ds("qT", [NH * 128, S], BF16)
    kT = ds("kT", [NH * 128, S], BF16)
    v = ds("v", [S, NH * 128], BF16)
    ksum = ds("ksum", [128, NA, cfg.NBLK], F32)
    cs = ds("cs", [32, 2, S], F32)
    oT = ds("oT", [D, S], BF16)
    gT = ds("gT", [DFF, S], BF16)
    xa = ds("xa", [D, S], F32)
    xb = ds("xb", [D, S], F32)
    wdnb = [ds("wdnb%d" % l, [DC, 128, FC, 128], BF16) for l in range(2)]
    wsT = ds("wsT", [128, GG, 128], BF16)
    bsb = ds("bsb", [128, GG, 128], F32)

    import os
    sel = os.environ.get("MK_STAGES")
    sel = set(int(t) for t in sel.split(",")) if sel else set(range(1, 13))
    stages = [
        lambda: (stage_rope(nc, cfg, C, pos, cs), stage_qkv(nc, cfg, C, xT, g_attn, wqk, wv, cs, qT, kT, v, ksum)),
        lambda: stage_moba(nc, cfg, C, qT, kT, v, ksum, oT),
        lambda: stage_dilated(nc, cfg, C, qT, kT, v, oT),
        lambda: stage_linear_res(nc, cfg, DC, oT, wo, xT, xa),
        lambda: stage_ffn_up(nc, cfg, xa, ffn[0]["g"], ffn[0]["wup"], ffn[0]["cw"], ffn[0]["cb"], gT, precast=(ffn[0]["wdn"], wdnb[0])),
        lambda: stage_linear_res(nc, cfg, FC, gT, wdnb[0], xa, xb),
        lambda: stage_sg_setup(nc, cfg, C, sg_ws, sg_bs, sg_lnb, wsT, bsb),
        lambda: stage_sg_main(nc, cfg, xb, g_sg, sg_wu, sg_wv, sg_bu, sg_bv, sg_lng, wsT, bsb, oT),
        lambda: stage_linear_res(nc, cfg, GG, oT, sg_wo, xb, xa),
        lambda: stage_ffn_up(nc, cfg, xa, ffn[1]["g"], ffn[1]["wup"], ffn[1]["cw"], ffn[1]["cb"], gT, precast=(ffn[1]["wdn"], wdnb[1])),
        lambda: stage_linear_res(nc, cfg, FC, gT, wdnb[1], xa, xb),
        lambda: stage_final_norm(nc, cfg, xb, g_fin, outT),
    ]
    for i, st in enumerate(stages):
        if i + 1 in sel:
            st()
    return nc


def host_prep(cfg, p):
    NH, FC, GG, E = cfg.NH, cfg.FC, cfg.GG, cfg.E
    f = lambda a: np.asarray(a, dtype=np.float32)
    w_in = f(p["attn_w_in"])[0]
    m = {}
    m["g_attn"] = vec_pc(f(p["attn_norm"])[0])
    m["wqk"] = tile_w(w_in[:, :2 * NH * 128])
    m["wv"] = tile_w(w_in[:, 2 * NH * 128:], cfg.VW)
    m["wo"] = tile_w(f(p["attn_w_out"])[0])
    for l in range(2):
        m["g_ffn%d" % l] = vec_pc(f(p["ffn_norm"])[l])
        m["wup%d" % l] = tile_w(f(p["ffn_w_up"])[l])
        m["cw%d" % l] = np.ascontiguousarray(f(p["ffn_conv_w"])[l].reshape(3, 2 * FC, 128).transpose(2, 0, 1))
        m["cb%d" % l] = vec_pc(f(p["ffn_conv_b"])[l])
        m["wdn%d" % l] = tile_w(f(p["ffn_w_down"])[l])
    sw = f(p["sg_w_in"])[0]
    sb = f(p["sg_b_in"])[0]
    m["g_sg"] = vec_pc(f(p["sg_norm"])[0])
    m["sg_wu"] = tile_w(sw[:, :E])
    m["sg_wv"] = tile_w(sw[:, E:], 256)
    m["sg_bu"] = vec_pc(sb[:E])
    m["sg_bv"] = np.ascontiguousarray(sb[E:].reshape(1, E))
    m["sg_lng"] = vec_pc(f(p["sg_v_gain"])[0])
    m["sg_lnb"] = np.ascontiguousarray(np.broadcast_to(f(p["sg_v_bias"])[0], (128, E)))
    m["sg_ws"] = np.ascontiguousarray(f(p["sg_w_s"])[0])
    m["sg_bs"] = np.ascontiguousarray(f(p["sg_b_s"])[0].reshape(1, GG * 128))
    m["sg_wo"] = tile_w(f(p["sg_w_out"])[0])
    m["g_fin"] = vec_pc(f(p["final_norm"]))
    return m


def run_module(cfg, inputs):
    x = np.asarray(inputs["x"], dtype=np.float32)
    positions = np.asarray(inputs["positions"]).astype(np.int32)
    B = x.shape[0]
    import time as _t
    t0 = _t.time()
    shared = host_prep(cfg, inputs)
    t1 = _t.time()
    nc = build_program(cfg)
    print("[mk] host_prep %.1fs build %.1fs" % (t1 - t0, _t.time() - t1), flush=True)
    in_maps = []
    for b in range(B):
        m = dict(shared)
        m["xT"] = np.ascontiguousarray(x[b].T)
        m["pos"] = np.ascontiguousarray(np.broadcast_to(positions[b], (32, cfg.S)))
        in_maps.append(m)
    t2 = _t.time()
    res = run_bass_kernel_spmd(nc, in_maps, core_ids=list(range(B)))
    print("[mk] launch %.1fs" % (_t.time() - t2), flush=True)
    return np.stack([np.ascontiguousarray(res.results[b]["outT"].T) for b in range(B)], 0)


def kernel(**inputs):
    cfg = Cfg()
    return run_module(cfg, inputs)
```

```python
import numpy as np
from contextlib import ExitStack
import concourse.bass as bass
import concourse.mybir as mybir
from concourse.bass_utils import run_bass_kernel_spmd

F32 = mybir.dt.float32
BF16 = mybir.dt.bfloat16
I32 = mybir.dt.int32
AF = mybir.ActivationFunctionType
ALU = mybir.AluOpType
AX = mybir.AxisListType

NEG = -30000.0
ENGS = ("pe", "act", "dve", "pool", "sp")
NRING = 12


class Cfg:
    def __init__(self, D=4096, S=4096, NA=24, NBG=8, DFF=14336, B=2):
        self.D, self.S, self.NA, self.NBG, self.DFF, self.B = D, S, NA, NBG, DFF, B
        self.DH = 128
        self.NB = 3 * NBG
        self.NH = NA + self.NB
        self.NHO = NA + NBG
        assert self.NHO * 128 == D
        self.QKV = 3 * self.NH * 128
        self.DC = D // 128
        self.FC = DFF // 128
        self.T = 512
        self.NT = S // self.T
        self.MB = 256
        self.NBLK = S // self.MB
        self.E = D
        self.GG = self.E // 128
        self.EPS = 1e-5
        self.PATS = ((128, 1), (512, 4), (2048, 16))
        self.VW = 512 if (self.NH * 128) % 512 == 0 else 256
        self.SCALE = 128 ** -0.5


class Buf:
    __slots__ = ("w", "r")

    def __init__(self):
        self.w = None
        self.r = []


class Op:
    __slots__ = ("eng", "fn", "deps", "signal", "ev", "dma", "di")

    def __init__(self, eng, fn, dma):
        self.eng, self.fn, self.dma = eng, fn, dma
        self.deps = []
        self.signal = False
        self.ev = None
        self.di = -1


class Prog:
    def __init__(self, nc):
        self.nc = nc
        self.ops = []

    def op(self, eng, fn, reads=(), writes=(), dma=False):
        o = Op(eng, fn, dma)
        deps = {}
        for b in reads:
            if b.w is not None:
                deps[id(b.w)] = b.w
        for b in writes:
            if b.w is not None:
                deps[id(b.w)] = b.w
            for r in b.r:
                deps[id(r)] = r
        for d in deps.values():
            if d is o:
                continue
            if d.eng == "pe" and eng == "pe" and not d.dma and not dma:
                continue
            o.deps.append(d)
            d.signal = True
        for b in writes:
            b.w = o
            b.r = []
        for b in reads:
            if b.w is not o:
                if not dma:
                    b.r = [r for r in b.r if r.dma or r.eng != eng]
                b.r.append(o)
        self.ops.append(o)
        return o

    def dma(self, eng, out, in_, reads=(), writes=()):
        return self.op(eng, lambda e: e.dma_start(out=out, in_=in_), reads, writes, dma=True)

    def emit(self, name=None):
        nc = self.nc
        with ExitStack() as es:
            if not hasattr(nc, "_mk_sems"):
                gs = ExitStack()
                c_ = {e: gs.enter_context(nc.semaphore("c_" + e)) for e in ENGS}
                d_ = {e: [gs.enter_context(nc.semaphore("d_%s%d" % (e, i))) for i in range(NRING)]
                      for e in ("sp", "pool")}
                nc._mk_sems = (gs, c_, d_)
                nc._mk_pool_n = 0
            _, csem, dsem = nc._mk_sems
            cnt = {e: 0 for e in ENGS}
            dcnt = {e: 0 for e in dsem}
            dcnt["pool"] = nc._mk_pool_n
            pool_base = nc._mk_pool_n
            by_eng = {e: [] for e in ENGS}
            dlist = {e: [] for e in dsem}
            for o in self.ops:
                if o.dma:
                    i = dcnt[o.eng]
                    dcnt[o.eng] += 1
                    o.di = i
                    o.ev = (dsem[o.eng][i % NRING], 16 * (i // NRING + 1))
                    dlist[o.eng].append(o)
                    if o.eng == "pool":
                        nc._mk_pool_n = i + 1
                elif o.signal:
                    cnt[o.eng] += 1
                    o.ev = (csem[o.eng], cnt[o.eng])
                by_eng[o.eng].append(o)
            block = es.enter_context(nc.Block())

            def body(e, eng):
                waited = {}

                def wait(ev):
                    sem, val = ev
                    k = id(sem)
                    if waited.get(k, 0) < val:
                        e.wait_ge(sem, val)
                        waited[k] = val

                for o in by_eng[eng]:
                    need = {}
                    for d in o.deps:
                        sm, val = d.ev
                        k_ = id(sm)
                        if k_ not in need or need[k_][1] < val:
                            need[k_] = (sm, val)
                    for ev in need.values():
                        wait(ev)
                    if o.dma:
                        li = o.di - (pool_base if eng == "pool" else 0)
                        if li >= NRING:
                            wait(dlist[eng][li - NRING].ev)
                    ins = o.fn(e)
                    if o.dma:
                        ins.then_inc(o.ev[0], 16)
                    elif o.signal:
                        ins.then_inc(o.ev[0], 1)
                if eng in dlist:
                    for o in dlist[eng][-NRING:]:
                        wait(o.ev)

            block.tensor(lambda e: body(e, "pe"))
            block.scalar(lambda e: body(e, "act"))
            block.vector(lambda e: body(e, "dve"))
            block.gpsimd(lambda e: body(e, "pool"))
            block.sync(lambda e: body(e, "sp"))
        _, csem, dsem = nc._mk_sems
        allsems = list(csem.values()) + list(dsem["sp"])
        with nc.Block() as blk2:
            def clr(e):
                for sm in allsems:
                    e.sem_clear(sm)
            blk2.sync(clr)
        self.ops = []


class Tile:
    def __init__(self, t, nsub=0):
        self.t = t
        self.b = Buf()
        self.sub = [Buf() for _ in range(nsub)]


class Ctx:
    _n = 0

    def __init__(self, nc, es):
        self.nc, self.es = nc, es
        Ctx._n += 1
        self.pre = "s%d_" % Ctx._n

    def sb(self, name, shape, dt, nsub=0):
        return Tile(self.es.enter_context(self.nc.sbuf_tensor(self.pre + name, list(shape), dt)), nsub)

    def ps(self, name, shape=(128, 512), dt=F32, nsub=0):
        return Tile(self.es.enter_context(self.nc.psum_tensor(self.pre + name, list(shape), dt)), nsub)


class Ring:
    def __init__(self, tiles):
        self.tiles = tiles
        self.i = 0

    def next(self):
        t = self.tiles[self.i % len(self.tiles)]
        self.i += 1
        return t


def ring(cx, name, n, shape, dt):
    return Ring([cx.sb("%s%d" % (name, i), shape, dt) for i in range(n)])


def psring(cx, name, n):
    return Ring([cx.ps("%s%d" % (name, i)) for i in range(n)])


def f_mm(out, lhsT, rhs, start=True, stop=True):
    return lambda e: e.matmul(out, lhsT, rhs, start=start, stop=stop)


def f_tr(out, in_, ident):
    return lambda e: e.transpose(out, in_, ident)


def f_act(out, in_, func, **kw):
    return lambda e: e.activation(out=out, in_=in_, func=func, **kw)


def f_ts(out, in0, s1, s2, op0, op1=None):
    if op1 is None:
        return lambda e: e.tensor_scalar(out=out, in0=in0, scalar1=s1, scalar2=None, op0=op0)
    return lambda e: e.tensor_scalar(out=out, in0=in0, scalar1=s1, scalar2=s2, op0=op0, op1=op1)


def f_stt(out, in0, scalar, in1, op0, op1):
    return lambda e: e.scalar_tensor_tensor(out=out, in0=in0, scalar=scalar, in1=in1, op0=op0, op1=op1)


def f_tt(out, in0, in1, op):
    return lambda e: e.tensor_tensor(out=out, in0=in0, in1=in1, op=op)


def f_copy(out, in_):
    return lambda e: e.tensor_copy(out=out, in_=in_)


def f_recip(out, in_):
    return lambda e: e.reciprocal(out=out, in_=in_)


def f_memset(ap, v):
    return lambda e: e.memset(ap, v)


def dview(h):
    return h.ap().rearrange("(c p) t -> p c t", p=128)


def emit_norm(P, cfg, K, xin, tok0, g, h, hcol0, xring, sqring, ones, psn, rstd, T=None):
    T, DC = (T or cfg.T), cfg.DC
    xv = dview(xin)
    for c in range(DC):
        xc = xring.next()
        P.dma("sp", xc.t[:, 0:T], xv[:, c, tok0:tok0 + T], writes=[xc.b])
        sq = sqring.next()
        P.op("act", f_act(sq.t[:, 0:T], xc.t[:, 0:T], AF.Square), reads=[xc.b], writes=[sq.b])
        P.op("pe", f_mm(psn.t[:, 0:T], ones.t[:, :], sq.t[:, 0:T], c == 0, c == DC - 1),
             reads=[ones.b, sq.b], writes=[psn.b])
    P.op("act", f_act(rstd.t[:, 0:T], psn.t[:, 0:T], AF.Sqrt, scale=1.0 / cfg.D, bias=K["eps"].t[:, 0:1]),
         reads=[psn.b, K["eps"].b], writes=[rstd.b])
    P.op("dve", f_recip(rstd.t[:, 0:T], rstd.t[:, 0:T]), reads=[rstd.b], writes=[rstd.b])
    for c in range(DC):
        xc = xring.next()
        P.dma("sp", xc.t[:, 0:T], xv[:, c, tok0:tok0 + T], writes=[xc.b])
        P.op("dve", f_stt(h.t[:, c, hcol0:hcol0 + T], xc.t[:, 0:T], g.t[:, c:c + 1], rstd.t[:, 0:T],
                          ALU.mult, ALU.mult),
             reads=[xc.b, g.b, rstd.b], writes=[h.b])


def load_consts(P, cx, cfg, consts):
    K = {}
    K["ones"] = cx.sb("k_ones", [128, 128], BF16)
    P.op("dve", f_memset(K["ones"].t[:, :], 1.0), writes=[K["ones"].b])
    K["eps"] = cx.sb("k_eps", [128, 1], F32)
    P.op("dve", f_memset(K["eps"].t[:, :], cfg.EPS), writes=[K["eps"].b])
    return K


def stage_ffn_up(nc, cfg, xin, g_d, wup_d, cw_d, cb_d, gT, precast=None):
    T, DC, FC, S = cfg.T, cfg.DC, cfg.FC, cfg.S
    NSUB = 2 if S % (2 * T) == 0 else 1
    TS = T * NSUB
    with ExitStack() as es:
        cx = Ctx(nc, es)
        P = Prog(nc)
        K = load_consts(P, cx, cfg, None)
        g = cx.sb("g", [128, DC], F32)
        P.dma("sp", g.t[:, :], g_d.ap(), writes=[g.b])
        cw = cx.sb("cw", [128, 3, 2 * FC], F32)
        P.dma("sp", cw.t[:, :, :], cw_d.ap(), writes=[cw.b])
        cb = cx.sb("cb", [128, 2 * FC], F32)
        P.dma("sp", cb.t[:, :], cb_d.ap(), writes=[cb.b])
        carry = [cx.sb("carry%d" % i, [128, FC, 2], F32, nsub=FC) for i in range(2)]
        for cr in carry:
            P.op("dve", f_memset(cr.t[:, :, :], 0.0), writes=cr.sub)
        hs = [cx.sb("h%d" % i, [128, DC, TS], BF16) for i in range(2)]
        xring = ring(cx, "xc", 4, [128, T], F32)
        sqring = ring(cx, "sq", 3, [128, T], BF16)
        rstd = cx.sb("rstd", [128, T], F32)
        wring = ring(cx, "w", 4, [128, DC, 128], BF16)
        abuf = ring(cx, "ab", 4, [128, T + 4], F32)
        cbuf = ring(cx, "cbf", 6, [128, T], F32)
        sgb = ring(cx, "sg", 2, [128, T], F32)
        gob = ring(cx, "go", 3, [128, T], BF16)
        psn = cx.ps("psn")
        psr = psring(cx, "ps", 7)
        NST = S // TS

        def do_norm(st_):
            for sub_ in range(NSUB):
                emit_norm(P, cfg, K, xin, st_ * TS + sub_ * T, g, hs[st_ % 2], sub_ * T, xring, sqring, K["ones"], psn, rstd)

        PCQ = 4
        n_pc = DC * PCQ
        pc_step = max(1, (NST * FC) // n_pc)
        pc_done = [0]
        PCW = FC * 128 // PCQ

        def do_precast(i):
            src, dst = precast
            c, q_ = i // PCQ, i % PCQ
            P.dma("pool", dst.ap()[c].rearrange("p k c -> p (k c)")[:, q_ * PCW:(q_ + 1) * PCW],
                  src.ap()[c].rearrange("p k c -> p (k c)")[:, q_ * PCW:(q_ + 1) * PCW])

        do_norm(0)
        pend = [None]
        for st in range(NST):
            h = hs[st % 2]
            for j in range(FC):
                if j == min(4, FC - 1) and st + 1 < NST:
                    do_norm(st + 1)
                ws = []
                for half in range(2):
                    w = wring.next()
                    P.dma("pool", w.t[:, :, :], wup_d.ap()[half * FC + j], writes=[w.b])
                    ws.append(w)
                it = st * FC + j
                if precast is not None and it % pc_step == 0 and pc_done[0] < n_pc:
                    do_precast(pc_done[0])
                    pc_done[0] += 1
                for sub in range(NSUB):
                    tok0 = st * TS + sub * T
                    cs = []
                    for half in range(2):
                        ps = psr.next()
                        for k in range(DC):
                            P.op("pe", f_mm(ps.t[:, :], ws[half].t[:, k, :], h.t[:, k, sub * T:(sub + 1) * T],
                                            k == 0, k == DC - 1),
                                 reads=[ws[half].b, h.b], writes=[ps.b])
                        ch = half * FC + j
                        ab = abuf.next()
                        cr = carry[half]
                        P.op("dve", f_copy(ab.t[:, 0:2], cr.t[:, j, :]), reads=[cr.sub[j]], writes=[ab.b])
                        P.op("act", f_act(ab.t[:, 2:T + 2], ps.t[:, :], AF.Copy), reads=[ps.b], writes=[ab.b])
                        P.op("dve", f_copy(cr.t[:, j, :], ab.t[:, T:T + 2]), reads=[ab.b], writes=[cr.sub[j]])
                        c_ = cbuf.next()
                        P.op("dve", f_ts(c_.t[:, :], ab.t[:, 2:T + 2], cw.t[:, 2, ch:ch + 1], cb.t[:, ch:ch + 1],
                                         ALU.mult, ALU.add), reads=[ab.b, cw.b, cb.b], writes=[c_.b])
                        P.op("dve", f_stt(c_.t[:, :], ab.t[:, 1:T + 1], cw.t[:, 1, ch:ch + 1], c_.t[:, :],
                                          ALU.mult, ALU.add), reads=[ab.b, c_.b, cw.b], writes=[c_.b])
                        P.op("dve", f_stt(c_.t[:, :], ab.t[:, 0:T], cw.t[:, 0, ch:ch + 1], c_.t[:, :],
                                          ALU.mult, ALU.add), reads=[ab.b, c_.b, cw.b], writes=[c_.b])
                        cs.append(c_)
                    def fin(cs=cs, j=j, tok0=tok0):
                        sg = sgb.next()
                        P.op("act", f_act(sg.t[:, :], cs[0].t[:, :], AF.Silu), reads=[cs[0].b], writes=[sg.b])
                        go = gob.next()
                        P.op("dve", f_tt(go.t[:, :], sg.t[:, :], cs[1].t[:, :], ALU.mult),
                             reads=[sg.b, cs[1].b], writes=[go.b])
                        P.dma("sp", gT.ap()[j * 128:(j + 1) * 128, tok0:tok0 + T], go.t[:, :], reads=[go.b])

                    if pend[0] is not None:
                        pend[0]()
                    pend[0] = fin
        if pend[0] is not None:
            pend[0]()
        while precast is not None and pc_done[0] < n_pc:
            do_precast(pc_done[0])
            pc_done[0] += 1
        P.emit()


def stage_linear_res(nc, cfg, KC, actT, w_d, xin, xout):
    T, DC, S = cfg.T, cfg.DC, cfg.S
    NSUB = 2 if (KC <= 32 and S % (2 * T) == 0) else 1
    TS = T * NSUB
    with ExitStack() as es:
        cx = Ctx(nc, es)
        P = Prog(nc)
        KG = 16 // NSUB
        NG = (KC + KG - 1) // KG
        abufs = [cx.sb("a%d" % i, [128, KC, TS], BF16, nsub=NG) for i in range(2 if KC <= 32 else 1)]
        wring = ring(cx, "w", 2, [128, KC, 128], BF16)
        xr = ring(cx, "xr", 3, [128, T], F32)
        xo = ring(cx, "xo", 3, [128, T], F32)
        psr = psring(cx, "ps", 4)
        av = dview(actT)
        xv = dview(xin)
        ov = dview(xout)
        for tt in range(S // TS):
            t0 = tt * TS
            a = abufs[tt % len(abufs)]
            for gk in range(NG):
                k0, k1 = gk * KG, min(KC, (gk + 1) * KG)
                P.dma("sp", a.t[:, k0:k1, :], av[:, k0:k1, t0:t0 + TS], writes=[a.sub[gk]])
            for c in range(DC):
                w = wring.next()
                P.dma("pool", w.t[:, :, :], w_d.ap()[c], writes=[w.b])
                for sub in range(NSUB):
                    tok0 = t0 + sub * T
                    ps = psr.next()
                    for k in range(KC):
                        P.op("pe", f_mm(ps.t[:, :], w.t[:, k, :], a.t[:, k, sub * T:(sub + 1) * T], k == 0, k == KC - 1),
                             reads=[w.b, a.sub[k // KG]], writes=[ps.b])
                    x_ = xr.next()
                    P.dma("sp", x_.t[:, :], xv[:, c, tok0:tok0 + T], writes=[x_.b])
                    o_ = xo.next()
                    P.op("dve", f_tt(o_.t[:, :], ps.t[:, :], x_.t[:, :], ALU.add), reads=[ps.b, x_.b], writes=[o_.b])
                    P.dma("sp", ov[:, c, tok0:tok0 + T], o_.t[:, :], reads=[o_.b])
        P.emit()


def tile_w(W, cw=128):
    K, N = W.shape
    return np.ascontiguousarray(W.reshape(K // 128, 128, N // cw, cw).transpose(2, 1, 0, 3))


def vec_pc(v):
    return np.ascontiguousarray(v.reshape(-1, 128).T)


def np_bf16(a):
    import ml_dtypes
    return np.asarray(a, dtype=np.float32).astype(ml_dtypes.bfloat16)


def make_consts(nc, cfg):
    C = {}
    half = 16
    invf = np.power(np.float32(500000.0), -np.arange(half, dtype=np.float32) * np.float32(2.0 / 32.0)).astype(np.float32)
    C["invf"] = nc.inline_tensor(np.concatenate([invf, invf]).reshape(32, 1).astype(np.float32), "k_invf")
    Pm = np.zeros((128, 32), np.float32)
    for m in range(16):
        Pm[m + 16, m] = -1.0
        Pm[m, m + 16] = 1.0
    C["Pm"] = nc.inline_tensor(Pm, "k_Pm")
    C["identf"] = nc.inline_tensor(np.eye(128, dtype=np.float32), "k_identf")
    C["identb"] = nc.inline_tensor(np_bf16(np.eye(128)), "k_identb")
    NBLK = cfg.NBLK
    E = np.zeros((128, NBLK * 128), np.float32)
    for n in range(NBLK):
        E[n % 16, n * 128:(n + 1) * 128] = 1.0
    C["esel"] = nc.inline_tensor(np_bf16(E), "k_esel")
    j = np.arange(128)[:, None, None]
    r = np.arange(4)[None, :, None]
    i = np.arange(512)[None, None, :]
    C["cmask"] = nc.inline_tensor(np_bf16(np.where(r * 128 + j <= i, 0.0, NEG)), "k_cmask")
    jj = np.arange(128)[:, None]
    ii = np.arange(128)[None, :]
    cur = np.where(jj <= ii, 0.0, NEG)
    prv = np.where(jj >= ii, 0.0, NEG)
    bm = np.stack([np.concatenate([cur, np.full((128, 128), NEG)], 1), np.concatenate([cur, prv], 1)], 1)
    C["bmask"] = nc.inline_tensor(np_bf16(bm), "k_bmask")
    NKT = cfg.S // 128
    pm = np.full((NKT, 16), NEG, np.float32)
    om = np.full((NKT, 16), -1.0e9, np.float32)
    for qi in range(NKT):
        qb = (qi * 128) // cfg.MB
        pm[qi, :qb] = 0.0
        om[qi, qb] = 0.0
    C["pastm"] = nc.inline_tensor(np.ascontiguousarray(np.broadcast_to(pm.reshape(1, NKT * 16), (128, NKT * 16))), "k_pastm")
    C["ownm"] = nc.inline_tensor(np.ascontiguousarray(np.broadcast_to(om.reshape(1, NKT * 16), (128, NKT * 16))), "k_ownm")
    C["tril"] = nc.inline_tensor(np.tril(np.ones((128, 128), np.float32)), "k_tril")
    return C


def stage_rope(nc, cfg, C, pos_d, cs_d):
    S = cfg.S
    W = min(S, 2048)
    with ExitStack() as es:
        cx = Ctx(nc, es)
        P = Prog(nc)
        invf = cx.sb("invf", [32, 1], F32)
        P.dma("sp", invf.t[:, :], C["invf"].ap(), writes=[invf.b])
        for c0 in range(0, S, W):
            S_ = W
            posi = cx.sb("posi%d" % c0, [32, S_], I32)
            P.dma("sp", posi.t[:, :], pos_d.ap()[:, c0:c0 + W], writes=[posi.b])
            ang = cx.sb("ang%d" % c0, [32, S_], F32)
            P.op("dve", f_copy(ang.t[:, :], posi.t[:, :]), reads=[posi.b], writes=[ang.b])
            P.op("dve", f_ts(ang.t[:, :], ang.t[:, :], invf.t[:, 0:1], None, ALU.mult), reads=[ang.b, invf.b], writes=[ang.b])
            ki = cx.sb("ki%d" % c0, [32, S_], I32)
            kf = cx.sb("kf%d" % c0, [32, S_], F32)
            mk = cx.sb("mk%d" % c0, [32, S_], F32)
            for idx, (nm, shift) in enumerate((("cos", 0.25), ("sin", 0.0))):
                y = cx.sb("y_%s%d" % (nm, c0), [32, S_], F32)
                P.op("dve", f_ts(y.t[:, :], ang.t[:, :], float(1.0 / (2.0 * np.pi)), shift, ALU.mult, ALU.add),
                     reads=[ang.b], writes=[y.b])
                P.op("dve", f_copy(ki.t[:, :], y.t[:, :]), reads=[y.b], writes=[ki.b])
                P.op("dve", f_copy(kf.t[:, :], ki.t[:, :]), reads=[ki.b], writes=[kf.b])
                P.op("dve", f_tt(y.t[:, :], y.t[:, :], kf.t[:, :], ALU.subtract), reads=[y.b, kf.b], writes=[y.b])
                P.op("dve", f_ts(mk.t[:, :], y.t[:, :], 0.5, None, ALU.is_gt), reads=[y.b], writes=[mk.b])
                P.op("dve", f_tt(y.t[:, :], y.t[:, :], mk.t[:, :], ALU.subtract), reads=[y.b, mk.b], writes=[y.b])
                P.op("dve", f_ts(mk.t[:, :], y.t[:, :], -0.5, None, ALU.is_lt), reads=[y.b], writes=[mk.b])
                P.op("dve", f_tt(y.t[:, :], y.t[:, :], mk.t[:, :], ALU.add), reads=[y.b, mk.b], writes=[y.b])
                P.op("act", f_act(y.t[:, :], y.t[:, :], AF.Sin, scale=float(2.0 * np.pi * (1.0 - 1e-6))),
                     reads=[y.b], writes=[y.b])
                P.dma("sp", cs_d.ap()[:, idx, c0:c0 + W], y.t[:, :], reads=[y.b])
        P.emit()


def stage_qkv(nc, cfg, C, xin, g_d, wqk_d, wv_d, cs_d, qT, kT, v, ksum_d):
    T, DC, S, NH, NA = cfg.T, cfg.DC, cfg.S, cfg.NH, cfg.NA
    VW = cfg.VW
    NSUB = 2 if S % (2 * T) == 0 else 1
    TS = T * NSUB
    with ExitStack() as es:
        cx = Ctx(nc, es)
        P = Prog(nc)
        K = load_consts(P, cx, cfg, None)
        g = cx.sb("g", [128, DC], F32)
        P.dma("sp", g.t[:, :], g_d.ap(), writes=[g.b])
        Pm = cx.sb("Pm", [128, 32], F32)
        P.dma("sp", Pm.t[:, :], C["Pm"].ap(), writes=[Pm.b])
        csr = ring(cx, "cs", 4, [32, 2, T], F32)
        ksum = cx.sb("ksum", [128, NA, cfg.NBLK], F32)
        h = cx.sb("h", [128, DC, TS], BF16)
        xring = ring(cx, "xc", 4, [128, T], F32)
        sqring = ring(cx, "sq", 3, [128, T], BF16)
        rstd = cx.sb("rstd", [128, T], F32)
        wring = ring(cx, "w", 3, [128, DC, 128], BF16)
        wvring = ring(cx, "wv", 2, [128, DC, VW], BF16)
        qfr = ring(cx, "qf", 4, [128, T], F32)
        tmr = ring(cx, "tm", 4, [32, T], F32)
        qbr = ring(cx, "qb", 3, [128, T], BF16)
        vbr = ring(cx, "vb", 3, [128, VW], BF16)
        psn = cx.ps("psn")
        psr = psring(cx, "ps", 5)
        psp = psring(cx, "pp", 2)
        for st in range(S // TS):
            cst = []
            for sub in range(NSUB):
                emit_norm(P, cfg, K, xin, st * TS + sub * T, g, h, sub * T, xring, sqring, K["ones"], psn, rstd)
                cs = csr.next()
                tok0 = st * TS + sub * T
                P.dma("sp", cs.t[:, :, :], cs_d.ap()[:, :, tok0:tok0 + T], writes=[cs.b])
                cst.append(cs)
            pending = [None]
            for j in range(2 * NH):
                isk = j >= NH
                hd = j - NH if isk else j
                w = wring.next()
                P.dma("pool", w.t[:, :, :], wqk_d.ap()[j], writes=[w.b])
                for sub in range(NSUB):
                    tok0 = st * TS + sub * T
                    ps = psr.next()
                    for k in range(DC):
                        P.op("pe", f_mm(ps.t[:, :], w.t[:, k, :], h.t[:, k, sub * T:(sub + 1) * T], k == 0, k == DC - 1),
                             reads=[w.b, h.b], writes=[ps.b])
                    qf = qfr.next()
                    P.op("act", f_act(qf.t[:, :], ps.t[:, :], AF.Copy), reads=[ps.b], writes=[qf.b])

                    def rot(qf=qf, sub=sub, tok0=tok0, isk=isk, hd=hd):
                        pp = psp.next()
                        P.op("pe", f_mm(pp.t[0:32, :], Pm.t[:, :], qf.t[:, :]), reads=[Pm.b, qf.b], writes=[pp.b])
                        t1 = tmr.next()
                        t2 = tmr.next()
                        P.op("dve", f_tt(t1.t[:, :], qf.t[0:32, :], cst[sub].t[:, 0, :], ALU.mult),
                             reads=[qf.b, cst[sub].b], writes=[t1.b])
                        P.op("dve", f_tt(t2.t[:, :], pp.t[0:32, :], cst[sub].t[:, 1, :], ALU.mult),
                             reads=[pp.b, cst[sub].b], writes=[t2.b])
                        P.op("dve", f_tt(qf.t[0:32, :], t1.t[:, :], t2.t[:, :], ALU.add),
                             reads=[t1.b, t2.b], writes=[qf.b])
                        qb = qbr.next()
                        P.op("act", f_act(qb.t[:, :], qf.t[:, :], AF.Copy), reads=[qf.b], writes=[qb.b])
                        if isk and hd < NA:
                            b0 = tok0 // cfg.MB
                            nb = T // cfg.MB
                            P.op("dve", lambda e, o_=ksum.t[:, hd, b0:b0 + nb], i_=qf.t[:, :].rearrange("p (b m) -> p b m", m=cfg.MB):
                                 e.tensor_reduce(out=o_, in_=i_, axis=AX.X, op=ALU.add),
                                 reads=[qf.b], writes=[ksum.b])
                        dst = kT if isk else qT
                        P.dma("sp", dst.ap()[hd * 128:(hd + 1) * 128, tok0:tok0 + T], qb.t[:, :], reads=[qb.b])

                    if pending[0] is not None:
                        pending[0]()
                    pending[0] = rot
            if pending[0] is not None:
                pending[0]()
                pending[0] = None
            for jb in range(NH * 128 // VW):
                wv = wvring.next()
                P.dma("pool", wv.t[:, :, :], wv_d.ap()[jb], writes=[wv.b])
                for tb in range(TS // 128):
                    tok0 = st * TS + tb * 128
                    ps = psr.next()
                    for k in range(DC):
                        P.op("pe", f_mm(ps.t[:, 0:VW], h.t[:, k, tb * 128:(tb + 1) * 128], wv.t[:, k, :], k == 0, k == DC - 1),
                             reads=[wv.b, h.b], writes=[ps.b])
                    vb = vbr.next()
                    P.op("act", f_act(vb.t[:, :], ps.t[:, 0:VW], AF.Copy), reads=[ps.b], writes=[vb.b])
                    P.dma("sp", v.ap()[tok0:tok0 + 128, jb * VW:(jb + 1) * VW], vb.t[:, :], reads=[vb.b])
        P.dma("sp", ksum_d.ap(), ksum.t[:, :, :], reads=[ksum.b])
        P.emit()


def stage_moba(nc, cfg, C, qT, kT, v, ksum_d, oT):
    S, NA, NBLK, T = cfg.S, cfg.NA, cfg.NBLK, cfg.T
    NKT = S // 128
    with ExitStack() as es:
        cx = Ctx(nc, es)
        P = Prog(nc)
        K = load_consts(P, cx, cfg, None)
        ones = K["ones"]
        identf = cx.sb("identf", [128, 128], F32)
        P.dma("sp", identf.t[:, :], C["identf"].ap(), writes=[identf.b])
        identb = cx.sb("identb", [128, 128], BF16)
        P.dma("sp", identb.t[:, :], C["identb"].ap(), writes=[identb.b])
        esel = cx.sb("esel", [128, NBLK * 128], BF16)
        P.dma("sp", esel.t[:, :], C["esel"].ap(), writes=[esel.b])
        cmask = cx.sb("cmask", [128, 4, 512], BF16)
        P.dma("sp", cmask.t[:, :, :], C["cmask"].ap(), writes=[cmask.b])
        ksum = cx.sb("ksum", [128, NA, NBLK], F32)
        P.dma("sp", ksum.t[:, :, :], ksum_d.ap(), writes=[ksum.b])
        qr = ring(cx, "q", 2, [128, S], BF16)
        kr = ring(cx, "k", 2, [128, S], BF16)
        vr = ring(cx, "v", 2, [128, NKT, 128], BF16)
        kmr = ring(cx, "km", 2, [128, 16], BF16)
        btr = ring(cx, "bt", 2, [128, S], BF16)
        gmr = ring(cx, "gm", 4, [128, 16], F32)
        mxr = ring(cx, "mx", 4, [128, 8], F32)
        bqr = ring(cx, "bq", 6, [128, 128], F32)
        ptr = ring(cx, "pt", 3, [128, T], BF16)
        rzr = ring(cx, "rz", 2, [128, T], F32)
        obr = ring(cx, "ob", 2, [128, T], BF16)
        psr = psring(cx, "ps", 3)
        por = psring(cx, "po", 2)
        pzr = psring(cx, "pz", 2)
        psm = cx.ps("psm", nsub=7)
        NB16 = min(NBLK, 16)
        assert NBLK <= 16 and NBLK >= 8
        pastm = cx.sb("pastm", [128, NKT * 16], F32)
        P.dma("sp", pastm.t[:, :], C["pastm"].ap(), writes=[pastm.b])
        ownm = cx.sb("ownm", [128, NKT * 16], F32)
        P.dma("sp", ownm.t[:, :], C["ownm"].ap(), writes=[ownm.b])
        gmar = ring(cx, "gma", 2, [128, NKT * 16], F32)
        mxar = ring(cx, "mxa", 2, [128, NKT, 8], F32)
        bqar = ring(cx, "bqa", 2, [128, NKT, 128], F32)
        for t_ in bqar.tiles:
            P.op("dve", f_memset(t_.t[:, :, :], 0.0), writes=[t_.b])
        v3 = lambda ap: ap.rearrange("p (a b) -> p a b", b=16)
        st = {}

        def load(hd):
            q, k, vv, km, bt = qr.next(), kr.next(), vr.next(), kmr.next(), btr.next()
            P.dma("sp", q.t[:, :], qT.ap()[hd * 128:(hd + 1) * 128, :], writes=[q.b])
            P.dma("sp", k.t[:, :], kT.ap()[hd * 128:(hd + 1) * 128, :], writes=[k.b])
            vsrc = v.ap()[:, hd * 128:(hd + 1) * 128].rearrange("(n p) d -> p n d", p=128)
            for n0 in range(0, NKT, 8):
                P.dma("sp", vv.t[:, n0:n0 + 8, :], vsrc[:, n0:n0 + 8, :], writes=[vv.b])
            P.op("dve", f_memset(km.t[:, :], 0.0), writes=[km.b])
            P.op("dve", f_copy(km.t[:, 0:NBLK], ksum.t[:, hd, :]), reads=[ksum.b], writes=[km.b])
            st[hd] = (q, k, vv, km, bt)

        def gate_a(hd):
            q, k, vv, km, bt = st[hd]
            for qi in range(NKT):
                P.op("pe", f_mm(psm.t[:, qi * 16:(qi + 1) * 16], q.t[:, qi * 128:(qi + 1) * 128], km.t[:, 0:16]),
                     reads=[q.b, km.b], writes=[psm.b])
            gma, mxa, bqa = gmar.next(), mxar.next(), bqar.next()
            P.op("dve", f_tt(gma.t[:, :], psm.t[:, 0:NKT * 16], pastm.t[:, :], ALU.add), reads=[psm.b, pastm.b], writes=[gma.b])
            for qi in range(NKT):
                P.op("dve", lambda e, o_=mxa.t[:, qi, :], i_=gma.t[:, qi * 16:(qi + 1) * 16]: e.max(out=o_, in_=i_),
                     reads=[gma.b], writes=[mxa.b])
            for qi in range(NKT):
                P.op("dve", f_ts(gma.t[:, qi * 16:(qi + 1) * 16], gma.t[:, qi * 16:(qi + 1) * 16], mxa.t[:, qi, 2:3], None, ALU.is_ge),
                     reads=[gma.b, mxa.b], writes=[gma.b])
            P.op("dve", f_ts(bqa.t[:, :, 0:16], v3(gma.t[:, :]), -NEG, NEG, ALU.mult, ALU.add), reads=[gma.b], writes=[bqa.b])
            P.op("dve", f_tt(bqa.t[:, :, 0:16], bqa.t[:, :, 0:16], v3(pastm.t[:, :]), ALU.min), reads=[bqa.b, pastm.b], writes=[bqa.b])
            P.op("dve", f_tt(bqa.t[:, :, 0:16], bqa.t[:, :, 0:16], v3(ownm.t[:, :]), ALU.max), reads=[bqa.b, ownm.b], writes=[bqa.b])
            st[hd] = (q, k, vv, km, bt, bqa)

        def gate_b(hd):
            q, k, vv, km, bt, bqa = st[hd]
            for grp in range(NKT // 4):
                pst = psr.next()
                for j_ in range(4):
                    P.op("pe", f_tr(pst.t[:, j_ * 128:(j_ + 1) * 128], bqa.t[:, grp * 4 + j_, :], identf.t[:, :]),
                         reads=[bqa.b, identf.b], writes=[pst.b])
                P.op("act", f_act(bt.t[:, grp * 512:(grp + 1) * 512], pst.t[:, :], AF.Copy), reads=[pst.b], writes=[bt.b])

        def attention(hd):
            q, k, vv, km, bt, bqa = st.pop(hd)
            for g in range(S // T):
                nkt = 4 * (g + 1)
                po, pz = por.next(), pzr.next()
                qs = q.t[:, g * T:(g + 1) * T]

                def qk(kt):
                    ps = psr.next()
                    n = kt // 2
                    diag = kt >= 4 * g
                    P.op("pe", f_mm(ps.t[:, :], k.t[:, kt * 128:(kt + 1) * 128], qs, True, False),
                         reads=[k.b, q.b], writes=[ps.b])
                    P.op("pe", f_mm(ps.t[:, :], esel.t[:, n * 128:(n + 1) * 128], bt.t[:, g * T:(g + 1) * T], False, not diag),
                         reads=[esel.b, bt.b], writes=[ps.b])
                    if diag:
                        P.op("pe", f_mm(ps.t[:, :], identb.t[:, :], cmask.t[:, kt - 4 * g, :], False, True),
                             reads=[identb.b, cmask.b], writes=[ps.b])
                    return ps

                ps_next = qk(0)
                for kt in range(nkt):
                    ps = ps_next
                    pt = ptr.next()
                    P.op("act", f_act(pt.t[:, :], ps.t[:, :], AF.Exp, scale=float(cfg.SCALE)), reads=[ps.b], writes=[pt.b])
                    if kt + 1 < nkt:
                        ps_next = qk(kt + 1)
                    P.op("pe", f_mm(po.t[:, :], vv.t[:, kt, :], pt.t[:, :], kt == 0, kt == nkt - 1),
                         reads=[vv.b, pt.b], writes=[po.b])
                    P.op("pe", f_mm(pz.t[:, :], ones.t[:, :], pt.t[:, :], kt == 0, kt == nkt - 1),
                         reads=[ones.b, pt.b], writes=[pz.b])
                rz = rzr.next()
                P.op("dve", f_recip(rz.t[:, :], pz.t[:, :]), reads=[pz.b], writes=[rz.b])
                ob = obr.next()
                P.op("dve", f_tt(ob.t[:, :], po.t[:, :], rz.t[:, :], ALU.mult), reads=[po.b, rz.b], writes=[ob.b])
                P.dma("sp", oT.ap()[hd * 128:(hd + 1) * 128, g * T:(g + 1) * T], ob.t[:, :], reads=[ob.b])

        load(0)
        gate_a(0)
        gate_b(0)
        for hd in range(NA):
            if hd + 1 < NA:
                load(hd + 1)
                gate_a(hd + 1)
            attention(hd)
            if hd + 1 < NA:
                gate_b(hd + 1)
        P.emit()


def stage_dilated(nc, cfg, C, qT, kT, v, oT):
    S, NA, NBG, T = cfg.S, cfg.NA, cfg.NBG, cfg.T
    NKT = S // 128
    with ExitStack() as es:
        cx = Ctx(nc, es)
        P = Prog(nc)
        K = load_consts(P, cx, cfg, None)
        ones = K["ones"]
        identb = cx.sb("identb", [128, 128], BF16)
        P.dma("sp", identb.t[:, :], C["identb"].ap(), writes=[identb.b])
        bmask = cx.sb("bmask", [128, 2, 256], BF16)
        P.dma("sp", bmask.t[:, :, :], C["bmask"].ap(), writes=[bmask.b])
        qr = ring(cx, "q", 4, [128, S], BF16)
        kr = ring(cx, "k", 4, [128, S], BF16)
        vr = ring(cx, "v", 2, [128, NKT, 128], BF16)
        uacc = cx.sb("uacc", [128, S], F32)
        zacc = cx.sb("zacc", [128, S], F32)
        ptr = ring(cx, "pt", 3, [128, 256], BF16)
        obr = ring(cx, "ob", 2, [128, T], BF16)
        psr = psring(cx, "ps", 4)
        pur = psring(cx, "pu", 4)
        for hh in range(NBG):
            for gi, (win, dil) in enumerate(cfg.PATS):
                hq = NA + gi * NBG + hh
                nblk = S // (128 * dil)
                q, k, vv = qr.next(), kr.next(), vr.next()
                P.dma("sp", q.t[:, :], qT.ap()[hq * 128:(hq + 1) * 128, :], writes=[q.b])
                P.dma("sp", k.t[:, :], kT.ap()[hq * 128:(hq + 1) * 128, :], writes=[k.b])
                vsrc = v.ap()[:, hq * 128:(hq + 1) * 128].rearrange("(n p r) d -> p r n d", p=128, r=dil)
                for r in range(dil):
                    for n0 in range(0, nblk, 8):
                        n1 = min(nblk, n0 + 8)
                        P.dma("sp", vv.t[:, r * nblk + n0:r * nblk + n1, :], vsrc[:, r, n0:n1, :], writes=[vv.b])
                if dil > 1:
                    q2, k2 = qr.next(), kr.next()
                    sub = S // dil
                    for r in range(dil):
                        P.op("dve", f_copy(q2.t[:, r * sub:(r + 1) * sub], q.t[:, r:r + (sub - 1) * dil + 1:dil]), reads=[q.b], writes=[q2.b])
                        P.op("act", f_act(k2.t[:, r * sub:(r + 1) * sub], k.t[:, r:r + (sub - 1) * dil + 1:dil], AF.Copy), reads=[k.b], writes=[k2.b])
                    q, k = q2, k2
                tiles = [(r, n) for r in range(dil) for n in range(nblk)]

                def qk(rn, q=q, k=k, nblk=nblk):
                    r, n = rn
                    c0 = (r * nblk + n) * 128
                    p0 = c0 - 128 if n > 0 else c0
                    ps = psr.next()
                    P.op("pe", f_mm(ps.t[:, 0:128], k.t[:, c0:c0 + 128], q.t[:, c0:c0 + 128], True, False), reads=[k.b, q.b], writes=[ps.b])
                    P.op("pe", f_mm(ps.t[:, 128:256], k.t[:, p0:p0 + 128], q.t[:, c0:c0 + 128], False, False), reads=[k.b, q.b], writes=[ps.b])
                    P.op("pe", f_mm(ps.t[:, 0:256], identb.t[:, :], bmask.t[:, 1 if n > 0 else 0, :], False, True),
                         reads=[identb.b, bmask.b], writes=[ps.b])
                    return ps

                ps_next = qk(tiles[0])
                for ti, (r, n) in enumerate(tiles):
                    b0 = n * 128 * dil + r
                    cur = slice(b0, b0 + 127 * dil + 1, dil)
                    ps = ps_next
                    pt = ptr.next()
                    P.op("act", f_act(pt.t[:, :], ps.t[:, 0:256], AF.Exp, scale=float(cfg.SCALE)), reads=[ps.b], writes=[pt.b])
                    if ti + 1 < len(tiles):
                        ps_next = qk(tiles[ti + 1])
                    pu = pur.next()
                    vi = r * nblk + n
                    vp = vi - 1 if n > 0 else vi
                    P.op("pe", f_mm(pu.t[:, 0:128], vv.t[:, vi, :], pt.t[:, 0:128], True, False), reads=[vv.b, pt.b], writes=[pu.b])
                    P.op("pe", f_mm(pu.t[:, 0:128], vv.t[:, vp, :], pt.t[:, 128:256], False, False), reads=[vv.b, pt.b], writes=[pu.b])
                    P.op("pe", f_mm(pu.t[:, 128:256], ones.t[:, :], pt.t[:, 0:128], False, False), reads=[ones.b, pt.b], writes=[pu.b])
                    P.op("pe", f_mm(pu.t[:, 128:256], ones.t[:, :], pt.t[:, 128:256], False, True), reads=[ones.b, pt.b], writes=[pu.b])
                    if gi == 0:
                        P.op("dve", f_copy(uacc.t[:, cur], pu.t[:, 0:128]), reads=[pu.b], writes=[uacc.b])
                        P.op("dve", f_copy(zacc.t[:, cur], pu.t[:, 128:256]), reads=[pu.b], writes=[zacc.b])
                    else:
                        P.op("dve", f_tt(uacc.t[:, cur], uacc.t[:, cur], pu.t[:, 0:128], ALU.add), reads=[pu.b, uacc.b], writes=[uacc.b])
                        P.op("dve", f_tt(zacc.t[:, cur], zacc.t[:, cur], pu.t[:, 128:256], ALU.add), reads=[pu.b, zacc.b], writes=[zacc.b])
            P.op("dve", f_recip(zacc.t[:, :], zacc.t[:, :]), reads=[zacc.b], writes=[zacc.b])
            for g in range(S // T):
                ob = obr.next()
                P.op("dve", f_tt(ob.t[:, :], uacc.t[:, g * T:(g + 1) * T], zacc.t[:, g * T:(g + 1) * T], ALU.mult),
                     reads=[uacc.b, zacc.b], writes=[ob.b])
                P.dma("sp", oT.ap()[(NA + hh) * 128:(NA + hh + 1) * 128, g * T:(g + 1) * T], ob.t[:, :], reads=[ob.b])
        P.emit()


def stage_sg_setup(nc, cfg, C, ws_d, bs_d, lnb_d, wsT_d, bsb_d):
    GG = cfg.GG
    with ExitStack() as es:
        cx = Ctx(nc, es)
        P = Prog(nc)
        identf = cx.sb("identf", [128, 128], F32)
        P.dma("sp", identf.t[:, :], C["identf"].ap(), writes=[identf.b])
        tril = cx.sb("tril", [128, 128], F32)
        P.dma("sp", tril.t[:, :], C["tril"].ap(), writes=[tril.b])
        e0 = cx.sb("e0", [128, 128], BF16)
        P.op("dve", f_memset(e0.t[:, :], 0.0), writes=[e0.b])
        P.op("dve", f_memset(e0.t[0:1, :], 1.0), writes=[e0.b])
        lnb = cx.sb("lnb", [128, cfg.E], BF16)
        P.dma("pool", lnb.t[:, :], lnb_d.ap(), writes=[lnb.b])
        bs0 = cx.sb("bs0", [128, GG * 128], BF16)
        P.op("dve", f_memset(bs0.t[:, :], 0.0), writes=[bs0.b])
        P.dma("pool", bs0.t[0:1, :], bs_d.ap(), writes=[bs0.b])
        wsT = cx.sb("wsT", [128, GG, 128], BF16)
        bsb = cx.sb("bsb", [128, GG, 128], F32)
        wr = ring(cx, "w", 3, [128, 128], F32)
        psr = psring(cx, "ps", 4)
        for g in range(GG):
            w = wr.next()
            P.dma("sp", w.t[:, :], ws_d.ap()[g], writes=[w.b])
            P.op("dve", f_tt(w.t[:, :], w.t[:, :], tril.t[:, :], ALU.mult), reads=[w.b, tril.b], writes=[w.b])
            ps = psr.next()
            P.op("pe", f_tr(ps.t[:, 0:128], w.t[:, :], identf.t[:, :]), reads=[w.b, identf.b], writes=[ps.b])
            P.op("act", f_act(wsT.t[:, g, :], ps.t[:, 0:128], AF.Copy), reads=[ps.b], writes=[wsT.b])
            ps2 = psr.next()
            P.op("pe", f_mm(ps2.t[:, 0:128], lnb.t[:, g * 128:(g + 1) * 128], wsT.t[:, g, :], True, False),
                 reads=[lnb.b, wsT.b], writes=[ps2.b])
            P.op("pe", f_mm(ps2.t[:, 0:128], e0.t[:, :], bs0.t[:, g * 128:(g + 1) * 128], False, True),
                 reads=[e0.b, bs0.b], writes=[ps2.b])
            P.op("act", f_act(bsb.t[:, g, :], ps2.t[:, 0:128], AF.Copy), reads=[ps2.b], writes=[bsb.b])
        P.dma("sp", wsT_d.ap(), wsT.t[:, :, :], reads=[wsT.b])
        P.dma("sp", bsb_d.ap(), bsb.t[:, :, :], reads=[bsb.b])
        P.emit()


def stage_sg_main(nc, cfg, xin, g_d, wu_d, wv_d, bu_d, bv_d, lng_d, wsT_d, bsb_d, mT):
    DC, S, GG, E = cfg.DC, cfg.S, cfg.GG, cfg.E
    T = 512
    VW = 256
    with ExitStack() as es:
        cx = Ctx(nc, es)
        P = Prog(nc)
        K = load_consts(P, cx, cfg, None)
        g = cx.sb("g", [128, DC], F32)
        P.dma("sp", g.t[:, :], g_d.ap(), writes=[g.b])
        bu = cx.sb("bu", [128, GG], F32)
        P.dma("sp", bu.t[:, :], bu_d.ap(), writes=[bu.b])
        lng = cx.sb("lng", [128, GG], F32)
        P.dma("sp", lng.t[:, :], lng_d.ap(), writes=[lng.b])
        e0 = cx.sb("e0", [128, 128], BF16)
        P.op("dve", f_memset(e0.t[:, :], 0.0), writes=[e0.b])
        P.op("dve", f_memset(e0.t[0:1, :], 1.0), writes=[e0.b])
        bv0 = cx.sb("bv0", [128, E], BF16)
        P.op("dve", f_memset(bv0.t[:, :], 0.0), writes=[bv0.b])
        P.dma("pool", bv0.t[0:1, :], bv_d.ap(), writes=[bv0.b])
        wsT = cx.sb("wsT", [128, GG, 128], BF16)
        P.dma("sp", wsT.t[:, :, :], wsT_d.ap(), writes=[wsT.b])
        bsb = cx.sb("bsb", [128, GG, 128], F32)
        P.dma("sp", bsb.t[:, :, :], bsb_d.ap(), writes=[bsb.b])
        h = cx.sb("h", [128, DC, T], BF16)
        uT = cx.sb("uT", [128, GG, T], BF16)
        xring = ring(cx, "xc", 3, [128, T], F32)
        sqring = ring(cx, "sq", 2, [128, T], BF16)
        rstd = cx.sb("rstd", [128, T], F32)
        wring = ring(cx, "w", 2, [128, DC, 128], BF16)
        wvring = ring(cx, "wv", 2, [128, DC, VW], BF16)
        vbufs = [cx.sb("vbuf%d" % i, [128, E], BF16) for i in range(T // 128)]
        vn = cx.sb("vn", [128, E], BF16)
        nst = max(1, E // 512)
        stats = cx.sb("stats", [128, nst, 6], F32)
        mv = cx.sb("mv", [128, 2], F32)
        fbr = ring(cx, "fb", 3, [128, 128], F32)
        mbr = ring(cx, "mb", 1, [128, GG, 128], BF16)
        psn = cx.ps("psn")
        psr = psring(cx, "ps", 5)
        psa = psring(cx, "pa", 2)
        mv_ = dview(mT)
        for tt in range(S // T):
            tok0 = tt * T
            emit_norm(P, cfg, K, xin, tok0, g, h, 0, xring, sqring, K["ones"], psn, rstd, T=T)
            for j in range(GG):
                w = wring.next()
                P.dma("pool", w.t[:, :, :], wu_d.ap()[j], writes=[w.b])
                ps = psr.next()
                for k in range(DC):
                    P.op("pe", f_mm(ps.t[:, 0:T], w.t[:, k, :], h.t[:, k, :], k == 0, k == DC - 1), reads=[w.b, h.b], writes=[ps.b])
                P.op("act", f_act(uT.t[:, j, :], ps.t[:, 0:T], AF.Gelu, bias=bu.t[:, j:j + 1]), reads=[ps.b, bu.b], writes=[uT.b])
            for jb in range(E // VW):
                wv = wvring.next()
                P.dma("pool", wv.t[:, :, :], wv_d.ap()[jb], writes=[wv.b])
                for tb in range(T // 128):
                    ps = psr.next()
                    for k in range(DC):
                        P.op("pe", f_mm(ps.t[:, 0:VW], h.t[:, k, tb * 128:(tb + 1) * 128], wv.t[:, k, :], k == 0, False),
                             reads=[wv.b, h.b], writes=[ps.b])
                    P.op("pe", f_mm(ps.t[:, 0:VW], e0.t[:, :], bv0.t[:, jb * VW:(jb + 1) * VW], False, True),
                         reads=[e0.b, bv0.b], writes=[ps.b])
                    P.op("act", f_act(vbufs[tb].t[:, jb * VW:(jb + 1) * VW], ps.t[:, 0:VW], AF.Gelu), reads=[ps.b], writes=[vbufs[tb].b])
            for tb in range(T // 128):
                vb = vbufs[tb]
                for i in range(nst):
                    w_ = min(512, E)
                    P.op("dve", lambda e, o_=stats.t[:, i, :], i_=vb.t[:, i * w_:(i + 1) * w_]: e.bn_stats(out=o_, in_=i_),
                         reads=[vb.b], writes=[stats.b])
                P.op("dve", lambda e, o_=mv.t[:, :], i_=stats.t[:, :, :]: e.bn_aggr(out=o_, in_=i_), reads=[stats.b], writes=[mv.b])
                P.op("act", f_act(mv.t[:, 1:2], mv.t[:, 1:2], AF.Sqrt, bias=K["eps"].t[:, 0:1]), reads=[mv.b, K["eps"].b], writes=[mv.b])
                P.op("dve", f_recip(mv.t[:, 1:2], mv.t[:, 1:2]), reads=[mv.b], writes=[mv.b])
                P.op("dve", f_ts(vn.t[:, :], vb.t[:, :], mv.t[:, 0:1], mv.t[:, 1:2], ALU.subtract, ALU.mult),
                     reads=[vb.b, mv.b], writes=[vn.b])
                mb = mbr.next()
                for gb in range(0, GG, 4):
                    pa = psa.next()
                    gis = list(range(gb, min(GG, gb + 4)))
                    for gi in gis:
                        c0 = (gi % 4) * 128
                        P.op("pe", f_mm(pa.t[:, c0:c0 + 128], vn.t[:, gi * 128:(gi + 1) * 128], wsT.t[:, gi, :], True, True),
                             reads=[vn.b, wsT.b], writes=[pa.b])
                    for gi in gis:
                        c0 = (gi % 4) * 128
                        fb = fbr.next()
                        P.op("dve", f_stt(fb.t[:, :], pa.t[:, c0:c0 + 128], lng.t[:, gi:gi + 1], bsb.t[:, gi, :], ALU.mult, ALU.add),
                             reads=[pa.b, lng.b, bsb.b], writes=[fb.b])
                        P.op("dve", f_tt(mb.t[:, gi, :], fb.t[:, :], uT.t[:, gi, tb * 128:(tb + 1) * 128], ALU.mult),
                             reads=[fb.b, uT.b], writes=[mb.b])
                for g0 in range(0, GG, 8):
                    g1 = min(GG, g0 + 8)
                    P.dma("sp", mv_[:, g0:g1, tok0 + tb * 128:tok0 + (tb + 1) * 128], mb.t[:, g0:g1, :], reads=[mb.b])
        P.emit()


def stage_final_norm(nc, cfg, xin, g_d, out):
    DC, S, T = cfg.DC, cfg.S, cfg.T
    CG = 8
    NG = (DC + CG - 1) // CG
    with ExitStack() as es:
        cx = Ctx(nc, es)
        P = Prog(nc)
        K = load_consts(P, cx, cfg, None)
        g = cx.sb("g", [128, DC], F32)
        P.dma("sp", g.t[:, :], g_d.ap(), writes=[g.b])
        xts = [cx.sb("x%d" % i, [128, DC, T], F32, nsub=NG) for i in range(2)]
        sqring = ring(cx, "sq", 3, [128, T], BF16)
        rsr = ring(cx, "rstd", 2, [128, T], F32)
        pnr = psring(cx, "psn", 2)
        xv = dview(xin)
        ov = dview(out)
        for tt in range(S // T):
            x = xts[tt % 2]
            tok0 = tt * T
            for gi in range(NG):
                c0, c1 = gi * CG, min(DC, (gi + 1) * CG)
                P.dma("sp", x.t[:, c0:c1, :], xv[:, c0:c1, tok0:tok0 + T], writes=[x.sub[gi]])
            psn, rstd = pnr.next(), rsr.next()
            for c in range(DC):
                sq = sqring.next()
                P.op("act", f_act(sq.t[:, :], x.t[:, c, :], AF.Square), reads=[x.sub[c // CG]], writes=[sq.b])
                P.op("pe", f_mm(psn.t[:, :], K["ones"].t[:, :], sq.t[:, :], c == 0, c == DC - 1),
                     reads=[K["ones"].b, sq.b], writes=[psn.b])
            P.op("act", f_act(rstd.t[:, :], psn.t[:, :], AF.Sqrt, scale=1.0 / cfg.D, bias=K["eps"].t[:, 0:1]),
                 reads=[psn.b, K["eps"].b], writes=[rstd.b])
            P.op("dve", f_recip(rstd.t[:, :], rstd.t[:, :]), reads=[rstd.b], writes=[rstd.b])
            for c in range(DC):
                P.op("dve", f_stt(x.t[:, c, :], x.t[:, c, :], g.t[:, c:c + 1], rstd.t[:, :], ALU.mult, ALU.mult),
                     reads=[x.sub[c // CG], g.b, rstd.b], writes=[x.sub[c // CG]])
            for gi in range(NG):
                c0, c1 = gi * CG, min(DC, (gi + 1) * CG)
                P.dma("sp", ov[:, c0:c1, tok0:tok0 + T], x.t[:, c0:c1, :], reads=[x.sub[gi]])
        P.emit()


def build_program(cfg):
    nc = bass.Bass("TRN2", target_bir_lowering=False)
    D, S, DC, FC, NH, NA, GG, E, DFF = cfg.D, cfg.S, cfg.DC, cfg.FC, cfg.NH, cfg.NA, cfg.GG, cfg.E, cfg.DFF
    C = make_consts(nc, cfg)

    def di(name, shape, dt=F32):
        return nc.dram_tensor(name, list(shape), dt, kind="ExternalInput")

    def ds(name, shape, dt):
        return nc.dram_tensor(name, list(shape), dt)

    xT = di("xT", [D, S])
    pos = di("pos", [32, S], I32)
    g_attn = di("g_attn", [128, DC])
    wqk = di("wqk", [2 * NH, 128, DC, 128])
    wv = di("wv", [NH * 128 // cfg.VW, 128, DC, cfg.VW])
    wo = di("wo", [DC, 128, DC, 128])
    ffn = []
    for l in range(2):
        ffn.append(dict(g=di("g_ffn%d" % l, [128, DC]), wup=di("wup%d" % l, [2 * FC, 128, DC, 128]),
                        cw=di("cw%d" % l, [128, 3, 2 * FC]), cb=di("cb%d" % l, [128, 2 * FC]),
                        wdn=di("wdn%d" % l, [DC, 128, FC, 128])))
    g_sg = di("g_sg", [128, DC])
    sg_wu = di("sg_wu", [GG, 128, DC, 128])
    sg_wv = di("sg_wv", [E // 256, 128, DC, 256])
    sg_bu = di("sg_bu", [128, GG])
    sg_bv = di("sg_bv", [1, E])
    sg_lng = di("sg_lng", [128, GG])
    sg_lnb = di("sg_lnb", [128, E])
    sg_ws = di("sg_ws", [GG, 128, 128])
    sg_bs = di("sg_bs", [1, GG * 128])
    sg_wo = di("sg_wo", [DC, 128, GG, 128])
    g_fin = di("g_fin", [128, DC])
    outT = nc.dram_tensor("outT", [D, S], F32, kind="ExternalOutput")

    qT = ds("qT", [NH * 128, S], BF16)
    kT = ds("kT", [NH * 128, S], BF16)
    v = ds("v", [S, NH * 128], BF16)
    ksum = ds("ksum", [128, NA, cfg.NBLK], F32)
    cs = ds("cs", [32, 2, S], F32)
    oT = ds("oT", [D, S], BF16)
    gT = ds("gT", [DFF, S], BF16)
    xa = ds("xa", [D, S], F32)
    xb = ds("xb", [D, S], F32)
    wdnb = [ds("wdnb%d" % l, [DC, 128, FC, 128], BF16) for l in range(2)]
    wsT = ds("wsT", [128, GG, 128], BF16)
    bsb = ds("bsb", [128, GG, 128], F32)

    import os
    sel = os.environ.get("MK_STAGES")
    sel = set(int(t) for t in sel.split(",")) if sel else set(range(1, 13))
    stages = [
        lambda: (stage_rope(nc, cfg, C, pos, cs), stage_qkv(nc, cfg, C, xT, g_attn, wqk, wv, cs, qT, kT, v, ksum)),
        lambda: stage_moba(nc, cfg, C, qT, kT, v, ksum, oT),
        lambda: stage_dilated(nc, cfg, C, qT, kT, v, oT),
        lambda: stage_linear_res(nc, cfg, DC, oT, wo, xT, xa),
        lambda: stage_ffn_up(nc, cfg, xa, ffn[0]["g"], ffn[0]["wup"], ffn[0]["cw"], ffn[0]["cb"], gT, precast=(ffn[0]["wdn"], wdnb[0])),
        lambda: stage_linear_res(nc, cfg, FC, gT, wdnb[0], xa, xb),
        lambda: stage_sg_setup(nc, cfg, C, sg_ws, sg_bs, sg_lnb, wsT, bsb),
        lambda: stage_sg_main(nc, cfg, xb, g_sg, sg_wu, sg_wv, sg_bu, sg_bv, sg_lng, wsT, bsb, oT),
        lambda: stage_linear_res(nc, cfg, GG, oT, sg_wo, xb, xa),
        lambda: stage_ffn_up(nc, cfg, xa, ffn[1]["g"], ffn[1]["wup"], ffn[1]["cw"], ffn[1]["cb"], gT, precast=(ffn[1]["wdn"], wdnb[1])),
        lambda: stage_linear_res(nc, cfg, FC, gT, wdnb[1], xa, xb),
        lambda: stage_final_norm(nc, cfg, xb, g_fin, outT),
    ]
    for i, st in enumerate(stages):
        if i + 1 in sel:
            st()
    return nc


def host_prep(cfg, p):
    NH, FC, GG, E = cfg.NH, cfg.FC, cfg.GG, cfg.E
    f = lambda a: np.asarray(a, dtype=np.float32)
    w_in = f(p["attn_w_in"])[0]
    m = {}
    m["g_attn"] = vec_pc(f(p["attn_norm"])[0])
    m["wqk"] = tile_w(w_in[:, :2 * NH * 128])
    m["wv"] = tile_w(w_in[:, 2 * NH * 128:], cfg.VW)
    m["wo"] = tile_w(f(p["attn_w_out"])[0])
    for l in range(2):
        m["g_ffn%d" % l] = vec_pc(f(p["ffn_norm"])[l])
        m["wup%d" % l] = tile_w(f(p["ffn_w_up"])[l])
        m["cw%d" % l] = np.ascontiguousarray(f(p["ffn_conv_w"])[l].reshape(3, 2 * FC, 128).transpose(2, 0, 1))
        m["cb%d" % l] = vec_pc(f(p["ffn_conv_b"])[l])
        m["wdn%d" % l] = tile_w(f(p["ffn_w_down"])[l])
    sw = f(p["sg_w_in"])[0]
    sb = f(p["sg_b_in"])[0]
    m["g_sg"] = vec_pc(f(p["sg_norm"])[0])
    m["sg_wu"] = tile_w(sw[:, :E])
    m["sg_wv"] = tile_w(sw[:, E:], 256)
    m["sg_bu"] = vec_pc(sb[:E])
    m["sg_bv"] = np.ascontiguousarray(sb[E:].reshape(1, E))
    m["sg_lng"] = vec_pc(f(p["sg_v_gain"])[0])
    m["sg_lnb"] = np.ascontiguousarray(np.broadcast_to(f(p["sg_v_bias"])[0], (128, E)))
    m["sg_ws"] = np.ascontiguousarray(f(p["sg_w_s"])[0])
    m["sg_bs"] = np.ascontiguousarray(f(p["sg_b_s"])[0].reshape(1, GG * 128))
    m["sg_wo"] = tile_w(f(p["sg_w_out"])[0])
    m["g_fin"] = vec_pc(f(p["final_norm"]))
    return m


def run_module(cfg, inputs):
    x = np.asarray(inputs["x"], dtype=np.float32)
    positions = np.asarray(inputs["positions"]).astype(np.int32)
    B = x.shape[0]
    import time as _t
    t0 = _t.time()
    shared = host_prep(cfg, inputs)
    t1 = _t.time()
    nc = build_program(cfg)
    print("[mk] host_prep %.1fs build %.1fs" % (t1 - t0, _t.time() - t1), flush=True)
    in_maps = []
    for b in range(B):
        m = dict(shared)
        m["xT"] = np.ascontiguousarray(x[b].T)
        m["pos"] = np.ascontiguousarray(np.broadcast_to(positions[b], (32, cfg.S)))
        in_maps.append(m)
    t2 = _t.time()
    res = run_bass_kernel_spmd(nc, in_maps, core_ids=list(range(B)))
    print("[mk] launch %.1fs" % (_t.time() - t2), flush=True)
    return np.stack([np.ascontiguousarray(res.results[b]["outT"].T) for b in range(B)], 0)


def kernel(**inputs):
    cfg = Cfg()
    return run_module(cfg, inputs)
```

```python
import numpy as np
from contextlib import ExitStack
import concourse.bass as bass
import concourse.mybir as mybir
from concourse.bass_utils import run_bass_kernel_spmd

F32 = mybir.dt.float32
BF16 = mybir.dt.bfloat16
I32 = mybir.dt.int32
AF = mybir.ActivationFunctionType
ALU = mybir.AluOpType
AX = mybir.AxisListType

NEG = -30000.0
ENGS = ("pe", "act", "dve", "pool", "sp")
NRING = 12


class Cfg:
    def __init__(self, D=4096, S=4096, NA=24, NBG=8, DFF=14336, B=2):
        self.D, self.S, self.NA, self.NBG, self.DFF, self.B = D, S, NA, NBG, DFF, B
        self.DH = 128
        self.NB = 3 * NBG
        self.NH = NA + self.NB
        self.NHO = NA + NBG
        assert self.NHO * 128 == D
        self.QKV = 3 * self.NH * 128
        self.DC = D // 128
        self.FC = DFF // 128
        self.T = 512
        self.NT = S // self.T
        self.MB = 256
        self.NBLK = S // self.MB
        self.E = D
        self.GG = self.E // 128
        self.EPS = 1e-5
        self.PATS = ((128, 1), (512, 4), (2048, 16))
        self.VW = 512 if (self.NH * 128) % 512 == 0 else 256
        self.SCALE = 128 ** -0.5


class Buf:
    __slots__ = ("w", "r")

    def __init__(self):
        self.w = None
        self.r = []


class Op:
    __slots__ = ("eng", "fn", "deps", "signal", "ev", "dma", "di")

    def __init__(self, eng, fn, dma):
        self.eng, self.fn, self.dma = eng, fn, dma
        self.deps = []
        self.signal = False
        self.ev = None
        self.di = -1


class Prog:
    def __init__(self, nc):
        self.nc = nc
        self.ops = []

    def op(self, eng, fn, reads=(), writes=(), dma=False):
        o = Op(eng, fn, dma)
        deps = {}
        for b in reads:
            if b.w is not None:
                deps[id(b.w)] = b.w
        for b in writes:
            if b.w is not None:
                deps[id(b.w)] = b.w
            for r in b.r:
                deps[id(r)] = r
        for d in deps.values():
            if d is o:
                continue
            if d.eng == "pe" and eng == "pe" and not d.dma and not dma:
                continue
            o.deps.append(d)
            d.signal = True
        for b in writes:
            b.w = o
            b.r = []
        for b in reads:
            if b.w is not o:
                if not dma:
                    b.r = [r for r in b.r if r.dma or r.eng != eng]
                b.r.append(o)
        self.ops.append(o)
        return o

    def dma(self, eng, out, in_, reads=(), writes=()):
        return self.op(eng, lambda e: e.dma_start(out=out, in_=in_), reads, writes, dma=True)

    def emit(self, name=None):
        nc = self.nc
        with ExitStack() as es:
            if not hasattr(nc, "_mk_sems"):
                gs = ExitStack()
                c_ = {e: gs.enter_context(nc.semaphore("c_" + e)) for e in ENGS}
                d_ = {e: [gs.enter_context(nc.semaphore("d_%s%d" % (e, i))) for i in range(NRING)]
                      for e in ("sp", "pool")}
                nc._mk_sems = (gs, c_, d_)
                nc._mk_pool_n = 0
            _, csem, dsem = nc._mk_sems
            cnt = {e: 0 for e in ENGS}
            dcnt = {e: 0 for e in dsem}
            dcnt["pool"] = nc._mk_pool_n
            pool_base = nc._mk_pool_n
            by_eng = {e: [] for e in ENGS}
            dlist = {e: [] for e in dsem}
            for o in self.ops:
                if o.dma:
                    i = dcnt[o.eng]
                    dcnt[o.eng] += 1
                    o.di = i
                    o.ev = (dsem[o.eng][i % NRING], 16 * (i // NRING + 1))
                    dlist[o.eng].append(o)
                    if o.eng == "pool":
                        nc._mk_pool_n = i + 1
                elif o.signal:
                    cnt[o.eng] += 1
                    o.ev = (csem[o.eng], cnt[o.eng])
                by_eng[o.eng].append(o)
            block = es.enter_context(nc.Block())

            def body(e, eng):
                waited = {}

                def wait(ev):
                    sem, val = ev
                    k = id(sem)
                    if waited.get(k, 0) < val:
                        e.wait_ge(sem, val)
                        waited[k] = val

                for o in by_eng[eng]:
                    need = {}
                    for d in o.deps:
                        sm, val = d.ev
                        k_ = id(sm)
                        if k_ not in need or need[k_][1] < val:
                            need[k_] = (sm, val)
                    for ev in need.values():
                        wait(ev)
                    if o.dma:
                        li = o.di - (pool_base if eng == "pool" else 0)
                        if li >= NRING:
                            wait(dlist[eng][li - NRING].ev)
                    ins = o.fn(e)
                    if o.dma:
                        ins.then_inc(o.ev[0], 16)
                    elif o.signal:
                        ins.then_inc(o.ev[0], 1)
                if eng in dlist:
                    for o in dlist[eng][-NRING:]:
                        wait(o.ev)

            block.tensor(lambda e: body(e, "pe"))
            block.scalar(lambda e: body(e, "act"))
            block.vector(lambda e: body(e, "dve"))
            block.gpsimd(lambda e: body(e, "pool"))
            block.sync(lambda e: body(e, "sp"))
        _, csem, dsem = nc._mk_sems
        allsems = list(csem.values()) + list(dsem["sp"])
        with nc.Block() as blk2:
            def clr(e):
                for sm in allsems:
                    e.sem_clear(sm)
            blk2.sync(clr)
        self.ops = []


class Tile:
    def __init__(self, t, nsub=0):
        self.t = t
        self.b = Buf()
        self.sub = [Buf() for _ in range(nsub)]


class Ctx:
    _n = 0

    def __init__(self, nc, es):
        self.nc, self.es = nc, es
        Ctx._n += 1
        self.pre = "s%d_" % Ctx._n

    def sb(self, name, shape, dt, nsub=0):
        return Tile(self.es.enter_context(self.nc.sbuf_tensor(self.pre + name, list(shape), dt)), nsub)

    def ps(self, name, shape=(128, 512), dt=F32, nsub=0):
        return Tile(self.es.enter_context(self.nc.psum_tensor(self.pre + name, list(shape), dt)), nsub)


class Ring:
    def __init__(self, tiles):
        self.tiles = tiles
        self.i = 0

    def next(self):
        t = self.tiles[self.i % len(self.tiles)]
        self.i += 1
        return t


def ring(cx, name, n, shape, dt):
    return Ring([cx.sb("%s%d" % (name, i), shape, dt) for i in range(n)])


def psring(cx, name, n):
    return Ring([cx.ps("%s%d" % (name, i)) for i in range(n)])


def f_mm(out, lhsT, rhs, start=True, stop=True):
    return lambda e: e.matmul(out, lhsT, rhs, start=start, stop=stop)


def f_tr(out, in_, ident):
    return lambda e: e.transpose(out, in_, ident)


def f_act(out, in_, func, **kw):
    return lambda e: e.activation(out=out, in_=in_, func=func, **kw)


def f_ts(out, in0, s1, s2, op0, op1=None):
    if op1 is None:
        return lambda e: e.tensor_scalar(out=out, in0=in0, scalar1=s1, scalar2=None, op0=op0)
    return lambda e: e.tensor_scalar(out=out, in0=in0, scalar1=s1, scalar2=s2, op0=op0, op1=op1)


def f_stt(out, in0, scalar, in1, op0, op1):
    return lambda e: e.scalar_tensor_tensor(out=out, in0=in0, scalar=scalar, in1=in1, op0=op0, op1=op1)


def f_tt(out, in0, in1, op):
    return lambda e: e.tensor_tensor(out=out, in0=in0, in1=in1, op=op)


def f_copy(out, in_):
    return lambda e: e.tensor_copy(out=out, in_=in_)


def f_recip(out, in_):
    return lambda e: e.reciprocal(out=out, in_=in_)


def f_memset(ap, v):
    return lambda e: e.memset(ap, v)


def dview(h):
    return h.ap().rearrange("(c p) t -> p c t", p=128)


def emit_norm(P, cfg, K, xin, tok0, g, h, hcol0, xring, sqring, ones, psn, rstd, T=None):
    T, DC = (T or cfg.T), cfg.DC
    xv = dview(xin)
    for c in range(DC):
        xc = xring.next()
        P.dma("sp", xc.t[:, 0:T], xv[:, c, tok0:tok0 + T], writes=[xc.b])
        sq = sqring.next()
        P.op("act", f_act(sq.t[:, 0:T], xc.t[:, 0:T], AF.Square), reads=[xc.b], writes=[sq.b])
        P.op("pe", f_mm(psn.t[:, 0:T], ones.t[:, :], sq.t[:, 0:T], c == 0, c == DC - 1),
             reads=[ones.b, sq.b], writes=[psn.b])
    P.op("act", f_act(rstd.t[:, 0:T], psn.t[:, 0:T], AF.Sqrt, scale=1.0 / cfg.D, bias=K["eps"].t[:, 0:1]),
         reads=[psn.b, K["eps"].b], writes=[rstd.b])
    P.op("dve", f_recip(rstd.t[:, 0:T], rstd.t[:, 0:T]), reads=[rstd.b], writes=[rstd.b])
    for c in range(DC):
        xc = xring.next()
        P.dma("sp", xc.t[:, 0:T], xv[:, c, tok0:tok0 + T], writes=[xc.b])
        P.op("dve", f_stt(h.t[:, c, hcol0:hcol0 + T], xc.t[:, 0:T], g.t[:, c:c + 1], rstd.t[:, 0:T],
                          ALU.mult, ALU.mult),
             reads=[xc.b, g.b, rstd.b], writes=[h.b])


def load_consts(P, cx, cfg, consts):
    K = {}
    K["ones"] = cx.sb("k_ones", [128, 128], BF16)
    P.op("dve", f_memset(K["ones"].t[:, :], 1.0), writes=[K["ones"].b])
    K["eps"] = cx.sb("k_eps", [128, 1], F32)
    P.op("dve", f_memset(K["eps"].t[:, :], cfg.EPS), writes=[K["eps"].b])
    return K


def stage_ffn_up(nc, cfg, xin, g_d, wup_d, cw_d, cb_d, gT, precast=None):
    T, DC, FC, S = cfg.T, cfg.DC, cfg.FC, cfg.S
    NSUB = 2 if S % (2 * T) == 0 else 1
    TS = T * NSUB
    with ExitStack() as es:
        cx = Ctx(nc, es)
        P = Prog(nc)
        K = load_consts(P, cx, cfg, None)
        g = cx.sb("g", [128, DC], F32)
        P.dma("sp", g.t[:, :], g_d.ap(), writes=[g.b])
        cw = cx.sb("cw", [128, 3, 2 * FC], F32)
        P.dma("sp", cw.t[:, :, :], cw_d.ap(), writes=[cw.b])
        cb = cx.sb("cb", [128, 2 * FC], F32)
        P.dma("sp", cb.t[:, :], cb_d.ap(), writes=[cb.b])
        carry = [cx.sb("carry%d" % i, [128, FC, 2], F32, nsub=FC) for i in range(2)]
        for cr in carry:
            P.op("dve", f_memset(cr.t[:, :, :], 0.0), writes=cr.sub)
        hs = [cx.sb("h%d" % i, [128, DC, TS], BF16) for i in range(2)]
        xring = ring(cx, "xc", 4, [128, T], F32)
        sqring = ring(cx, "sq", 3, [128, T], BF16)
        rstd = cx.sb("rstd", [128, T], F32)
        wring = ring(cx, "w", 4, [128, DC, 128], BF16)
        abuf = ring(cx, "ab", 4, [128, T + 4], F32)
        cbuf = ring(cx, "cbf", 6, [128, T], F32)
        sgb = ring(cx, "sg", 2, [128, T], F32)
        gob = ring(cx, "go", 3, [128, T], BF16)
        psn = cx.ps("psn")
        psr = psring(cx, "ps", 7)
        NST = S // TS

        def do_norm(st_):
            for sub_ in range(NSUB):
                emit_norm(P, cfg, K, xin, st_ * TS + sub_ * T, g, hs[st_ % 2], sub_ * T, xring, sqring, K["ones"], psn, rstd)

        PCQ = 4
        n_pc = DC * PCQ
        pc_step = max(1, (NST * FC) // n_pc)
        pc_done = [0]
        PCW = FC * 128 // PCQ

        def do_precast(i):
            src, dst = precast
            c, q_ = i // PCQ, i % PCQ
            P.dma("pool", dst.ap()[c].rearrange("p k c -> p (k c)")[:, q_ * PCW:(q_ + 1) * PCW],
                  src.ap()[c].rearrange("p k c -> p (k c)")[:, q_ * PCW:(q_ + 1) * PCW])

        do_norm(0)
        pend = [None]
        for st in range(NST):
            h = hs[st % 2]
            for j in range(FC):
                if j == min(4, FC - 1) and st + 1 < NST:
                    do_norm(st + 1)
                ws = []
                for half in range(2):
                    w = wring.next()
                    P.dma("pool", w.t[:, :, :], wup_d.ap()[half * FC + j], writes=[w.b])
                    ws.append(w)
                it = st * FC + j
                if precast is not None and it % pc_step == 0 and pc_done[0] < n_pc:
                    do_precast(pc_done[0])
                    pc_done[0] += 1
                for sub in range(NSUB):
                    tok0 = st * TS + sub * T
                    cs = []
                    for half in range(2):
                        ps = psr.next()
                        for k in range(DC):
                            P.op("pe", f_mm(ps.t[:, :], ws[half].t[:, k, :], h.t[:, k, sub * T:(sub + 1) * T],
                                            k == 0, k == DC - 1),
                                 reads=[ws[half].b, h.b], writes=[ps.b])
                        ch = half * FC + j
                        ab = abuf.next()
                        cr = carry[half]
                        P.op("dve", f_copy(ab.t[:, 0:2], cr.t[:, j, :]), reads=[cr.sub[j]], writes=[ab.b])
                        P.op("act", f_act(ab.t[:, 2:T + 2], ps.t[:, :], AF.Copy), reads=[ps.b], writes=[ab.b])
                        P.op("dve", f_copy(cr.t[:, j, :], ab.t[:, T:T + 2]), reads=[ab.b], writes=[cr.sub[j]])
                        c_ = cbuf.next()
                        P.op("dve", f_ts(c_.t[:, :], ab.t[:, 2:T + 2], cw.t[:, 2, ch:ch + 1], cb.t[:, ch:ch + 1],
                                         ALU.mult, ALU.add), reads=[ab.b, cw.b, cb.b], writes=[c_.b])
                        P.op("dve", f_stt(c_.t[:, :], ab.t[:, 1:T + 1], cw.t[:, 1, ch:ch + 1], c_.t[:, :],
                                          ALU.mult, ALU.add), reads=[ab.b, c_.b, cw.b], writes=[c_.b])
                        P.op("dve", f_stt(c_.t[:, :], ab.t[:, 0:T], cw.t[:, 0, ch:ch + 1], c_.t[:, :],
                                          ALU.mult, ALU.add), reads=[ab.b, c_.b, cw.b], writes=[c_.b])
                        cs.append(c_)
                    def fin(cs=cs, j=j, tok0=tok0):
                        sg = sgb.next()
                        P.op("act", f_act(sg.t[:, :], cs[0].t[:, :], AF.Silu), reads=[cs[0].b], writes=[sg.b])
                        go = gob.next()
                        P.op("dve", f_tt(go.t[:, :], sg.t[:, :], cs[1].t[:, :], ALU.mult),
                             reads=[sg.b, cs[1].b], writes=[go.b])
                        P.dma("sp", gT.ap()[j * 128:(j + 1) * 128, tok0:tok0 + T], go.t[:, :], reads=[go.b])

                    if pend[0] is not None:
                        pend[0]()
                    pend[0] = fin
        if pend[0] is not None:
            pend[0]()
        while precast is not None and pc_done[0] < n_pc:
            do_precast(pc_done[0])
            pc_done[0] += 1
        P.emit()


def stage_linear_res(nc, cfg, KC, actT, w_d, xin, xout):
    T, DC, S = cfg.T, cfg.DC, cfg.S
    NSUB = 2 if (KC <= 32 and S % (2 * T) == 0) else 1
    TS = T * NSUB
    with ExitStack() as es:
        cx = Ctx(nc, es)
        P = Prog(nc)
        KG = 16 // NSUB
        NG = (KC + KG - 1) // KG
        abufs = [cx.sb("a%d" % i, [128, KC, TS], BF16, nsub=NG) for i in range(2 if KC <= 32 else 1)]
        wring = ring(cx, "w", 2, [128, KC, 128], BF16)
        xr = ring(cx, "xr", 3, [128, T], F32)
        xo = ring(cx, "xo", 3, [128, T], F32)
        psr = psring(cx, "ps", 4)
        av = dview(actT)
        xv = dview(xin)
        ov = dview(xout)
        for tt in range(S // TS):
            t0 = tt * TS
            a = abufs[tt % len(abufs)]
            for gk in range(NG):
                k0, k1 = gk * KG, min(KC, (gk + 1) * KG)
                P.dma("sp", a.t[:, k0:k1, :], av[:, k0:k1, t0:t0 + TS], writes=[a.sub[gk]])
            for c in range(DC):
                w = wring.next()
                P.dma("pool", w.t[:, :, :], w_d.ap()[c], writes=[w.b])
                for sub in range(NSUB):
                    tok0 = t0 + sub * T
                    ps = psr.next()
                    for k in range(KC):
                        P.op("pe", f_mm(ps.t[:, :], w.t[:, k, :], a.t[:, k, sub * T:(sub + 1) * T], k == 0, k == KC - 1),
                             reads=[w.b, a.sub[k // KG]], writes=[ps.b])
                    x_ = xr.next()
                    P.dma("sp", x_.t[:, :], xv[:, c, tok0:tok0 + T], writes=[x_.b])
                    o_ = xo.next()
                    P.op("dve", f_tt(o_.t[:, :], ps.t[:, :], x_.t[:, :], ALU.add), reads=[ps.b, x_.b], writes=[o_.b])
                    P.dma("sp", ov[:, c, tok0:tok0 + T], o_.t[:, :], reads=[o_.b])
        P.emit()


def tile_w(W, cw=128):
    K, N = W.shape
    return np.ascontiguousarray(W.reshape(K // 128, 128, N // cw, cw).transpose(2, 1, 0, 3))


def vec_pc(v):
    return np.ascontiguousarray(v.reshape(-1, 128).T)


def np_bf16(a):
    import ml_dtypes
    return np.asarray(a, dtype=np.float32).astype(ml_dtypes.bfloat16)


def make_consts(nc, cfg):
    C = {}
    half = 16
    invf = np.power(np.float32(500000.0), -np.arange(half, dtype=np.float32) * np.float32(2.0 / 32.0)).astype(np.float32)
    C["invf"] = nc.inline_tensor(np.concatenate([invf, invf]).reshape(32, 1).astype(np.float32), "k_invf")
    Pm = np.zeros((128, 32), np.float32)
    for m in range(16):
        Pm[m + 16, m] = -1.0
        Pm[m, m + 16] = 1.0
    C["Pm"] = nc.inline_tensor(Pm, "k_Pm")
    C["identf"] = nc.inline_tensor(np.eye(128, dtype=np.float32), "k_identf")
    C["identb"] = nc.inline_tensor(np_bf16(np.eye(128)), "k_identb")
    NBLK = cfg.NBLK
    E = np.zeros((128, NBLK * 128), np.float32)
    for n in range(NBLK):
        E[n % 16, n * 128:(n + 1) * 128] = 1.0
    C["esel"] = nc.inline_tensor(np_bf16(E), "k_esel")
    j = np.arange(128)[:, None, None]
    r = np.arange(4)[None, :, None]
    i = np.arange(512)[None, None, :]
    C["cmask"] = nc.inline_tensor(np_bf16(np.where(r * 128 + j <= i, 0.0, NEG)), "k_cmask")
    jj = np.arange(128)[:, None]
    ii = np.arange(128)[None, :]
    cur = np.where(jj <= ii, 0.0, NEG)
    prv = np.where(jj >= ii, 0.0, NEG)
    bm = np.stack([np.concatenate([cur, np.full((128, 128), NEG)], 1), np.concatenate([cur, prv], 1)], 1)
    C["bmask"] = nc.inline_tensor(np_bf16(bm), "k_bmask")
    NKT = cfg.S // 128
    pm = np.full((NKT, 16), NEG, np.float32)
    om = np.full((NKT, 16), -1.0e9, np.float32)
    for qi in range(NKT):
        qb = (qi * 128) // cfg.MB
        pm[qi, :qb] = 0.0
        om[qi, qb] = 0.0
    C["pastm"] = nc.inline_tensor(np.ascontiguousarray(np.broadcast_to(pm.reshape(1, NKT * 16), (128, NKT * 16))), "k_pastm")
    C["ownm"] = nc.inline_tensor(np.ascontiguousarray(np.broadcast_to(om.reshape(1, NKT * 16), (128, NKT * 16))), "k_ownm")
    C["tril"] = nc.inline_tensor(np.tril(np.ones((128, 128), np.float32)), "k_tril")
    return C


def stage_rope(nc, cfg, C, pos_d, cs_d):
    S = cfg.S
    W = min(S, 2048)
    with ExitStack() as es:
        cx = Ctx(nc, es)
        P = Prog(nc)
        invf = cx.sb("invf", [32, 1], F32)
        P.dma("sp", invf.t[:, :], C["invf"].ap(), writes=[invf.b])
        for c0 in range(0, S, W):
            S_ = W
            posi = cx.sb("posi%d" % c0, [32, S_], I32)
            P.dma("sp", posi.t[:, :], pos_d.ap()[:, c0:c0 + W], writes=[posi.b])
            ang = cx.sb("ang%d" % c0, [32, S_], F32)
            P.op("dve", f_copy(ang.t[:, :], posi.t[:, :]), reads=[posi.b], writes=[ang.b])
            P.op("dve", f_ts(ang.t[:, :], ang.t[:, :], invf.t[:, 0:1], None, ALU.mult), reads=[ang.b, invf.b], writes=[ang.b])
            ki = cx.sb("ki%d" % c0, [32, S_], I32)
            kf = cx.sb("kf%d" % c0, [32, S_], F32)
            mk = cx.sb("mk%d" % c0, [32, S_], F32)
            for idx, (nm, shift) in enumerate((("cos", 0.25), ("sin", 0.0))):
                y = cx.sb("y_%s%d" % (nm, c0), [32, S_], F32)
                P.op("dve", f_ts(y.t[:, :], ang.t[:, :], float(1.0 / (2.0 * np.pi)), shift, ALU.mult, ALU.add),
                     reads=[ang.b], writes=[y.b])
                P.op("dve", f_copy(ki.t[:, :], y.t[:, :]), reads=[y.b], writes=[ki.b])
                P.op("dve", f_copy(kf.t[:, :], ki.t[:, :]), reads=[ki.b], writes=[kf.b])
                P.op("dve", f_tt(y.t[:, :], y.t[:, :], kf.t[:, :], ALU.subtract), reads=[y.b, kf.b], writes=[y.b])
                P.op("dve", f_ts(mk.t[:, :], y.t[:, :], 0.5, None, ALU.is_gt), reads=[y.b], writes=[mk.b])
                P.op("dve", f_tt(y.t[:, :], y.t[:, :], mk.t[:, :], ALU.subtract), reads=[y.b, mk.b], writes=[y.b])
                P.op("dve", f_ts(mk.t[:, :], y.t[:, :], -0.5, None, ALU.is_lt), reads=[y.b], writes=[mk.b])
                P.op("dve", f_tt(y.t[:, :], y.t[:, :], mk.t[:, :], ALU.add), reads=[y.b, mk.b], writes=[y.b])
                P.op("act", f_act(y.t[:, :], y.t[:, :], AF.Sin, scale=float(2.0 * np.pi * (1.0 - 1e-6))),
                     reads=[y.b], writes=[y.b])
                P.dma("sp", cs_d.ap()[:, idx, c0:c0 + W], y.t[:, :], reads=[y.b])
        P.emit()


def stage_qkv(nc, cfg, C, xin, g_d, wqk_d, wv_d, cs_d, qT, kT, v, ksum_d):
    T, DC, S, NH, NA = cfg.T, cfg.DC, cfg.S, cfg.NH, cfg.NA
    VW = cfg.VW
    NSUB = 2 if S % (2 * T) == 0 else 1
    TS = T * NSUB
    with ExitStack() as es:
        cx = Ctx(nc, es)
        P = Prog(nc)
        K = load_consts(P, cx, cfg, None)
        g = cx.sb("g", [128, DC], F32)
        P.dma("sp", g.t[:, :], g_d.ap(), writes=[g.b])
        Pm = cx.sb("Pm", [128, 32], F32)
        P.dma("sp", Pm.t[:, :], C["Pm"].ap(), writes=[Pm.b])
        csr = ring(cx, "cs", 4, [32, 2, T], F32)
        ksum = cx.sb("ksum", [128, NA, cfg.NBLK], F32)
        h = cx.sb("h", [128, DC, TS], BF16)
        xring = ring(cx, "xc", 4, [128, T], F32)
        sqring = ring(cx, "sq", 3, [128, T], BF16)
        rstd = cx.sb("rstd", [128, T], F32)
        wring = ring(cx, "w", 3, [128, DC, 128], BF16)
        wvring = ring(cx, "wv", 2, [128, DC, VW], BF16)
        qfr = ring(cx, "qf", 4, [128, T], F32)
        tmr = ring(cx, "tm", 4, [32, T], F32)
        qbr = ring(cx, "qb", 3, [128, T], BF16)
        vbr = ring(cx, "vb", 3, [128, VW], BF16)
        psn = cx.ps("psn")
        psr = psring(cx, "ps", 5)
        psp = psring(cx, "pp", 2)
        for st in range(S // TS):
            cst = []
            for sub in range(NSUB):
                emit_norm(P, cfg, K, xin, st * TS + sub * T, g, h, sub * T, xring, sqring, K["ones"], psn, rstd)
                cs = csr.next()
                tok0 = st * TS + sub * T
                P.dma("sp", cs.t[:, :, :], cs_d.ap()[:, :, tok0:tok0 + T], writes=[cs.b])
                cst.append(cs)
            pending = [None]
            for j in range(2 * NH):
                isk = j >= NH
                hd = j - NH if isk else j
                w = wring.next()
                P.dma("pool", w.t[:, :, :], wqk_d.ap()[j], writes=[w.b])
                for sub in range(NSUB):
                    tok0 = st * TS + sub * T
                    ps = psr.next()
                    for k in range(DC):
                        P.op("pe", f_mm(ps.t[:, :], w.t[:, k, :], h.t[:, k, sub * T:(sub + 1) * T], k == 0, k == DC - 1),
                             reads=[w.b, h.b], writes=[ps.b])
                    qf = qfr.next()
                    P.op("act", f_act(qf.t[:, :], ps.t[:, :], AF.Copy), reads=[ps.b], writes=[qf.b])

                    def rot(qf=qf, sub=sub, tok0=tok0, isk=isk, hd=hd):
                        pp = psp.next()
                        P.op("pe", f_mm(pp.t[0:32, :], Pm.t[:, :], qf.t[:, :]), reads=[Pm.b, qf.b], writes=[pp.b])
                        t1 = tmr.next()
                        t2 = tmr.next()
                        P.op("dve", f_tt(t1.t[:, :], qf.t[0:32, :], cst[sub].t[:, 0, :], ALU.mult),
                             reads=[qf.b, cst[sub].b], writes=[t1.b])
                        P.op("dve", f_tt(t2.t[:, :], pp.t[0:32, :], cst[sub].t[:, 1, :], ALU.mult),
                             reads=[pp.b, cst[sub].b], writes=[t2.b])
                        P.op("dve", f_tt(qf.t[0:32, :], t1.t[:, :], t2.t[:, :], ALU.add),
                             reads=[t1.b, t2.b], writes=[qf.b])
                        qb = qbr.next()
                        P.op("act", f_act(qb.t[:, :], qf.t[:, :], AF.Copy), reads=[qf.b], writes=[qb.b])
                        if isk and hd < NA:
                            b0 = tok0 // cfg.MB
                            nb = T // cfg.MB
                            P.op("dve", lambda e, o_=ksum.t[:, hd, b0:b0 + nb], i_=qf.t[:, :].rearrange("p (b m) -> p b m", m=cfg.MB):
                                 e.tensor_reduce(out=o_, in_=i_, axis=AX.X, op=ALU.add),
                                 reads=[qf.b], writes=[ksum.b])
                        dst = kT if isk else qT
                        P.dma("sp", dst.ap()[hd * 128:(hd + 1) * 128, tok0:tok0 + T], qb.t[:, :], reads=[qb.b])

                    if pending[0] is not None:
                        pending[0]()
                    pending[0] = rot
            if pending[0] is not None:
                pending[0]()
                pending[0] = None
            for jb in range(NH * 128 // VW):
                wv = wvring.next()
                P.dma("pool", wv.t[:, :, :], wv_d.ap()[jb], writes=[wv.b])
                for tb in range(TS // 128):
                    tok0 = st * TS + tb * 128
                    ps = psr.next()
                    for k in range(DC):
                        P.op("pe", f_mm(ps.t[:, 0:VW], h.t[:, k, tb * 128:(tb + 1) * 128], wv.t[:, k, :], k == 0, k == DC - 1),
                             reads=[wv.b, h.b], writes=[ps.b])
                    vb = vbr.next()
                    P.op("act", f_act(vb.t[:, :], ps.t[:, 0:VW], AF.Copy), reads=[ps.b], writes=[vb.b])
                    P.dma("sp", v.ap()[tok0:tok0 + 128, jb * VW:(jb + 1) * VW], vb.t[:, :], reads=[vb.b])
        P.dma("sp", ksum_d.ap(), ksum.t[:, :, :], reads=[ksum.b])
        P.emit()


def stage_moba(nc, cfg, C, qT, kT, v, ksum_d, oT):
    S, NA, NBLK, T = cfg.S, cfg.NA, cfg.NBLK, cfg.T
    NKT = S // 128
    with ExitStack() as es:
        cx = Ctx(nc, es)
        P = Prog(nc)
        K = load_consts(P, cx, cfg, None)
        ones = K["ones"]
        identf = cx.sb("identf", [128, 128], F32)
        P.dma("sp", identf.t[:, :], C["identf"].ap(), writes=[identf.b])
        identb = cx.sb("identb", [128, 128], BF16)
        P.dma("sp", identb.t[:, :], C["identb"].ap(), writes=[identb.b])
        esel = cx.sb("esel", [128, NBLK * 128], BF16)
        P.dma("sp", esel.t[:, :], C["esel"].ap(), writes=[esel.b])
        cmask = cx.sb("cmask", [128, 4, 512], BF16)
        P.dma("sp", cmask.t[:, :, :], C["cmask"].ap(), writes=[cmask.b])
        ksum = cx.sb("ksum", [128, NA, NBLK], F32)
        P.dma("sp", ksum.t[:, :, :], ksum_d.ap(), writes=[ksum.b])
        qr = ring(cx, "q", 2, [128, S], BF16)
        kr = ring(cx, "k", 2, [128, S], BF16)
        vr = ring(cx, "v", 2, [128, NKT, 128], BF16)
        kmr = ring(cx, "km", 2, [128, 16], BF16)
        btr = ring(cx, "bt", 2, [128, S], BF16)
        gmr = ring(cx, "gm", 4, [128, 16], F32)
        mxr = ring(cx, "mx", 4, [128, 8], F32)
        bqr = ring(cx, "bq", 6, [128, 128], F32)
        ptr = ring(cx, "pt", 4, [128, T], BF16)
        rzr = ring(cx, "rz", 2, [128, T], F32)
        obr = ring(cx, "ob", 2, [128, T], BF16)
        psr = psring(cx, "ps", 5)
        por = psring(cx, "po", 1)
        pzr = psring(cx, "pz", 1)
        osr = ring(cx, "os", 2, [128, T], F32)
        zsr = ring(cx, "zs", 2, [128, T], F32)
        psm = cx.ps("psm", nsub=7)
        NB16 = min(NBLK, 16)
        assert NBLK <= 16 and NBLK >= 8
        pastm = cx.sb("pastm", [128, NKT * 16], F32)
        P.dma("sp", pastm.t[:, :], C["pastm"].ap(), writes=[pastm.b])
        ownm = cx.sb("ownm", [128, NKT * 16], F32)
        P.dma("sp", ownm.t[:, :], C["ownm"].ap(), writes=[ownm.b])
        gmar = ring(cx, "gma", 2, [128, NKT * 16], F32)
        mxar = ring(cx, "mxa", 2, [128, NKT, 8], F32)
        bqar = ring(cx, "bqa", 2, [128, NKT, 128], F32)
        for t_ in bqar.tiles:
            P.op("dve", f_memset(t_.t[:, :, :], 0.0), writes=[t_.b])
        v3 = lambda ap: ap.rearrange("p (a b) -> p a b", b=16)
        st = {}

        def load(hd):
            q, k, vv, km, bt = qr.next(), kr.next(), vr.next(), kmr.next(), btr.next()
            P.dma("sp", q.t[:, :], qT.ap()[hd * 128:(hd + 1) * 128, :], writes=[q.b])
            P.dma("sp", k.t[:, :], kT.ap()[hd * 128:(hd + 1) * 128, :], writes=[k.b])
            vsrc = v.ap()[:, hd * 128:(hd + 1) * 128].rearrange("(n p) d -> p n d", p=128)
            for n0 in range(0, NKT, 8):
                P.dma("sp", vv.t[:, n0:n0 + 8, :], vsrc[:, n0:n0 + 8, :], writes=[vv.b])
            P.op("dve", f_memset(km.t[:, :], 0.0), writes=[km.b])
            P.op("dve", f_copy(km.t[:, 0:NBLK], ksum.t[:, hd, :]), reads=[ksum.b], writes=[km.b])
            st[hd] = (q, k, vv, km, bt)

        def gate_a(hd):
            q, k, vv, km, bt = st[hd]
            for qi in range(NKT):
                P.op("pe", f_mm(psm.t[:, qi * 16:(qi + 1) * 16], q.t[:, qi * 128:(qi + 1) * 128], km.t[:, 0:16]),
                     reads=[q.b, km.b], writes=[psm.b])
            gma, mxa, bqa = gmar.next(), mxar.next(), bqar.next()
            P.op("dve", f_tt(gma.t[:, :], psm.t[:, 0:NKT * 16], pastm.t[:, :], ALU.add), reads=[psm.b, pastm.b], writes=[gma.b])
            for qi in range(NKT):
                P.op("dve", lambda e, o_=mxa.t[:, qi, :], i_=gma.t[:, qi * 16:(qi + 1) * 16]: e.max(out=o_, in_=i_),
                     reads=[gma.b], writes=[mxa.b])
            for qi in range(NKT):
                P.op("dve", f_ts(gma.t[:, qi * 16:(qi + 1) * 16], gma.t[:, qi * 16:(qi + 1) * 16], mxa.t[:, qi, 2:3], None, ALU.is_ge),
                     reads=[gma.b, mxa.b], writes=[gma.b])
            P.op("dve", f_ts(bqa.t[:, :, 0:16], v3(gma.t[:, :]), -NEG, NEG, ALU.mult, ALU.add), reads=[gma.b], writes=[bqa.b])
            P.op("dve", f_tt(bqa.t[:, :, 0:16], bqa.t[:, :, 0:16], v3(pastm.t[:, :]), ALU.min), reads=[bqa.b, pastm.b], writes=[bqa.b])
            P.op("dve", f_tt(bqa.t[:, :, 0:16], bqa.t[:, :, 0:16], v3(ownm.t[:, :]), ALU.max), reads=[bqa.b, ownm.b], writes=[bqa.b])
            st[hd] = (q, k, vv, km, bt, bqa)

        def gate_b(hd):
            q, k, vv, km, bt, bqa = st[hd]
            for grp in range(NKT // 4):
                pst = psr.next()
                for j_ in range(4):
                    P.op("pe", f_tr(pst.t[:, j_ * 128:(j_ + 1) * 128], bqa.t[:, grp * 4 + j_, :], identf.t[:, :]),
                         reads=[bqa.b, identf.b], writes=[pst.b])
                P.op("act", f_act(bt.t[:, grp * 512:(grp + 1) * 512], pst.t[:, :], AF.Copy), reads=[pst.b], writes=[bt.b])

        def attention(hd):
            q, k, vv, km, bt, bqa = st.pop(hd)
            for g in range(S // T):
                nkt = 4 * (g + 1)
                po, pz = por.next(), pzr.next()
                qs = q.t[:, g * T:(g + 1) * T]

                def qk(kt):
                    ps = psr.next()
                    n = kt // 2
                    diag = kt >= 4 * g
                    P.op("pe", f_mm(ps.t[:, :], k.t[:, kt * 128:(kt + 1) * 128], qs, True, False),
                         reads=[k.b, q.b], writes=[ps.b])
                    P.op("pe", f_mm(ps.t[:, :], esel.t[:, n * 128:(n + 1) * 128], bt.t[:, g * T:(g + 1) * T], False, not diag),
                         reads=[esel.b, bt.b], writes=[ps.b])
                    if diag:
                        P.op("pe", f_mm(ps.t[:, :], identb.t[:, :], cmask.t[:, kt - 4 * g, :], False, True),
                             reads=[identb.b, cmask.b], writes=[ps.b])
                    return ps

                psq = [qk(i_) for i_ in range(min(2, nkt))]
                for kt in range(nkt):
                    ps = psq.pop(0)
                    pt = ptr.next()
                    P.op("act", f_act(pt.t[:, :], ps.t[:, :], AF.Exp, scale=float(cfg.SCALE)), reads=[ps.b], writes=[pt.b])
                    if kt + 2 < nkt:
                        psq.append(qk(kt + 2))
                    P.op("pe", f_mm(po.t[:, :], vv.t[:, kt, :], pt.t[:, :], kt == 0, kt == nkt - 1),
                         reads=[vv.b, pt.b], writes=[po.b])
                    P.op("pe", f_mm(pz.t[:, :], ones.t[:, :], pt.t[:, :], kt == 0, kt == nkt - 1),
                         reads=[ones.b, pt.b], writes=[pz.b])
                zs, os_ = zsr.next(), osr.next()
                P.op("act", f_act(zs.t[:, :], pz.t[:, :], AF.Copy), reads=[pz.b], writes=[zs.b])
                P.op("act", f_act(os_.t[:, :], po.t[:, :], AF.Copy), reads=[po.b], writes=[os_.b])
                rz = rzr.next()
                P.op("dve", f_recip(rz.t[:, :], zs.t[:, :]), reads=[zs.b], writes=[rz.b])
                ob = obr.next()
                P.op("dve", f_tt(ob.t[:, :], os_.t[:, :], rz.t[:, :], ALU.mult), reads=[os_.b, rz.b], writes=[ob.b])
                P.dma("sp", oT.ap()[hd * 128:(hd + 1) * 128, g * T:(g + 1) * T], ob.t[:, :], reads=[ob.b])

        load(0)
        gate_a(0)
        gate_b(0)
        for hd in range(NA):
            if hd + 1 < NA:
                load(hd + 1)
                gate_a(hd + 1)
            attention(hd)
            if hd + 1 < NA:
                gate_b(hd + 1)
        P.emit()


def stage_dilated(nc, cfg, C, qT, kT, v, oT):
    S, NA, NBG, T = cfg.S, cfg.NA, cfg.NBG, cfg.T
    NKT = S // 128
    with ExitStack() as es:
        cx = Ctx(nc, es)
        P = Prog(nc)
        K = load_consts(P, cx, cfg, None)
        ones = K["ones"]
        identb = cx.sb("identb", [128, 128], BF16)
        P.dma("sp", identb.t[:, :], C["identb"].ap(), writes=[identb.b])
        bmask = cx.sb("bmask", [128, 2, 256], BF16)
        P.dma("sp", bmask.t[:, :, :], C["bmask"].ap(), writes=[bmask.b])
        qr = ring(cx, "q", 4, [128, S], BF16)
        kr = ring(cx, "k", 4, [128, S], BF16)
        vr = ring(cx, "v", 2, [128, NKT, 128], BF16)
        uacc = cx.sb("uacc", [128, S], F32)
        zacc = cx.sb("zacc", [128, S], F32)
        ptr = ring(cx, "pt", 3, [128, 256], BF16)
        obr = ring(cx, "ob", 2, [128, T], BF16)
        psr = psring(cx, "ps", 4)
        pur = psring(cx, "pu", 4)
        for hh in range(NBG):
            for gi, (win, dil) in enumerate(cfg.PATS):
                hq = NA + gi * NBG + hh
                nblk = S // (128 * dil)
                q, k, vv = qr.next(), kr.next(), vr.next()
                P.dma("sp", q.t[:, :], qT.ap()[hq * 128:(hq + 1) * 128, :], writes=[q.b])
                P.dma("sp", k.t[:, :], kT.ap()[hq * 128:(hq + 1) * 128, :], writes=[k.b])
                vsrc = v.ap()[:, hq * 128:(hq + 1) * 128].rearrange("(n p r) d -> p r n d", p=128, r=dil)
                for r in range(dil):
                    for n0 in range(0, nblk, 8):
                        n1 = min(nblk, n0 + 8)
                        P.dma("sp", vv.t[:, r * nblk + n0:r * nblk + n1, :], vsrc[:, r, n0:n1, :], writes=[vv.b])
                if dil > 1:
                    q2, k2 = qr.next(), kr.next()
                    sub = S // dil
                    for r in range(dil):
                        P.op("dve", f_copy(q2.t[:, r * sub:(r + 1) * sub], q.t[:, r:r + (sub - 1) * dil + 1:dil]), reads=[q.b], writes=[q2.b])
                        P.op("act", f_act(k2.t[:, r * sub:(r + 1) * sub], k.t[:, r:r + (sub - 1) * dil + 1:dil], AF.Copy), reads=[k.b], writes=[k2.b])
                    q, k = q2, k2
                tiles = [(r, n) for r in range(dil) for n in range(nblk)]

                def qk(rn, q=q, k=k, nblk=nblk):
                    r, n = rn
                    c0 = (r * nblk + n) * 128
                    p0 = c0 - 128 if n > 0 else c0
                    ps = psr.next()
                    P.op("pe", f_mm(ps.t[:, 0:128], k.t[:, c0:c0 + 128], q.t[:, c0:c0 + 128], True, False), reads=[k.b, q.b], writes=[ps.b])
                    P.op("pe", f_mm(ps.t[:, 128:256], k.t[:, p0:p0 + 128], q.t[:, c0:c0 + 128], False, False), reads=[k.b, q.b], writes=[ps.b])
                    P.op("pe", f_mm(ps.t[:, 0:256], identb.t[:, :], bmask.t[:, 1 if n > 0 else 0, :], False, True),
                         reads=[identb.b, bmask.b], writes=[ps.b])
                    return ps

                ps_next = qk(tiles[0])
                for ti, (r, n) in enumerate(tiles):
                    b0 = n * 128 * dil + r
                    cur = slice(b0, b0 + 127 * dil + 1, dil)
                    ps = ps_next
                    pt = ptr.next()
                    P.op("act", f_act(pt.t[:, :], ps.t[:, 0:256], AF.Exp, scale=float(cfg.SCALE)), reads=[ps.b], writes=[pt.b])
                    if ti + 1 < len(tiles):
                        ps_next = qk(tiles[ti + 1])
                    pu = pur.next()
                    vi = r * nblk + n
                    vp = vi - 1 if n > 0 else vi
                    P.op("pe", f_mm(pu.t[:, 0:128], vv.t[:, vi, :], pt.t[:, 0:128], True, False), reads=[vv.b, pt.b], writes=[pu.b])
                    P.op("pe", f_mm(pu.t[:, 0:128], vv.t[:, vp, :], pt.t[:, 128:256], False, False), reads=[vv.b, pt.b], writes=[pu.b])
                    P.op("pe", f_mm(pu.t[:, 128:256], ones.t[:, :], pt.t[:, 0:128], False, False), reads=[ones.b, pt.b], writes=[pu.b])
                    P.op("pe", f_mm(pu.t[:, 128:256], ones.t[:, :], pt.t[:, 128:256], False, True), reads=[ones.b, pt.b], writes=[pu.b])
                    if gi == 0:
                        P.op("dve", f_copy(uacc.t[:, cur], pu.t[:, 0:128]), reads=[pu.b], writes=[uacc.b])
                        P.op("dve", f_copy(zacc.t[:, cur], pu.t[:, 128:256]), reads=[pu.b], writes=[zacc.b])
                    else:
                        P.op("dve", f_tt(uacc.t[:, cur], uacc.t[:, cur], pu.t[:, 0:128], ALU.add), reads=[pu.b, uacc.b], writes=[uacc.b])
                        P.op("dve", f_tt(zacc.t[:, cur], zacc.t[:, cur], pu.t[:, 128:256], ALU.add), reads=[pu.b, zacc.b], writes=[zacc.b])
            P.op("dve", f_recip(zacc.t[:, :], zacc.t[:, :]), reads=[zacc.b], writes=[zacc.b])
            for g in range(S // T):
                ob = obr.next()
                P.op("dve", f_tt(ob.t[:, :], uacc.t[:, g * T:(g + 1) * T], zacc.t[:, g * T:(g + 1) * T], ALU.mult),
                     reads=[uacc.b, zacc.b], writes=[ob.b])
                P.dma("sp", oT.ap()[(NA + hh) * 128:(NA + hh + 1) * 128, g * T:(g + 1) * T], ob.t[:, :], reads=[ob.b])
        P.emit()


def stage_sg_setup(nc, cfg, C, ws_d, bs_d, lnb_d, wsT_d, bsb_d):
    GG = cfg.GG
    with ExitStack() as es:
        cx = Ctx(nc, es)
        P = Prog(nc)
        identf = cx.sb("identf", [128, 128], F32)
        P.dma("sp", identf.t[:, :], C["identf"].ap(), writes=[identf.b])
        tril = cx.sb("tril", [128, 128], F32)
        P.dma("sp", tril.t[:, :], C["tril"].ap(), writes=[tril.b])
        e0 = cx.sb("e0", [128, 128], BF16)
        P.op("dve", f_memset(e0.t[:, :], 0.0), writes=[e0.b])
        P.op("dve", f_memset(e0.t[0:1, :], 1.0), writes=[e0.b])
        lnb = cx.sb("lnb", [128, cfg.E], BF16)
        P.dma("pool", lnb.t[:, :], lnb_d.ap(), writes=[lnb.b])
        bs0 = cx.sb("bs0", [128, GG * 128], BF16)
        P.op("dve", f_memset(bs0.t[:, :], 0.0), writes=[bs0.b])
        P.dma("pool", bs0.t[0:1, :], bs_d.ap(), writes=[bs0.b])
        wsT = cx.sb("wsT", [128, GG, 128], BF16)
        bsb = cx.sb("bsb", [128, GG, 128], F32)
        wr = ring(cx, "w", 3, [128, 128], F32)
        psr = psring(cx, "ps", 4)
        for g in range(GG):
            w = wr.next()
            P.dma("sp", w.t[:, :], ws_d.ap()[g], writes=[w.b])
            P.op("dve", f_tt(w.t[:, :], w.t[:, :], tril.t[:, :], ALU.mult), reads=[w.b, tril.b], writes=[w.b])
            ps = psr.next()
            P.op("pe", f_tr(ps.t[:, 0:128], w.t[:, :], identf.t[:, :]), reads=[w.b, identf.b], writes=[ps.b])
            P.op("act", f_act(wsT.t[:, g, :], ps.t[:, 0:128], AF.Copy), reads=[ps.b], writes=[wsT.b])
            ps2 = psr.next()
            P.op("pe", f_mm(ps2.t[:, 0:128], lnb.t[:, g * 128:(g + 1) * 128], wsT.t[:, g, :], True, False),
                 reads=[lnb.b, wsT.b], writes=[ps2.b])
            P.op("pe", f_mm(ps2.t[:, 0:128], e0.t[:, :], bs0.t[:, g * 128:(g + 1) * 128], False, True),
                 reads=[e0.b, bs0.b], writes=[ps2.b])
            P.op("act", f_act(bsb.t[:, g, :], ps2.t[:, 0:128], AF.Copy), reads=[ps2.b], writes=[bsb.b])
        P.dma("sp", wsT_d.ap(), wsT.t[:, :, :], reads=[wsT.b])
        P.dma("sp", bsb_d.ap(), bsb.t[:, :, :], reads=[bsb.b])
        P.emit()


def stage_sg_main(nc, cfg, xin, g_d, wu_d, wv_d, bu_d, bv_d, lng_d, wsT_d, bsb_d, mT):
    DC, S, GG, E = cfg.DC, cfg.S, cfg.GG, cfg.E
    T = 512
    VW = 256
    with ExitStack() as es:
        cx = Ctx(nc, es)
        P = Prog(nc)
        K = load_consts(P, cx, cfg, None)
        g = cx.sb("g", [128, DC], F32)
        P.dma("sp", g.t[:, :], g_d.ap(), writes=[g.b])
        bu = cx.sb("bu", [128, GG], F32)
        P.dma("sp", bu.t[:, :], bu_d.ap(), writes=[bu.b])
        lng = cx.sb("lng", [128, GG], F32)
        P.dma("sp", lng.t[:, :], lng_d.ap(), writes=[lng.b])
        e0 = cx.sb("e0", [128, 128], BF16)
        P.op("dve", f_memset(e0.t[:, :], 0.0), writes=[e0.b])
        P.op("dve", f_memset(e0.t[0:1, :], 1.0), writes=[e0.b])
        bv0 = cx.sb("bv0", [128, E], BF16)
        P.op("dve", f_memset(bv0.t[:, :], 0.0), writes=[bv0.b])
        P.dma("pool", bv0.t[0:1, :], bv_d.ap(), writes=[bv0.b])
        wsT = cx.sb("wsT", [128, GG, 128], BF16)
        P.dma("sp", wsT.t[:, :, :], wsT_d.ap(), writes=[wsT.b])
        bsb = cx.sb("bsb", [128, GG, 128], F32)
        P.dma("sp", bsb.t[:, :, :], bsb_d.ap(), writes=[bsb.b])
        h = cx.sb("h", [128, DC, T], BF16)
        uT = cx.sb("uT", [128, GG, T], BF16)
        xring = ring(cx, "xc", 3, [128, T], F32)
        sqring = ring(cx, "sq", 2, [128, T], BF16)
        rstd = cx.sb("rstd", [128, T], F32)
        wring = ring(cx, "w", 2, [128, DC, 128], BF16)
        wvring = ring(cx, "wv", 2, [128, DC, VW], BF16)
        vbufs = [cx.sb("vbuf%d" % i, [128, E], BF16) for i in range(T // 128)]
        vn = cx.sb("vn", [128, E], BF16)
        nst = max(1, E // 512)
        stats = cx.sb("stats", [128, nst, 6], F32)
        mv = cx.sb("mv", [128, 2], F32)
        fbr = ring(cx, "fb", 3, [128, 128], F32)
        mbr = ring(cx, "mb", 1, [128, GG, 128], BF16)
        psn = cx.ps("psn")
        psr = psring(cx, "ps", 5)
        psa = psring(cx, "pa", 2)
        mv_ = dview(mT)
        for tt in range(S // T):
            tok0 = tt * T
            emit_norm(P, cfg, K, xin, tok0, g, h, 0, xring, sqring, K["ones"], psn, rstd, T=T)
            for j in range(GG):
                w = wring.next()
                P.dma("pool", w.t[:, :, :], wu_d.ap()[j], writes=[w.b])
                ps = psr.next()
                for k in range(DC):
                    P.op("pe", f_mm(ps.t[:, 0:T], w.t[:, k, :], h.t[:, k, :], k == 0, k == DC - 1), reads=[w.b, h.b], writes=[ps.b])
                P.op("act", f_act(uT.t[:, j, :], ps.t[:, 0:T], AF.Gelu, bias=bu.t[:, j:j + 1]), reads=[ps.b, bu.b], writes=[uT.b])
            for jb in range(E // VW):
                wv = wvring.next()
                P.dma("pool", wv.t[:, :, :], wv_d.ap()[jb], writes=[wv.b])
                for tb in range(T // 128):
                    ps = psr.next()
                    for k in range(DC):
                        P.op("pe", f_mm(ps.t[:, 0:VW], h.t[:, k, tb * 128:(tb + 1) * 128], wv.t[:, k, :], k == 0, False),
                             reads=[wv.b, h.b], writes=[ps.b])
                    P.op("pe", f_mm(ps.t[:, 0:VW], e0.t[:, :], bv0.t[:, jb * VW:(jb + 1) * VW], False, True),
                         reads=[e0.b, bv0.b], writes=[ps.b])
                    P.op("act", f_act(vbufs[tb].t[:, jb * VW:(jb + 1) * VW], ps.t[:, 0:VW], AF.Gelu), reads=[ps.b], writes=[vbufs[tb].b])
            for tb in range(T // 128):
                vb = vbufs[tb]
                for i in range(nst):
                    w_ = min(512, E)
                    P.op("dve", lambda e, o_=stats.t[:, i, :], i_=vb.t[:, i * w_:(i + 1) * w_]: e.bn_stats(out=o_, in_=i_),
                         reads=[vb.b], writes=[stats.b])
                P.op("dve", lambda e, o_=mv.t[:, :], i_=stats.t[:, :, :]: e.bn_aggr(out=o_, in_=i_), reads=[stats.b], writes=[mv.b])
                P.op("act", f_act(mv.t[:, 1:2], mv.t[:, 1:2], AF.Sqrt, bias=K["eps"].t[:, 0:1]), reads=[mv.b, K["eps"].b], writes=[mv.b])
                P.op("dve", f_recip(mv.t[:, 1:2], mv.t[:, 1:2]), reads=[mv.b], writes=[mv.b])
                P.op("dve", f_ts(vn.t[:, :], vb.t[:, :], mv.t[:, 0:1], mv.t[:, 1:2], ALU.subtract, ALU.mult),
                     reads=[vb.b, mv.b], writes=[vn.b])
                mb = mbr.next()
                for gb in range(0, GG, 4):
                    pa = psa.next()
                    gis = list(range(gb, min(GG, gb + 4)))
                    for gi in gis:
                        c0 = (gi % 4) * 128
                        P.op("pe", f_mm(pa.t[:, c0:c0 + 128], vn.t[:, gi * 128:(gi + 1) * 128], wsT.t[:, gi, :], True, True),
                             reads=[vn.b, wsT.b], writes=[pa.b])
                    for gi in gis:
                        c0 = (gi % 4) * 128
                        fb = fbr.next()
                        P.op("dve", f_stt(fb.t[:, :], pa.t[:, c0:c0 + 128], lng.t[:, gi:gi + 1], bsb.t[:, gi, :], ALU.mult, ALU.add),
                             reads=[pa.b, lng.b, bsb.b], writes=[fb.b])
                        P.op("dve", f_tt(mb.t[:, gi, :], fb.t[:, :], uT.t[:, gi, tb * 128:(tb + 1) * 128], ALU.mult),
                             reads=[fb.b, uT.b], writes=[mb.b])
                for g0 in range(0, GG, 8):
                    g1 = min(GG, g0 + 8)
                    P.dma("sp", mv_[:, g0:g1, tok0 + tb * 128:tok0 + (tb + 1) * 128], mb.t[:, g0:g1, :], reads=[mb.b])
        P.emit()


def stage_final_norm(nc, cfg, xin, g_d, out):
    DC, S, T = cfg.DC, cfg.S, cfg.T
    CG = 8
    NG = (DC + CG - 1) // CG
    with ExitStack() as es:
        cx = Ctx(nc, es)
        P = Prog(nc)
        K = load_consts(P, cx, cfg, None)
        g = cx.sb("g", [128, DC], F32)
        P.dma("sp", g.t[:, :], g_d.ap(), writes=[g.b])
        xts = [cx.sb("x%d" % i, [128, DC, T], F32, nsub=NG) for i in range(2)]
        sqring = ring(cx, "sq", 3, [128, T], BF16)
        rsr = ring(cx, "rstd", 2, [128, T], F32)
        pnr = psring(cx, "psn", 2)
        xv = dview(xin)
        ov = dview(out)
        for tt in range(S // T):
            x = xts[tt % 2]
            tok0 = tt * T
            for gi in range(NG):
                c0, c1 = gi * CG, min(DC, (gi + 1) * CG)
                P.dma("sp", x.t[:, c0:c1, :], xv[:, c0:c1, tok0:tok0 + T], writes=[x.sub[gi]])
            psn, rstd = pnr.next(), rsr.next()
            for c in range(DC):
                sq = sqring.next()
                P.op("act", f_act(sq.t[:, :], x.t[:, c, :], AF.Square), reads=[x.sub[c // CG]], writes=[sq.b])
                P.op("pe", f_mm(psn.t[:, :], K["ones"].t[:, :], sq.t[:, :], c == 0, c == DC - 1),
                     reads=[K["ones"].b, sq.b], writes=[psn.b])
            P.op("act", f_act(rstd.t[:, :], psn.t[:, :], AF.Sqrt, scale=1.0 / cfg.D, bias=K["eps"].t[:, 0:1]),
                 reads=[psn.b, K["eps"].b], writes=[rstd.b])
            P.op("dve", f_recip(rstd.t[:, :], rstd.t[:, :]), reads=[rstd.b], writes=[rstd.b])
            for c in range(DC):
                P.op("dve", f_stt(x.t[:, c, :], x.t[:, c, :], g.t[:, c:c + 1], rstd.t[:, :], ALU.mult, ALU.mult),
                     reads=[x.sub[c // CG], g.b, rstd.b], writes=[x.sub[c // CG]])
            for gi in range(NG):
                c0, c1 = gi * CG, min(DC, (gi + 1) * CG)
                P.dma("sp", ov[:, c0:c1, tok0:tok0 + T], x.t[:, c0:c1, :], reads=[x.sub[gi]])
        P.emit()


def build_program(cfg):
    nc = bass.Bass("TRN2", target_bir_lowering=False)
    D, S, DC, FC, NH, NA, GG, E, DFF = cfg.D, cfg.S, cfg.DC, cfg.FC, cfg.NH, cfg.NA, cfg.GG, cfg.E, cfg.DFF
    C = make_consts(nc, cfg)

    def di(name, shape, dt=F32):
        return nc.dram_tensor(name, list(shape), dt, kind="ExternalInput")

    def ds(name, shape, dt):
        return nc.dram_tensor(name, list(shape), dt)

    xT = di("xT", [D, S])
    pos = di("pos", [32, S], I32)
    g_attn = di("g_attn", [128, DC])
    wqk = di("wqk", [2 * NH, 128, DC, 128])
    wv = di("wv", [NH * 128 // cfg.VW, 128, DC, cfg.VW])
    wo = di("wo", [DC, 128, DC, 128])
    ffn = []
    for l in range(2):
        ffn.append(dict(g=di("g_ffn%d" % l, [128, DC]), wup=di("wup%d" % l, [2 * FC, 128, DC, 128]),
                        cw=di("cw%d" % l, [128, 3, 2 * FC]), cb=di("cb%d" % l, [128, 2 * FC]),
                        wdn=di("wdn%d" % l, [DC, 128, FC, 128])))
    g_sg = di("g_sg", [128, DC])
    sg_wu = di("sg_wu", [GG, 128, DC, 128])
    sg_wv = di("sg_wv", [E // 256, 128, DC, 256])
    sg_bu = di("sg_bu", [128, GG])
    sg_bv = di("sg_bv", [1, E])
    sg_lng = di("sg_lng", [128, GG])
    sg_lnb = di("sg_lnb", [128, E])
    sg_ws = di("sg_ws", [GG, 128, 128])
    sg_bs = di("sg_bs", [1, GG * 128])
    sg_wo = di("sg_wo", [DC, 128, GG, 128])
    g_fin = di("g_fin", [128, DC])
    outT = nc.dram_tensor("outT", [D, S], F32, kind="ExternalOutput")

    qT = ds("qT", [NH * 128, S], BF16)
    kT = ds("kT", [NH * 128, S], BF16)
    v = ds("v", [S, NH * 128], BF16)
    ksum = ds("ksum", [128, NA, cfg.NBLK], F32)
    cs = ds("cs", [32, 2, S], F32)
    oT = ds("oT", [D, S], BF16)
    gT = ds("gT", [DFF, S], BF16)
    xa = ds("xa", [D, S], F32)
    xb = ds("xb", [D, S], F32)
    wdnb = [ds("wdnb%d" % l, [DC, 128, FC, 128], BF16) for l in range(2)]
    wsT = ds("wsT", [128, GG, 128], BF16)
    bsb = ds("bsb", [128, GG, 128], F32)

    import os
    sel = os.environ.get("MK_STAGES")
    sel = set(int(t) for t in sel.split(",")) if sel else set(range(1, 13))
    stages = [
        lambda: (stage_rope(nc, cfg, C, pos, cs), stage_qkv(nc, cfg, C, xT, g_attn, wqk, wv, cs, qT, kT, v, ksum)),
        lambda: stage_moba(nc, cfg, C, qT, kT, v, ksum, oT),
        lambda: stage_dilated(nc, cfg, C, qT, kT, v, oT),
        lambda: stage_linear_res(nc, cfg, DC, oT, wo, xT, xa),
        lambda: stage_ffn_up(nc, cfg, xa, ffn[0]["g"], ffn[0]["wup"], ffn[0]["cw"], ffn[0]["cb"], gT, precast=(ffn[0]["wdn"], wdnb[0])),
        lambda: stage_linear_res(nc, cfg, FC, gT, wdnb[0], xa, xb),
        lambda: stage_sg_setup(nc, cfg, C, sg_ws, sg_bs, sg_lnb, wsT, bsb),
        lambda: stage_sg_main(nc, cfg, xb, g_sg, sg_wu, sg_wv, sg_bu, sg_bv, sg_lng, wsT, bsb, oT),
        lambda: stage_linear_res(nc, cfg, GG, oT, sg_wo, xb, xa),
        lambda: stage_ffn_up(nc, cfg, xa, ffn[1]["g"], ffn[1]["wup"], ffn[1]["cw"], ffn[1]["cb"], gT, precast=(ffn[1]["wdn"], wdnb[1])),
        lambda: stage_linear_res(nc, cfg, FC, gT, wdnb[1], xa, xb),
        lambda: stage_final_norm(nc, cfg, xb, g_fin, outT),
    ]
    for i, st in enumerate(stages):
        if i + 1 in sel:
            st()
    return nc


def host_prep(cfg, p):
    NH, FC, GG, E = cfg.NH, cfg.FC, cfg.GG, cfg.E
    f = lambda a: np.asarray(a, dtype=np.float32)
    w_in = f(p["attn_w_in"])[0]
    m = {}
    m["g_attn"] = vec_pc(f(p["attn_norm"])[0])
    m["wqk"] = tile_w(w_in[:, :2 * NH * 128])
    m["wv"] = tile_w(w_in[:, 2 * NH * 128:], cfg.VW)
    m["wo"] = tile_w(f(p["attn_w_out"])[0])
    for l in range(2):
        m["g_ffn%d" % l] = vec_pc(f(p["ffn_norm"])[l])
        m["wup%d" % l] = tile_w(f(p["ffn_w_up"])[l])
        m["cw%d" % l] = np.ascontiguousarray(f(p["ffn_conv_w"])[l].reshape(3, 2 * FC, 128).transpose(2, 0, 1))
        m["cb%d" % l] = vec_pc(f(p["ffn_conv_b"])[l])
        m["wdn%d" % l] = tile_w(f(p["ffn_w_down"])[l])
    sw = f(p["sg_w_in"])[0]
    sb = f(p["sg_b_in"])[0]
    m["g_sg"] = vec_pc(f(p["sg_norm"])[0])
    m["sg_wu"] = tile_w(sw[:, :E])
    m["sg_wv"] = tile_w(sw[:, E:], 256)
    m["sg_bu"] = vec_pc(sb[:E])
    m["sg_bv"] = np.ascontiguousarray(sb[E:].reshape(1, E))
    m["sg_lng"] = vec_pc(f(p["sg_v_gain"])[0])
    m["sg_lnb"] = np.ascontiguousarray(np.broadcast_to(f(p["sg_v_bias"])[0], (128, E)))
    m["sg_ws"] = np.ascontiguousarray(f(p["sg_w_s"])[0])
    m["sg_bs"] = np.ascontiguousarray(f(p["sg_b_s"])[0].reshape(1, GG * 128))
    m["sg_wo"] = tile_w(f(p["sg_w_out"])[0])
    m["g_fin"] = vec_pc(f(p["final_norm"]))
    return m


def run_module(cfg, inputs):
    x = np.asarray(inputs["x"], dtype=np.float32)
    positions = np.asarray(inputs["positions"]).astype(np.int32)
    B = x.shape[0]
    import time as _t
    t0 = _t.time()
    shared = host_prep(cfg, inputs)
    t1 = _t.time()
    nc = build_program(cfg)
    print("[mk] host_prep %.1fs build %.1fs" % (t1 - t0, _t.time() - t1), flush=True)
    in_maps = []
    for b in range(B):
        m = dict(shared)
        m["xT"] = np.ascontiguousarray(x[b].T)
        m["pos"] = np.ascontiguousarray(np.broadcast_to(positions[b], (32, cfg.S)))
        in_maps.append(m)
    t2 = _t.time()
    res = run_bass_kernel_spmd(nc, in_maps, core_ids=list(range(B)))
    print("[mk] launch %.1fs" % (_t.time() - t2), flush=True)
    return np.stack([np.ascontiguousarray(res.results[b]["outT"].T) for b in range(B)], 0)


def kernel(**inputs):
    cfg = Cfg()
    return run_module(cfg, inputs)
```

```python
import numpy as np
from contextlib import ExitStack
import concourse.bass as bass
import concourse.mybir as mybir
from concourse.bass_utils import run_bass_kernel_spmd

F32 = mybir.dt.float32
BF16 = mybir.dt.bfloat16
I32 = mybir.dt.int32
AF = mybir.ActivationFunctionType
ALU = mybir.AluOpType
AX = mybir.AxisListType

NEG = -30000.0
ENGS = ("pe", "act", "dve", "pool", "sp")
NRING = 12


class Cfg:
    def __init__(self, D=4096, S=4096, NA=24, NBG=8, DFF=14336, B=2):
        self.D, self.S, self.NA, self.NBG, self.DFF, self.B = D, S, NA, NBG, DFF, B
        self.DH = 128
        self.NB = 3 * NBG
        self.NH = NA + self.NB
        self.NHO = NA + NBG
        assert self.NHO * 128 == D
        self.QKV = 3 * self.NH * 128
        self.DC = D // 128
        self.FC = DFF // 128
        self.T = 512
        self.NT = S // self.T
        self.MB = 256
        self.NBLK = S // self.MB
        self.E = D
        self.GG = self.E // 128
        self.EPS = 1e-5
        self.PATS = ((128, 1), (512, 4), (2048, 16))
        self.VW = 512 if (self.NH * 128) % 512 == 0 else 256
        self.SCALE = 128 ** -0.5


class Buf:
    __slots__ = ("w", "r")

    def __init__(self):
        self.w = None
        self.r = []


class Op:
    __slots__ = ("eng", "fn", "deps", "signal", "ev", "dma", "di")

    def __init__(self, eng, fn, dma):
        self.eng, self.fn, self.dma = eng, fn, dma
        self.deps = []
        self.signal = False
        self.ev = None
        self.di = -1


class Prog:
    def __init__(self, nc):
        self.nc = nc
        self.ops = []

    def op(self, eng, fn, reads=(), writes=(), dma=False):
        o = Op(eng, fn, dma)
        deps = {}
        for b in reads:
            if b.w is not None:
                deps[id(b.w)] = b.w
        for b in writes:
            if b.w is not None:
                deps[id(b.w)] = b.w
            for r in b.r:
                deps[id(r)] = r
        for d in deps.values():
            if d is o:
                continue
            if d.eng == "pe" and eng == "pe" and not d.dma and not dma:
                continue
            o.deps.append(d)
            d.signal = True
        for b in writes:
            b.w = o
            b.r = []
        for b in reads:
            if b.w is not o:
                if not dma:
                    b.r = [r for r in b.r if r.dma or r.eng != eng]
                b.r.append(o)
        self.ops.append(o)
        return o

    def dma(self, eng, out, in_, reads=(), writes=()):
        return self.op(eng, lambda e: e.dma_start(out=out, in_=in_), reads, writes, dma=True)

    def emit(self, name=None):
        nc = self.nc
        with ExitStack() as es:
            if not hasattr(nc, "_mk_sems"):
                gs = ExitStack()
                c_ = {e: gs.enter_context(nc.semaphore("c_" + e)) for e in ENGS}
                d_ = {e: [gs.enter_context(nc.semaphore("d_%s%d" % (e, i))) for i in range(NRING)]
                      for e in ("sp", "pool")}
                nc._mk_sems = (gs, c_, d_)
                nc._mk_pool_n = 0
            _, csem, dsem = nc._mk_sems
            cnt = {e: 0 for e in ENGS}
            dcnt = {e: 0 for e in dsem}
            dcnt["pool"] = nc._mk_pool_n
            pool_base = nc._mk_pool_n
            by_eng = {e: [] for e in ENGS}
            dlist = {e: [] for e in dsem}
            for o in self.ops:
                if o.dma:
                    i = dcnt[o.eng]
                    dcnt[o.eng] += 1
                    o.di = i
                    o.ev = (dsem[o.eng][i % NRING], 16 * (i // NRING + 1))
                    dlist[o.eng].append(o)
                    if o.eng == "pool":
                        nc._mk_pool_n = i + 1
                elif o.signal:
                    cnt[o.eng] += 1
                    o.ev = (csem[o.eng], cnt[o.eng])
                by_eng[o.eng].append(o)
            block = es.enter_context(nc.Block())

            def body(e, eng):
                waited = {}

                def wait(ev):
                    sem, val = ev
                    k = id(sem)
                    if waited.get(k, 0) < val:
                        e.wait_ge(sem, val)
                        waited[k] = val

                for o in by_eng[eng]:
                    need = {}
                    for d in o.deps:
                        sm, val = d.ev
                        k_ = id(sm)
                        if k_ not in need or need[k_][1] < val:
                            need[k_] = (sm, val)
                    for ev in need.values():
                        wait(ev)
                    if o.dma:
                        li = o.di - (pool_base if eng == "pool" else 0)
                        if li >= NRING:
                            wait(dlist[eng][li - NRING].ev)
                    ins = o.fn(e)
                    if o.dma:
                        ins.then_inc(o.ev[0], 16)
                    elif o.signal:
                        ins.then_inc(o.ev[0], 1)
                if eng in dlist:
                    for o in dlist[eng][-NRING:]:
                        wait(o.ev)

            block.tensor(lambda e: body(e, "pe"))
            block.scalar(lambda e: body(e, "act"))
            block.vector(lambda e: body(e, "dve"))
            block.gpsimd(lambda e: body(e, "pool"))
            block.sync(lambda e: body(e, "sp"))
        _, csem, dsem = nc._mk_sems
        allsems = list(csem.values()) + list(dsem["sp"])
        with nc.Block() as blk2:
            def clr(e):
                for sm in allsems:
                    e.sem_clear(sm)
            blk2.sync(clr)
        self.ops = []


class Tile:
    def __init__(self, t, nsub=0):
        self.t = t
        self.b = Buf()
        self.sub = [Buf() for _ in range(nsub)]


class Ctx:
    _n = 0

    def __init__(self, nc, es):
        self.nc, self.es = nc, es
        Ctx._n += 1
        self.pre = "s%d_" % Ctx._n

    def sb(self, name, shape, dt, nsub=0):
        return Tile(self.es.enter_context(self.nc.sbuf_tensor(self.pre + name, list(shape), dt)), nsub)

    def ps(self, name, shape=(128, 512), dt=F32, nsub=0):
        return Tile(self.es.enter_context(self.nc.psum_tensor(self.pre + name, list(shape), dt)), nsub)


class Ring:
    def __init__(self, tiles):
        self.tiles = tiles
        self.i = 0

    def next(self):
        t = self.tiles[self.i % len(self.tiles)]
        self.i += 1
        return t


def ring(cx, name, n, shape, dt):
    return Ring([cx.sb("%s%d" % (name, i), shape, dt) for i in range(n)])


def psring(cx, name, n):
    return Ring([cx.ps("%s%d" % (name, i)) for i in range(n)])


def f_mm(out, lhsT, rhs, start=True, stop=True):
    return lambda e: e.matmul(out, lhsT, rhs, start=start, stop=stop)


def f_tr(out, in_, ident):
    return lambda e: e.transpose(out, in_, ident)


def f_act(out, in_, func, **kw):
    return lambda e: e.activation(out=out, in_=in_, func=func, **kw)


def f_ts(out, in0, s1, s2, op0, op1=None):
    if op1 is None:
        return lambda e: e.tensor_scalar(out=out, in0=in0, scalar1=s1, scalar2=None, op0=op0)
    return lambda e: e.tensor_scalar(out=out, in0=in0, scalar1=s1, scalar2=s2, op0=op0, op1=op1)


def f_stt(out, in0, scalar, in1, op0, op1):
    return lambda e: e.scalar_tensor_tensor(out=out, in0=in0, scalar=scalar, in1=in1, op0=op0, op1=op1)


def f_tt(out, in0, in1, op):
    return lambda e: e.tensor_tensor(out=out, in0=in0, in1=in1, op=op)


def f_copy(out, in_):
    return lambda e: e.tensor_copy(out=out, in_=in_)


def f_recip(out, in_):
    return lambda e: e.reciprocal(out=out, in_=in_)


def f_memset(ap, v):
    return lambda e: e.memset(ap, v)


def dview(h):
    return h.ap().rearrange("(c p) t -> p c t", p=128)


def emit_norm(P, cfg, K, xin, tok0, g, h, hcol0, xring, sqring, ones, psn, rstd, T=None):
    T, DC = (T or cfg.T), cfg.DC
    xv = dview(xin)
    for c in range(DC):
        xc = xring.next()
        P.dma("sp", xc.t[:, 0:T], xv[:, c, tok0:tok0 + T], writes=[xc.b])
        sq = sqring.next()
        P.op("act", f_act(sq.t[:, 0:T], xc.t[:, 0:T], AF.Square), reads=[xc.b], writes=[sq.b])
        P.op("pe", f_mm(psn.t[:, 0:T], ones.t[:, :], sq.t[:, 0:T], c == 0, c == DC - 1),
             reads=[ones.b, sq.b], writes=[psn.b])
    P.op("act", f_act(rstd.t[:, 0:T], psn.t[:, 0:T], AF.Sqrt, scale=1.0 / cfg.D, bias=K["eps"].t[:, 0:1]),
         reads=[psn.b, K["eps"].b], writes=[rstd.b])
    P.op("dve", f_recip(rstd.t[:, 0:T], rstd.t[:, 0:T]), reads=[rstd.b], writes=[rstd.b])
    for c in range(DC):
        xc = xring.next()
        P.dma("sp", xc.t[:, 0:T], xv[:, c, tok0:tok0 + T], writes=[xc.b])
        P.op("dve", f_stt(h.t[:, c, hcol0:hcol0 + T], xc.t[:, 0:T], g.t[:, c:c + 1], rstd.t[:, 0:T],
                          ALU.mult, ALU.mult),
             reads=[xc.b, g.b, rstd.b], writes=[h.b])


def load_consts(P, cx, cfg, consts):
    K = {}
    K["ones"] = cx.sb("k_ones", [128, 128], BF16)
    P.op("dve", f_memset(K["ones"].t[:, :], 1.0), writes=[K["ones"].b])
    K["eps"] = cx.sb("k_eps", [128, 1], F32)
    P.op("dve", f_memset(K["eps"].t[:, :], cfg.EPS), writes=[K["eps"].b])
    return K


def stage_ffn_up(nc, cfg, xin, g_d, wup_d, cw_d, cb_d, gT, precast=None):
    T, DC, FC, S = cfg.T, cfg.DC, cfg.FC, cfg.S
    NSUB = 2 if S % (2 * T) == 0 else 1
    TS = T * NSUB
    with ExitStack() as es:
        cx = Ctx(nc, es)
        P = Prog(nc)
        K = load_consts(P, cx, cfg, None)
        g = cx.sb("g", [128, DC], F32)
        P.dma("sp", g.t[:, :], g_d.ap(), writes=[g.b])
        cw = cx.sb("cw", [128, 3, 2 * FC], F32)
        P.dma("sp", cw.t[:, :, :], cw_d.ap(), writes=[cw.b])
        cb = cx.sb("cb", [128, 2 * FC], F32)
        P.dma("sp", cb.t[:, :], cb_d.ap(), writes=[cb.b])
        carry = [cx.sb("carry%d" % i, [128, FC, 2], F32, nsub=FC) for i in range(2)]
        for cr in carry:
            P.op("dve", f_memset(cr.t[:, :, :], 0.0), writes=cr.sub)
        hs = [cx.sb("h%d" % i, [128, DC, TS], BF16) for i in range(2)]
        xring = ring(cx, "xc", 4, [128, T], F32)
        sqring = ring(cx, "sq", 3, [128, T], BF16)
        rstd = cx.sb("rstd", [128, T], F32)
        wring = ring(cx, "w", 4, [128, DC, 128], BF16)
        abuf = ring(cx, "ab", 4, [128, T + 4], F32)
        cbuf = ring(cx, "cbf", 6, [128, T], F32)
        sgb = ring(cx, "sg", 2, [128, T], F32)
        gob = ring(cx, "go", 3, [128, T], BF16)
        psn = cx.ps("psn")
        psr = psring(cx, "ps", 7)
        NST = S // TS

        def do_norm(st_):
            for sub_ in range(NSUB):
                emit_norm(P, cfg, K, xin, st_ * TS + sub_ * T, g, hs[st_ % 2], sub_ * T, xring, sqring, K["ones"], psn, rstd)

        PCQ = 4
        n_pc = DC * PCQ
        pc_step = max(1, (NST * FC) // n_pc)
        pc_done = [0]
        PCW = FC * 128 // PCQ

        def do_precast(i):
            src, dst = precast
            c, q_ = i // PCQ, i % PCQ
            P.dma("pool", dst.ap()[c].rearrange("p k c -> p (k c)")[:, q_ * PCW:(q_ + 1) * PCW],
                  src.ap()[c].rearrange("p k c -> p (k c)")[:, q_ * PCW:(q_ + 1) * PCW])

        do_norm(0)
        pend = [None]
        for st in range(NST):
            h = hs[st % 2]
            for j in range(FC):
                if j == min(4, FC - 1) and st + 1 < NST:
                    do_norm(st + 1)
                ws = []
                for half in range(2):
                    w = wring.next()
                    P.dma("pool", w.t[:, :, :], wup_d.ap()[half * FC + j], writes=[w.b])
                    ws.append(w)
                it = st * FC + j
                if precast is not None and it % pc_step == 0 and pc_done[0] < n_pc:
                    do_precast(pc_done[0])
                    pc_done[0] += 1
                for sub in range(NSUB):
                    tok0 = st * TS + sub * T
                    cs = []
                    for half in range(2):
                        ps = psr.next()
                        for k in range(DC):
                            P.op("pe", f_mm(ps.t[:, :], ws[half].t[:, k, :], h.t[:, k, sub * T:(sub + 1) * T],
                                            k == 0, k == DC - 1),
                                 reads=[ws[half].b, h.b], writes=[ps.b])
                        ch = half * FC + j
                        ab = abuf.next()
                        cr = carry[half]
                        P.op("dve", f_copy(ab.t[:, 0:2], cr.t[:, j, :]), reads=[cr.sub[j]], writes=[ab.b])
                        P.op("act", f_act(ab.t[:, 2:T + 2], ps.t[:, :], AF.Copy), reads=[ps.b], writes=[ab.b])
                        P.op("dve", f_copy(cr.t[:, j, :], ab.t[:, T:T + 2]), reads=[ab.b], writes=[cr.sub[j]])
                        c_ = cbuf.next()
                        P.op("dve", f_ts(c_.t[:, :], ab.t[:, 2:T + 2], cw.t[:, 2, ch:ch + 1], cb.t[:, ch:ch + 1],
                                         ALU.mult, ALU.add), reads=[ab.b, cw.b, cb.b], writes=[c_.b])
                        P.op("dve", f_stt(c_.t[:, :], ab.t[:, 1:T + 1], cw.t[:, 1, ch:ch + 1], c_.t[:, :],
                                          ALU.mult, ALU.add), reads=[ab.b, c_.b, cw.b], writes=[c_.b])
                        P.op("dve", f_stt(c_.t[:, :], ab.t[:, 0:T], cw.t[:, 0, ch:ch + 1], c_.t[:, :],
                                          ALU.mult, ALU.add), reads=[ab.b, c_.b, cw.b], writes=[c_.b])
                        cs.append(c_)
                    def fin(cs=cs, j=j, tok0=tok0):
                        sg = sgb.next()
                        P.op("act", f_act(sg.t[:, :], cs[0].t[:, :], AF.Silu), reads=[cs[0].b], writes=[sg.b])
                        go = gob.next()
                        P.op("dve", f_tt(go.t[:, :], sg.t[:, :], cs[1].t[:, :], ALU.mult),
                             reads=[sg.b, cs[1].b], writes=[go.b])
                        P.dma("sp", gT.ap()[j * 128:(j + 1) * 128, tok0:tok0 + T], go.t[:, :], reads=[go.b])

                    if pend[0] is not None:
                        pend[0]()
                    pend[0] = fin
        if pend[0] is not None:
            pend[0]()
        while precast is not None and pc_done[0] < n_pc:
            do_precast(pc_done[0])
            pc_done[0] += 1
        P.emit()


def stage_linear_res(nc, cfg, KC, actT, w_d, xin, xout):
    T, DC, S = cfg.T, cfg.DC, cfg.S
    NSUB = 2 if (KC <= 32 and S % (2 * T) == 0) else 1
    TS = T * NSUB
    with ExitStack() as es:
        cx = Ctx(nc, es)
        P = Prog(nc)
        KG = 16 // NSUB
        NG = (KC + KG - 1) // KG
        abufs = [cx.sb("a%d" % i, [128, KC, TS], BF16, nsub=NG) for i in range(2 if KC <= 32 else 1)]
        wring = ring(cx, "w", 2, [128, KC, 128], BF16)
        xr = ring(cx, "xr", 3, [128, T], F32)
        xo = ring(cx, "xo", 3, [128, T], F32)
        psr = psring(cx, "ps", 4)
        av = dview(actT)
        xv = dview(xin)
        ov = dview(xout)
        for tt in range(S // TS):
            t0 = tt * TS
            a = abufs[tt % len(abufs)]
            for gk in range(NG):
                k0, k1 = gk * KG, min(KC, (gk + 1) * KG)
                P.dma("sp", a.t[:, k0:k1, :], av[:, k0:k1, t0:t0 + TS], writes=[a.sub[gk]])
            for c in range(DC):
                w = wring.next()
                P.dma("pool", w.t[:, :, :], w_d.ap()[c], writes=[w.b])
                for sub in range(NSUB):
                    tok0 = t0 + sub * T
                    ps = psr.next()
                    for k in range(KC):
                        P.op("pe", f_mm(ps.t[:, :], w.t[:, k, :], a.t[:, k, sub * T:(sub + 1) * T], k == 0, k == KC - 1),
                             reads=[w.b, a.sub[k // KG]], writes=[ps.b])
                    x_ = xr.next()
                    P.dma("sp", x_.t[:, :], xv[:, c, tok0:tok0 + T], writes=[x_.b])
                    o_ = xo.next()
                    P.op("dve", f_tt(o_.t[:, :], ps.t[:, :], x_.t[:, :], ALU.add), reads=[ps.b, x_.b], writes=[o_.b])
                    P.dma("sp", ov[:, c, tok0:tok0 + T], o_.t[:, :], reads=[o_.b])
        P.emit()


def tile_w(W, cw=128):
    K, N = W.shape
    return np.ascontiguousarray(W.reshape(K // 128, 128, N // cw, cw).transpose(2, 1, 0, 3))


def vec_pc(v):
    return np.ascontiguousarray(v.reshape(-1, 128).T)


def np_bf16(a):
    import ml_dtypes
    return np.asarray(a, dtype=np.float32).astype(ml_dtypes.bfloat16)


def make_consts(nc, cfg):
    C = {}
    half = 16
    invf = np.power(np.float32(500000.0), -np.arange(half, dtype=np.float32) * np.float32(2.0 / 32.0)).astype(np.float32)
    C["invf"] = nc.inline_tensor(np.concatenate([invf, invf]).reshape(32, 1).astype(np.float32), "k_invf")
    Pm = np.zeros((128, 128), np.float32)
    for m in range(16):
        Pm[m + 16, m] = -1.0
        Pm[m, m + 16] = 1.0
    C["Pm"] = nc.inline_tensor(Pm, "k_Pm")
    C["identf"] = nc.inline_tensor(np.eye(128, dtype=np.float32), "k_identf")
    C["identb"] = nc.inline_tensor(np_bf16(np.eye(128)), "k_identb")
    NBLK = cfg.NBLK
    E = np.zeros((128, NBLK * 128), np.float32)
    for n in range(NBLK):
        E[n % 16, n * 128:(n + 1) * 128] = 1.0
    C["esel"] = nc.inline_tensor(np_bf16(E), "k_esel")
    j = np.arange(128)[:, None, None]
    r = np.arange(4)[None, :, None]
    i = np.arange(512)[None, None, :]
    C["cmask"] = nc.inline_tensor(np_bf16(np.where(r * 128 + j <= i, 0.0, NEG)), "k_cmask")
    jj = np.arange(128)[:, None]
    ii = np.arange(128)[None, :]
    cur = np.where(jj <= ii, 0.0, NEG)
    prv = np.where(jj >= ii, 0.0, NEG)
    bm = np.stack([np.concatenate([cur, np.full((128, 128), NEG)], 1), np.concatenate([cur, prv], 1)], 1)
    C["bmask"] = nc.inline_tensor(np_bf16(bm), "k_bmask")
    NKT = cfg.S // 128
    pm = np.full((NKT, 16), NEG, np.float32)
    om = np.full((NKT, 16), -1.0e9, np.float32)
    for qi in range(NKT):
        qb = (qi * 128) // cfg.MB
        pm[qi, :qb] = 0.0
        om[qi, qb] = 0.0
    C["pastm"] = nc.inline_tensor(np.ascontiguousarray(np.broadcast_to(pm.reshape(1, NKT * 16), (128, NKT * 16))), "k_pastm")
    C["ownm"] = nc.inline_tensor(np.ascontiguousarray(np.broadcast_to(om.reshape(1, NKT * 16), (128, NKT * 16))), "k_ownm")
    C["tril"] = nc.inline_tensor(np.tril(np.ones((128, 128), np.float32)), "k_tril")
    return C


def stage_rope(nc, cfg, C, pos_d, cs_d):
    S = cfg.S
    W = min(S, 2048)
    with ExitStack() as es:
        cx = Ctx(nc, es)
        P = Prog(nc)
        invf = cx.sb("invf", [32, 1], F32)
        P.dma("sp", invf.t[:, :], C["invf"].ap(), writes=[invf.b])
        for c0 in range(0, S, W):
            S_ = W
            posi = cx.sb("posi%d" % c0, [32, S_], I32)
            P.dma("sp", posi.t[:, :], pos_d.ap()[:, c0:c0 + W], writes=[posi.b])
            ang = cx.sb("ang%d" % c0, [32, S_], F32)
            P.op("dve", f_copy(ang.t[:, :], posi.t[:, :]), reads=[posi.b], writes=[ang.b])
            P.op("dve", f_ts(ang.t[:, :], ang.t[:, :], invf.t[:, 0:1], None, ALU.mult), reads=[ang.b, invf.b], writes=[ang.b])
            ki = cx.sb("ki%d" % c0, [32, S_], I32)
            kf = cx.sb("kf%d" % c0, [32, S_], F32)
            mk = cx.sb("mk%d" % c0, [32, S_], F32)
            for idx, (nm, shift) in enumerate((("cos", 0.25), ("sin", 0.0))):
                y = cx.sb("y_%s%d" % (nm, c0), [32, S_], F32)
                P.op("dve", f_ts(y.t[:, :], ang.t[:, :], float(1.0 / (2.0 * np.pi)), shift, ALU.mult, ALU.add),
                     reads=[ang.b], writes=[y.b])
                P.op("dve", f_copy(ki.t[:, :], y.t[:, :]), reads=[y.b], writes=[ki.b])
                P.op("dve", f_copy(kf.t[:, :], ki.t[:, :]), reads=[ki.b], writes=[kf.b])
                P.op("dve", f_tt(y.t[:, :], y.t[:, :], kf.t[:, :], ALU.subtract), reads=[y.b, kf.b], writes=[y.b])
                P.op("dve", f_ts(mk.t[:, :], y.t[:, :], 0.5, None, ALU.is_gt), reads=[y.b], writes=[mk.b])
                P.op("dve", f_tt(y.t[:, :], y.t[:, :], mk.t[:, :], ALU.subtract), reads=[y.b, mk.b], writes=[y.b])
                P.op("dve", f_ts(mk.t[:, :], y.t[:, :], -0.5, None, ALU.is_lt), reads=[y.b], writes=[mk.b])
                P.op("dve", f_tt(y.t[:, :], y.t[:, :], mk.t[:, :], ALU.add), reads=[y.b, mk.b], writes=[y.b])
                P.op("act", f_act(y.t[:, :], y.t[:, :], AF.Sin, scale=float(2.0 * np.pi * (1.0 - 1e-6))),
                     reads=[y.b], writes=[y.b])
                P.dma("sp", cs_d.ap()[:, idx, c0:c0 + W], y.t[:, :], reads=[y.b])
        P.emit()


def stage_qkv(nc, cfg, C, xin, g_d, wqk_d, wv_d, cs_d, qT, kT, v, ksum_d):
    T, DC, S, NH, NA = cfg.T, cfg.DC, cfg.S, cfg.NH, cfg.NA
    VW = cfg.VW
    NSUB = 2 if S % (2 * T) == 0 else 1
    TS = T * NSUB
    with ExitStack() as es:
        cx = Ctx(nc, es)
        P = Prog(nc)
        K = load_consts(P, cx, cfg, None)
        g = cx.sb("g", [128, DC], F32)
        P.dma("sp", g.t[:, :], g_d.ap(), writes=[g.b])
        Pm = cx.sb("Pm", [128, 128], F32)
        P.dma("sp", Pm.t[:, :], C["Pm"].ap(), writes=[Pm.b])
        csr = ring(cx, "cs", 4, [32, 2, T], F32)
        ksum = cx.sb("ksum", [128, NA, cfg.NBLK], F32)
        h = cx.sb("h", [128, DC, TS], BF16)
        xring = ring(cx, "xc", 4, [128, T], F32)
        sqring = ring(cx, "sq", 3, [128, T], BF16)
        rstd = cx.sb("rstd", [128, T], F32)
        wring = ring(cx, "w", 3, [128, DC, 128], BF16)
        wvring = ring(cx, "wv", 2, [128, DC, VW], BF16)
        qfr = ring(cx, "qf", 4, [128, T], F32)
        tmr = ring(cx, "tm", 4, [32, T], F32)
        qbr = ring(cx, "qb", 3, [128, T], BF16)
        vbr = ring(cx, "vb", 3, [128, VW], BF16)
        psn = cx.ps("psn")
        psr = psring(cx, "ps", 5)
        psp = psring(cx, "pp", 2)
        for st in range(S // TS):
            cst = []
            for sub in range(NSUB):
                emit_norm(P, cfg, K, xin, st * TS + sub * T, g, h, sub * T, xring, sqring, K["ones"], psn, rstd)
                cs = csr.next()
                tok0 = st * TS + sub * T
                P.dma("sp", cs.t[:, :, :], cs_d.ap()[:, :, tok0:tok0 + T], writes=[cs.b])
                cst.append(cs)
            pending = [None]
            for j in range(2 * NH):
                isk = j >= NH
                hd = j - NH if isk else j
                w = wring.next()
                P.dma("pool", w.t[:, :, :], wqk_d.ap()[j], writes=[w.b])
                for sub in range(NSUB):
                    tok0 = st * TS + sub * T
                    ps = psr.next()
                    for k in range(DC):
                        P.op("pe", f_mm(ps.t[:, :], w.t[:, k, :], h.t[:, k, sub * T:(sub + 1) * T], k == 0, k == DC - 1),
                             reads=[w.b, h.b], writes=[ps.b])
                    qf = qfr.next()
                    P.op("act", f_act(qf.t[:, :], ps.t[:, :], AF.Copy), reads=[ps.b], writes=[qf.b])

                    def rot(qf=qf, sub=sub, tok0=tok0, isk=isk, hd=hd):
                        pp = psp.next()
                        P.op("pe", f_mm(pp.t[:, :], Pm.t[:, :], qf.t[:, :]), reads=[Pm.b, qf.b], writes=[pp.b])
                        t1 = tmr.next()
                        t2 = tmr.next()
                        P.op("dve", f_tt(t1.t[:, :], qf.t[0:32, :], cst[sub].t[:, 0, :], ALU.mult),
                             reads=[qf.b, cst[sub].b], writes=[t1.b])
                        P.op("dve", f_tt(t2.t[:, :], pp.t[0:32, :], cst[sub].t[:, 1, :], ALU.mult),
                             reads=[pp.b, cst[sub].b], writes=[t2.b])
                        P.op("dve", f_tt(qf.t[0:32, :], t1.t[:, :], t2.t[:, :], ALU.add),
                             reads=[t1.b, t2.b], writes=[qf.b])
                        qb = qbr.next()
                        P.op("act", f_act(qb.t[:, :], qf.t[:, :], AF.Copy), reads=[qf.b], writes=[qb.b])
                        if isk and hd < NA:
                            b0 = tok0 // cfg.MB
                            nb = T // cfg.MB
                            P.op("dve", lambda e, o_=ksum.t[:, hd, b0:b0 + nb], i_=qf.t[:, :].rearrange("p (b m) -> p b m", m=cfg.MB):
                                 e.tensor_reduce(out=o_, in_=i_, axis=AX.X, op=ALU.add),
                                 reads=[qf.b], writes=[ksum.b])
                        dst = kT if isk else qT
                        P.dma("sp", dst.ap()[hd * 128:(hd + 1) * 128, tok0:tok0 + T], qb.t[:, :], reads=[qb.b])

                    if pending[0] is not None:
                        pending[0]()
                    pending[0] = rot
            if pending[0] is not None:
                pending[0]()
                pending[0] = None
            for jb in range(NH * 128 // VW):
                wv = wvring.next()
                P.dma("pool", wv.t[:, :, :], wv_d.ap()[jb], writes=[wv.b])
                for tb in range(TS // 128):
                    tok0 = st * TS + tb * 128
                    ps = psr.next()
                    for k in range(DC):
                        P.op("pe", f_mm(ps.t[:, 0:VW], h.t[:, k, tb * 128:(tb + 1) * 128], wv.t[:, k, :], k == 0, k == DC - 1),
                             reads=[wv.b, h.b], writes=[ps.b])
                    vb = vbr.next()
                    P.op("act", f_act(vb.t[:, :], ps.t[:, 0:VW], AF.Copy), reads=[ps.b], writes=[vb.b])
                    P.dma("sp", v.ap()[tok0:tok0 + 128, jb * VW:(jb + 1) * VW], vb.t[:, :], reads=[vb.b])
        P.dma("sp", ksum_d.ap(), ksum.t[:, :, :], reads=[ksum.b])
        P.emit()


def stage_moba(nc, cfg, C, qT, kT, v, ksum_d, oT):
    S, NA, NBLK, T = cfg.S, cfg.NA, cfg.NBLK, cfg.T
    NKT = S // 128
    with ExitStack() as es:
        cx = Ctx(nc, es)
        P = Prog(nc)
        K = load_consts(P, cx, cfg, None)
        ones = K["ones"]
        identf = cx.sb("identf", [128, 128], F32)
        P.dma("sp", identf.t[:, :], C["identf"].ap(), writes=[identf.b])
        identb = cx.sb("identb", [128, 128], BF16)
        P.dma("sp", identb.t[:, :], C["identb"].ap(), writes=[identb.b])
        esel = cx.sb("esel", [128, NBLK * 128], BF16)
        P.dma("sp", esel.t[:, :], C["esel"].ap(), writes=[esel.b])
        cmask = cx.sb("cmask", [128, 4, 512], BF16)
        P.dma("sp", cmask.t[:, :, :], C["cmask"].ap(), writes=[cmask.b])
        ksum = cx.sb("ksum", [128, NA, NBLK], F32)
        P.dma("sp", ksum.t[:, :, :], ksum_d.ap(), writes=[ksum.b])
        qr = ring(cx, "q", 2, [128, S], BF16)
        kr = ring(cx, "k", 2, [128, S], BF16)
        vr = ring(cx, "v", 2, [128, NKT, 128], BF16)
        kmr = ring(cx, "km", 2, [128, 16], BF16)
        btr = ring(cx, "bt", 2, [128, S], BF16)
        gmr = ring(cx, "gm", 4, [128, 16], F32)
        mxr = ring(cx, "mx", 4, [128, 8], F32)
        bqr = ring(cx, "bq", 6, [128, 128], F32)
        ptr = ring(cx, "pt", 4, [128, T], BF16)
        rzr = ring(cx, "rz", 2, [128, T], F32)
        obr = ring(cx, "ob", 2, [128, T], BF16)
        psr = psring(cx, "ps", 5)
        por = psring(cx, "po", 1)
        pzr = psring(cx, "pz", 1)
        osr = ring(cx, "os", 2, [128, T], F32)
        zsr = ring(cx, "zs", 2, [128, T], F32)
        psm = cx.ps("psm", nsub=7)
        NB16 = min(NBLK, 16)
        assert NBLK <= 16 and NBLK >= 8
        pastm = cx.sb("pastm", [128, NKT * 16], F32)
        P.dma("sp", pastm.t[:, :], C["pastm"].ap(), writes=[pastm.b])
        ownm = cx.sb("ownm", [128, NKT * 16], F32)
        P.dma("sp", ownm.t[:, :], C["ownm"].ap(), writes=[ownm.b])
        gmar = ring(cx, "gma", 2, [128, NKT * 16], F32)
        mxar = ring(cx, "mxa", 2, [128, NKT, 8], F32)
        bqar = ring(cx, "bqa", 2, [128, NKT, 128], F32)
        for t_ in bqar.tiles:
            P.op("dve", f_memset(t_.t[:, :, :], 0.0), writes=[t_.b])
        v3 = lambda ap: ap.rearrange("p (a b) -> p a b", b=16)
        st = {}

        def load(hd):
            q, k, vv, km, bt = qr.next(), kr.next(), vr.next(), kmr.next(), btr.next()
            P.dma("sp", q.t[:, :], qT.ap()[hd * 128:(hd + 1) * 128, :], writes=[q.b])
            P.dma("sp", k.t[:, :], kT.ap()[hd * 128:(hd + 1) * 128, :], writes=[k.b])
            vsrc = v.ap()[:, hd * 128:(hd + 1) * 128].rearrange("(n p) d -> p n d", p=128)
            for n0 in range(0, NKT, 8):
                P.dma("sp", vv.t[:, n0:n0 + 8, :], vsrc[:, n0:n0 + 8, :], writes=[vv.b])
            P.op("dve", f_memset(km.t[:, :], 0.0), writes=[km.b])
            P.op("dve", f_copy(km.t[:, 0:NBLK], ksum.t[:, hd, :]), reads=[ksum.b], writes=[km.b])
            st[hd] = (q, k, vv, km, bt)

        def gate_a(hd):
            q, k, vv, km, bt = st[hd]
            for qi in range(NKT):
                P.op("pe", f_mm(psm.t[:, qi * 16:(qi + 1) * 16], q.t[:, qi * 128:(qi + 1) * 128], km.t[:, 0:16]),
                     reads=[q.b, km.b], writes=[psm.b])
            gma, mxa, bqa = gmar.next(), mxar.next(), bqar.next()
            P.op("dve", f_tt(gma.t[:, :], psm.t[:, 0:NKT * 16], pastm.t[:, :], ALU.add), reads=[psm.b, pastm.b], writes=[gma.b])
            for qi in range(NKT):
                P.op("dve", lambda e, o_=mxa.t[:, qi, :], i_=gma.t[:, qi * 16:(qi + 1) * 16]: e.max(out=o_, in_=i_),
                     reads=[gma.b], writes=[mxa.b])
            for qi in range(NKT):
                P.op("dve", f_ts(gma.t[:, qi * 16:(qi + 1) * 16], gma.t[:, qi * 16:(qi + 1) * 16], mxa.t[:, qi, 2:3], None, ALU.is_ge),
                     reads=[gma.b, mxa.b], writes=[gma.b])
            P.op("dve", f_ts(bqa.t[:, :, 0:16], v3(gma.t[:, :]), -NEG, NEG, ALU.mult, ALU.add), reads=[gma.b], writes=[bqa.b])
            P.op("dve", f_tt(bqa.t[:, :, 0:16], bqa.t[:, :, 0:16], v3(pastm.t[:, :]), ALU.min), reads=[bqa.b, pastm.b], writes=[bqa.b])
            P.op("dve", f_tt(bqa.t[:, :, 0:16], bqa.t[:, :, 0:16], v3(ownm.t[:, :]), ALU.max), reads=[bqa.b, ownm.b], writes=[bqa.b])
            st[hd] = (q, k, vv, km, bt, bqa)

        def gate_b(hd):
            q, k, vv, km, bt, bqa = st[hd]
            for grp in range(NKT // 4):
                pst = psr.next()
                for j_ in range(4):
                    P.op("pe", f_tr(pst.t[:, j_ * 128:(j_ + 1) * 128], bqa.t[:, grp * 4 + j_, :], identf.t[:, :]),
                         reads=[bqa.b, identf.b], writes=[pst.b])
                P.op("act", f_act(bt.t[:, grp * 512:(grp + 1) * 512], pst.t[:, :], AF.Copy), reads=[pst.b], writes=[bt.b])

        def attention(hd):
            q, k, vv, km, bt, bqa = st.pop(hd)
            for g in range(S // T):
                nkt = 4 * (g + 1)
                po, pz = por.next(), pzr.next()
                qs = q.t[:, g * T:(g + 1) * T]

                def qk(kt):
                    ps = psr.next()
                    n = kt // 2
                    diag = kt >= 4 * g
                    P.op("pe", f_mm(ps.t[:, :], k.t[:, kt * 128:(kt + 1) * 128], qs, True, False),
                         reads=[k.b, q.b], writes=[ps.b])
                    P.op("pe", f_mm(ps.t[:, :], esel.t[:, n * 128:(n + 1) * 128], bt.t[:, g * T:(g + 1) * T], False, not diag),
                         reads=[esel.b, bt.b], writes=[ps.b])
                    if diag:
                        P.op("pe", f_mm(ps.t[:, :], identb.t[:, :], cmask.t[:, kt - 4 * g, :], False, True),
                             reads=[identb.b, cmask.b], writes=[ps.b])
                    return ps

                psq = [qk(i_) for i_ in range(min(2, nkt))]
                for kt in range(nkt):
                    ps = psq.pop(0)
                    pt = ptr.next()
                    P.op("act", f_act(pt.t[:, :], ps.t[:, :], AF.Exp, scale=float(cfg.SCALE)), reads=[ps.b], writes=[pt.b])
                    if kt + 2 < nkt:
                        psq.append(qk(kt + 2))
                    P.op("pe", f_mm(po.t[:, :], vv.t[:, kt, :], pt.t[:, :], kt == 0, kt == nkt - 1),
                         reads=[vv.b, pt.b], writes=[po.b])
                    P.op("pe", f_mm(pz.t[:, :], ones.t[:, :], pt.t[:, :], kt == 0, kt == nkt - 1),
                         reads=[ones.b, pt.b], writes=[pz.b])
                zs, os_ = zsr.next(), osr.next()
                P.op("act", f_act(zs.t[:, :], pz.t[:, :], AF.Copy), reads=[pz.b], writes=[zs.b])
                P.op("act", f_act(os_.t[:, :], po.t[:, :], AF.Copy), reads=[po.b], writes=[os_.b])
                rz = rzr.next()
                P.op("dve", f_recip(rz.t[:, :], zs.t[:, :]), reads=[zs.b], writes=[rz.b])
                ob = obr.next()
                P.op("dve", f_tt(ob.t[:, :], os_.t[:, :], rz.t[:, :], ALU.mult), reads=[os_.b, rz.b], writes=[ob.b])
                P.dma("sp", oT.ap()[hd * 128:(hd + 1) * 128, g * T:(g + 1) * T], ob.t[:, :], reads=[ob.b])

        load(0)
        gate_a(0)
        gate_b(0)
        for hd in range(NA):
            if hd + 1 < NA:
                load(hd + 1)
                gate_a(hd + 1)
            attention(hd)
            if hd + 1 < NA:
                gate_b(hd + 1)
        P.emit()


def stage_dilated(nc, cfg, C, qT, kT, v, oT):
    S, NA, NBG, T = cfg.S, cfg.NA, cfg.NBG, cfg.T
    NKT = S // 128
    with ExitStack() as es:
        cx = Ctx(nc, es)
        P = Prog(nc)
        K = load_consts(P, cx, cfg, None)
        ones = K["ones"]
        identb = cx.sb("identb", [128, 128], BF16)
        P.dma("sp", identb.t[:, :], C["identb"].ap(), writes=[identb.b])
        bmask = cx.sb("bmask", [128, 2, 256], BF16)
        P.dma("sp", bmask.t[:, :, :], C["bmask"].ap(), writes=[bmask.b])
        qr = ring(cx, "q", 4, [128, S], BF16)
        kr = ring(cx, "k", 4, [128, S], BF16)
        vr = ring(cx, "v", 2, [128, NKT, 128], BF16)
        uacc = cx.sb("uacc", [128, S], F32)
        zacc = cx.sb("zacc", [128, S], F32)
        ptr = ring(cx, "pt", 3, [128, 256], BF16)
        obr = ring(cx, "ob", 2, [128, T], BF16)
        psr = psring(cx, "ps", 4)
        pur = psring(cx, "pu", 4)
        for hh in range(NBG):
            for gi, (win, dil) in enumerate(cfg.PATS):
                hq = NA + gi * NBG + hh
                nblk = S // (128 * dil)
                q, k, vv = qr.next(), kr.next(), vr.next()
                P.dma("sp", q.t[:, :], qT.ap()[hq * 128:(hq + 1) * 128, :], writes=[q.b])
                P.dma("sp", k.t[:, :], kT.ap()[hq * 128:(hq + 1) * 128, :], writes=[k.b])
                vsrc = v.ap()[:, hq * 128:(hq + 1) * 128].rearrange("(n p r) d -> p r n d", p=128, r=dil)
                for r in range(dil):
                    for n0 in range(0, nblk, 8):
                        n1 = min(nblk, n0 + 8)
                        P.dma("sp", vv.t[:, r * nblk + n0:r * nblk + n1, :], vsrc[:, r, n0:n1, :], writes=[vv.b])
                if dil > 1:
                    q2, k2 = qr.next(), kr.next()
                    sub = S // dil
                    for r in range(dil):
                        P.op("dve", f_copy(q2.t[:, r * sub:(r + 1) * sub], q.t[:, r:r + (sub - 1) * dil + 1:dil]), reads=[q.b], writes=[q2.b])
                        P.op("act", f_act(k2.t[:, r * sub:(r + 1) * sub], k.t[:, r:r + (sub - 1) * dil + 1:dil], AF.Copy), reads=[k.b], writes=[k2.b])
                    q, k = q2, k2
                tiles = [(r, n) for r in range(dil) for n in range(nblk)]

                def qk(rn, q=q, k=k, nblk=nblk):
                    r, n = rn
                    c0 = (r * nblk + n) * 128
                    p0 = c0 - 128 if n > 0 else c0
                    ps = psr.next()
                    P.op("pe", f_mm(ps.t[:, 0:128], k.t[:, c0:c0 + 128], q.t[:, c0:c0 + 128], True, False), reads=[k.b, q.b], writes=[ps.b])
                    P.op("pe", f_mm(ps.t[:, 128:256], k.t[:, p0:p0 + 128], q.t[:, c0:c0 + 128], False, False), reads=[k.b, q.b], writes=[ps.b])
                    P.op("pe", f_mm(ps.t[:, 0:256], identb.t[:, :], bmask.t[:, 1 if n > 0 else 0, :], False, True),
                         reads=[identb.b, bmask.b], writes=[ps.b])
                    return ps

                ps_next = qk(tiles[0])
                for ti, (r, n) in enumerate(tiles):
                    b0 = n * 128 * dil + r
                    cur = slice(b0, b0 + 127 * dil + 1, dil)
                    ps = ps_next
                    pt = ptr.next()
                    P.op("act", f_act(pt.t[:, :], ps.t[:, 0:256], AF.Exp, scale=float(cfg.SCALE)), reads=[ps.b], writes=[pt.b])
                    if ti + 1 < len(tiles):
                        ps_next = qk(tiles[ti + 1])
                    pu = pur.next()
                    vi = r * nblk + n
                    vp = vi - 1 if n > 0 else vi
                    P.op("pe", f_mm(pu.t[:, 0:128], vv.t[:, vi, :], pt.t[:, 0:128], True, False), reads=[vv.b, pt.b], writes=[pu.b])
                    P.op("pe", f_mm(pu.t[:, 0:128], vv.t[:, vp, :], pt.t[:, 128:256], False, False), reads=[vv.b, pt.b], writes=[pu.b])
                    P.op("pe", f_mm(pu.t[:, 128:256], ones.t[:, :], pt.t[:, 0:128], False, False), reads=[ones.b, pt.b], writes=[pu.b])
                    P.op("pe", f_mm(pu.t[:, 128:256], ones.t[:, :], pt.t[:, 128:256], False, True), reads=[ones.b, pt.b], writes=[pu.b])
                    if gi == 0:
                        P.op("dve", f_copy(uacc.t[:, cur], pu.t[:, 0:128]), reads=[pu.b], writes=[uacc.b])
                        P.op("dve", f_copy(zacc.t[:, cur], pu.t[:, 128:256]), reads=[pu.b], writes=[zacc.b])
                    else:
                        P.op("dve", f_tt(uacc.t[:, cur], uacc.t[:, cur], pu.t[:, 0:128], ALU.add), reads=[pu.b, uacc.b], writes=[uacc.b])
                        P.op("dve", f_tt(zacc.t[:, cur], zacc.t[:, cur], pu.t[:, 128:256], ALU.add), reads=[pu.b, zacc.b], writes=[zacc.b])
            P.op("dve", f_recip(zacc.t[:, :], zacc.t[:, :]), reads=[zacc.b], writes=[zacc.b])
            for g in range(S // T):
                ob = obr.next()
                P.op("dve", f_tt(ob.t[:, :], uacc.t[:, g * T:(g + 1) * T], zacc.t[:, g * T:(g + 1) * T], ALU.mult),
                     reads=[uacc.b, zacc.b], writes=[ob.b])
                P.dma("sp", oT.ap()[(NA + hh) * 128:(NA + hh + 1) * 128, g * T:(g + 1) * T], ob.t[:, :], reads=[ob.b])
        P.emit()


def stage_sg_setup(nc, cfg, C, ws_d, bs_d, lnb_d, wsT_d, bsb_d):
    GG = cfg.GG
    with ExitStack() as es:
        cx = Ctx(nc, es)
        P = Prog(nc)
        identf = cx.sb("identf", [128, 128], F32)
        P.dma("sp", identf.t[:, :], C["identf"].ap(), writes=[identf.b])
        tril = cx.sb("tril", [128, 128], F32)
        P.dma("sp", tril.t[:, :], C["tril"].ap(), writes=[tril.b])
        e0 = cx.sb("e0", [128, 128], BF16)
        P.op("dve", f_memset(e0.t[:, :], 0.0), writes=[e0.b])
        P.op("dve", f_memset(e0.t[0:1, :], 1.0), writes=[e0.b])
        lnb = cx.sb("lnb", [128, cfg.E], BF16)
        P.dma("pool", lnb.t[:, :], lnb_d.ap(), writes=[lnb.b])
        bs0 = cx.sb("bs0", [128, GG * 128], BF16)
        P.op("dve", f_memset(bs0.t[:, :], 0.0), writes=[bs0.b])
        P.dma("pool", bs0.t[0:1, :], bs_d.ap(), writes=[bs0.b])
        wsT = cx.sb("wsT", [128, GG, 128], BF16)
        bsb = cx.sb("bsb", [128, GG, 128], F32)
        wr = ring(cx, "w", 3, [128, 128], F32)
        psr = psring(cx, "ps", 4)
        for g in range(GG):
            w = wr.next()
            P.dma("sp", w.t[:, :], ws_d.ap()[g], writes=[w.b])
            P.op("dve", f_tt(w.t[:, :], w.t[:, :], tril.t[:, :], ALU.mult), reads=[w.b, tril.b], writes=[w.b])
            ps = psr.next()
            P.op("pe", f_tr(ps.t[:, 0:128], w.t[:, :], identf.t[:, :]), reads=[w.b, identf.b], writes=[ps.b])
            P.op("act", f_act(wsT.t[:, g, :], ps.t[:, 0:128], AF.Copy), reads=[ps.b], writes=[wsT.b])
            ps2 = psr.next()
            P.op("pe", f_mm(ps2.t[:, 0:128], lnb.t[:, g * 128:(g + 1) * 128], wsT.t[:, g, :], True, False),
                 reads=[lnb.b, wsT.b], writes=[ps2.b])
            P.op("pe", f_mm(ps2.t[:, 0:128], e0.t[:, :], bs0.t[:, g * 128:(g + 1) * 128], False, True),
                 reads=[e0.b, bs0.b], writes=[ps2.b])
            P.op("act", f_act(bsb.t[:, g, :], ps2.t[:, 0:128], AF.Copy), reads=[ps2.b], writes=[bsb.b])
        P.dma("sp", wsT_d.ap(), wsT.t[:, :, :], reads=[wsT.b])
        P.dma("sp", bsb_d.ap(), bsb.t[:, :, :], reads=[bsb.b])
        P.emit()


def stage_sg_main(nc, cfg, xin, g_d, wu_d, wv_d, bu_d, bv_d, lng_d, wsT_d, bsb_d, mT):
    DC, S, GG, E = cfg.DC, cfg.S, cfg.GG, cfg.E
    T = 512
    VW = 256
    with ExitStack() as es:
        cx = Ctx(nc, es)
        P = Prog(nc)
        K = load_consts(P, cx, cfg, None)
        g = cx.sb("g", [128, DC], F32)
        P.dma("sp", g.t[:, :], g_d.ap(), writes=[g.b])
        bu = cx.sb("bu", [128, GG], F32)
        P.dma("sp", bu.t[:, :], bu_d.ap(), writes=[bu.b])
        lng = cx.sb("lng", [128, GG], F32)
        P.dma("sp", lng.t[:, :], lng_d.ap(), writes=[lng.b])
        e0 = cx.sb("e0", [128, 128], BF16)
        P.op("dve", f_memset(e0.t[:, :], 0.0), writes=[e0.b])
        P.op("dve", f_memset(e0.t[0:1, :], 1.0), writes=[e0.b])
        bv0 = cx.sb("bv0", [128, E], BF16)
        P.op("dve", f_memset(bv0.t[:, :], 0.0), writes=[bv0.b])
        P.dma("pool", bv0.t[0:1, :], bv_d.ap(), writes=[bv0.b])
        wsT = cx.sb("wsT", [128, GG, 128], BF16)
        P.dma("sp", wsT.t[:, :, :], wsT_d.ap(), writes=[wsT.b])
        bsb = cx.sb("bsb", [128, GG, 128], F32)
        P.dma("sp", bsb.t[:, :, :], bsb_d.ap(), writes=[bsb.b])
        h = cx.sb("h", [128, DC, T], BF16)
        uT = cx.sb("uT", [128, GG, T], BF16)
        xring = ring(cx, "xc", 3, [128, T], F32)
        sqring = ring(cx, "sq", 2, [128, T], BF16)
        rstd = cx.sb("rstd", [128, T], F32)
        wring = ring(cx, "w", 2, [128, DC, 128], BF16)
        wvring = ring(cx, "wv", 2, [128, DC, VW], BF16)
        vbufs = [cx.sb("vbuf%d" % i, [128, E], BF16) for i in range(T // 128)]
        vn = cx.sb("vn", [128, E], BF16)
        nst = max(1, E // 512)
        stats = cx.sb("stats", [128, nst, 6], F32)
        mv = cx.sb("mv", [128, 2], F32)
        fbr = ring(cx, "fb", 3, [128, 128], F32)
        mbr = ring(cx, "mb", 1, [128, GG, 128], BF16)
        psn = cx.ps("psn")
        psr = psring(cx, "ps", 5)
        psa = psring(cx, "pa", 2)
        mv_ = dview(mT)
        for tt in range(S // T):
            tok0 = tt * T
            emit_norm(P, cfg, K, xin, tok0, g, h, 0, xring, sqring, K["ones"], psn, rstd, T=T)
            for j in range(GG):
                w = wring.next()
                P.dma("pool", w.t[:, :, :], wu_d.ap()[j], writes=[w.b])
                ps = psr.next()
                for k in range(DC):
                    P.op("pe", f_mm(ps.t[:, 0:T], w.t[:, k, :], h.t[:, k, :], k == 0, k == DC - 1), reads=[w.b, h.b], writes=[ps.b])
                P.op("act", f_act(uT.t[:, j, :], ps.t[:, 0:T], AF.Gelu, bias=bu.t[:, j:j + 1]), reads=[ps.b, bu.b], writes=[uT.b])
            for jb in range(E // VW):
                wv = wvring.next()
                P.dma("pool", wv.t[:, :, :], wv_d.ap()[jb], writes=[wv.b])
                for tb in range(T // 128):
                    ps = psr.next()
                    for k in range(DC):
                        P.op("pe", f_mm(ps.t[:, 0:VW], h.t[:, k, tb * 128:(tb + 1) * 128], wv.t[:, k, :], k == 0, False),
                             reads=[wv.b, h.b], writes=[ps.b])
                    P.op("pe", f_mm(ps.t[:, 0:VW], e0.t[:, :], bv0.t[:, jb * VW:(jb + 1) * VW], False, True),
                         reads=[e0.b, bv0.b], writes=[ps.b])
                    P.op("act", f_act(vbufs[tb].t[:, jb * VW:(jb + 1) * VW], ps.t[:, 0:VW], AF.Gelu), reads=[ps.b], writes=[vbufs[tb].b])
            for tb in range(T // 128):
                vb = vbufs[tb]
                for i in range(nst):
                    w_ = min(512, E)
                    P.op("dve", lambda e, o_=stats.t[:, i, :], i_=vb.t[:, i * w_:(i + 1) * w_]: e.bn_stats(out=o_, in_=i_),
                         reads=[vb.b], writes=[stats.b])
                P.op("dve", lambda e, o_=mv.t[:, :], i_=stats.t[:, :, :]: e.bn_aggr(out=o_, in_=i_), reads=[stats.b], writes=[mv.b])
                P.op("act", f_act(mv.t[:, 1:2], mv.t[:, 1:2], AF.Sqrt, bias=K["eps"].t[:, 0:1]), reads=[mv.b, K["eps"].b], writes=[mv.b])
                P.op("dve", f_recip(mv.t[:, 1:2], mv.t[:, 1:2]), reads=[mv.b], writes=[mv.b])
                P.op("dve", f_ts(vn.t[:, :], vb.t[:, :], mv.t[:, 0:1], mv.t[:, 1:2], ALU.subtract, ALU.mult),
                     reads=[vb.b, mv.b], writes=[vn.b])
                mb = mbr.next()
                for gb in range(0, GG, 4):
                    pa = psa.next()
                    gis = list(range(gb, min(GG, gb + 4)))
                    for gi in gis:
                        c0 = (gi % 4) * 128
                        P.op("pe", f_mm(pa.t[:, c0:c0 + 128], vn.t[:, gi * 128:(gi + 1) * 128], wsT.t[:, gi, :], True, True),
                             reads=[vn.b, wsT.b], writes=[pa.b])
                    for gi in gis:
                        c0 = (gi % 4) * 128
                        fb = fbr.next()
                        P.op("dve", f_stt(fb.t[:, :], pa.t[:, c0:c0 + 128], lng.t[:, gi:gi + 1], bsb.t[:, gi, :], ALU.mult, ALU.add),
                             reads=[pa.b, lng.b, bsb.b], writes=[fb.b])
                        P.op("dve", f_tt(mb.t[:, gi, :], fb.t[:, :], uT.t[:, gi, tb * 128:(tb + 1) * 128], ALU.mult),
                             reads=[fb.b, uT.b], writes=[mb.b])
                for g0 in range(0, GG, 8):
                    g1 = min(GG, g0 + 8)
                    P.dma("sp", mv_[:, g0:g1, tok0 + tb * 128:tok0 + (tb + 1) * 128], mb.t[:, g0:g1, :], reads=[mb.b])
        P.emit()


def stage_final_norm(nc, cfg, xin, g_d, out):
    DC, S, T = cfg.DC, cfg.S, cfg.T
    CG = 8
    NG = (DC + CG - 1) // CG
    with ExitStack() as es:
        cx = Ctx(nc, es)
        P = Prog(nc)
        K = load_consts(P, cx, cfg, None)
        g = cx.sb("g", [128, DC], F32)
        P.dma("sp", g.t[:, :], g_d.ap(), writes=[g.b])
        xts = [cx.sb("x%d" % i, [128, DC, T], F32, nsub=NG) for i in range(2)]
        sqring = ring(cx, "sq", 3, [128, T], BF16)
        rsr = ring(cx, "rstd", 2, [128, T], F32)
        pnr = psring(cx, "psn", 2)
        xv = dview(xin)
        ov = dview(out)
        for tt in range(S // T):
            x = xts[tt % 2]
            tok0 = tt * T
            for gi in range(NG):
                c0, c1 = gi * CG, min(DC, (gi + 1) * CG)
                P.dma("sp", x.t[:, c0:c1, :], xv[:, c0:c1, tok0:tok0 + T], writes=[x.sub[gi]])
            psn, rstd = pnr.next(), rsr.next()
            for c in range(DC):
                sq = sqring.next()
                P.op("act", f_act(sq.t[:, :], x.t[:, c, :], AF.Square), reads=[x.sub[c // CG]], writes=[sq.b])
                P.op("pe", f_mm(psn.t[:, :], K["ones"].t[:, :], sq.t[:, :], c == 0, c == DC - 1),
                     reads=[K["ones"].b, sq.b], writes=[psn.b])
            P.op("act", f_act(rstd.t[:, :], psn.t[:, :], AF.Sqrt, scale=1.0 / cfg.D, bias=K["eps"].t[:, 0:1]),
                 reads=[psn.b, K["eps"].b], writes=[rstd.b])
            P.op("dve", f_recip(rstd.t[:, :], rstd.t[:, :]), reads=[rstd.b], writes=[rstd.b])
            for c in range(DC):
                P.op("dve", f_stt(x.t[:, c, :], x.t[:, c, :], g.t[:, c:c + 1], rstd.t[:, :], ALU.mult, ALU.mult),
                     reads=[x.sub[c // CG], g.b, rstd.b], writes=[x.sub[c // CG]])
            for gi in range(NG):
                c0, c1 = gi * CG, min(DC, (gi + 1) * CG)
                P.dma("sp", ov[:, c0:c1, tok0:tok0 + T], x.t[:, c0:c1, :], reads=[x.sub[gi]])
        P.emit()


def build_program(cfg):
    nc = bass.Bass("TRN2", target_bir_lowering=False)
    D, S, DC, FC, NH, NA, GG, E, DFF = cfg.D, cfg.S, cfg.DC, cfg.FC, cfg.NH, cfg.NA, cfg.GG, cfg.E, cfg.DFF
    C = make_consts(nc, cfg)

    def di(name, shape, dt=F32):
        return nc.dram_tensor(name, list(shape), dt, kind="ExternalInput")

    def ds(name, shape, dt):
        return nc.dram_tensor(name, list(shape), dt)

    xT = di("xT", [D, S])
    pos = di("pos", [32, S], I32)
    g_attn = di("g_attn", [128, DC])
    wqk = di("wqk", [2 * NH, 128, DC, 128])
    wv = di("wv", [NH * 128 // cfg.VW, 128, DC, cfg.VW])
    wo = di("wo", [DC, 128, DC, 128])
    ffn = []
    for l in range(2):
        ffn.append(dict(g=di("g_ffn%d" % l, [128, DC]), wup=di("wup%d" % l, [2 * FC, 128, DC, 128]),
                        cw=di("cw%d" % l, [128, 3, 2 * FC]), cb=di("cb%d" % l, [128, 2 * FC]),
                        wdn=di("wdn%d" % l, [DC, 128, FC, 128])))
    g_sg = di("g_sg", [128, DC])
    sg_wu = di("sg_wu", [GG, 128, DC, 128])
    sg_wv = di("sg_wv", [E // 256, 128, DC, 256])
    sg_bu = di("sg_bu", [128, GG])
    sg_bv = di("sg_bv", [1, E])
    sg_lng = di("sg_lng", [128, GG])
    sg_lnb = di("sg_lnb", [128, E])
    sg_ws = di("sg_ws", [GG, 128, 128])
    sg_bs = di("sg_bs", [1, GG * 128])
    sg_wo = di("sg_wo", [DC, 128, GG, 128])
    g_fin = di("g_fin", [128, DC])
    outT = nc.dram_tensor("outT", [D, S], F32, kind="ExternalOutput")

    qT = ds("qT", [NH * 128, S], BF16)
    kT = ds("kT", [NH * 128, S], BF16)
    v = ds("v", [S, NH * 128], BF16)
    ksum = ds("ksum", [128, NA, cfg.NBLK], F32)
    cs = ds("cs", [32, 2, S], F32)
    oT = ds("oT", [D, S], BF16)
    gT = ds("gT", [DFF, S], BF16)
    xa = ds("xa", [D, S], F32)
    xb = ds("xb", [D, S], F32)
    wdnb = [ds("wdnb%d" % l, [DC, 128, FC, 128], BF16) for l in range(2)]
    wsT = ds("wsT", [128, GG, 128], BF16)
    bsb = ds("bsb", [128, GG, 128], F32)

    import os
    sel = os.environ.get("MK_STAGES")
    sel = set(int(t) for t in sel.split(",")) if sel else set(range(1, 13))
    stages = [
        lambda: (stage_rope(nc, cfg, C, pos, cs), stage_qkv(nc, cfg, C, xT, g_attn, wqk, wv, cs, qT, kT, v, ksum)),
        lambda: stage_moba(nc, cfg, C, qT, kT, v, ksum, oT),
        lambda: stage_dilated(nc, cfg, C, qT, kT, v, oT),
        lambda: stage_linear_res(nc, cfg, DC, oT, wo, xT, xa),
        lambda: stage_ffn_up(nc, cfg, xa, ffn[0]["g"], ffn[0]["wup"], ffn[0]["cw"], ffn[0]["cb"], gT, precast=(ffn[0]["wdn"], wdnb[0])),
        lambda: stage_linear_res(nc, cfg, FC, gT, wdnb[0], xa, xb),
        lambda: stage_sg_setup(nc, cfg, C, sg_ws, sg_bs, sg_lnb, wsT, bsb),
        lambda: stage_sg_main(nc, cfg, xb, g_sg, sg_wu, sg_wv, sg_bu, sg_bv, sg_lng, wsT, bsb, oT),
        lambda: stage_linear_res(nc, cfg, GG, oT, sg_wo, xb, xa),
        lambda: stage_ffn_up(nc, cfg, xa, ffn[1]["g"], ffn[1]["wup"], ffn[1]["cw"], ffn[1]["cb"], gT, precast=(ffn[1]["wdn"], wdnb[1])),
        lambda: stage_linear_res(nc, cfg, FC, gT, wdnb[1], xa, xb),
        lambda: stage_final_norm(nc, cfg, xb, g_fin, outT),
    ]
    for i, st in enumerate(stages):
        if i + 1 in sel:
            st()
    return nc


def host_prep(cfg, p):
    NH, FC, GG, E = cfg.NH, cfg.FC, cfg.GG, cfg.E
    f = lambda a: np.asarray(a, dtype=np.float32)
    w_in = f(p["attn_w_in"])[0]
    m = {}
    m["g_attn"] = vec_pc(f(p["attn_norm"])[0])
    m["wqk"] = tile_w(w_in[:, :2 * NH * 128])
    m["wv"] = tile_w(w_in[:, 2 * NH * 128:], cfg.VW)
    m["wo"] = tile_w(f(p["attn_w_out"])[0])
    for l in range(2):
        m["g_ffn%d" % l] = vec_pc(f(p["ffn_norm"])[l])
        m["wup%d" % l] = tile_w(f(p["ffn_w_up"])[l])
        m["cw%d" % l] = np.ascontiguousarray(f(p["ffn_conv_w"])[l].reshape(3, 2 * FC, 128).transpose(2, 0, 1))
        m["cb%d" % l] = vec_pc(f(p["ffn_conv_b"])[l])
        m["wdn%d" % l] = tile_w(f(p["ffn_w_down"])[l])
    sw = f(p["sg_w_in"])[0]
    sb = f(p["sg_b_in"])[0]
    m["g_sg"] = vec_pc(f(p["sg_norm"])[0])
    m["sg_wu"] = tile_w(sw[:, :E])
    m["sg_wv"] = tile_w(sw[:, E:], 256)
    m["sg_bu"] = vec_pc(sb[:E])
    m["sg_bv"] = np.ascontiguousarray(sb[E:].reshape(1, E))
    m["sg_lng"] = vec_pc(f(p["sg_v_gain"])[0])
    m["sg_lnb"] = np.ascontiguousarray(np.broadcast_to(f(p["sg_v_bias"])[0], (128, E)))
    m["sg_ws"] = np.ascontiguousarray(f(p["sg_w_s"])[0])
    m["sg_bs"] = np.ascontiguousarray(f(p["sg_b_s"])[0].reshape(1, GG * 128))
    m["sg_wo"] = tile_w(f(p["sg_w_out"])[0])
    m["g_fin"] = vec_pc(f(p["final_norm"]))
    return m


def run_module(cfg, inputs):
    x = np.asarray(inputs["x"], dtype=np.float32)
    positions = np.asarray(inputs["positions"]).astype(np.int32)
    B = x.shape[0]
    import time as _t
    t0 = _t.time()
    shared = host_prep(cfg, inputs)
    t1 = _t.time()
    nc = build_program(cfg)
    print("[mk] host_prep %.1fs build %.1fs" % (t1 - t0, _t.time() - t1), flush=True)
    in_maps = []
    for b in range(B):
        m = dict(shared)
        m["xT"] = np.ascontiguousarray(x[b].T)
        m["pos"] = np.ascontiguousarray(np.broadcast_to(positions[b], (32, cfg.S)))
        in_maps.append(m)
    t2 = _t.time()
    res = run_bass_kernel_spmd(nc, in_maps, core_ids=list(range(B)))
    print("[mk] launch %.1fs" % (_t.time() - t2), flush=True)
    return np.stack([np.ascontiguousarray(res.results[b]["outT"].T) for b in range(B)], 0)


def kernel(**inputs):
    cfg = Cfg()
    return run_module(cfg, inputs)
```
